# Optimizing a Trainium2 kernel written in Bass

```python
import jax, jax.numpy as jnp
from jax import lax
import numpy as np

D_MODEL = 1024
BATCH = 2
SEQ = 8192
DEPTH = 1

MLA_HEADS = 8
MLA_Q_LORA = 384
MLA_KV_LORA = 256
MLA_NOPE = 64
MLA_ROPE = 32
MLA_V = 64
MLA_W = MLA_HEADS * MLA_V
ROPE_THETA = 10000.0
Q_BLOCK = 128
MLSTM_HEADS = 4
MLSTM_DH = 128
MLSTM_W = MLSTM_HEADS * MLSTM_DH
MLSTM_CHUNK = 64
CONV_WIDTH = 5
D_FF = 4 * D_MODEL
NORM_EPS = 1e-6
IN_SPLITS = (MLA_Q_LORA, MLA_KV_LORA, MLA_ROPE, MLSTM_W, MLSTM_W, MLSTM_W, MLSTM_W, 4 * MLSTM_HEADS, D_MODEL, D_MODEL)
IN_COLS = MLA_Q_LORA + MLA_KV_LORA + MLA_ROPE + 4 * MLSTM_W + 4 * MLSTM_HEADS + 2 * D_MODEL

kernel_name = "hybrid_mla_mlstm_gated_block"


def rmsnorm(x, g):
    xf = x.astype(jnp.float32)
    y = xf * lax.rsqrt(jnp.mean(xf * xf, axis=-1, keepdims=True) + NORM_EPS)
    return (y * g.astype(jnp.float32)).astype(x.dtype)


def split_cols(h):
    outs, off = [], 0
    for w in IN_SPLITS:
        outs.append(h[..., off:off + w])
        off += w
    return outs


def rope_tables(positions):
    inv_freq = ROPE_THETA ** (-jnp.arange(0, MLA_ROPE, 2, dtype=jnp.float32) / MLA_ROPE)
    ang = positions.astype(jnp.float32)[..., None] * inv_freq
    return jnp.cos(ang), jnp.sin(ang)


def apply_rope(x, cos, sin):
    xf = x.astype(jnp.float32)
    half = xf.shape[-1] // 2
    x1, x2 = xf[..., :half], xf[..., half:]
    return jnp.concatenate([x1 * cos - x2 * sin, x2 * cos + x1 * sin], axis=-1).astype(x.dtype)


def mla_attention(q_nope, q_rope, k_nope, k_rope, v):
    B, S, H, _ = q_nope.shape
    nb = S // Q_BLOCK
    scale = (MLA_NOPE + MLA_ROPE) ** -0.5
    qn = q_nope.reshape(B, nb, Q_BLOCK, H, MLA_NOPE).swapaxes(0, 1)
    qr = q_rope.reshape(B, nb, Q_BLOCK, H, MLA_ROPE).swapaxes(0, 1)
    kn = k_nope.astype(jnp.float32)
    kr = k_rope.astype(jnp.float32)
    vf = v.astype(jnp.float32)

    def block(args):
        qn_b, qr_b = args
        s = (jnp.einsum('bqhd,bkhd->bhqk', qn_b.astype(jnp.float32), kn)
             + jnp.einsum('bqhd,bkd->bhqk', qr_b.astype(jnp.float32), kr))
        p = jax.nn.softmax(s * scale, axis=-1)
        return jnp.einsum('bhqk,bkhd->bqhd', p, vf)

    o = lax.map(block, (qn, qr))
    return o.swapaxes(0, 1).reshape(B, S, H * MLA_V).astype(v.dtype)


def mla_branch(c_q, c_kv, k_rope_raw, cos, sin, q_norm_g, w_uq, kv_norm_g, w_ukv):
    B, S, _ = c_q.shape
    q = (rmsnorm(c_q, q_norm_g) @ w_uq).reshape(B, S, MLA_HEADS, MLA_NOPE + MLA_ROPE)
    q_nope = q[..., :MLA_NOPE]
    q_rope = apply_rope(q[..., MLA_NOPE:], cos[:, :, None, :], sin[:, :, None, :])
    k_rope = apply_rope(k_rope_raw, cos, sin)
    kv = (rmsnorm(c_kv, kv_norm_g) @ w_ukv).reshape(B, S, MLA_HEADS, MLA_NOPE + MLA_V)
    k_nope, v = kv[..., :MLA_NOPE], kv[..., MLA_NOPE:]
    return mla_attention(q_nope, q_rope, k_nope, k_rope, v)


def mlstm_chunkwise(q, k, v, i_pre, f_pre):
    B, H, S, DK = q.shape
    DV = v.shape[-1]
    L = MLSTM_CHUNK
    NC = S // L
    qc = q.astype(jnp.float32).reshape(B, H, NC, L, DK)
    kc = k.astype(jnp.float32).reshape(B, H, NC, L, DK) * (DK ** -0.5)
    vc = v.astype(jnp.float32).reshape(B, H, NC, L, DV)
    log_f = jax.nn.log_sigmoid(f_pre.astype(jnp.float32)).reshape(B, H, NC, L)
    log_i = i_pre.astype(jnp.float32).reshape(B, H, NC, L)
    b = jnp.cumsum(log_f, axis=-1)
    b_last = b[..., -1]
    w_end = b_last[..., None] - b + log_i

    def step(carry, inp):
        C, n, m = carry
        k_c, v_c, w_c, bl = inp
        m_new = jnp.maximum(bl + m, jnp.max(w_c, axis=-1))
        decay = jnp.exp(bl + m - m_new)
        w = jnp.exp(w_c - m_new[..., None])
        C_new = decay[..., None, None] * C + jnp.einsum('bhl,bhlk,bhlv->bhkv', w, k_c, v_c)
        n_new = decay[..., None] * n + jnp.einsum('bhl,bhlk->bhk', w, k_c)
        return (C_new, n_new, m_new), (C, n, m)

    init = (jnp.zeros((B, H, DK, DV), jnp.float32), jnp.zeros((B, H, DK), jnp.float32),
            jnp.zeros((B, H), jnp.float32))
    xs = (jnp.moveaxis(kc, 2, 0), jnp.moveaxis(vc, 2, 0), jnp.moveaxis(w_end, 2, 0), jnp.moveaxis(b_last, 2, 0))
    _, (C_prev, n_prev, m_prev) = lax.scan(step, init, xs)
    C_prev = jnp.moveaxis(C_prev, 0, 2)
    n_prev = jnp.moveaxis(n_prev, 0, 2)
    m_prev = jnp.moveaxis(m_prev, 0, 2)

    mask = jnp.tril(jnp.ones((L, L), dtype=bool))
    D = jnp.where(mask, b[..., :, None] - b[..., None, :] + log_i[..., None, :], -jnp.inf)
    inter_log = b + m_prev[..., None]
    m_t = jnp.maximum(inter_log, jnp.max(D, axis=-1))
    inter_w = jnp.exp(inter_log - m_t)
    qk = jnp.einsum('bhctd,bhcsd->bhcts', qc, kc) * jnp.exp(D - m_t[..., None])
    num = inter_w[..., None] * jnp.einsum('bhctk,bhckv->bhctv', qc, C_prev) + jnp.einsum('bhcts,bhcsv->bhctv', qk, vc)
    den = inter_w * jnp.einsum('bhctk,bhck->bhct', qc, n_prev) + jnp.sum(qk, axis=-1)
    h = num / jnp.maximum(jnp.abs(den), jnp.exp(-m_t))[..., None]
    return h.reshape(B, H, S, DV)


def mlstm_branch(mq, mk, mv, mo, mgates, conv_w, conv_b, ig_b, fg_b, out_norm_g):
    B, S, _ = mq.shape
    qk = jnp.concatenate([mq, mk], axis=-1)
    qk = lax.conv_general_dilated(qk, conv_w, window_strides=(1,),
                                  padding=[(CONV_WIDTH // 2, CONV_WIDTH // 2)],
                                  dimension_numbers=('NWC', 'WIO', 'NWC'),
                                  feature_group_count=2 * MLSTM_W) + conv_b
    qk = jax.nn.silu(qk)

    def heads(t):
        return t.reshape(B, S, MLSTM_HEADS, MLSTM_DH).transpose(0, 2, 1, 3)

    q, k, v = heads(qk[..., :MLSTM_W]), heads(qk[..., MLSTM_W:]), heads(mv)
    g = mgates.reshape(B, S, 2, 2, MLSTM_HEADS)
    i_pre = (g[:, :, :, 0, :] + ig_b).transpose(2, 0, 3, 1)
    f_pre = (g[:, :, :, 1, :] + fg_b).transpose(2, 0, 3, 1)
    h_fwd = mlstm_chunkwise(q, k, v, i_pre[0], f_pre[0])
    h_bwd = jnp.flip(mlstm_chunkwise(jnp.flip(q, 2), jnp.flip(k, 2), jnp.flip(v, 2),
                                     jnp.flip(i_pre[1], -1), jnp.flip(f_pre[1], -1)), 2)
    h = h_fwd + h_bwd
    h = h * lax.rsqrt(jnp.mean(h * h, axis=-1, keepdims=True) + NORM_EPS)
    h = h.transpose(0, 2, 1, 3).reshape(B, S, MLSTM_W) * out_norm_g.astype(jnp.float32)
    return (h * jax.nn.sigmoid(mo.astype(jnp.float32))).astype(mq.dtype)


def setup_inputs(seed: int = 0) -> dict:
    key = jax.random.key(seed)
    ks = jax.random.split(key, 24)

    def nrm(k, shape, fan_in):
        return jax.random.normal(k, shape, jnp.float32) * (fan_in ** -0.5)

    def gain(k, shape):
        return 1.0 + 0.02 * jax.random.normal(k, shape, jnp.float32)

    x = jax.random.normal(ks[0], (BATCH, SEQ, D_MODEL), jnp.float32)
    offset = jax.random.randint(ks[1], (BATCH, 1), 0, 1024, dtype=jnp.int32)
    positions = jnp.arange(SEQ, dtype=jnp.int32)[None, :] + offset
    fgate_base = jnp.linspace(3.0, 6.0, MLSTM_HEADS, dtype=jnp.float32)
    return {
        "x": x,
        "positions": positions,
        "norm_mix_g": gain(ks[2], (DEPTH, D_MODEL)),
        "w_in": nrm(ks[3], (DEPTH, D_MODEL, IN_COLS), D_MODEL),
        "mla_q_norm_g": gain(ks[4], (DEPTH, MLA_Q_LORA)),
        "mla_w_uq": nrm(ks[5], (DEPTH, MLA_Q_LORA, MLA_HEADS * (MLA_NOPE + MLA_ROPE)), MLA_Q_LORA),
        "mla_kv_norm_g": gain(ks[6], (DEPTH, MLA_KV_LORA)),
        "mla_w_ukv": nrm(ks[7], (DEPTH, MLA_KV_LORA, MLA_HEADS * (MLA_NOPE + MLA_V)), MLA_KV_LORA),
        "mlstm_conv_w": nrm(ks[8], (DEPTH, CONV_WIDTH, 1, 2 * MLSTM_W), CONV_WIDTH),
        "mlstm_conv_b": 0.02 * jax.random.normal(ks[9], (DEPTH, 2 * MLSTM_W), jnp.float32),
        "mlstm_igate_b": 0.1 * jax.random.normal(ks[10], (DEPTH, 2, MLSTM_HEADS), jnp.float32),
        "mlstm_fgate_b": fgate_base + 0.1 * jax.random.normal(ks[11], (DEPTH, 2, MLSTM_HEADS), jnp.float32),
        "mlstm_out_norm_g": gain(ks[12], (DEPTH, MLSTM_W)),
        "w_branch_mla": nrm(ks[13], (DEPTH, MLA_W, D_MODEL), MLA_W),
        "w_branch_mlstm": nrm(ks[14], (DEPTH, MLSTM_W, D_MODEL), MLSTM_W),
        "w_out": nrm(ks[15], (DEPTH, D_MODEL, D_MODEL), D_MODEL),
        "norm_mlp_g": gain(ks[16], (DEPTH, D_MODEL)),
        "w_mlp_up": nrm(ks[17], (DEPTH, D_MODEL, D_FF), D_MODEL),
        "w_mlp_down": nrm(ks[18], (DEPTH, D_FF, D_MODEL), D_FF),
        "norm_final_g": gain(ks[19], (D_MODEL,)),
    }


def reference(x, positions, norm_mix_g, w_in, mla_q_norm_g, mla_w_uq, mla_kv_norm_g, mla_w_ukv,
              mlstm_conv_w, mlstm_conv_b, mlstm_igate_b, mlstm_fgate_b, mlstm_out_norm_g,
              w_branch_mla, w_branch_mlstm, w_out, norm_mlp_g, w_mlp_up, w_mlp_down, norm_final_g):
    cos, sin = rope_tables(positions)
    for l in range(DEPTH):
        h = rmsnorm(x, norm_mix_g[l])
        c_q, c_kv, k_rope, mq, mk, mv, mo, mgates, gate_a, gate_b = split_cols(h @ w_in[l])
        y_attn = mla_branch(c_q, c_kv, k_rope, cos, sin, mla_q_norm_g[l], mla_w_uq[l],
                            mla_kv_norm_g[l], mla_w_ukv[l])
        y_mlstm = mlstm_branch(mq, mk, mv, mo, mgates, mlstm_conv_w[l], mlstm_conv_b[l],
                               mlstm_igate_b[l], mlstm_fgate_b[l], mlstm_out_norm_g[l])
        merged = (jax.nn.sigmoid(gate_a) * (y_attn @ w_branch_mla[l])
                  + jax.nn.sigmoid(gate_b) * (y_mlstm @ w_branch_mlstm[l]))
        x = x + merged @ w_out[l]
        u = rmsnorm(x, norm_mlp_g[l]) @ w_mlp_up[l]
        x = x + jnp.square(jax.nn.relu(u)) @ w_mlp_down[l]
    return rmsnorm(x, norm_final_g)
```

```python
import math
from contextlib import ExitStack
import numpy as np
import concourse.bass as bass
import concourse.mybir as mybir
from concourse.bass_utils import run_bass_kernel_spmd

F32 = mybir.dt.float32
BF16 = mybir.dt.bfloat16
I32 = mybir.dt.int32
AF = mybir.ActivationFunctionType
ALU = mybir.AluOpType

SEM_LIMIT = 20000
N_DSEM = 24
SB_LO = 16512
SB_HI = 229376
NOWN = 2048
NOTH = 6144
EPS = 1e-6
LNSC = -0.5 * math.log(128.0)
BIG = 30000.0


class Op:
    __slots__ = ("eng", "fn", "deps", "signal", "is_dma", "idx", "sem", "val", "dslot")

    def __init__(self, eng, fn, is_dma):
        self.eng = eng
        self.fn = fn
        self.deps = []
        self.signal = False
        self.is_dma = is_dma
        self.sem = None
        self.val = None
        self.dslot = None


class _Rec:
    def __init__(self):
        self.call = None

    def __getattr__(self, name):
        def f(*a, **k):
            self.call = (name, a, k)
            return self
        return f


class Sched:
    def __init__(self):
        self.ops = []
        self.last_w = {}
        self.readers = {}
        self.n_dma = 0
        self.dslot_last = {}
        self.fence_op = None

    def add(self, eng, fn, reads=(), writes=(), dma=False):
        rec = _Rec()
        fn(rec)
        call = rec.call
        op = Op(eng, call, dma)
        psk = [k for k in reads if isinstance(k, tuple) and k[0] == "ps"]
        if psk:
            reads = [k for k in reads if k not in psk]
            writes = list(writes) + psk
        deps = set()
        for k in reads:
            w = self.last_w.get(k)
            if w is not None:
                deps.add(w)
        for k in writes:
            w = self.last_w.get(k)
            if w is not None:
                deps.add(w)
            for r in self.readers.get(k, ()):
                deps.add(r)
        if self.fence_op is not None:
            deps.add(self.fence_op)
        if dma:
            slot = (eng, self.n_dma % N_DSEM)
            self.n_dma += 1
            prev = self.dslot_last.get(slot)
            if prev is not None:
                deps.add(prev)
            self.dslot_last[slot] = op
            op.dslot = slot
        for d in deps:
            if d is op:
                continue
            if (not d.is_dma) and d.eng == "pe" and eng == "pe" and not dma:
                continue
            d.signal = True
            op.deps.append(d)
        for k in reads:
            self.readers.setdefault(k, []).append(op)
        for k in writes:
            self.last_w[k] = op
            self.readers[k] = []
        self.ops.append(op)
        return op

    def barrier(self, fn):
        keys = set(self.last_w.keys()) | set(self.readers.keys())
        self.fence_op = None
        op = self.add("dve", fn, writes=list(keys))
        for o in self.dslot_last.values():
            if o is not op and o not in op.deps:
                o.signal = True
                op.deps.append(o)
        self.fence_op = op
        return op

    def emit(self, nc, stack):
        engs = ["pe", "act", "dve", "pool", "sp"]
        counts = {e: 0 for e in engs}
        dcount = {}
        for op in self.ops:
            if op.is_dma:
                c = dcount.get(op.dslot, 0) + 1
                dcount[op.dslot] = c
                op.sem = ("d", op.dslot)
                op.val = 16 * c
            elif op.signal:
                c = counts[op.eng]
                counts[op.eng] = c + 1
                op.sem = (op.eng, c // SEM_LIMIT)
                op.val = c % SEM_LIMIT + 1
        sems = {}
        for op in self.ops:
            if op.sem is not None and op.sem not in sems:
                sems[op.sem] = stack.enter_context(nc.semaphore("s_%d" % len(sems)))
        block = stack.enter_context(nc.Block())
        ops = self.ops

        def run(engname, e):
            waited = {}
            for op in ops:
                if op.eng != engname:
                    continue
                for d in op.deps:
                    key = d.sem
                    if waited.get(key, 0) >= d.val:
                        continue
                    e.wait_ge(sems[key], d.val)
                    waited[key] = d.val
                name, a_, k_ = op.fn
                ins = getattr(e, name)(*a_, **k_)
                if op.is_dma:
                    ins.then_inc(sems[op.sem], 16)
                elif op.signal:
                    ins.then_inc(sems[op.sem], 1)

        @block.tensor
        def _(e):
            run("pe", e)

        @block.scalar
        def _(e):
            run("act", e)

        @block.vector
        def _(e):
            run("dve", e)

        @block.gpsimd
        def _(e):
            run("pool", e)

        @block.sync
        def _(e):
            run("sp", e)


class Alloc:
    def __init__(self, nc):
        self.nc = nc
        self.base = SB_LO
        self.top = SB_LO
        self.n = 0

    def mark(self):
        return self.top

    def reset(self, m):
        self.top = m

    def set_regions(self, regions):
        self.regions = [list(r) for r in regions]

    def ralloc(self, shape, dt):
        nb = 1
        for s_ in shape[1:]:
            nb *= s_
        nb *= 2 if dt == BF16 else 4
        nb = (nb + 63) // 64 * 64
        for r in self.regions:
            if r[0] + nb <= r[1]:
                off = r[0]
                r[0] += nb
                self.n += 1
                return self.nc.alloc_sbuf_tensor_at("t%d" % self.n, list(shape), dt, offset=off)
        raise AssertionError(("SBUF overflow", shape, self.regions))

    def __call__(self, shape, dt):
        nb = 1
        for s in shape[1:]:
            nb *= s
        nb *= 2 if dt == BF16 else 4
        nb = (nb + 63) // 64 * 64
        off = self.top
        self.top += nb
        assert self.top <= SB_HI, ("SBUF overflow", self.top)
        self.n += 1
        return self.nc.alloc_sbuf_tensor_at("t%d" % self.n, list(shape), dt, offset=off)


def build_program(debug=None):
    nc = bass.Bass("TRN2", target_bir_lowering=False)

    def din(name, shape, dt=F32):
        return nc.dram_tensor(name, list(shape), dt, kind="ExternalInput").ap()

    xo = din("xo", [NOTH, 1024])
    xown = din("xown", [NOWN, 1024])
    xh = din("xh", [128, 1024])
    posT = din("posT", [128, 80], I32)
    tfd = din("tf", [128, 48])
    cstd = din("cst", [128, 880])
    gmixd = din("gmix", [128, 1024])
    gmlpd = din("gmlp", [128, 1024])
    gfind = din("gfin", [128, 1024])
    gqd = din("gq", [128, 384])
    gkvd = din("gkv", [128, 256])
    gond = din("gon", [128, 512])
    gbd = din("gb", [128, 64])
    cwd = din("cw", [128, 40])
    cbd = din("cb", [128, 8])
    w_in = din("w_in", [1024, 4784])
    w_uq = din("w_uq", [384, 768])
    w_ukv = din("w_ukv", [256, 1024])
    w_bm = din("w_bm", [512, 1024])
    w_bl = din("w_bl", [512, 1024])
    w_out = din("w_out", [1024, 1024])
    w_up = din("w_up", [1024, 4096])
    w_down = din("w_down", [4096, 1024])
    y = nc.dram_tensor("y", [NOWN, 1024], F32, kind="ExternalOutput").ap()
    x1d = nc.dram_tensor("x1d", [NOWN, 1024], F32).ap()
    dbg = None
    if debug:
        dbg = nc.dram_tensor("dbg", [NOWN, 1024], F32, kind="ExternalOutput").ap()

    S = Sched()
    A = S.add
    al = Alloc(nc)
    st = ExitStack()
    ps = [st.enter_context(nc.psum_tensor("ps%d" % i, [128, 512], F32)) for i in range(8)]

    def psb(i):
        return ps[i][:].bitcast(BF16)

    cst = al([128, 880], F32)
    idb = al([128, 128], BF16)
    mLEb = al([128, 128], BF16)
    mGEb = al([128, 128], BF16)
    gmix = al([128, 1024], F32)
    xt = [al([128, 1024], F32) for _ in range(2)]
    junk = al([128, 1024], BF16)
    hb = [al([128, 1024], BF16) for _ in range(2)]
    sst = al([128, 8], F32)
    wst = [al([128, 1024], F32) for _ in range(2)]
    LNSCt = al([128, 1], F32)
    ONEt = al([128, 1], F32)
    EPSt = al([128, 1], F32)
    idf = cst[:, 0:128]
    mLE = cst[:, 128:256]
    mGE = cst[:, 256:384]
    mSU = cst[:, 384:512]
    mSL = cst[:, 512:640]
    ones = cst[:, 640:768]
    sel = cst[0:32, 784:880]

    A("sp", lambda e: e.dma_start(out=cst[:], in_=cstd), writes=["cst"], dma=True)
    A("sp", lambda e: e.dma_start(out=gmix[:], in_=gmixd), writes=["gmix"], dma=True)
    A("dve", lambda e: e.memset(ONEt[:], 1.0), writes=["onet"])
    A("dve", lambda e: e.memset(EPSt[:], EPS), writes=["epst"])
    A("dve", lambda e: e.tensor_copy(out=idb[:], in_=idf), reads=["cst"], writes=["idb"])
    A("dve", lambda e: e.tensor_copy(out=mLEb[:], in_=mLE), reads=["cst"], writes=["mLEb"])
    A("dve", lambda e: e.tensor_copy(out=mGEb[:], in_=mGE), reads=["cst"], writes=["mGEb"])

    wstate = {"i": 0}

    def load_w(dst, dkey, src, K, c_lo, c_hi, dcol=0, queue="sp"):
        for k in range(K):
            c0 = c_lo
            while c0 < c_hi:
                cc = min(1024, c_hi - c0)
                sl = wstate["i"] % 2
                wstate["i"] += 1
                A(queue, lambda e, sl=sl, k=k, c0=c0, cc=cc: e.dma_start(out=wst[sl][:, 0:cc], in_=src[k * 128:(k + 1) * 128, c0:c0 + cc]),
                  writes=[("wst", sl)], dma=True)
                d0 = dcol + (c0 - c_lo)
                A("pool", lambda e, sl=sl, k=k, d0=d0, cc=cc: e.tensor_copy(out=dst[:, k, d0:d0 + cc], in_=wst[sl][:, 0:cc]),
                  reads=[("wst", sl)], writes=[(dkey, k)])
                c0 += cc

    xstate = {"i": 0}

    def rstd_of(sskey_ap, n, key):
        A("act", lambda e: e.activation(out=sskey_ap, in_=sskey_ap, func=AF.Ln, scale=1.0 / n, bias=EPSt[:, 0:1]), reads=[key, "epst"], writes=[key])
        A("act", lambda e: e.activation(out=sskey_ap, in_=sskey_ap, func=AF.Exp, scale=-0.5), reads=[key], writes=[key])

    def xpipe(src_rows, g_tile, gkey, hT_dst, hkey, tps, keep=False):
        i = xstate["i"]
        xstate["i"] += 1
        sl = i % 2
        A("sp", lambda e: e.dma_start(out=xt[sl][:], in_=src_rows), writes=[("xt", sl)], dma=True)
        ssap = sst[:, sl:sl + 1]
        A("act", lambda e: e.activation(out=junk[:], in_=xt[sl][:], func=AF.Square, accum_out=ssap), reads=[("xt", sl)], writes=[("ss", sl)])
        rstd_of(ssap, 1024, ("ss", sl))
        A("dve", lambda e: e.scalar_tensor_tensor(out=hb[sl][:], in0=xt[sl][:], scalar=ssap, in1=g_tile[:], op0=ALU.mult, op1=ALU.mult),
          reads=[("xt", sl), ("ss", sl), gkey], writes=[("hb", sl)])
        pb = psb(tps)
        for k in range(8):
            A("pe", lambda e, k=k: e.transpose(out=pb[:, k * 128:(k + 1) * 128], in_=hb[sl][:, k * 128:(k + 1) * 128], identity=idb[:]),
              reads=[("hb", sl), "idb"], writes=[("ps", tps)])
        A("act", lambda e: e.copy(out=hT_dst, in_=pb.rearrange("p (k t) -> p k t", k=8)), reads=[("ps", tps)], writes=[hkey])
        return sl

    m_phase = al.mark()
    LGe = [al([128, 8], F32) for _ in range(2)]
    Ie = [al([128, 8], F32) for _ in range(2)]
    exg = [al([128, 8], F32) for _ in range(2)]
    Wt = [al([128, 8], F32) for _ in range(2)]
    dect = [al([128, 4], F32) for _ in range(2)]
    Arun = al([128, 4], F32)
    ktil = [al([128, 128], BF16) for _ in range(4)]
    Cf = al([128, 4, 129], F32)
    Cb = al([128, 4, 129], F32)
    Cfb = al([128, 4, 129], BF16)
    Cbb = al([128, 4, 129], BF16)
    qT = al([128, 4, NOWN], BF16)
    kT = al([128, 4, NOWN], BF16)
    ktok = al([128, 16, 512], BF16)
    vaug = al([128, 16, 4, 129], BF16)
    sgo = al([128, 16, 512], BF16)
    Gown = al([128, 16, 16], F32)
    LFo = al([128, 16, 8], F32)
    m_alias = al.mark()
    wl = al([128, 8, 2064], BF16)
    tf = al([128, 48], F32)
    tb = al([128, 48], F32)
    pf = al([128, 48], F32)
    pbk = al([128, 48], F32)
    gbias = al([128, 64], F32)
    cw = al([128, 40], F32)
    cb = al([128, 8], F32)
    gon = al([128, 512], F32)
    haloT = al([128, 8, 128], F32)
    hTg = [al([128, 8, 512], BF16) for _ in range(2)]
    padb = [al([128, 516], F32) for _ in range(2)]
    acc = [al([128, 512], F32) for _ in range(2)]
    kTg1 = al([128, 4, 512], BF16)
    ktokg1 = al([128, 4, 512], BF16)
    vaugg1 = al([128, 4, 4, 129], BF16)
    kTg = [kTg1, kTg1]
    ktokg = [ktokg1, ktokg1]
    vaugg = [vaugg1, vaugg1]
    Gg = [al([128, 4, 16], F32) for _ in range(2)]
    LFg = [al([128, 4, 8], F32) for _ in range(2)]
    sgt = [al([128, 512], BF16) for _ in range(2)]
    m_p12 = al.mark()
    al.reset(m_alias)
    ymT = al([128, 4, NOWN], BF16)
    m_keep = al.mark()
    EBt = al([128, 16, 8], F32)
    ECt = al([128, 16, 8], F32)
    WSt = al([128, 16, 8], F32)
    DECt = al([128, 16, 8], F32)
    cmt = al([128, 8], F32)
    HB = al([128, 16, 512], F32)
    PT = [al([128, 4, 128], BF16) for _ in range(2)]
    den = al([128, 4], F32)
    scl = al([128, 4], F32)
    hsum = [al([128, 4, 128], F32) for _ in range(2)]
    ssq = al([128, 4], F32)
    ymt = [al([128, 4, 128], BF16) for _ in range(2)]
    assert al.mark() <= m_p12
    al.reset(m_p12)

    for (t_, d_, k_) in ((tf, tfd, "tf"), (gbias, gbd, "gbias"), (cw, cwd, "cw"), (cb, cbd, "cb"), (gon, gond, "gon")):
        A("sp", lambda e, t_=t_, d_=d_: e.dma_start(out=t_[:], in_=d_), writes=[k_], dma=True)
    A("dve", lambda e: e.tensor_scalar(out=tb[:], in0=tf[:], scalar1=-1.0, scalar2=1.0, op0=ALU.mult, op1=ALU.add), reads=["tf"], writes=["tb"])
    A("dve", lambda e: e.tensor_scalar(out=pf[:], in0=tf[:], scalar1=-1.0, scalar2=BIG, op0=ALU.add, op1=ALU.mult), reads=["tf"], writes=["pf"])
    A("dve", lambda e: e.tensor_scalar(out=pbk[:], in0=tf[:], scalar1=-BIG, scalar2=None, op0=ALU.mult), reads=["tf"], writes=["pbk"])
    for t_, k_ in ((Cf, "Cf"), (Cb, "Cb"), (Arun, "Arun")):
        A("dve", lambda e, t_=t_: e.memset(t_[:], 0.0), writes=[k_])
    A("pool", lambda e: e.memset(vaug[:, :, :, 128:129], 1.0), writes=["vaug1"])
    A("pool", lambda e: e.memset(vaugg1[:, :, :, 128:129], 1.0), writes=[("vaugg1", 0)])

    load_w(wl, "wl", w_in, 8, 672, 2720, 0)
    for (s0, d0) in ((2720, 2048), (2728, 2052), (2724, 2056), (2732, 2060)):
        load_w(wl, "wl", w_in, 8, s0, s0 + 4, d0)
    WLK = [("wl", k) for k in range(8)]

    TPSX, TPSK, CPS0, CPS1, MVPS, UPS0 = 0, 1, 2, 3, 4, 5

    hTh = hTg[1][:, :, 0:128]
    xpipe(xh, gmix, "gmix", hTh, ("hTg", 1), TPSX)
    for c in range(8):
        bank = CPS0 + (c % 2)
        for k in range(8):
            A("pe", lambda e, c=c, k=k, bank=bank: e.matmul(ps[bank][:, 0:128], lhsT=wl[:, k, c * 128:(c + 1) * 128], rhs=hTg[1][:, k, 0:128],
                                                           start=(k == 0), stop=(k == 7)),
              reads=[("hTg", 1), ("wl", k)], writes=[("ps", bank)])
        A("act", lambda e, c=c, bank=bank: e.copy(out=haloT[:, c, :], in_=ps[bank][:, 0:128]), reads=[("ps", bank)], writes=["haloT"])

    cstate = {"i": 0}

    def conv_chunk(par, wcol, chunk, hidx, dst_ap, dst_key):
        ci = cstate["i"]
        cstate["i"] += 1
        bank = CPS0 + (ci % 2)
        pz = ci % 2
        for k in range(8):
            A("pe", lambda e, k=k: e.matmul(ps[bank][:], lhsT=wl[:, k, wcol:wcol + 128], rhs=hTg[par][:, k, :], start=(k == 0), stop=(k == 7)),
              reads=[("hTg", par), ("wl", k)], writes=[("ps", bank)])
        A("act", lambda e: e.copy(out=padb[pz][:, 2:514], in_=ps[bank][:]), reads=[("ps", bank)], writes=[("padb", pz)])
        A("pool", lambda e: e.tensor_copy(out=padb[pz][:, 0:2], in_=haloT[:, chunk, hidx:hidx + 2]), reads=["haloT"], writes=[("padbL", pz)])
        A("pool", lambda e: e.tensor_copy(out=padb[pz][:, 514:516], in_=haloT[:, chunk, hidx + 2:hidx + 4]), reads=["haloT"], writes=[("padbR", pz)])
        rk = [("padb", pz), ("padbL", pz), ("padbR", pz), "cw", "cb"]
        A("dve", lambda e: e.tensor_scalar(out=acc[pz][:], in0=padb[pz][:, 0:512], scalar1=cw[:, chunk * 5:chunk * 5 + 1], scalar2=cb[:, chunk:chunk + 1],
                                           op0=ALU.mult, op1=ALU.add), reads=rk, writes=[("acc", pz)])
        for j in range(1, 5):
            A("dve", lambda e, j=j: e.scalar_tensor_tensor(out=acc[pz][:], in0=padb[pz][:, j:j + 512], scalar=cw[:, chunk * 5 + j:chunk * 5 + j + 1],
                                                           in1=acc[pz][:], op0=ALU.mult, op1=ALU.add), reads=rk + [("acc", pz)], writes=[("acc", pz)])
        A("act", lambda e: e.activation(out=dst_ap, in_=acc[pz][:], func=AF.Silu), reads=[("acc", pz)], writes=[dst_key])

    def mlstm_group(gi, own):
        par = gi % 2
        src = xown if own else xo
        g0 = (gi - 12) if own else gi
        for i in range(4):
            r0 = g0 * 512 + i * 128
            xpipe(src[r0:r0 + 128, :], gmix, "gmix", hTg[par][:, :, i * 128:(i + 1) * 128], ("hTg", par), TPSX)
        hidx = (48 + 4 * g0) if own else 4 * g0
        for h in range(4):
            if own:
                conv_chunk(par, h * 128, h, hidx, qT[:, h, g0 * 512:(g0 + 1) * 512], "qT")
                conv_chunk(par, 512 + h * 128, 4 + h, hidx, kT[:, h, g0 * 512:(g0 + 1) * 512], "kT")
            else:
                conv_chunk(par, 512 + h * 128, 4 + h, hidx, kTg[par][:, h, :], ("kTg", 0))
        for b2 in range(2):
            pb = psb(TPSK)
            for bb in range(2):
                blk = b2 * 2 + bb
                for h in range(4):
                    if own:
                        src_ap = kT[:, h, g0 * 512 + blk * 128:g0 * 512 + (blk + 1) * 128]
                        rkey = "kT"
                    else:
                        src_ap = kTg[par][:, h, blk * 128:(blk + 1) * 128]
                        rkey = ("kTg", 0)
                    o0 = (bb * 4 + h) * 128
                    A("pe", lambda e, src_ap=src_ap, o0=o0: e.transpose(out=pb[:, o0:o0 + 128], in_=src_ap, identity=idb[:]),
                      reads=[rkey, "idb"], writes=[("ps", TPSK)])
            if own:
                dst = ktok[:, g0 * 4 + b2 * 2:g0 * 4 + b2 * 2 + 2, :]
                dk = "ktok"
            else:
                dst = ktokg[par][:, b2 * 2:b2 * 2 + 2, :]
                dk = ("ktokg", 0)
            A("act", lambda e, dst=dst, pb=pb: e.copy(out=dst, in_=pb.rearrange("p (b c) -> p b c", b=2)), reads=[("ps", TPSK)], writes=[dk])
        for i in range(4):
            for k in range(8):
                A("pe", lambda e, i=i, k=k: e.matmul(ps[MVPS][:], lhsT=hTg[par][:, k, i * 128:(i + 1) * 128], rhs=wl[:, k, 1024:1536],
                                                     start=(k == 0), stop=(k == 7)), reads=[("hTg", par), ("wl", k)], writes=[("ps", MVPS)])
            if own:
                dst = vaug[:, g0 * 4 + i, :, 0:128]
                dk = "vaug"
            else:
                dst = vaugg[par][:, i, :, 0:128]
                dk = ("vaugg", 0)
            A("dve", lambda e, dst=dst: e.tensor_copy(out=dst, in_=ps[MVPS][:].rearrange("p (h d) -> p h d", h=4)), reads=[("ps", MVPS)], writes=[dk])
            if own:
                for k in range(8):
                    A("pe", lambda e, i=i, k=k: e.matmul(ps[MVPS][:], lhsT=hTg[par][:, k, i * 128:(i + 1) * 128], rhs=wl[:, k, 1536:2048],
                                                         start=(k == 0), stop=(k == 7)), reads=[("hTg", par), ("wl", k)], writes=[("ps", MVPS)])
                sp_ = i % 2
                A("act", lambda e, sp_=sp_: e.activation(out=sgt[sp_][:], in_=ps[MVPS][:], func=AF.Sigmoid), reads=[("ps", MVPS)], writes=[("sgt", sp_)])
                A("pool", lambda e, sp_=sp_, i=i: e.tensor_tensor(out=sgo[:, g0 * 4 + i, :], in0=sgt[sp_][:], in1=gon[:], op=ALU.mult),
                  reads=[("sgt", sp_), "gon"], writes=["sgo"])
        for i in range(4):
            for k in range(8):
                A("pe", lambda e, i=i, k=k: e.matmul(ps[MVPS][:, i * 16:(i + 1) * 16], lhsT=hTg[par][:, k, i * 128:(i + 1) * 128], rhs=wl[:, k, 2048:2064],
                                                     start=(k == 0), stop=(k == 7)), reads=[("hTg", par), ("wl", k)], writes=[("ps", MVPS)])
        if own:
            Gd = Gown[:, g0 * 4:g0 * 4 + 4, :]
            gk = "Gown"
            LFd = LFo[:, g0 * 4:g0 * 4 + 4, :]
            lk = "LFo"
        else:
            Gd = Gg[par][:]
            gk = ("Gg", par)
            LFd = LFg[par][:]
            lk = ("LFg", par)
        A("dve", lambda e: e.tensor_tensor(out=Gd, in0=ps[MVPS][:, 0:64].rearrange("p (b c) -> p b c", b=4),
                                           in1=gbias[:].rearrange("p (b c) -> p b c", b=4), op=ALU.add), reads=[("ps", MVPS), "gbias"], writes=[gk])
        A("act", lambda e: e.activation(out=LFd, in_=Gd[:, :, 8:16], func=AF.Exp, scale=-1.0), reads=[gk], writes=[lk])
        A("act", lambda e: e.activation(out=LFd, in_=LFd, func=AF.Ln, bias=ONEt[:, 0:1]), reads=[lk, "onet"], writes=[lk])
        if own:
            return
        for blk in range(4):
            gb_ = gi * 4 + blk
            z = blk % 2
            A("dve", lambda e: e.tensor_scalar(out=LGe[z][:, 0:4], in0=LFg[par][:, blk, 0:4], scalar1=tf[:, gb_:gb_ + 1], scalar2=-1.0, op0=ALU.mult, op1=ALU.mult),
              reads=[lk, "tf"], writes=[("LGe", z)])
            A("dve", lambda e: e.tensor_scalar(out=LGe[z][:, 4:8], in0=LFg[par][:, blk, 4:8], scalar1=tb[:, gb_:gb_ + 1], scalar2=-1.0, op0=ALU.mult, op1=ALU.mult),
              reads=[lk, "tb"], writes=[("LGe", z)])
            A("dve", lambda e: e.tensor_scalar(out=Ie[z][:, 0:4], in0=Gg[par][:, blk, 0:4], scalar1=pf[:, gb_:gb_ + 1], scalar2=None, op0=ALU.add),
              reads=[gk, "pf"], writes=[("Ie", z)])
            A("dve", lambda e: e.tensor_scalar(out=Ie[z][:, 4:8], in0=Gg[par][:, blk, 4:8], scalar1=pbk[:, gb_:gb_ + 1], scalar2=None, op0=ALU.add),
              reads=[gk, "pbk"], writes=[("Ie", z)])
            gp = UPS0 + 2
            A("pe", lambda e: e.matmul(ps[gp][:, 0:4], lhsT=mSU, rhs=LGe[z][:, 0:4], start=True, stop=True), reads=["cst", ("LGe", z)], writes=[("ps", gp)])
            A("pe", lambda e: e.matmul(ps[gp][:, 4:8], lhsT=mSL, rhs=LGe[z][:, 4:8], start=True, stop=True), reads=["cst", ("LGe", z)], writes=[("ps", gp)])
            A("pe", lambda e: e.matmul(ps[gp][:, 8:16], lhsT=ones, rhs=LGe[z][:, 0:8], start=True, stop=True), reads=["cst", ("LGe", z)], writes=[("ps", gp)])
            A("dve", lambda e: e.tensor_tensor(out=exg[z][:], in0=ps[gp][:, 0:8], in1=Ie[z][:], op=ALU.add), reads=[("ps", gp), ("Ie", z)], writes=[("exg", z)])
            A("dve", lambda e: e.tensor_tensor(out=exg[z][:, 4:8], in0=exg[z][:, 4:8], in1=Arun[:], op=ALU.add), reads=[("exg", z), "Arun"], writes=[("exg", z)])
            A("act", lambda e: e.activation(out=Wt[z][:], in_=exg[z][:], func=AF.Exp, bias=LNSCt[:, 0:1]), reads=[("exg", z), "lnsc"], writes=[("Wt", z)])
            A("act", lambda e: e.activation(out=dect[z][:], in_=ps[gp][:, 8:12], func=AF.Exp), reads=[("ps", gp)], writes=[("dect", z)])
            A("dve", lambda e: e.tensor_tensor(out=Arun[:], in0=Arun[:], in1=ps[gp][:, 12:16], op=ALU.add), reads=["Arun", ("ps", gp)], writes=["Arun"])
            for u in range(8):
                ch, h = divmod(u, 4)
                kz = u % 4
                A("dve", lambda e, u=u, h=h, kz=kz: e.tensor_scalar(out=ktil[kz][:], in0=ktokg[par][:, blk, h * 128:(h + 1) * 128],
                                                                    scalar1=Wt[z][:, u:u + 1], scalar2=None, op0=ALU.mult),
                  reads=[("ktokg", 0), ("Wt", z)], writes=[("ktil", kz)])
                bank = UPS0 + (u // 3) if u < 6 else UPS0 + 2
                if u < 6:
                    c0 = (u % 3) * 129
                else:
                    c0 = 128 + (u - 6) * 129
                A("pe", lambda e, h=h, kz=kz, bank=bank, c0=c0: e.matmul(ps[bank][:, c0:c0 + 129], lhsT=ktil[kz][:], rhs=vaugg[par][:, blk, h, :],
                                                                         start=True, stop=True),
                  reads=[("ktil", kz), ("vaugg", 0), ("vaugg1", 0)], writes=[("ps", bank)])
                if ch == 0:
                    A("dve", lambda e, h=h, bank=bank, c0=c0: e.scalar_tensor_tensor(out=Cf[:, h, :], in0=Cf[:, h, :], scalar=dect[z][:, h:h + 1],
                                                                                     in1=ps[bank][:, c0:c0 + 129], op0=ALU.mult, op1=ALU.add),
                      reads=["Cf", ("dect", z), ("ps", bank)], writes=["Cf"])
                else:
                    A("dve", lambda e, h=h, bank=bank, c0=c0: e.tensor_tensor(out=Cb[:, h, :], in0=Cb[:, h, :], in1=ps[bank][:, c0:c0 + 129], op=ALU.add),
                      reads=["Cb", ("ps", bank)], writes=["Cb"])

    A("dve", lambda e: e.memset(LNSCt[:], LNSC), writes=["lnsc"])

    for gi in range(12 if not (debug and (debug.endswith("_fast") or debug.startswith("p4"))) else 0):
        mlstm_group(gi, False)
    for gi in range(12, 16 if not (debug and debug.startswith("p4")) else 12):
        mlstm_group(gi, True)

    if debug and debug.split("_")[0] in ("qTe", "kTe"):
        src_t = qT if debug.startswith("qTe") else kT
        dt_ = al([128, 1024], F32)
        for h in range(4):
            for half in range(2):
                A("dve", lambda e, h=h, half=half: e.tensor_copy(out=dt_[:], in_=src_t[:, h, half * 1024:(half + 1) * 1024]), reads=["qT", "kT"], writes=["dt_"])
                r0 = (h * 2 + half) * 128
                A("sp", lambda e, r0=r0: e.dma_start(out=dbg[r0:r0 + 128, :], in_=dt_[:]), reads=["dt_"], writes=["dbgo"], dma=True)
        A("sp", lambda e: e.nop(), reads=["dbgo"])
        S.emit(nc, st)
        st.close()
        return nc
    p12keys = [("wl", k) for k in range(8)] + ["tf", "tb", "pf", "pbk", "gbias", "cw", "cb", "gon", "haloT", ("hTg", 0), ("hTg", 1),
               ("padb", 0), ("padb", 1), ("padbL", 0), ("padbL", 1), ("padbR", 0), ("padbR", 1), ("acc", 0), ("acc", 1),
               ("kTg", 0), ("ktokg", 0), ("vaugg", 0), ("vaugg1", 0), ("Gg", 0), ("Gg", 1), ("LFg", 0), ("LFg", 1), ("sgt", 0), ("sgt", 1)]
    p3keys = ["EBt", "ECt", "WSt", "DECt", "cmt", "HB", ("PT", 0), ("PT", 1), "den", "scl", ("hsum", 0), ("hsum", 1), "ssq", ("ymt", 0), ("ymt", 1), "ymT"]
    A("dve", lambda e: e.memset(cmt[:], 0.0), writes=p12keys + p3keys)
    A("pool", lambda e: e.tensor_copy(out=Cfb[:], in_=Cf[:]), reads=["Cf"], writes=["Cfb"])
    A("pool", lambda e: e.tensor_copy(out=Cbb[:], in_=Cb[:]), reads=["Cb"], writes=["Cbb"])
    GP = 7
    for blk in range(16 if not (debug and debug.startswith("p4")) else 0):
        z = blk % 2
        A("dve", lambda e, blk=blk: e.tensor_scalar(out=LGe[z][:], in0=LFo[:, blk, :], scalar1=-1.0, scalar2=None, op0=ALU.mult), reads=["LFo"], writes=[("LGe", z)])
        A("pe", lambda e: e.matmul(ps[GP][:, 0:4], lhsT=mLE, rhs=LGe[z][:, 0:4], start=True, stop=True), reads=["cst", ("LGe", z)], writes=[("ps", GP)])
        A("pe", lambda e: e.matmul(ps[GP][:, 4:8], lhsT=mGE, rhs=LGe[z][:, 4:8], start=True, stop=True), reads=["cst", ("LGe", z)], writes=[("ps", GP)])
        A("pe", lambda e: e.matmul(ps[GP][:, 8:16], lhsT=ones, rhs=LGe[z][:, 0:8], start=True, stop=True), reads=["cst", ("LGe", z)], writes=[("ps", GP)])
        A("act", lambda e, blk=blk: e.activation(out=EBt[:, blk, :], in_=ps[GP][:, 0:8], func=AF.Exp), reads=[("ps", GP)], writes=["EBt"])
        A("dve", lambda e, blk=blk: e.tensor_tensor(out=cmt[:], in0=Gown[:, blk, 0:8], in1=ps[GP][:, 0:8], op=ALU.subtract), reads=["Gown", ("ps", GP)], writes=["cmt"])
        A("act", lambda e, blk=blk: e.activation(out=ECt[:, blk, :], in_=cmt[:], func=AF.Exp, bias=LNSCt[:, 0:1]), reads=["cmt", "lnsc"], writes=["ECt"])
        A("act", lambda e, blk=blk: e.activation(out=DECt[:, blk, :], in_=ps[GP][:, 8:16], func=AF.Exp), reads=[("ps", GP)], writes=["DECt"])
        A("dve", lambda e, blk=blk: e.tensor_tensor(out=cmt[:], in0=cmt[:], in1=ps[GP][:, 8:16], op=ALU.add), reads=["cmt", ("ps", GP)], writes=["cmt"])
        A("act", lambda e, blk=blk: e.activation(out=WSt[:, blk, :], in_=cmt[:], func=AF.Exp, bias=LNSCt[:, 0:1]), reads=["cmt", "lnsc"], writes=["WSt"])

    SPS, ND0, ND1, UB0, UB1, TPY = 0, 1, 2, 3, 4, 5
    def dirpass(d):
        blocks = list(range(16)) if d == 0 else list(range(15, -1, -1))
        Cx, Cxb, ck, ckb = (Cf, Cfb, "Cf", "Cfb") if d == 0 else (Cb, Cbb, "Cb", "Cbb")
        maskb = mLEb if d == 0 else mGEb
        mk_ = "mLEb" if d == 0 else "mGEb"
        final = (d == 0)
        for bi, blk in enumerate(blocks):
            z = bi % 2
            tsl = slice(blk * 128, (blk + 1) * 128)
            for h in range(4):
                A("pe", lambda e, h=h: e.matmul(ps[SPS][:, h * 128:(h + 1) * 128], lhsT=kT[:, h, tsl], rhs=qT[:, h, tsl], start=True, stop=True),
                  reads=["kT", "qT"], writes=[("ps", SPS)])
            for h in range(4):
                A("dve", lambda e, h=h: e.scalar_tensor_tensor(out=PT[z][:, h, :], in0=ps[SPS][:, h * 128:(h + 1) * 128], scalar=ECt[:, blk, d * 4 + h:d * 4 + h + 1],
                                                               in1=maskb[:], op0=ALU.mult, op1=ALU.mult),
                  reads=[("ps", SPS), "ECt", mk_], writes=[("PT", z)])
            for h in range(4):
                bank = ND0 + h // 2
                c0 = (h % 2) * 129
                A("pe", lambda e, h=h, bank=bank, c0=c0: e.matmul(ps[bank][:, c0:c0 + 129], lhsT=PT[z][:, h, :], rhs=vaug[:, blk, h, :], start=True, stop=False),
                  reads=[("PT", z), "vaug", "vaug1"], writes=[("ps", bank)])
                A("pe", lambda e, h=h, bank=bank, c0=c0: e.matmul(ps[bank][:, c0:c0 + 129], lhsT=qT[:, h, tsl], rhs=Cxb[:, h, :], start=False, stop=True),
                  reads=["qT", ckb], writes=[("ps", bank)])
            for hh in range(2):
                bank = ND0 + hh
                A("dve", lambda e, hh=hh, bank=bank: e.tensor_tensor(out=den[:, hh * 2:hh * 2 + 2],
                                                                     in0=ps[bank][:, 0:258].rearrange("p (a c) -> p a c", a=2)[:, :, 128],
                                                                     in1=EBt[:, blk, d * 4 + hh * 2:d * 4 + hh * 2 + 2], op=ALU.mult),
                  reads=[("ps", bank), "EBt"], writes=["den"])
            A("dve", lambda e: e.tensor_scalar(out=scl[:], in0=den[:], scalar1=-1.0, scalar2=None, op0=ALU.mult), reads=["den"], writes=["scl"])
            A("dve", lambda e: e.tensor_tensor(out=den[:], in0=den[:], in1=scl[:], op=ALU.max), reads=["den", "scl"], writes=["den"])
            A("dve", lambda e: e.tensor_scalar(out=den[:], in0=den[:], scalar1=1.0, scalar2=None, op0=ALU.max), reads=["den"], writes=["den"])
            A("dve", lambda e: e.reciprocal(out=den[:], in_=den[:]), reads=["den"], writes=["den"])
            A("dve", lambda e: e.tensor_tensor(out=scl[:], in0=den[:], in1=EBt[:, blk, d * 4:d * 4 + 4], op=ALU.mult), reads=["den", "EBt"], writes=["scl"])
            for h in range(4):
                bank = ND0 + h // 2
                c0 = (h % 2) * 129
                if not final:
                    A("dve", lambda e, h=h, bank=bank, c0=c0: e.tensor_scalar(out=HB[:, blk, h * 128:(h + 1) * 128], in0=ps[bank][:, c0:c0 + 128],
                                                                              scalar1=scl[:, h:h + 1], scalar2=None, op0=ALU.mult),
                      reads=[("ps", bank), "scl"], writes=["HB"])
                else:
                    A("dve", lambda e, h=h, bank=bank, c0=c0: e.scalar_tensor_tensor(out=hsum[z][:, h, :], in0=ps[bank][:, c0:c0 + 128], scalar=scl[:, h:h + 1],
                                                                                     in1=HB[:, blk, h * 128:(h + 1) * 128], op0=ALU.mult, op1=ALU.add),
                      reads=[("ps", bank), "scl", "HB"], writes=[("hsum", z)])
            if bi < 15:
                for h in range(4):
                    kz = h
                    bank = UB0 + h // 2
                    c0 = (h % 2) * 129
                    A("dve", lambda e, h=h, kz=kz: e.tensor_scalar(out=ktil[kz][:], in0=ktok[:, blk, h * 128:(h + 1) * 128],
                                                                   scalar1=WSt[:, blk, d * 4 + h:d * 4 + h + 1], scalar2=None, op0=ALU.mult),
                      reads=["ktok", "WSt"], writes=[("ktil", kz)])
                    A("pe", lambda e, h=h, kz=kz, bank=bank, c0=c0: e.matmul(ps[bank][:, c0:c0 + 129], lhsT=ktil[kz][:], rhs=vaug[:, blk, h, :], start=True, stop=True),
                      reads=[("ktil", kz), "vaug", "vaug1"], writes=[("ps", bank)])
                    A("dve", lambda e, h=h, bank=bank, c0=c0: e.scalar_tensor_tensor(out=Cx[:, h, :], in0=Cx[:, h, :], scalar=DECt[:, blk, d * 4 + h:d * 4 + h + 1],
                                                                                     in1=ps[bank][:, c0:c0 + 129], op0=ALU.mult, op1=ALU.add),
                      reads=[ck, "DECt", ("ps", bank)], writes=[ck])
                A("pool", lambda e: e.tensor_copy(out=Cxb[:], in_=Cx[:]), reads=[ck], writes=[ckb])
            if final:
                for h in range(4):
                    A("act", lambda e, h=h: e.activation(out=junk[:, 0:128], in_=hsum[z][:, h, :], func=AF.Square, accum_out=ssq[:, h:h + 1]),
                      reads=[("hsum", z)], writes=["ssq"])
                rstd_of(ssq[:], 128, "ssq")
                for h in range(4):
                    A("dve", lambda e, h=h: e.scalar_tensor_tensor(out=ymt[z][:, h, :], in0=hsum[z][:, h, :], scalar=ssq[:, h:h + 1],
                                                                   in1=sgo[:, blk, h * 128:(h + 1) * 128], op0=ALU.mult, op1=ALU.mult),
                      reads=[("hsum", z), "ssq", "sgo"], writes=[("ymt", z)])
                pb = psb(TPY)
                for h in range(4):
                    A("pe", lambda e, h=h: e.transpose(out=pb[:, h * 128:(h + 1) * 128], in_=ymt[z][:, h, :], identity=idb[:]),
                      reads=[("ymt", z), "idb"], writes=[("ps", TPY)])
                A("act", lambda e: e.copy(out=ymT[:, :, tsl], in_=pb[:, 0:512].rearrange("p (h t) -> p h t", h=4)), reads=[("ps", TPY)], writes=["ymT"])

    if not (debug and debug.startswith("p4")):
        dirpass(1)
    if debug and debug.startswith("HB"):
        for blk in range(16):
            A("sp", lambda e, blk=blk: e.dma_start(out=dbg[blk * 128:(blk + 1) * 128, 0:512], in_=HB[:, blk, :]), reads=["HB"], writes=["dbgo"], dma=True)
        A("sp", lambda e: e.nop(), reads=["dbgo"])
        S.emit(nc, st)
        st.close()
        return nc
    if not (debug and debug.startswith("p4")):
        dirpass(0)

    if debug and debug.split("_")[0] in ("mlstm", "qT", "kT"):
        debug = debug.split("_")[0]
        if debug == "qT":
            ymT = qT
        elif debug == "kT":
            ymT = kT
        ymk = {"mlstm": "ymT", "qT": "qT", "kT": "kT"}[debug]
        dt_ = al([128, 1024], F32)
        for h in range(4):
            for half in range(2):
                A("dve", lambda e, h=h, half=half: e.tensor_copy(out=dt_[:], in_=ymT[:, h, half * 1024:(half + 1) * 1024]), reads=[ymk], writes=["dt_"])
                r0 = (h * 2 + half) * 128
                A("sp", lambda e, r0=r0: e.dma_start(out=dbg[r0:r0 + 128, :], in_=dt_[:]), reads=["dt_"], writes=["dbgo"], dma=True)
        A("sp", lambda e: e.nop(), reads=["dbgo"])
        S.emit(nc, st)
        st.close()
        return nc

    S.barrier(lambda e: e.memset(sst[:, 7:8], 0.0))
    al.set_regions([(m_phase, m_alias), (m_keep, SB_HI)])
    R = al.ralloc
    ckvT = R([128, 2, 8192], BF16)
    kropeT = R([32, 8192], BF16)
    QT = R([96, 8, NOWN], BF16)
    yaT = R([128, 4, NOWN], BF16)
    reg_p4 = [list(r) for r in al.regions]
    wkv = R([128, 8, 288], BF16)
    wq = R([128, 8, 384], BF16)
    wuq = R([128, 3, 768], BF16)
    gq = R([128, 384], F32)
    gkv = R([128, 256], F32)
    hT4 = [R([128, 8, 128], BF16) for _ in range(2)]
    sinT = R([128, 80, 16], F32)
    cosT = R([128, 80, 16], F32)
    sinq = R([128, 16, 4, 16], F32)
    cosq = R([128, 16, 4, 16], F32)
    reg_tmp = [list(r) for r in al.regions]
    posi = R([128, 80], I32)
    posf = R([128, 80], F32)
    ang = R([128, 80, 16], F32)
    tq = R([128, 1280], F32)
    tk = R([128, 1280], I32)
    tkf = R([128, 1280], F32)
    tm = R([128, 1280], F32)

    load_w(wkv, "wkv", w_in, 8, 384, 672)
    load_w(wq, "wq", w_in, 8, 0, 384)
    load_w(wuq, "wuq", w_uq, 3, 0, 768)
    A("sp", lambda e: e.dma_start(out=gq[:], in_=gqd), writes=["gq"], dma=True)
    A("sp", lambda e: e.dma_start(out=gkv[:], in_=gkvd), writes=["gkv"], dma=True)
    A("sp", lambda e: e.dma_start(out=posi[:], in_=posT), writes=["posi"], dma=True)
    A("dve", lambda e: e.tensor_copy(out=posf[:], in_=posi[:]), reads=["posi"], writes=["posf"])
    invf = (np.float32(10000.0) ** (-np.arange(0, 32, 2, dtype=np.float32) / np.float32(32))).astype(np.float32)
    for f in range(16):
        A("dve", lambda e, f=f: e.tensor_scalar(out=ang[:, :, f], in0=posf[:], scalar1=float(invf[f]), scalar2=None, op0=ALU.mult), reads=["posf"], writes=["ang"])
    angf = ang[:].rearrange("p a b -> p (a b)")
    TWO_PI = 2.0 * math.pi
    for (dst, off) in ((sinT, 0.0), (cosT, 0.25)):
        dstf = dst[:].rearrange("p a b -> p (a b)")
        A("dve", lambda e, off=off: e.tensor_scalar(out=tq[:], in0=angf, scalar1=1.0 / TWO_PI, scalar2=off, op0=ALU.mult, op1=ALU.add), reads=["ang"], writes=["tq"])
        A("dve", lambda e: e.tensor_copy(out=tk[:], in_=tq[:]), reads=["tq"], writes=["tk"])
        A("dve", lambda e: e.tensor_copy(out=tkf[:], in_=tk[:]), reads=["tk"], writes=["tkf"])
        A("dve", lambda e: e.tensor_tensor(out=tq[:], in0=tq[:], in1=tkf[:], op=ALU.subtract), reads=["tq", "tkf"], writes=["tq"])
        A("dve", lambda e: e.tensor_scalar(out=tm[:], in0=tq[:], scalar1=0.5, scalar2=None, op0=ALU.is_gt), reads=["tq"], writes=["tm"])
        A("dve", lambda e: e.tensor_tensor(out=tq[:], in0=tq[:], in1=tm[:], op=ALU.subtract), reads=["tq", "tm"], writes=["tq"])
        A("dve", lambda e: e.tensor_scalar(out=tm[:], in0=tq[:], scalar1=-0.5, scalar2=None, op0=ALU.is_lt), reads=["tq"], writes=["tm"])
        A("dve", lambda e: e.tensor_tensor(out=tq[:], in0=tq[:], in1=tm[:], op=ALU.add), reads=["tq", "tm"], writes=["tq"])
        A("dve", lambda e: e.tensor_scalar(out=tq[:], in0=tq[:], scalar1=-0.4999, scalar2=0.4999, op0=ALU.max, op1=ALU.min), reads=["tq"], writes=["tq"])
        A("act", lambda e, dstf=dstf: e.activation(out=dstf, in_=tq[:], func=AF.Sin, scale=TWO_PI), reads=["tq"], writes=["sincos"])
    for hh in range(4):
        A("dve", lambda e, hh=hh: e.tensor_copy(out=sinq[:, :, hh, :], in_=sinT[:, 64:80, :]), reads=["sincos"], writes=["sinq"])
        A("dve", lambda e, hh=hh: e.tensor_copy(out=cosq[:, :, hh, :], in_=cosT[:, 64:80, :]), reads=["sincos"], writes=["cosq"])

    if debug == "p4a":
        A("sp", lambda e: e.dma_start(out=dbg[0:128, 0:1024], in_=sinT[:].rearrange("p a b -> p (a b)")[:, 0:1024]), reads=["sincos"], writes=["dbgo"], dma=True)
        A("sp", lambda e: e.nop(), reads=["dbgo"])
        S.emit(nc, st)
        st.close()
        return nc
    S.barrier(lambda e: e.memset(sst[:, 7:8], 0.0))
    al.regions = [list(r) for r in reg_tmp]
    cn = [R([128, 384], BF16) for _ in range(2)]
    krr = [R([128, 32], BF16) for _ in range(2)]
    rt = [R([128, 4, 16], F32) for _ in range(4)]
    cqnT = R([128, 3, 128], BF16)
    qtok = R([128, 8, 96], BF16)
    TPSX, LAT0, TPC, Q0, Q1, TPQ = 0, 1, 3, 4, 5, 6
    lstate = {"i": 0}

    def rope(x1, x2, cs, sn, o1, o2, rkeys, okey, shape4):
        ta, tb_, tc, td = [t[:] if shape4 else t[:, 0, :] for t in rt]
        A("dve", lambda e: e.tensor_tensor(out=ta, in0=x1, in1=cs, op=ALU.mult), reads=rkeys, writes=["rt0"])
        A("dve", lambda e: e.tensor_tensor(out=tb_, in0=x2, in1=sn, op=ALU.mult), reads=rkeys, writes=["rt1"])
        A("dve", lambda e: e.tensor_tensor(out=o1, in0=ta, in1=tb_, op=ALU.subtract), reads=["rt0", "rt1"], writes=[okey])
        A("dve", lambda e: e.tensor_tensor(out=tc, in0=x2, in1=cs, op=ALU.mult), reads=rkeys, writes=["rt2"])
        A("dve", lambda e: e.tensor_tensor(out=td, in0=x1, in1=sn, op=ALU.mult), reads=rkeys, writes=["rt3"])
        A("dve", lambda e: e.tensor_tensor(out=o2, in0=tc, in1=td, op=ALU.add), reads=["rt2", "rt3"], writes=[okey])

    for kt in range(64):
        z = kt % 2
        src = xo[kt * 128:(kt + 1) * 128, :] if kt < 48 else xown[(kt - 48) * 128:(kt - 47) * 128, :]
        xpipe(src, gmix, "gmix", hT4[z][:], ("hT4", z), TPSX)
        lat = LAT0 + z
        for k in range(8):
            A("pe", lambda e, k=k: e.matmul(ps[lat][:, 0:288], lhsT=hT4[z][:, k, :], rhs=wkv[:, k, :], start=(k == 0), stop=(k == 7)),
              reads=[("hT4", z), ("wkv", k)], writes=[("ps", lat)])
        ssap = sst[:, 2 + z:3 + z]
        A("act", lambda e: e.activation(out=junk[:, 0:256], in_=ps[lat][:, 0:256], func=AF.Square, accum_out=ssap), reads=[("ps", lat)], writes=[("ssl", z)])
        rstd_of(ssap, 256, ("ssl", z))
        A("dve", lambda e: e.scalar_tensor_tensor(out=cn[z][:, 0:256], in0=ps[lat][:, 0:256], scalar=ssap, in1=gkv[:], op0=ALU.mult, op1=ALU.mult),
          reads=[("ps", lat), ("ssl", z), "gkv"], writes=[("cn", z)])
        rope(ps[lat][:, 256:272], ps[lat][:, 272:288], cosT[:, kt, :], sinT[:, kt, :], krr[z][:, 0:16], krr[z][:, 16:32],
             [("ps", lat), "sincos"], ("krr", z), False)
        pb = psb(TPC)
        for c in range(2):
            A("pe", lambda e, c=c: e.transpose(out=pb[:, c * 128:(c + 1) * 128], in_=cn[z][:, c * 128:(c + 1) * 128], identity=idb[:]),
              reads=[("cn", z), "idb"], writes=[("ps", TPC)])
        A("pe", lambda e: e.transpose(out=pb[0:32, 256:384], in_=krr[z][:], identity=idb[:]), reads=[("krr", z), "idb"], writes=[("ps", TPC)])
        A("act", lambda e: e.copy(out=ckvT[:, :, kt * 128:(kt + 1) * 128], in_=pb[:, 0:256].rearrange("p (c t) -> p c t", c=2)), reads=[("ps", TPC)], writes=["ckvT"])
        A("act", lambda e: e.copy(out=kropeT[:, kt * 128:(kt + 1) * 128], in_=pb[0:32, 256:384]), reads=[("ps", TPC)], writes=["kropeT"])

    for ot in range(16 if debug != "p4b" else 0):
        z = ot % 2
        xpipe(xown[ot * 128:(ot + 1) * 128, :], gmix, "gmix", hT4[z][:], ("hT4", z), TPSX)
        lat = LAT0 + z
        for k in range(8):
            A("pe", lambda e, k=k: e.matmul(ps[lat][:, 0:384], lhsT=hT4[z][:, k, :], rhs=wq[:, k, :], start=(k == 0), stop=(k == 7)),
              reads=[("hT4", z), ("wq", k)], writes=[("ps", lat)])
        ssap = sst[:, 2 + z:3 + z]
        A("act", lambda e: e.activation(out=junk[:, 0:384], in_=ps[lat][:, 0:384], func=AF.Square, accum_out=ssap), reads=[("ps", lat)], writes=[("ssl", z)])
        rstd_of(ssap, 384, ("ssl", z))
        A("dve", lambda e: e.scalar_tensor_tensor(out=cn[z][:], in0=ps[lat][:, 0:384], scalar=ssap, in1=gq[:], op0=ALU.mult, op1=ALU.mult),
          reads=[("ps", lat), ("ssl", z), "gq"], writes=[("cn", z)])
        pb = psb(TPC)
        for c in range(3):
            A("pe", lambda e, c=c: e.transpose(out=pb[:, c * 128:(c + 1) * 128], in_=cn[z][:, c * 128:(c + 1) * 128], identity=idb[:]),
              reads=[("cn", z), "idb"], writes=[("ps", TPC)])
        A("act", lambda e: e.copy(out=cqnT[:], in_=pb[:, 0:384].rearrange("p (c t) -> p c t", c=3)), reads=[("ps", TPC)], writes=["cqnT"])
        qlvl = int(debug[3:]) if (debug and debug.startswith("p4q")) else 9
        if qlvl < 2:
            continue
        for x_ in range(2):
            qb = Q0 + x_
            for c in range(3):
                A("pe", lambda e, c=c: e.matmul(ps[qb][:, 0:384], lhsT=cqnT[:, c, :], rhs=wuq[:, c, x_ * 384:(x_ + 1) * 384], start=(c == 0), stop=(c == 2)),
                  reads=["cqnT", ("wuq", c)], writes=[("ps", qb)])
            V4 = ps[qb][:, 0:384].rearrange("p (h d) -> p h d", h=4)
            A("act", lambda e: e.copy(out=qtok[:, x_ * 4:x_ * 4 + 4, 0:64], in_=V4[:, :, 0:64]), reads=[("ps", qb)], writes=["qtokn"])
            if qlvl < 3:
                continue
            for hh in range(4):
                c0 = hh * 96
                rope(ps[qb][:, c0 + 64:c0 + 80], ps[qb][:, c0 + 80:c0 + 96], cosT[:, 64 + ot, :], sinT[:, 64 + ot, :],
                     qtok[:, x_ * 4 + hh, 64:80], qtok[:, x_ * 4 + hh, 80:96], [("ps", qb), "sincos"], "qtokr", False)
        if qlvl < 4:
            continue
        pq = psb(TPQ)
        for h in range(8):
            A("pe", lambda e, h=h: e.transpose(out=pq[0:96, h * 128:(h + 1) * 128], in_=qtok[:, h, :], identity=idb[:]),
              reads=["qtokn", "qtokr", "idb"], writes=[("ps", TPQ)])
        A("act", lambda e: e.copy(out=QT[:, :, ot * 128:(ot + 1) * 128], in_=pq[0:96, :].rearrange("p (h t) -> p h t", h=8)), reads=[("ps", TPQ)], writes=["QT"])

    if debug and debug.startswith("p4"):
        dtt = R([128, 1024], F32)
        for h in range(8):
            for half in range(2):
                A("dve", lambda e, h=h, half=half: e.tensor_copy(out=dtt[0:96, :], in_=QT[:, h, half * 1024:(half + 1) * 1024]), reads=["QT"], writes=["dtt"])
                r0 = (h * 2 + half) * 96
                A("sp", lambda e, r0=r0: e.dma_start(out=dbg[r0:r0 + 96, :], in_=dtt[0:96, :]), reads=["dtt"], writes=["dbgo"], dma=True)
        A("dve", lambda e: e.tensor_copy(out=dtt[0:32, :], in_=kropeT[:, 7168:8192]), reads=["kropeT"], writes=["dtt"])
        A("sp", lambda e: e.dma_start(out=dbg[1536:1568, :], in_=dtt[0:32, :]), reads=["dtt"], writes=["dbgo"], dma=True)
        A("dve", lambda e: e.tensor_copy(out=dtt[:], in_=ckvT[:, 1, 7168:8192]), reads=["ckvT"], writes=["dtt"])
        A("sp", lambda e: e.dma_start(out=dbg[1664:1792, :], in_=dtt[:]), reads=["dtt"], writes=["dbgo"], dma=True)
        A("sp", lambda e: e.nop(), reads=["dbgo"])
        S.emit(nc, st)
        st.close()
        return nc
    S.barrier(lambda e: e.memset(sst[:, 7:8], 0.0))
    al.regions = [list(r) for r in reg_p4]
    wkp = R([128, 2, 8, 96], BF16)
    wv = R([128, 2, 512], BF16)
    selb = R([32, 96], BF16)
    KT0 = R([96, 8192], BF16)
    KT = [KT0, KT0]
    VA = [R([128, 64, 65], BF16) for _ in range(2)]
    Pb = [R([128, 512], BF16) for _ in range(3)]
    yattn = R([128, 16, 512], BF16)
    rden = R([128, 4], F32)
    A("pool", lambda e: e.memset(wkp[:], 0.0), writes=["wkp"])
    A("dve", lambda e: e.tensor_copy(out=selb[:], in_=sel), reads=["cst"], writes=["selb"])
    for b_ in range(2):
        A("pool", lambda e, b_=b_: e.memset(VA[b_][:, :, 64:65], 1.0), writes=[("VA1", b_)])
    for c in range(2):
        sl = wstate["i"] % 2
        wstate["i"] += 1
        A("sp", lambda e, c=c, sl=sl: e.dma_start(out=wst[sl][:], in_=w_ukv[c * 128:(c + 1) * 128, :]), writes=[("wst", sl)], dma=True)
        W3 = wst[sl][:].rearrange("p (h d) -> p h d", h=8)
        A("pool", lambda e, c=c: e.tensor_copy(out=wkp[:, c, :, 0:64], in_=W3[:, :, 0:64]), reads=[("wst", sl), "wkp"], writes=["wkp"])
        A("pool", lambda e, c=c: e.tensor_copy(out=wv[:, c, :].rearrange("p (h d) -> p h d", h=8), in_=W3[:, :, 64:128]), reads=[("wst", sl)], writes=["wv"])

    SB_, OB_, KB_, VB_ = (0, 1, 2), (3, 4), (5, 6), 7
    SCALE = 96.0 ** -0.5
    kbs = {"i": 0}
    for h in range(8):
        bf = h % 2
        for grp in range(16):
            kb = KB_[kbs["i"] % 2]
            kbs["i"] += 1
            gs = slice(grp * 512, (grp + 1) * 512)
            A("pe", lambda e: e.matmul(ps[kb][0:96, :], lhsT=wkp[:, 0, h, :], rhs=ckvT[:, 0, gs], start=True, stop=False), reads=["wkp", "ckvT"], writes=[("ps", kb)])
            A("pe", lambda e: e.matmul(ps[kb][0:96, :], lhsT=wkp[:, 1, h, :], rhs=ckvT[:, 1, gs], start=False, stop=False), reads=["wkp", "ckvT"], writes=[("ps", kb)])
            A("pe", lambda e: e.matmul(ps[kb][0:96, :], lhsT=selb[:], rhs=kropeT[:, gs], start=False, stop=True), reads=["selb", "kropeT"], writes=[("ps", kb)])
            A("dve", lambda e: e.tensor_copy(out=KT[bf][:, gs], in_=ps[kb][0:96, :]), reads=[("ps", kb)], writes=[("KT", 0)])
        for tg in range(8):
            for tl in range(8):
                kt = tg * 8 + tl
                for c in range(2):
                    A("pe", lambda e, c=c, tl=tl, kt=kt: e.matmul(ps[VB_][:, tl * 64:(tl + 1) * 64], lhsT=ckvT[:, c, kt * 128:(kt + 1) * 128],
                                                                 rhs=wv[:, c, h * 64:(h + 1) * 64], start=(c == 0), stop=(c == 1)),
                      reads=["ckvT", "wv"], writes=[("ps", VB_)])
            A("act", lambda e: e.copy(out=VA[bf][:, tg * 8:(tg + 1) * 8, 0:64], in_=ps[VB_][:].rearrange("p (t d) -> p t d", t=8)), reads=[("ps", VB_)], writes=[("VA", bf)])
        for tt in range(4):
            ob = OB_[(h * 4 + tt) % 2]
            qs = slice(tt * 512, (tt + 1) * 512)

            def s_mm(st_):
                sb_ = SB_[st_ % 3]
                A("pe", lambda e: e.matmul(ps[sb_][:], lhsT=KT[bf][:, st_ * 128:(st_ + 1) * 128], rhs=QT[:, h, qs], start=True, stop=True),
                  reads=[("KT", 0), "QT"], writes=[("ps", sb_)])
                A("act", lambda e: e.activation(out=Pb[st_ % 3][:], in_=ps[sb_][:], func=AF.Exp, scale=SCALE), reads=[("ps", sb_)], writes=[("Pb", st_ % 3)])

            s_mm(0)
            s_mm(1)
            for st_ in range(64):
                for qq in range(4):
                    A("pe", lambda e, qq=qq: e.matmul(ps[ob][:, qq * 65:(qq + 1) * 65], lhsT=Pb[st_ % 3][:, qq * 128:(qq + 1) * 128], rhs=VA[bf][:, st_, :],
                                                      start=(st_ == 0 and qq == 0), stop=(st_ == 63), skip_group_check=True),
                      reads=[("Pb", st_ % 3), ("VA", bf), ("VA1", bf)], writes=[("ps", ob)])
                if st_ + 2 < 64:
                    s_mm(st_ + 2)
            O3 = ps[ob][:, 0:260].rearrange("p (q d) -> p q d", q=4)
            A("dve", lambda e: e.reciprocal(out=rden[:], in_=O3[:, :, 64]), reads=[("ps", ob)], writes=["rden"])
            for qq in range(4):
                A("dve", lambda e, qq=qq: e.tensor_scalar(out=yattn[:, tt * 4 + qq, h * 64:(h + 1) * 64], in0=O3[:, qq, 0:64], scalar1=rden[:, qq:qq + 1],
                                                          scalar2=None, op0=ALU.mult), reads=[("ps", ob), "rden"], writes=["yattn"])
    for tile in range(16):
        pb = psb(VB_)
        for c in range(4):
            A("pe", lambda e, c=c: e.transpose(out=pb[:, c * 128:(c + 1) * 128], in_=yattn[:, tile, c * 128:(c + 1) * 128], identity=idb[:]),
              reads=["yattn", "idb"], writes=[("ps", VB_)])
        A("act", lambda e: e.copy(out=yaT[:, :, tile * 128:(tile + 1) * 128], in_=pb[:, 0:512].rearrange("p (c t) -> p c t", c=4)), reads=[("ps", VB_)], writes=["yaT"])

    S.barrier(lambda e: e.memset(sst[:, 7:8], 0.0))
    al.regions = [[m_phase, m_alias], [reg_p4[1][0], SB_HI]]
    wgab = R([128, 8, 2048], BF16)
    wbm = R([128, 4, 1024], BF16)
    wbl = R([128, 4, 1024], BF16)
    wo = R([128, 8, 1024], BF16)
    hT6 = [R([128, 8, 128], BF16) for _ in range(2)]
    sg = R([128, 2048], BF16)
    t1 = R([128, 1024], F32)
    t2 = R([128, 1024], F32)
    mrg = R([128, 1024], BF16)
    mT = R([128, 8, 128], BF16)
    x1t = [R([128, 1024], F32) for _ in range(2)]
    load_w(wgab, "wgab", w_in, 8, 2736, 4784)
    load_w(wbm, "wbm", w_bm, 4, 0, 1024)
    load_w(wbl, "wbl", w_bl, 4, 0, 1024)
    load_w(wo, "wo", w_out, 8, 0, 1024)
    for ot in range(16):
        z = ot % 2
        tsl = slice(ot * 128, (ot + 1) * 128)
        sl = xpipe(xown[tsl, :], gmix, "gmix", hT6[z][:], ("hT6", z), 7)
        for cb_ in range(4):
            for k in range(8):
                A("pe", lambda e, k=k, cb_=cb_: e.matmul(ps[cb_][:], lhsT=hT6[z][:, k, :], rhs=wgab[:, k, cb_ * 512:(cb_ + 1) * 512], start=(k == 0), stop=(k == 7)),
                  reads=[("hT6", z), ("wgab", k)], writes=[("ps", cb_)])
            A("act", lambda e, cb_=cb_: e.activation(out=sg[:, cb_ * 512:(cb_ + 1) * 512], in_=ps[cb_][:], func=AF.Sigmoid), reads=[("ps", cb_)], writes=[("sg", cb_)])
        for half in range(2):
            for (wt, wk, aT_, ak, bank0) in ((wbm, "wbm", yaT, "yaT", 4), (wbl, "wbl", ymT, "ymT", 5)):
                bank = bank0
                for c in range(4):
                    A("pe", lambda e, c=c, wt=wt, aT_=aT_, bank=bank: e.matmul(ps[bank][:], lhsT=aT_[:, c, tsl], rhs=wt[:, c, half * 512:(half + 1) * 512],
                                                                               start=(c == 0), stop=(c == 3)), reads=[ak, (wk, c)], writes=[("ps", bank)])
            hs = slice(half * 512, (half + 1) * 512)
            A("dve", lambda e: e.tensor_tensor(out=t1[:, hs], in0=ps[4][:], in1=sg[:, half * 512:(half + 1) * 512], op=ALU.mult), reads=[("ps", 4), ("sg", half)], writes=[("t1", half)])
            A("dve", lambda e: e.tensor_tensor(out=t2[:, hs], in0=ps[5][:], in1=sg[:, 1024 + half * 512:1024 + (half + 1) * 512], op=ALU.mult),
              reads=[("ps", 5), ("sg", 2 + half)], writes=[("t2", half)])
            A("pool", lambda e: e.tensor_tensor(out=mrg[:, hs], in0=t1[:, hs], in1=t2[:, hs], op=ALU.add), reads=[("t1", half), ("t2", half)], writes=[("mrg", half)])
        pb = psb(6)
        for k in range(8):
            A("pe", lambda e, k=k: e.transpose(out=pb[:, k * 128:(k + 1) * 128], in_=mrg[:, k * 128:(k + 1) * 128], identity=idb[:]),
              reads=[("mrg", k // 4), "idb"], writes=[("ps", 6)])
        A("act", lambda e: e.copy(out=mT[:], in_=pb.rearrange("p (k t) -> p k t", k=8)), reads=[("ps", 6)], writes=["mT"])
        for half in range(2):
            bank = 4 + half
            for k in range(8):
                A("pe", lambda e, k=k, bank=bank, half=half: e.matmul(ps[bank][:], lhsT=mT[:, k, :], rhs=wo[:, k, half * 512:(half + 1) * 512], start=(k == 0), stop=(k == 7)),
                  reads=["mT", ("wo", k)], writes=[("ps", bank)])
            A("dve", lambda e, bank=bank, half=half: e.tensor_tensor(out=x1t[z][:, half * 512:(half + 1) * 512], in0=ps[bank][:], in1=xt[sl][:, half * 512:(half + 1) * 512], op=ALU.add),
              reads=[("ps", bank), ("xt", sl)], writes=[("x1t", z)])
        A("sp", lambda e: e.dma_start(out=x1d[tsl, :], in_=x1t[z][:]), reads=[("x1t", z)], writes=["x1d"], dma=True)

    S.barrier(lambda e: e.memset(sst[:, 7:8], 0.0))
    al.regions = [[m_phase, SB_HI]]
    wup = R([128, 8, 4096], BF16)
    wdn = R([128, 32, 1024], BF16)
    gmlp = R([128, 1024], F32)
    gfin = R([128, 1024], F32)
    hTm = [R([128, 8, 256], BF16) for _ in range(2)]
    aT = R([128, 32, 256], BF16)
    rr = [R([128, 256], F32) for _ in range(2)]
    xres = [R([128, 1024], F32) for _ in range(2)]
    otile = xres
    A("sp", lambda e: e.dma_start(out=gmlp[:], in_=gmlpd), writes=["gmlp"], dma=True)
    A("sp", lambda e: e.dma_start(out=gfin[:], in_=gfind), writes=["gfin"], dma=True)
    load_w(wup, "wup", w_up, 8, 0, 4096)
    load_w(wdn, "wdn", w_down, 32, 0, 1024)
    for g in range(8):
        z = g % 2
        for i in range(2):
            r0 = g * 256 + i * 128
            xpipe(x1d[r0:r0 + 128, :], gmlp, "gmlp", hTm[z][:, :, i * 128:(i + 1) * 128], ("hTm", z), 7)
        for f in range(32):
            bank = f % 2
            for k in range(8):
                A("pe", lambda e, k=k, f=f, bank=bank: e.matmul(ps[bank][:, 0:256], lhsT=wup[:, k, f * 128:(f + 1) * 128], rhs=hTm[z][:, k, :], start=(k == 0), stop=(k == 7)),
                  reads=[("hTm", z), ("wup", k)], writes=[("ps", bank)])
            A("act", lambda e, bank=bank: e.activation(out=rr[bank][:], in_=ps[bank][:, 0:256], func=AF.Relu), reads=[("ps", bank)], writes=[("rr", bank)])
            A("dve", lambda e, f=f, bank=bank: e.tensor_tensor(out=aT[:, f, :], in0=rr[bank][:], in1=rr[bank][:], op=ALU.mult), reads=[("rr", bank)], writes=["aT"])
        for i in range(2):
            r0 = g * 256 + i * 128
            zz = (g * 2 + i) % 2
            A("sp", lambda e, r0=r0, zz=zz: e.dma_start(out=xres[zz][:], in_=x1d[r0:r0 + 128, :]), reads=["x1d"], writes=[("xres", zz)], dma=True)
            for half in range(2):
                bank = 2 + half
                for f in range(32):
                    A("pe", lambda e, f=f, bank=bank, half=half, i=i: e.matmul(ps[bank][:], lhsT=aT[:, f, i * 128:(i + 1) * 128], rhs=wdn[:, f, half * 512:(half + 1) * 512],
                                                                               start=(f == 0), stop=(f == 31)), reads=["aT", ("wdn", f)], writes=[("ps", bank)])
                A("dve", lambda e, bank=bank, half=half, zz=zz: e.tensor_tensor(out=xres[zz][:, half * 512:(half + 1) * 512], in0=ps[bank][:], in1=xres[zz][:, half * 512:(half + 1) * 512], op=ALU.add),
                  reads=[("ps", bank), ("xres", zz)], writes=[("xres", zz)])
            ssap = sst[:, 4 + zz:5 + zz]
            A("act", lambda e, zz=zz, ssap=ssap: e.activation(out=junk[:], in_=xres[zz][:], func=AF.Square, accum_out=ssap), reads=[("xres", zz)], writes=[("ssf", zz)])
            rstd_of(ssap, 1024, ("ssf", zz))
            A("dve", lambda e, zz=zz, ssap=ssap: e.scalar_tensor_tensor(out=otile[zz][:], in0=xres[zz][:], scalar=ssap, in1=gfin[:], op0=ALU.mult, op1=ALU.mult),
              reads=[("xres", zz), ("ssf", zz), "gfin"], writes=[("xres", zz)])
            A("sp", lambda e, r0=r0, zz=zz: e.dma_start(out=y[r0:r0 + 128, :], in_=otile[zz][:]), reads=[("xres", zz)], writes=[("yout", g * 2 + i)], dma=True)
    A("sp", lambda e: e.nop(), reads=[("yout", i_) for i_ in range(16)])
    S.emit(nc, st)
    st.close()
    return nc


def make_consts():
    c = np.zeros((128, 880), np.float32)
    r = np.arange(128)
    c[:, 0:128] = np.eye(128)
    c[:, 128:256] = (r[:, None] <= r[None, :])
    c[:, 256:384] = (r[:, None] >= r[None, :])
    c[:, 384:512] = (r[:, None] > r[None, :])
    c[:, 512:640] = (r[:, None] < r[None, :])
    c[:, 640:768] = 1.0
    for i in range(32):
        c[i, 784 + 64 + i] = 1.0
    return c


def bc(v, n=128):
    return np.ascontiguousarray(np.broadcast_to(np.asarray(v, np.float32).reshape(1, -1), (n, np.asarray(v).size)))


def prep_inputs(inp, core):
    b, j = divmod(core, 4)
    x = inp["x"][b]
    pos = inp["positions"][b]
    o0, o1 = NOWN * j, NOWN * (j + 1)
    xo = np.concatenate([x[:o0], x[o1:]], axis=0)
    xown = x[o0:o1]
    xh = np.zeros((128, 1024), np.float32)

    def row(n):
        return x[n] if 0 <= n < 8192 else np.zeros(1024, np.float32)

    for g in range(12):
        n0 = 512 * g if 512 * g < o0 else 512 * g + NOWN
        for q, n in enumerate((n0 - 2, n0 - 1, n0 + 512, n0 + 513)):
            xh[4 * g + q] = row(n)
    for g in range(4):
        n0 = o0 + 512 * g
        for q, n in enumerate((n0 - 2, n0 - 1, n0 + 512, n0 + 513)):
            xh[48 + 4 * g + q] = row(n)
    pos_all = np.concatenate([pos[:o0], pos[o1:], pos[o0:o1]])
    posT = np.concatenate([pos_all.reshape(64, 128).T, pos[o0:o1].reshape(16, 128).T], axis=1).astype(np.int32)
    tfv = (np.arange(48) < 16 * j).astype(np.float32)
    igb = inp["mlstm_igate_b"][0]
    fgb = inp["mlstm_fgate_b"][0]
    gb16 = np.concatenate([igb[0], igb[1], fgb[0], fgb[1]])
    cwv = inp["mlstm_conv_w"][0][:, 0, :]
    cw = np.ascontiguousarray(cwv.reshape(5, 8, 128).transpose(2, 1, 0)).reshape(128, 40)
    cb = np.ascontiguousarray(inp["mlstm_conv_b"][0].reshape(8, 128).T)
    d = {
        "xo": np.ascontiguousarray(xo), "xown": np.ascontiguousarray(xown), "xh": xh,
        "posT": np.ascontiguousarray(posT), "tf": bc(tfv), "cst": make_consts(),
        "gmix": bc(inp["norm_mix_g"][0]), "gmlp": bc(inp["norm_mlp_g"][0]), "gfin": bc(inp["norm_final_g"]),
        "gq": bc(inp["mla_q_norm_g"][0]), "gkv": bc(inp["mla_kv_norm_g"][0]), "gon": bc(inp["mlstm_out_norm_g"][0]),
        "gb": bc(np.tile(gb16, 4)), "cw": cw.astype(np.float32), "cb": cb.astype(np.float32),
        "w_in": inp["w_in"][0], "w_uq": inp["mla_w_uq"][0], "w_ukv": inp["mla_w_ukv"][0],
        "w_bm": inp["w_branch_mla"][0], "w_bl": inp["w_branch_mlstm"][0], "w_out": inp["w_out"][0],
        "w_up": inp["w_mlp_up"][0], "w_down": inp["w_mlp_down"][0],
    }
    return {k: np.ascontiguousarray(v) for k, v in d.items()}


def run(inputs, debug=None, cores=8):
    inp = {k: np.asarray(v) for k, v in inputs.items()}
    nc = build_program(debug)
    in_maps = [prep_inputs(inp, c) for c in range(cores)]
    res = run_bass_kernel_spmd(nc, in_maps, core_ids=list(range(cores)))
    return res


def kernel(**inputs):
    res = run(inputs)
    out = np.zeros((2, 8192, 1024), np.float32)
    for c in range(8):
        b, j = divmod(c, 4)
        out[b, NOWN * j:NOWN * (j + 1)] = res.results[c]["y"]
    return out
```

```python
import math
from contextlib import ExitStack
import numpy as np
import concourse.bass as bass
import concourse.mybir as mybir
from concourse.bass_utils import run_bass_kernel_spmd

F32 = mybir.dt.float32
BF16 = mybir.dt.bfloat16
I32 = mybir.dt.int32
AF = mybir.ActivationFunctionType
ALU = mybir.AluOpType

SEM_LIMIT = 20000
N_DSEM = 24
SB_LO = 16512
SB_HI = 229376
NOWN = 2048
NOTH = 6144
EPS = 1e-6
LNSC = -0.5 * math.log(128.0)
BIG = 30000.0


class Op:
    __slots__ = ("eng", "fn", "deps", "signal", "is_dma", "idx", "sem", "val", "dslot")

    def __init__(self, eng, fn, is_dma):
        self.eng = eng
        self.fn = fn
        self.deps = []
        self.signal = False
        self.is_dma = is_dma
        self.sem = None
        self.val = None
        self.dslot = None


class _Rec:
    def __init__(self):
        self.call = None

    def __getattr__(self, name):
        def f(*a, **k):
            self.call = (name, a, k)
            return self
        return f


class Sched:
    def __init__(self):
        self.ops = []
        self.last_w = {}
        self.readers = {}
        self.n_dma = 0
        self.dslot_last = {}
        self.fence_op = None

    def add(self, eng, fn, reads=(), writes=(), dma=False):
        rec = _Rec()
        fn(rec)
        call = rec.call
        op = Op(eng, call, dma)
        psk = [k for k in reads if isinstance(k, tuple) and k[0] == "ps"]
        if psk:
            reads = [k for k in reads if k not in psk]
            writes = list(writes) + psk
        deps = set()
        for k in reads:
            w = self.last_w.get(k)
            if w is not None:
                deps.add(w)
        for k in writes:
            w = self.last_w.get(k)
            if w is not None:
                deps.add(w)
            for r in self.readers.get(k, ()):
                deps.add(r)
        if self.fence_op is not None:
            deps.add(self.fence_op)
        if dma:
            slot = (eng, self.n_dma % N_DSEM)
            self.n_dma += 1
            prev = self.dslot_last.get(slot)
            if prev is not None:
                deps.add(prev)
            self.dslot_last[slot] = op
            op.dslot = slot
        for d in deps:
            if d is op:
                continue
            if (not d.is_dma) and d.eng == "pe" and eng == "pe" and not dma:
                continue
            d.signal = True
            op.deps.append(d)
        for k in reads:
            self.readers.setdefault(k, []).append(op)
        for k in writes:
            self.last_w[k] = op
            self.readers[k] = []
        self.ops.append(op)
        return op

    def barrier(self, fn):
        keys = set(self.last_w.keys()) | set(self.readers.keys())
        self.fence_op = None
        op = self.add("dve", fn, writes=list(keys))
        for o in self.dslot_last.values():
            if o is not op and o not in op.deps:
                o.signal = True
                op.deps.append(o)
        self.fence_op = op
        return op

    def emit(self, nc, stack):
        engs = ["pe", "act", "dve", "pool", "sp"]
        counts = {e: 0 for e in engs}
        dcount = {}
        for op in self.ops:
            if op.is_dma:
                c = dcount.get(op.dslot, 0) + 1
                dcount[op.dslot] = c
                op.sem = ("d", op.dslot)
                op.val = 16 * c
            elif op.signal:
                c = counts[op.eng]
                counts[op.eng] = c + 1
                op.sem = (op.eng, c // SEM_LIMIT)
                op.val = c % SEM_LIMIT + 1
        sems = {}
        for op in self.ops:
            if op.sem is not None and op.sem not in sems:
                sems[op.sem] = stack.enter_context(nc.semaphore("s_%d" % len(sems)))
        block = stack.enter_context(nc.Block())
        ops = self.ops

        def run(engname, e):
            waited = {}
            for op in ops:
                if op.eng != engname:
                    continue
                for d in op.deps:
                    key = d.sem
                    if waited.get(key, 0) >= d.val:
                        continue
                    e.wait_ge(sems[key], d.val)
                    waited[key] = d.val
                name, a_, k_ = op.fn
                ins = getattr(e, name)(*a_, **k_)
                if op.is_dma:
                    ins.then_inc(sems[op.sem], 16)
                elif op.signal:
                    ins.then_inc(sems[op.sem], 1)

        @block.tensor
        def _(e):
            run("pe", e)

        @block.scalar
        def _(e):
            run("act", e)

        @block.vector
        def _(e):
            run("dve", e)

        @block.gpsimd
        def _(e):
            run("pool", e)

        @block.sync
        def _(e):
            run("sp", e)


class Alloc:
    def __init__(self, nc):
        self.nc = nc
        self.base = SB_LO
        self.top = SB_LO
        self.n = 0

    def mark(self):
        return self.top

    def reset(self, m):
        self.top = m

    def set_regions(self, regions):
        self.regions = [list(r) for r in regions]

    def ralloc(self, shape, dt):
        nb = 1
        for s_ in shape[1:]:
            nb *= s_
        nb *= 2 if dt == BF16 else 4
        nb = (nb + 63) // 64 * 64
        for r in self.regions:
            if r[0] + nb <= r[1]:
                off = r[0]
                r[0] += nb
                self.n += 1
                return self.nc.alloc_sbuf_tensor_at("t%d" % self.n, list(shape), dt, offset=off)
        raise AssertionError(("SBUF overflow", shape, self.regions))

    def __call__(self, shape, dt):
        nb = 1
        for s in shape[1:]:
            nb *= s
        nb *= 2 if dt == BF16 else 4
        nb = (nb + 63) // 64 * 64
        off = self.top
        self.top += nb
        assert self.top <= SB_HI, ("SBUF overflow", self.top)
        self.n += 1
        return self.nc.alloc_sbuf_tensor_at("t%d" % self.n, list(shape), dt, offset=off)


def build_program(debug=None):
    nc = bass.Bass("TRN2", target_bir_lowering=False)

    def din(name, shape, dt=F32):
        return nc.dram_tensor(name, list(shape), dt, kind="ExternalInput").ap()

    xo = din("xo", [NOTH, 1024])
    xown = din("xown", [NOWN, 1024])
    xh = din("xh", [128, 1024])
    posT = din("posT", [128, 80], I32)
    tfd = din("tf", [128, 48])
    cstd = din("cst", [128, 880])
    gmixd = din("gmix", [128, 1024])
    gmlpd = din("gmlp", [128, 1024])
    gfind = din("gfin", [128, 1024])
    gqd = din("gq", [128, 384])
    gkvd = din("gkv", [128, 256])
    gond = din("gon", [128, 512])
    gbd = din("gb", [128, 64])
    cwd = din("cw", [128, 40])
    cbd = din("cb", [128, 8])
    w_in = din("w_in", [1024, 4784])
    w_uq = din("w_uq", [384, 768])
    w_ukv = din("w_ukv", [256, 1024])
    w_bm = din("w_bm", [512, 1024])
    w_bl = din("w_bl", [512, 1024])
    w_out = din("w_out", [1024, 1024])
    w_up = din("w_up", [1024, 4096])
    w_down = din("w_down", [4096, 1024])
    y = nc.dram_tensor("y", [NOWN, 1024], F32, kind="ExternalOutput").ap()
    x1d = nc.dram_tensor("x1d", [NOWN, 1024], F32).ap()
    dbg = None
    if debug:
        dbg = nc.dram_tensor("dbg", [NOWN, 1024], F32, kind="ExternalOutput").ap()

    S = Sched()
    A = S.add
    al = Alloc(nc)
    st = ExitStack()
    psbig = [st.enter_context(nc.psum_tensor("psb%d" % i, [128, 1024], F32)) for i in range(4)]
    ps = [psbig[i // 2][:, (i % 2) * 512:(i % 2 + 1) * 512] for i in range(8)]

    def psb(i):
        return ps[i][:].bitcast(BF16)

    cst = al([128, 880], F32)
    idb = al([128, 128], BF16)
    mLEb = al([128, 128], BF16)
    mGEb = al([128, 128], BF16)
    gmix = al([128, 1024], F32)
    xt = [al([128, 1024], F32) for _ in range(2)]
    junk = al([128, 1024], BF16)
    hb = [al([128, 1024], BF16) for _ in range(2)]
    sst = al([128, 8], F32)
    wst = [al([128, 1024], F32) for _ in range(2)]
    LNSCt = al([128, 1], F32)
    ONEt = al([128, 1], F32)
    EPSt = al([128, 1], F32)
    idf = cst[:, 0:128]
    mLE = cst[:, 128:256]
    mGE = cst[:, 256:384]
    mSU = cst[:, 384:512]
    mSL = cst[:, 512:640]
    ones = cst[:, 640:768]
    sel = cst[0:32, 784:880]

    A("sp", lambda e: e.dma_start(out=cst[:], in_=cstd), writes=["cst"], dma=True)
    A("sp", lambda e: e.dma_start(out=gmix[:], in_=gmixd), writes=["gmix"], dma=True)
    A("dve", lambda e: e.memset(ONEt[:], 1.0), writes=["onet"])
    A("dve", lambda e: e.memset(EPSt[:], EPS), writes=["epst"])
    A("dve", lambda e: e.tensor_copy(out=idb[:], in_=idf), reads=["cst"], writes=["idb"])
    A("dve", lambda e: e.tensor_copy(out=mLEb[:], in_=mLE), reads=["cst"], writes=["mLEb"])
    A("dve", lambda e: e.tensor_copy(out=mGEb[:], in_=mGE), reads=["cst"], writes=["mGEb"])

    wstate = {"i": 0}

    def load_w(dst, dkey, src, K, c_lo, c_hi, dcol=0, queue="sp"):
        for k in range(K):
            c0 = c_lo
            while c0 < c_hi:
                cc = min(1024, c_hi - c0)
                sl = wstate["i"] % 2
                wstate["i"] += 1
                A(queue, lambda e, sl=sl, k=k, c0=c0, cc=cc: e.dma_start(out=wst[sl][:, 0:cc], in_=src[k * 128:(k + 1) * 128, c0:c0 + cc]),
                  writes=[("wst", sl)], dma=True)
                d0 = dcol + (c0 - c_lo)
                A("pool", lambda e, sl=sl, k=k, d0=d0, cc=cc: e.tensor_copy(out=dst[:, k, d0:d0 + cc], in_=wst[sl][:, 0:cc]),
                  reads=[("wst", sl)], writes=[(dkey, k)])
                c0 += cc

    xstate = {"i": 0}

    def rstd_of(sskey_ap, n, key):
        A("act", lambda e: e.activation(out=sskey_ap, in_=sskey_ap, func=AF.Ln, scale=1.0 / n, bias=EPSt[:, 0:1]), reads=[key, "epst"], writes=[key])
        A("act", lambda e: e.activation(out=sskey_ap, in_=sskey_ap, func=AF.Exp, scale=-0.5), reads=[key], writes=[key])

    def xpipe(src_rows, g_tile, gkey, hT_dst, hkey, tps, keep=False):
        i = xstate["i"]
        xstate["i"] += 1
        sl = i % 2
        A("sp", lambda e: e.dma_start(out=xt[sl][:], in_=src_rows), writes=[("xt", sl)], dma=True)
        ssap = sst[:, sl:sl + 1]
        A("act", lambda e: e.activation(out=junk[:], in_=xt[sl][:], func=AF.Square, accum_out=ssap), reads=[("xt", sl)], writes=[("ss", sl)])
        rstd_of(ssap, 1024, ("ss", sl))
        A("dve", lambda e: e.scalar_tensor_tensor(out=hb[sl][:], in0=xt[sl][:], scalar=ssap, in1=g_tile[:], op0=ALU.mult, op1=ALU.mult),
          reads=[("xt", sl), ("ss", sl), gkey], writes=[("hb", sl)])
        pb = psb(tps)
        for k in range(8):
            A("pe", lambda e, k=k: e.transpose(out=pb[:, k * 128:(k + 1) * 128], in_=hb[sl][:, k * 128:(k + 1) * 128], identity=idb[:]),
              reads=[("hb", sl), "idb"], writes=[("ps", tps)])
        A("act", lambda e: e.copy(out=hT_dst, in_=pb.rearrange("p (k t) -> p k t", k=8)), reads=[("ps", tps)], writes=[hkey])
        return sl

    m_phase = al.mark()
    LGe = [al([128, 8], F32) for _ in range(2)]
    Ie = [al([128, 8], F32) for _ in range(2)]
    exg = [al([128, 8], F32) for _ in range(2)]
    Wt = [al([128, 8], F32) for _ in range(2)]
    dect = [al([128, 4], F32) for _ in range(2)]
    Arun = al([128, 4], F32)
    ktil = [al([128, 128], BF16) for _ in range(4)]
    Cf = al([128, 4, 129], F32)
    Cb = al([128, 4, 129], F32)
    Cfb = al([128, 4, 129], BF16)
    Cbb = al([128, 4, 129], BF16)
    qT = al([128, 4, NOWN], BF16)
    kT = al([128, 4, NOWN], BF16)
    ktok = al([128, 16, 512], BF16)
    vaug = al([128, 16, 4, 129], BF16)
    sgo = al([128, 16, 512], BF16)
    Gown = al([128, 16, 16], F32)
    LFo = al([128, 16, 8], F32)
    m_alias = al.mark()
    wl = al([128, 8, 2064], BF16)
    tf = al([128, 48], F32)
    tb = al([128, 48], F32)
    pf = al([128, 48], F32)
    pbk = al([128, 48], F32)
    gbias = al([128, 64], F32)
    cw = al([128, 40], F32)
    cb = al([128, 8], F32)
    gon = al([128, 512], F32)
    haloT = al([128, 8, 128], F32)
    hTg = [al([128, 8, 512], BF16) for _ in range(2)]
    padb = [al([128, 516], F32) for _ in range(2)]
    acc = [al([128, 512], F32) for _ in range(2)]
    kTg1 = al([128, 4, 512], BF16)
    ktokg1 = al([128, 4, 512], BF16)
    vaugg1 = al([128, 4, 4, 129], BF16)
    kTg = [kTg1, kTg1]
    ktokg = [ktokg1, ktokg1]
    vaugg = [vaugg1, vaugg1]
    Gg = [al([128, 4, 16], F32) for _ in range(2)]
    LFg = [al([128, 4, 8], F32) for _ in range(2)]
    sgt = [al([128, 512], BF16) for _ in range(2)]
    m_p12 = al.mark()
    al.reset(m_alias)
    ymT = al([128, 4, NOWN], BF16)
    m_keep = al.mark()
    EBt = al([128, 16, 8], F32)
    ECt = al([128, 16, 8], F32)
    WSt = al([128, 16, 8], F32)
    DECt = al([128, 16, 8], F32)
    cmt = al([128, 8], F32)
    HB = al([128, 16, 512], F32)
    PT = [al([128, 4, 128], BF16) for _ in range(2)]
    den = al([128, 4], F32)
    scl = al([128, 4], F32)
    hsum = [al([128, 4, 128], F32) for _ in range(2)]
    ssq = al([128, 4], F32)
    ymt = [al([128, 4, 128], BF16) for _ in range(2)]
    assert al.mark() <= m_p12
    al.reset(m_p12)

    for (t_, d_, k_) in ((tf, tfd, "tf"), (gbias, gbd, "gbias"), (cw, cwd, "cw"), (cb, cbd, "cb"), (gon, gond, "gon")):
        A("sp", lambda e, t_=t_, d_=d_: e.dma_start(out=t_[:], in_=d_), writes=[k_], dma=True)
    A("dve", lambda e: e.tensor_scalar(out=tb[:], in0=tf[:], scalar1=-1.0, scalar2=1.0, op0=ALU.mult, op1=ALU.add), reads=["tf"], writes=["tb"])
    A("dve", lambda e: e.tensor_scalar(out=pf[:], in0=tf[:], scalar1=-1.0, scalar2=BIG, op0=ALU.add, op1=ALU.mult), reads=["tf"], writes=["pf"])
    A("dve", lambda e: e.tensor_scalar(out=pbk[:], in0=tf[:], scalar1=-BIG, scalar2=None, op0=ALU.mult), reads=["tf"], writes=["pbk"])
    for t_, k_ in ((Cf, "Cf"), (Cb, "Cb"), (Arun, "Arun")):
        A("dve", lambda e, t_=t_: e.memset(t_[:], 0.0), writes=[k_])
    A("pool", lambda e: e.memset(vaug[:, :, :, 128:129], 1.0), writes=["vaug1"])
    A("pool", lambda e: e.memset(vaugg1[:, :, :, 128:129], 1.0), writes=[("vaugg1", 0)])

    load_w(wl, "wl", w_in, 8, 672, 2720, 0)
    for (s0, d0) in ((2720, 2048), (2728, 2052), (2724, 2056), (2732, 2060)):
        load_w(wl, "wl", w_in, 8, s0, s0 + 4, d0)
    WLK = [("wl", k) for k in range(8)]

    TPSX, TPSK, CPS0, CPS1, MVPS, UPS0 = 0, 1, 2, 3, 4, 5

    hTh = hTg[1][:, :, 0:128]
    xpipe(xh, gmix, "gmix", hTh, ("hTg", 1), TPSX)
    for c in range(8):
        bank = CPS0 + (c % 2)
        for k in range(8):
            A("pe", lambda e, c=c, k=k, bank=bank: e.matmul(ps[bank][:, 0:128], lhsT=wl[:, k, c * 128:(c + 1) * 128], rhs=hTg[1][:, k, 0:128],
                                                           start=(k == 0), stop=(k == 7)),
              reads=[("hTg", 1), ("wl", k)], writes=[("ps", bank)])
        A("act", lambda e, c=c, bank=bank: e.copy(out=haloT[:, c, :], in_=ps[bank][:, 0:128]), reads=[("ps", bank)], writes=["haloT"])

    cstate = {"i": 0}

    def conv_chunk(par, wcol, chunk, hidx, dst_ap, dst_key):
        ci = cstate["i"]
        cstate["i"] += 1
        bank = CPS0 + (ci % 2)
        pz = ci % 2
        for k in range(8):
            A("pe", lambda e, k=k: e.matmul(ps[bank][:], lhsT=wl[:, k, wcol:wcol + 128], rhs=hTg[par][:, k, :], start=(k == 0), stop=(k == 7)),
              reads=[("hTg", par), ("wl", k)], writes=[("ps", bank)])
        A("act", lambda e: e.copy(out=padb[pz][:, 2:514], in_=ps[bank][:]), reads=[("ps", bank)], writes=[("padb", pz)])
        A("pool", lambda e: e.tensor_copy(out=padb[pz][:, 0:2], in_=haloT[:, chunk, hidx:hidx + 2]), reads=["haloT"], writes=[("padbL", pz)])
        A("pool", lambda e: e.tensor_copy(out=padb[pz][:, 514:516], in_=haloT[:, chunk, hidx + 2:hidx + 4]), reads=["haloT"], writes=[("padbR", pz)])
        rk = [("padb", pz), ("padbL", pz), ("padbR", pz), "cw", "cb"]
        A("dve", lambda e: e.tensor_scalar(out=acc[pz][:], in0=padb[pz][:, 0:512], scalar1=cw[:, chunk * 5:chunk * 5 + 1], scalar2=cb[:, chunk:chunk + 1],
                                           op0=ALU.mult, op1=ALU.add), reads=rk, writes=[("acc", pz)])
        for j in range(1, 5):
            A("dve", lambda e, j=j: e.scalar_tensor_tensor(out=acc[pz][:], in0=padb[pz][:, j:j + 512], scalar=cw[:, chunk * 5 + j:chunk * 5 + j + 1],
                                                           in1=acc[pz][:], op0=ALU.mult, op1=ALU.add), reads=rk + [("acc", pz)], writes=[("acc", pz)])
        A("act", lambda e: e.activation(out=dst_ap, in_=acc[pz][:], func=AF.Silu), reads=[("acc", pz)], writes=[dst_key])

    def mlstm_group(gi, own):
        par = gi % 2
        src = xown if own else xo
        g0 = (gi - 12) if own else gi
        for i in range(4):
            r0 = g0 * 512 + i * 128
            xpipe(src[r0:r0 + 128, :], gmix, "gmix", hTg[par][:, :, i * 128:(i + 1) * 128], ("hTg", par), TPSX)
        hidx = (48 + 4 * g0) if own else 4 * g0
        for h in range(4):
            if own:
                conv_chunk(par, h * 128, h, hidx, qT[:, h, g0 * 512:(g0 + 1) * 512], "qT")
                conv_chunk(par, 512 + h * 128, 4 + h, hidx, kT[:, h, g0 * 512:(g0 + 1) * 512], "kT")
            else:
                conv_chunk(par, 512 + h * 128, 4 + h, hidx, kTg[par][:, h, :], ("kTg", 0))
        for b2 in range(2):
            pb = psb(TPSK)
            for bb in range(2):
                blk = b2 * 2 + bb
                for h in range(4):
                    if own:
                        src_ap = kT[:, h, g0 * 512 + blk * 128:g0 * 512 + (blk + 1) * 128]
                        rkey = "kT"
                    else:
                        src_ap = kTg[par][:, h, blk * 128:(blk + 1) * 128]
                        rkey = ("kTg", 0)
                    o0 = (bb * 4 + h) * 128
                    A("pe", lambda e, src_ap=src_ap, o0=o0: e.transpose(out=pb[:, o0:o0 + 128], in_=src_ap, identity=idb[:]),
                      reads=[rkey, "idb"], writes=[("ps", TPSK)])
            if own:
                dst = ktok[:, g0 * 4 + b2 * 2:g0 * 4 + b2 * 2 + 2, :]
                dk = "ktok"
            else:
                dst = ktokg[par][:, b2 * 2:b2 * 2 + 2, :]
                dk = ("ktokg", 0)
            A("act", lambda e, dst=dst, pb=pb: e.copy(out=dst, in_=pb.rearrange("p (b c) -> p b c", b=2)), reads=[("ps", TPSK)], writes=[dk])
        for i in range(4):
            for k in range(8):
                A("pe", lambda e, i=i, k=k: e.matmul(ps[MVPS][:], lhsT=hTg[par][:, k, i * 128:(i + 1) * 128], rhs=wl[:, k, 1024:1536],
                                                     start=(k == 0), stop=(k == 7)), reads=[("hTg", par), ("wl", k)], writes=[("ps", MVPS)])
            if own:
                dst = vaug[:, g0 * 4 + i, :, 0:128]
                dk = "vaug"
            else:
                dst = vaugg[par][:, i, :, 0:128]
                dk = ("vaugg", 0)
            A("dve", lambda e, dst=dst: e.tensor_copy(out=dst, in_=ps[MVPS][:].rearrange("p (h d) -> p h d", h=4)), reads=[("ps", MVPS)], writes=[dk])
            if own:
                for k in range(8):
                    A("pe", lambda e, i=i, k=k: e.matmul(ps[MVPS][:], lhsT=hTg[par][:, k, i * 128:(i + 1) * 128], rhs=wl[:, k, 1536:2048],
                                                         start=(k == 0), stop=(k == 7)), reads=[("hTg", par), ("wl", k)], writes=[("ps", MVPS)])
                sp_ = i % 2
                A("act", lambda e, sp_=sp_: e.activation(out=sgt[sp_][:], in_=ps[MVPS][:], func=AF.Sigmoid), reads=[("ps", MVPS)], writes=[("sgt", sp_)])
                A("pool", lambda e, sp_=sp_, i=i: e.tensor_tensor(out=sgo[:, g0 * 4 + i, :], in0=sgt[sp_][:], in1=gon[:], op=ALU.mult),
                  reads=[("sgt", sp_), "gon"], writes=["sgo"])
        for i in range(4):
            for k in range(8):
                A("pe", lambda e, i=i, k=k: e.matmul(ps[MVPS][:, i * 16:(i + 1) * 16], lhsT=hTg[par][:, k, i * 128:(i + 1) * 128], rhs=wl[:, k, 2048:2064],
                                                     start=(k == 0), stop=(k == 7)), reads=[("hTg", par), ("wl", k)], writes=[("ps", MVPS)])
        if own:
            Gd = Gown[:, g0 * 4:g0 * 4 + 4, :]
            gk = "Gown"
            LFd = LFo[:, g0 * 4:g0 * 4 + 4, :]
            lk = "LFo"
        else:
            Gd = Gg[par][:]
            gk = ("Gg", par)
            LFd = LFg[par][:]
            lk = ("LFg", par)
        A("dve", lambda e: e.tensor_tensor(out=Gd, in0=ps[MVPS][:, 0:64].rearrange("p (b c) -> p b c", b=4),
                                           in1=gbias[:].rearrange("p (b c) -> p b c", b=4), op=ALU.add), reads=[("ps", MVPS), "gbias"], writes=[gk])
        A("act", lambda e: e.activation(out=LFd, in_=Gd[:, :, 8:16], func=AF.Exp, scale=-1.0), reads=[gk], writes=[lk])
        A("act", lambda e: e.activation(out=LFd, in_=LFd, func=AF.Ln, bias=ONEt[:, 0:1]), reads=[lk, "onet"], writes=[lk])
        if own:
            return
        for blk in range(4):
            gb_ = gi * 4 + blk
            z = blk % 2
            A("dve", lambda e: e.tensor_scalar(out=LGe[z][:, 0:4], in0=LFg[par][:, blk, 0:4], scalar1=tf[:, gb_:gb_ + 1], scalar2=-1.0, op0=ALU.mult, op1=ALU.mult),
              reads=[lk, "tf"], writes=[("LGe", z)])
            A("dve", lambda e: e.tensor_scalar(out=LGe[z][:, 4:8], in0=LFg[par][:, blk, 4:8], scalar1=tb[:, gb_:gb_ + 1], scalar2=-1.0, op0=ALU.mult, op1=ALU.mult),
              reads=[lk, "tb"], writes=[("LGe", z)])
            A("dve", lambda e: e.tensor_scalar(out=Ie[z][:, 0:4], in0=Gg[par][:, blk, 0:4], scalar1=pf[:, gb_:gb_ + 1], scalar2=None, op0=ALU.add),
              reads=[gk, "pf"], writes=[("Ie", z)])
            A("dve", lambda e: e.tensor_scalar(out=Ie[z][:, 4:8], in0=Gg[par][:, blk, 4:8], scalar1=pbk[:, gb_:gb_ + 1], scalar2=None, op0=ALU.add),
              reads=[gk, "pbk"], writes=[("Ie", z)])
            gp = UPS0 + 2
            A("pe", lambda e: e.matmul(ps[gp][:, 0:4], lhsT=mSU, rhs=LGe[z][:, 0:4], start=True, stop=True), reads=["cst", ("LGe", z)], writes=[("ps", gp)])
            A("pe", lambda e: e.matmul(ps[gp][:, 4:8], lhsT=mSL, rhs=LGe[z][:, 4:8], start=True, stop=True), reads=["cst", ("LGe", z)], writes=[("ps", gp)])
            A("pe", lambda e: e.matmul(ps[gp][:, 8:16], lhsT=ones, rhs=LGe[z][:, 0:8], start=True, stop=True), reads=["cst", ("LGe", z)], writes=[("ps", gp)])
            A("dve", lambda e: e.tensor_tensor(out=exg[z][:], in0=ps[gp][:, 0:8], in1=Ie[z][:], op=ALU.add), reads=[("ps", gp), ("Ie", z)], writes=[("exg", z)])
            A("dve", lambda e: e.tensor_tensor(out=exg[z][:, 4:8], in0=exg[z][:, 4:8], in1=Arun[:], op=ALU.add), reads=[("exg", z), "Arun"], writes=[("exg", z)])
            A("act", lambda e: e.activation(out=Wt[z][:], in_=exg[z][:], func=AF.Exp, bias=LNSCt[:, 0:1]), reads=[("exg", z), "lnsc"], writes=[("Wt", z)])
            A("act", lambda e: e.activation(out=dect[z][:], in_=ps[gp][:, 8:12], func=AF.Exp), reads=[("ps", gp)], writes=[("dect", z)])
            A("dve", lambda e: e.tensor_tensor(out=Arun[:], in0=Arun[:], in1=ps[gp][:, 12:16], op=ALU.add), reads=["Arun", ("ps", gp)], writes=["Arun"])
            for u in range(8):
                ch, h = divmod(u, 4)
                kz = u % 4
                A("dve", lambda e, u=u, h=h, kz=kz: e.tensor_scalar(out=ktil[kz][:], in0=ktokg[par][:, blk, h * 128:(h + 1) * 128],
                                                                    scalar1=Wt[z][:, u:u + 1], scalar2=None, op0=ALU.mult),
                  reads=[("ktokg", 0), ("Wt", z)], writes=[("ktil", kz)])
                bank = UPS0 + (u // 3) if u < 6 else UPS0 + 2
                if u < 6:
                    c0 = (u % 3) * 129
                else:
                    c0 = 128 + (u - 6) * 129
                A("pe", lambda e, h=h, kz=kz, bank=bank, c0=c0: e.matmul(ps[bank][:, c0:c0 + 129], lhsT=ktil[kz][:], rhs=vaugg[par][:, blk, h, :],
                                                                         start=True, stop=True),
                  reads=[("ktil", kz), ("vaugg", 0), ("vaugg1", 0)], writes=[("ps", bank)])
                if ch == 0:
                    A("dve", lambda e, h=h, bank=bank, c0=c0: e.scalar_tensor_tensor(out=Cf[:, h, :], in0=Cf[:, h, :], scalar=dect[z][:, h:h + 1],
                                                                                     in1=ps[bank][:, c0:c0 + 129], op0=ALU.mult, op1=ALU.add),
                      reads=["Cf", ("dect", z), ("ps", bank)], writes=["Cf"])
                else:
                    A("dve", lambda e, h=h, bank=bank, c0=c0: e.tensor_tensor(out=Cb[:, h, :], in0=Cb[:, h, :], in1=ps[bank][:, c0:c0 + 129], op=ALU.add),
                      reads=["Cb", ("ps", bank)], writes=["Cb"])

    A("dve", lambda e: e.memset(LNSCt[:], LNSC), writes=["lnsc"])

    for gi in range(12 if not (debug and (debug.endswith("_fast") or debug.startswith("p4"))) else 0):
        mlstm_group(gi, False)
    for gi in range(12, 16 if not (debug and debug.startswith("p4")) else 12):
        mlstm_group(gi, True)

    if debug and debug.split("_")[0] in ("qTe", "kTe"):
        src_t = qT if debug.startswith("qTe") else kT
        dt_ = al([128, 1024], F32)
        for h in range(4):
            for half in range(2):
                A("dve", lambda e, h=h, half=half: e.tensor_copy(out=dt_[:], in_=src_t[:, h, half * 1024:(half + 1) * 1024]), reads=["qT", "kT"], writes=["dt_"])
                r0 = (h * 2 + half) * 128
                A("sp", lambda e, r0=r0: e.dma_start(out=dbg[r0:r0 + 128, :], in_=dt_[:]), reads=["dt_"], writes=["dbgo"], dma=True)
        A("sp", lambda e: e.nop(), reads=["dbgo"])
        S.emit(nc, st)
        st.close()
        return nc
    p12keys = [("wl", k) for k in range(8)] + ["tf", "tb", "pf", "pbk", "gbias", "cw", "cb", "gon", "haloT", ("hTg", 0), ("hTg", 1),
               ("padb", 0), ("padb", 1), ("padbL", 0), ("padbL", 1), ("padbR", 0), ("padbR", 1), ("acc", 0), ("acc", 1),
               ("kTg", 0), ("ktokg", 0), ("vaugg", 0), ("vaugg1", 0), ("Gg", 0), ("Gg", 1), ("LFg", 0), ("LFg", 1), ("sgt", 0), ("sgt", 1)]
    p3keys = ["EBt", "ECt", "WSt", "DECt", "cmt", "HB", ("PT", 0), ("PT", 1), "den", "scl", ("hsum", 0), ("hsum", 1), "ssq", ("ymt", 0), ("ymt", 1), "ymT"]
    A("dve", lambda e: e.memset(cmt[:], 0.0), writes=p12keys + p3keys)
    A("pool", lambda e: e.tensor_copy(out=Cfb[:], in_=Cf[:]), reads=["Cf"], writes=["Cfb"])
    A("pool", lambda e: e.tensor_copy(out=Cbb[:], in_=Cb[:]), reads=["Cb"], writes=["Cbb"])
    GP = 7
    for blk in range(16 if not (debug and debug.startswith("p4")) else 0):
        z = blk % 2
        A("dve", lambda e, blk=blk: e.tensor_scalar(out=LGe[z][:], in0=LFo[:, blk, :], scalar1=-1.0, scalar2=None, op0=ALU.mult), reads=["LFo"], writes=[("LGe", z)])
        A("pe", lambda e: e.matmul(ps[GP][:, 0:4], lhsT=mLE, rhs=LGe[z][:, 0:4], start=True, stop=True), reads=["cst", ("LGe", z)], writes=[("ps", GP)])
        A("pe", lambda e: e.matmul(ps[GP][:, 4:8], lhsT=mGE, rhs=LGe[z][:, 4:8], start=True, stop=True), reads=["cst", ("LGe", z)], writes=[("ps", GP)])
        A("pe", lambda e: e.matmul(ps[GP][:, 8:16], lhsT=ones, rhs=LGe[z][:, 0:8], start=True, stop=True), reads=["cst", ("LGe", z)], writes=[("ps", GP)])
        A("act", lambda e, blk=blk: e.activation(out=EBt[:, blk, :], in_=ps[GP][:, 0:8], func=AF.Exp), reads=[("ps", GP)], writes=["EBt"])
        A("dve", lambda e, blk=blk: e.tensor_tensor(out=cmt[:], in0=Gown[:, blk, 0:8], in1=ps[GP][:, 0:8], op=ALU.subtract), reads=["Gown", ("ps", GP)], writes=["cmt"])
        A("act", lambda e, blk=blk: e.activation(out=ECt[:, blk, :], in_=cmt[:], func=AF.Exp, bias=LNSCt[:, 0:1]), reads=["cmt", "lnsc"], writes=["ECt"])
        A("act", lambda e, blk=blk: e.activation(out=DECt[:, blk, :], in_=ps[GP][:, 8:16], func=AF.Exp), reads=[("ps", GP)], writes=["DECt"])
        A("dve", lambda e, blk=blk: e.tensor_tensor(out=cmt[:], in0=cmt[:], in1=ps[GP][:, 8:16], op=ALU.add), reads=["cmt", ("ps", GP)], writes=["cmt"])
        A("act", lambda e, blk=blk: e.activation(out=WSt[:, blk, :], in_=cmt[:], func=AF.Exp, bias=LNSCt[:, 0:1]), reads=["cmt", "lnsc"], writes=["WSt"])

    SPS, ND0, ND1, UB0, UB1, TPY = 0, 1, 2, 3, 4, 5
    def dirpass(d):
        blocks = list(range(16)) if d == 0 else list(range(15, -1, -1))
        Cx, Cxb, ck, ckb = (Cf, Cfb, "Cf", "Cfb") if d == 0 else (Cb, Cbb, "Cb", "Cbb")
        maskb = mLEb if d == 0 else mGEb
        mk_ = "mLEb" if d == 0 else "mGEb"
        final = (d == 0)
        for bi, blk in enumerate(blocks):
            z = bi % 2
            tsl = slice(blk * 128, (blk + 1) * 128)
            for h in range(4):
                A("pe", lambda e, h=h: e.matmul(ps[SPS][:, h * 128:(h + 1) * 128], lhsT=kT[:, h, tsl], rhs=qT[:, h, tsl], start=True, stop=True),
                  reads=["kT", "qT"], writes=[("ps", SPS)])
            for h in range(4):
                A("dve", lambda e, h=h: e.scalar_tensor_tensor(out=PT[z][:, h, :], in0=ps[SPS][:, h * 128:(h + 1) * 128], scalar=ECt[:, blk, d * 4 + h:d * 4 + h + 1],
                                                               in1=maskb[:], op0=ALU.mult, op1=ALU.mult),
                  reads=[("ps", SPS), "ECt", mk_], writes=[("PT", z)])
            for h in range(4):
                bank = ND0 + h // 2
                c0 = (h % 2) * 129
                A("pe", lambda e, h=h, bank=bank, c0=c0: e.matmul(ps[bank][:, c0:c0 + 129], lhsT=PT[z][:, h, :], rhs=vaug[:, blk, h, :], start=True, stop=False),
                  reads=[("PT", z), "vaug", "vaug1"], writes=[("ps", bank)])
                A("pe", lambda e, h=h, bank=bank, c0=c0: e.matmul(ps[bank][:, c0:c0 + 129], lhsT=qT[:, h, tsl], rhs=Cxb[:, h, :], start=False, stop=True),
                  reads=["qT", ckb], writes=[("ps", bank)])
            for hh in range(2):
                bank = ND0 + hh
                A("dve", lambda e, hh=hh, bank=bank: e.tensor_tensor(out=den[:, hh * 2:hh * 2 + 2],
                                                                     in0=ps[bank][:, 0:258].rearrange("p (a c) -> p a c", a=2)[:, :, 128],
                                                                     in1=EBt[:, blk, d * 4 + hh * 2:d * 4 + hh * 2 + 2], op=ALU.mult),
                  reads=[("ps", bank), "EBt"], writes=["den"])
            A("dve", lambda e: e.tensor_scalar(out=scl[:], in0=den[:], scalar1=-1.0, scalar2=None, op0=ALU.mult), reads=["den"], writes=["scl"])
            A("dve", lambda e: e.tensor_tensor(out=den[:], in0=den[:], in1=scl[:], op=ALU.max), reads=["den", "scl"], writes=["den"])
            A("dve", lambda e: e.tensor_scalar(out=den[:], in0=den[:], scalar1=1.0, scalar2=None, op0=ALU.max), reads=["den"], writes=["den"])
            A("dve", lambda e: e.reciprocal(out=den[:], in_=den[:]), reads=["den"], writes=["den"])
            A("dve", lambda e: e.tensor_tensor(out=scl[:], in0=den[:], in1=EBt[:, blk, d * 4:d * 4 + 4], op=ALU.mult), reads=["den", "EBt"], writes=["scl"])
            for h in range(4):
                bank = ND0 + h // 2
                c0 = (h % 2) * 129
                if not final:
                    A("dve", lambda e, h=h, bank=bank, c0=c0: e.tensor_scalar(out=HB[:, blk, h * 128:(h + 1) * 128], in0=ps[bank][:, c0:c0 + 128],
                                                                              scalar1=scl[:, h:h + 1], scalar2=None, op0=ALU.mult),
                      reads=[("ps", bank), "scl"], writes=["HB"])
                else:
                    A("dve", lambda e, h=h, bank=bank, c0=c0: e.scalar_tensor_tensor(out=hsum[z][:, h, :], in0=ps[bank][:, c0:c0 + 128], scalar=scl[:, h:h + 1],
                                                                                     in1=HB[:, blk, h * 128:(h + 1) * 128], op0=ALU.mult, op1=ALU.add),
                      reads=[("ps", bank), "scl", "HB"], writes=[("hsum", z)])
            if bi < 15:
                for h in range(4):
                    kz = h
                    bank = UB0 + h // 2
                    c0 = (h % 2) * 129
                    A("dve", lambda e, h=h, kz=kz: e.tensor_scalar(out=ktil[kz][:], in0=ktok[:, blk, h * 128:(h + 1) * 128],
                                                                   scalar1=WSt[:, blk, d * 4 + h:d * 4 + h + 1], scalar2=None, op0=ALU.mult),
                      reads=["ktok", "WSt"], writes=[("ktil", kz)])
                    A("pe", lambda e, h=h, kz=kz, bank=bank, c0=c0: e.matmul(ps[bank][:, c0:c0 + 129], lhsT=ktil[kz][:], rhs=vaug[:, blk, h, :], start=True, stop=True),
                      reads=[("ktil", kz), "vaug", "vaug1"], writes=[("ps", bank)])
                    A("dve", lambda e, h=h, bank=bank, c0=c0: e.scalar_tensor_tensor(out=Cx[:, h, :], in0=Cx[:, h, :], scalar=DECt[:, blk, d * 4 + h:d * 4 + h + 1],
                                                                                     in1=ps[bank][:, c0:c0 + 129], op0=ALU.mult, op1=ALU.add),
                      reads=[ck, "DECt", ("ps", bank)], writes=[ck])
                A("pool", lambda e: e.tensor_copy(out=Cxb[:], in_=Cx[:]), reads=[ck], writes=[ckb])
            if final:
                for h in range(4):
                    A("act", lambda e, h=h: e.activation(out=junk[:, 0:128], in_=hsum[z][:, h, :], func=AF.Square, accum_out=ssq[:, h:h + 1]),
                      reads=[("hsum", z)], writes=["ssq"])
                rstd_of(ssq[:], 128, "ssq")
                for h in range(4):
                    A("dve", lambda e, h=h: e.scalar_tensor_tensor(out=ymt[z][:, h, :], in0=hsum[z][:, h, :], scalar=ssq[:, h:h + 1],
                                                                   in1=sgo[:, blk, h * 128:(h + 1) * 128], op0=ALU.mult, op1=ALU.mult),
                      reads=[("hsum", z), "ssq", "sgo"], writes=[("ymt", z)])
                pb = psb(TPY)
                for h in range(4):
                    A("pe", lambda e, h=h: e.transpose(out=pb[:, h * 128:(h + 1) * 128], in_=ymt[z][:, h, :], identity=idb[:]),
                      reads=[("ymt", z), "idb"], writes=[("ps", TPY)])
                A("act", lambda e: e.copy(out=ymT[:, :, tsl], in_=pb[:, 0:512].rearrange("p (h t) -> p h t", h=4)), reads=[("ps", TPY)], writes=["ymT"])

    if not (debug and debug.startswith("p4")):
        dirpass(1)
    if debug and debug.startswith("HB"):
        for blk in range(16):
            A("sp", lambda e, blk=blk: e.dma_start(out=dbg[blk * 128:(blk + 1) * 128, 0:512], in_=HB[:, blk, :]), reads=["HB"], writes=["dbgo"], dma=True)
        A("sp", lambda e: e.nop(), reads=["dbgo"])
        S.emit(nc, st)
        st.close()
        return nc
    if not (debug and debug.startswith("p4")):
        dirpass(0)

    if debug and debug.split("_")[0] in ("mlstm", "qT", "kT"):
        debug = debug.split("_")[0]
        if debug == "qT":
            ymT = qT
        elif debug == "kT":
            ymT = kT
        ymk = {"mlstm": "ymT", "qT": "qT", "kT": "kT"}[debug]
        dt_ = al([128, 1024], F32)
        for h in range(4):
            for half in range(2):
                A("dve", lambda e, h=h, half=half: e.tensor_copy(out=dt_[:], in_=ymT[:, h, half * 1024:(half + 1) * 1024]), reads=[ymk], writes=["dt_"])
                r0 = (h * 2 + half) * 128
                A("sp", lambda e, r0=r0: e.dma_start(out=dbg[r0:r0 + 128, :], in_=dt_[:]), reads=["dt_"], writes=["dbgo"], dma=True)
        A("sp", lambda e: e.nop(), reads=["dbgo"])
        S.emit(nc, st)
        st.close()
        return nc

    S.barrier(lambda e: e.memset(sst[:, 7:8], 0.0))
    al.set_regions([(m_phase, m_alias), (m_keep, SB_HI)])
    R = al.ralloc
    ckvT = R([128, 2, 8192], BF16)
    kropeT = R([32, 8192], BF16)
    QT = R([96, 8, NOWN], BF16)
    yaT = R([128, 4, NOWN], BF16)
    reg_p4 = [list(r) for r in al.regions]
    wkv = R([128, 8, 288], BF16)
    wq = R([128, 8, 384], BF16)
    wuq = R([128, 3, 768], BF16)
    gq = R([128, 384], F32)
    gkv = R([128, 256], F32)
    hT4 = [R([128, 8, 128], BF16) for _ in range(2)]
    sinT = R([128, 80, 16], F32)
    cosT = R([128, 80, 16], F32)
    sinq = R([128, 16, 4, 16], F32)
    cosq = R([128, 16, 4, 16], F32)
    reg_tmp = [list(r) for r in al.regions]
    posi = R([128, 80], I32)
    posf = R([128, 80], F32)
    ang = R([128, 80, 16], F32)
    tq = R([128, 1280], F32)
    tk = R([128, 1280], I32)
    tkf = R([128, 1280], F32)
    tm = R([128, 1280], F32)

    load_w(wkv, "wkv", w_in, 8, 384, 672)
    load_w(wq, "wq", w_in, 8, 0, 384)
    load_w(wuq, "wuq", w_uq, 3, 0, 768)
    A("sp", lambda e: e.dma_start(out=gq[:], in_=gqd), writes=["gq"], dma=True)
    A("sp", lambda e: e.dma_start(out=gkv[:], in_=gkvd), writes=["gkv"], dma=True)
    A("sp", lambda e: e.dma_start(out=posi[:], in_=posT), writes=["posi"], dma=True)
    A("dve", lambda e: e.tensor_copy(out=posf[:], in_=posi[:]), reads=["posi"], writes=["posf"])
    invf = (np.float32(10000.0) ** (-np.arange(0, 32, 2, dtype=np.float32) / np.float32(32))).astype(np.float32)
    for f in range(16):
        A("dve", lambda e, f=f: e.tensor_scalar(out=ang[:, :, f], in0=posf[:], scalar1=float(invf[f]), scalar2=None, op0=ALU.mult), reads=["posf"], writes=["ang"])
    angf = ang[:].rearrange("p a b -> p (a b)")
    TWO_PI = 2.0 * math.pi
    for (dst, off) in ((sinT, 0.0), (cosT, 0.25)):
        dstf = dst[:].rearrange("p a b -> p (a b)")
        A("dve", lambda e, off=off: e.tensor_scalar(out=tq[:], in0=angf, scalar1=1.0 / TWO_PI, scalar2=off, op0=ALU.mult, op1=ALU.add), reads=["ang"], writes=["tq"])
        A("dve", lambda e: e.tensor_copy(out=tk[:], in_=tq[:]), reads=["tq"], writes=["tk"])
        A("dve", lambda e: e.tensor_copy(out=tkf[:], in_=tk[:]), reads=["tk"], writes=["tkf"])
        A("dve", lambda e: e.tensor_tensor(out=tq[:], in0=tq[:], in1=tkf[:], op=ALU.subtract), reads=["tq", "tkf"], writes=["tq"])
        A("dve", lambda e: e.tensor_scalar(out=tm[:], in0=tq[:], scalar1=0.5, scalar2=None, op0=ALU.is_gt), reads=["tq"], writes=["tm"])
        A("dve", lambda e: e.tensor_tensor(out=tq[:], in0=tq[:], in1=tm[:], op=ALU.subtract), reads=["tq", "tm"], writes=["tq"])
        A("dve", lambda e: e.tensor_scalar(out=tm[:], in0=tq[:], scalar1=-0.5, scalar2=None, op0=ALU.is_lt), reads=["tq"], writes=["tm"])
        A("dve", lambda e: e.tensor_tensor(out=tq[:], in0=tq[:], in1=tm[:], op=ALU.add), reads=["tq", "tm"], writes=["tq"])
        A("dve", lambda e: e.tensor_scalar(out=tq[:], in0=tq[:], scalar1=-0.4999, scalar2=0.4999, op0=ALU.max, op1=ALU.min), reads=["tq"], writes=["tq"])
        A("act", lambda e, dstf=dstf: e.activation(out=dstf, in_=tq[:], func=AF.Sin, scale=TWO_PI), reads=["tq"], writes=["sincos"])
    for hh in range(4):
        A("dve", lambda e, hh=hh: e.tensor_copy(out=sinq[:, :, hh, :], in_=sinT[:, 64:80, :]), reads=["sincos"], writes=["sinq"])
        A("dve", lambda e, hh=hh: e.tensor_copy(out=cosq[:, :, hh, :], in_=cosT[:, 64:80, :]), reads=["sincos"], writes=["cosq"])

    if debug == "p4a":
        A("sp", lambda e: e.dma_start(out=dbg[0:128, 0:1024], in_=sinT[:].rearrange("p a b -> p (a b)")[:, 0:1024]), reads=["sincos"], writes=["dbgo"], dma=True)
        A("sp", lambda e: e.nop(), reads=["dbgo"])
        S.emit(nc, st)
        st.close()
        return nc
    S.barrier(lambda e: e.memset(sst[:, 7:8], 0.0))
    al.regions = [list(r) for r in reg_tmp]
    cn = [R([128, 384], BF16) for _ in range(2)]
    krr = [R([128, 32], BF16) for _ in range(2)]
    rt = [R([128, 4, 16], F32) for _ in range(4)]
    cqnT = R([128, 3, 128], BF16)
    qtok = R([128, 8, 96], BF16)
    TPSX, LAT0, TPC, Q0, Q1, TPQ = 0, 1, 3, 4, 5, 6
    lstate = {"i": 0}

    def rope(x1, x2, cs, sn, o1, o2, rkeys, okey, shape4):
        ta, tb_, tc, td = [t[:] if shape4 else t[:, 0, :] for t in rt]
        A("dve", lambda e: e.tensor_tensor(out=ta, in0=x1, in1=cs, op=ALU.mult), reads=rkeys, writes=["rt0"])
        A("dve", lambda e: e.tensor_tensor(out=tb_, in0=x2, in1=sn, op=ALU.mult), reads=rkeys, writes=["rt1"])
        A("dve", lambda e: e.tensor_tensor(out=o1, in0=ta, in1=tb_, op=ALU.subtract), reads=["rt0", "rt1"], writes=[okey])
        A("dve", lambda e: e.tensor_tensor(out=tc, in0=x2, in1=cs, op=ALU.mult), reads=rkeys, writes=["rt2"])
        A("dve", lambda e: e.tensor_tensor(out=td, in0=x1, in1=sn, op=ALU.mult), reads=rkeys, writes=["rt3"])
        A("dve", lambda e: e.tensor_tensor(out=o2, in0=tc, in1=td, op=ALU.add), reads=["rt2", "rt3"], writes=[okey])

    for kt in range(64):
        z = kt % 2
        src = xo[kt * 128:(kt + 1) * 128, :] if kt < 48 else xown[(kt - 48) * 128:(kt - 47) * 128, :]
        xpipe(src, gmix, "gmix", hT4[z][:], ("hT4", z), TPSX)
        lat = LAT0 + z
        for k in range(8):
            A("pe", lambda e, k=k: e.matmul(ps[lat][:, 0:288], lhsT=hT4[z][:, k, :], rhs=wkv[:, k, :], start=(k == 0), stop=(k == 7)),
              reads=[("hT4", z), ("wkv", k)], writes=[("ps", lat)])
        ssap = sst[:, 2 + z:3 + z]
        A("act", lambda e: e.activation(out=junk[:, 0:256], in_=ps[lat][:, 0:256], func=AF.Square, accum_out=ssap), reads=[("ps", lat)], writes=[("ssl", z)])
        rstd_of(ssap, 256, ("ssl", z))
        A("dve", lambda e: e.scalar_tensor_tensor(out=cn[z][:, 0:256], in0=ps[lat][:, 0:256], scalar=ssap, in1=gkv[:], op0=ALU.mult, op1=ALU.mult),
          reads=[("ps", lat), ("ssl", z), "gkv"], writes=[("cn", z)])
        rope(ps[lat][:, 256:272], ps[lat][:, 272:288], cosT[:, kt, :], sinT[:, kt, :], krr[z][:, 0:16], krr[z][:, 16:32],
             [("ps", lat), "sincos"], ("krr", z), False)
        pb = psb(TPC)
        for c in range(2):
            A("pe", lambda e, c=c: e.transpose(out=pb[:, c * 128:(c + 1) * 128], in_=cn[z][:, c * 128:(c + 1) * 128], identity=idb[:]),
              reads=[("cn", z), "idb"], writes=[("ps", TPC)])
        A("pe", lambda e: e.transpose(out=pb[0:32, 256:384], in_=krr[z][:], identity=idb[:]), reads=[("krr", z), "idb"], writes=[("ps", TPC)])
        A("act", lambda e: e.copy(out=ckvT[:, :, kt * 128:(kt + 1) * 128], in_=pb[:, 0:256].rearrange("p (c t) -> p c t", c=2)), reads=[("ps", TPC)], writes=["ckvT"])
        A("act", lambda e: e.copy(out=kropeT[:, kt * 128:(kt + 1) * 128], in_=pb[0:32, 256:384]), reads=[("ps", TPC)], writes=["kropeT"])

    for ot in range(16 if debug != "p4b" else 0):
        z = ot % 2
        xpipe(xown[ot * 128:(ot + 1) * 128, :], gmix, "gmix", hT4[z][:], ("hT4", z), TPSX)
        lat = LAT0 + z
        for k in range(8):
            A("pe", lambda e, k=k: e.matmul(ps[lat][:, 0:384], lhsT=hT4[z][:, k, :], rhs=wq[:, k, :], start=(k == 0), stop=(k == 7)),
              reads=[("hT4", z), ("wq", k)], writes=[("ps", lat)])
        ssap = sst[:, 2 + z:3 + z]
        A("act", lambda e: e.activation(out=junk[:, 0:384], in_=ps[lat][:, 0:384], func=AF.Square, accum_out=ssap), reads=[("ps", lat)], writes=[("ssl", z)])
        rstd_of(ssap, 384, ("ssl", z))
        A("dve", lambda e: e.scalar_tensor_tensor(out=cn[z][:], in0=ps[lat][:, 0:384], scalar=ssap, in1=gq[:], op0=ALU.mult, op1=ALU.mult),
          reads=[("ps", lat), ("ssl", z), "gq"], writes=[("cn", z)])
        pb = psb(TPC)
        for c in range(3):
            A("pe", lambda e, c=c: e.transpose(out=pb[:, c * 128:(c + 1) * 128], in_=cn[z][:, c * 128:(c + 1) * 128], identity=idb[:]),
              reads=[("cn", z), "idb"], writes=[("ps", TPC)])
        A("act", lambda e: e.copy(out=cqnT[:], in_=pb[:, 0:384].rearrange("p (c t) -> p c t", c=3)), reads=[("ps", TPC)], writes=["cqnT"])
        qlvl = int(debug[3:]) if (debug and debug.startswith("p4q")) else 9
        if qlvl < 2:
            continue
        for x_ in range(2):
            qb = Q0 + x_
            for c in range(3):
                A("pe", lambda e, c=c: e.matmul(ps[qb][:, 0:384], lhsT=cqnT[:, c, :], rhs=wuq[:, c, x_ * 384:(x_ + 1) * 384], start=(c == 0), stop=(c == 2)),
                  reads=["cqnT", ("wuq", c)], writes=[("ps", qb)])
            V4 = ps[qb][:, 0:384].rearrange("p (h d) -> p h d", h=4)
            A("act", lambda e: e.copy(out=qtok[:, x_ * 4:x_ * 4 + 4, 0:64], in_=V4[:, :, 0:64]), reads=[("ps", qb)], writes=["qtokn"])
            if qlvl < 3:
                continue
            for hh in range(4):
                c0 = hh * 96
                rope(ps[qb][:, c0 + 64:c0 + 80], ps[qb][:, c0 + 80:c0 + 96], cosT[:, 64 + ot, :], sinT[:, 64 + ot, :],
                     qtok[:, x_ * 4 + hh, 64:80], qtok[:, x_ * 4 + hh, 80:96], [("ps", qb), "sincos"], "qtokr", False)
        if qlvl < 4:
            continue
        pq = psb(TPQ)
        for h in range(8):
            A("pe", lambda e, h=h: e.transpose(out=pq[0:96, h * 128:(h + 1) * 128], in_=qtok[:, h, :], identity=idb[:]),
              reads=["qtokn", "qtokr", "idb"], writes=[("ps", TPQ)])
        A("act", lambda e: e.copy(out=QT[:, :, ot * 128:(ot + 1) * 128], in_=pq[0:96, :].rearrange("p (h t) -> p h t", h=8)), reads=[("ps", TPQ)], writes=["QT"])

    if debug and debug.startswith("p4"):
        dtt = R([128, 1024], F32)
        for h in range(8):
            for half in range(2):
                A("dve", lambda e, h=h, half=half: e.tensor_copy(out=dtt[0:96, :], in_=QT[:, h, half * 1024:(half + 1) * 1024]), reads=["QT"], writes=["dtt"])
                r0 = (h * 2 + half) * 96
                A("sp", lambda e, r0=r0: e.dma_start(out=dbg[r0:r0 + 96, :], in_=dtt[0:96, :]), reads=["dtt"], writes=["dbgo"], dma=True)
        A("dve", lambda e: e.tensor_copy(out=dtt[0:32, :], in_=kropeT[:, 7168:8192]), reads=["kropeT"], writes=["dtt"])
        A("sp", lambda e: e.dma_start(out=dbg[1536:1568, :], in_=dtt[0:32, :]), reads=["dtt"], writes=["dbgo"], dma=True)
        A("dve", lambda e: e.tensor_copy(out=dtt[:], in_=ckvT[:, 1, 7168:8192]), reads=["ckvT"], writes=["dtt"])
        A("sp", lambda e: e.dma_start(out=dbg[1664:1792, :], in_=dtt[:]), reads=["dtt"], writes=["dbgo"], dma=True)
        A("sp", lambda e: e.nop(), reads=["dbgo"])
        S.emit(nc, st)
        st.close()
        return nc
    S.barrier(lambda e: e.memset(sst[:, 7:8], 0.0))
    al.regions = [list(r) for r in reg_p4]
    wkp = R([128, 2, 8, 96], BF16)
    wv = R([128, 2, 512], BF16)
    selb = R([32, 96], BF16)
    KT0 = R([96, 8192], BF16)
    KT = [KT0, KT0]
    VA = [R([128, 64, 65], BF16) for _ in range(2)]
    yattn = R([128, 16, 512], BF16)
    rden = R([128, 4], F32)
    A("pool", lambda e: e.memset(wkp[:], 0.0), writes=["wkp"])
    A("dve", lambda e: e.tensor_copy(out=selb[:], in_=sel), reads=["cst"], writes=["selb"])
    for b_ in range(2):
        A("pool", lambda e, b_=b_: e.memset(VA[b_][:, :, 64:65], 1.0), writes=[("VA1", b_)])
    for c in range(2):
        sl = wstate["i"] % 2
        wstate["i"] += 1
        A("sp", lambda e, c=c, sl=sl: e.dma_start(out=wst[sl][:], in_=w_ukv[c * 128:(c + 1) * 128, :]), writes=[("wst", sl)], dma=True)
        W3 = wst[sl][:].rearrange("p (h d) -> p h d", h=8)
        A("pool", lambda e, c=c: e.tensor_copy(out=wkp[:, c, :, 0:64], in_=W3[:, :, 0:64]), reads=[("wst", sl), "wkp"], writes=["wkp"])
        A("pool", lambda e, c=c: e.tensor_copy(out=wv[:, c, :].rearrange("p (h d) -> p h d", h=8), in_=W3[:, :, 64:128]), reads=[("wst", sl)], writes=["wv"])

    OB_, KB_, VB_ = (6, 7), (0, 1), 2
    SCALE = 96.0 ** -0.5
    Pb2 = [R([128, 1024], BF16) for _ in range(3)]
    kbs = {"i": 0}
    for h in range(8):
        bf = h % 2
        for grp in range(16):
            kb = KB_[kbs["i"] % 2]
            kbs["i"] += 1
            gs = slice(grp * 512, (grp + 1) * 512)
            A("pe", lambda e: e.matmul(ps[kb][0:96, :], lhsT=wkp[:, 0, h, :], rhs=ckvT[:, 0, gs], start=True, stop=False), reads=["wkp", "ckvT"], writes=[("ps", kb)])
            A("pe", lambda e: e.matmul(ps[kb][0:96, :], lhsT=wkp[:, 1, h, :], rhs=ckvT[:, 1, gs], start=False, stop=False), reads=["wkp", "ckvT"], writes=[("ps", kb)])
            A("pe", lambda e: e.matmul(ps[kb][0:96, :], lhsT=selb[:], rhs=kropeT[:, gs], start=False, stop=True), reads=["selb", "kropeT"], writes=[("ps", kb)])
            A("dve", lambda e: e.tensor_copy(out=KT[bf][:, gs], in_=ps[kb][0:96, :]), reads=[("ps", kb)], writes=[("KT", 0)])
        for tg in range(8):
            vb = 2 + (tg % 2)
            for tl in range(8):
                kt = tg * 8 + tl
                for c in range(2):
                    A("pe", lambda e, c=c, tl=tl, kt=kt: e.matmul(ps[vb][:, tl * 64:(tl + 1) * 64], lhsT=ckvT[:, c, kt * 128:(kt + 1) * 128],
                                                                 rhs=wv[:, c, h * 64:(h + 1) * 64], start=(c == 0), stop=(c == 1)),
                      reads=["ckvT", "wv"], writes=[("ps", vb)])
            A("act", lambda e: e.copy(out=VA[bf][:, tg * 8:(tg + 1) * 8, 0:64], in_=ps[vb][:].rearrange("p (t d) -> p t d", t=8)), reads=[("ps", vb)], writes=[("VA", bf)])
        for tt in range(4):
            ob = OB_[(h * 4 + tt) % 2]
            qs = slice(tt * 512, (tt + 1) * 512)

            def s_mm(sp_):
                j = sp_ % 3
                for half in range(2):
                    st_ = 2 * sp_ + half
                    A("pe", lambda e: e.matmul(ps[2 * j + half][:], lhsT=KT[bf][:, st_ * 128:(st_ + 1) * 128], rhs=QT[:, h, qs], start=True, stop=True),
                      reads=[("KT", 0), "QT"], writes=[("ps", 2 * j + half)])
                A("act", lambda e: e.activation(out=Pb2[j][:], in_=psbig[j][:], func=AF.Exp, scale=SCALE),
                  reads=[("ps", 2 * j), ("ps", 2 * j + 1)], writes=[("Pb", j)])

            s_mm(0)
            s_mm(1)
            for sp_ in range(32):
                j = sp_ % 3
                for half in range(2):
                    st_ = 2 * sp_ + half
                    for qq in range(4):
                        A("pe", lambda e, qq=qq: e.matmul(ps[ob][:, qq * 65:(qq + 1) * 65], lhsT=Pb2[j][:, half * 512 + qq * 128:half * 512 + (qq + 1) * 128],
                                                          rhs=VA[bf][:, st_, :], start=(st_ == 0 and qq == 0), stop=(st_ == 63), skip_group_check=True),
                          reads=[("Pb", j), ("VA", bf), ("VA1", bf)], writes=[("ps", ob)])
                if sp_ + 2 < 32:
                    s_mm(sp_ + 2)
            O3 = ps[ob][:, 0:260].rearrange("p (q d) -> p q d", q=4)
            A("dve", lambda e: e.reciprocal(out=rden[:], in_=O3[:, :, 64]), reads=[("ps", ob)], writes=["rden"])
            for qq in range(4):
                A("dve", lambda e, qq=qq: e.tensor_scalar(out=yattn[:, tt * 4 + qq, h * 64:(h + 1) * 64], in0=O3[:, qq, 0:64], scalar1=rden[:, qq:qq + 1],
                                                          scalar2=None, op0=ALU.mult), reads=[("ps", ob), "rden"], writes=["yattn"])
    for tile in range(16):
        pb = psb(VB_)
        for c in range(4):
            A("pe", lambda e, c=c: e.transpose(out=pb[:, c * 128:(c + 1) * 128], in_=yattn[:, tile, c * 128:(c + 1) * 128], identity=idb[:]),
              reads=["yattn", "idb"], writes=[("ps", VB_)])
        A("act", lambda e: e.copy(out=yaT[:, :, tile * 128:(tile + 1) * 128], in_=pb[:, 0:512].rearrange("p (c t) -> p c t", c=4)), reads=[("ps", VB_)], writes=["yaT"])

    S.barrier(lambda e: e.memset(sst[:, 7:8], 0.0))
    al.regions = [[m_phase, m_alias], [reg_p4[1][0], SB_HI]]
    wgab = R([128, 8, 2048], BF16)
    wbm = R([128, 4, 1024], BF16)
    wbl = R([128, 4, 1024], BF16)
    wo = R([128, 8, 1024], BF16)
    hT6 = [R([128, 8, 128], BF16) for _ in range(2)]
    sg = R([128, 2048], BF16)
    t1 = R([128, 1024], F32)
    t2 = R([128, 1024], F32)
    mrg = R([128, 1024], BF16)
    mT = R([128, 8, 128], BF16)
    x1t = [R([128, 1024], F32) for _ in range(2)]
    load_w(wgab, "wgab", w_in, 8, 2736, 4784)
    load_w(wbm, "wbm", w_bm, 4, 0, 1024)
    load_w(wbl, "wbl", w_bl, 4, 0, 1024)
    load_w(wo, "wo", w_out, 8, 0, 1024)
    for ot in range(16):
        z = ot % 2
        tsl = slice(ot * 128, (ot + 1) * 128)
        sl = xpipe(xown[tsl, :], gmix, "gmix", hT6[z][:], ("hT6", z), 7)
        for cb_ in range(4):
            for k in range(8):
                A("pe", lambda e, k=k, cb_=cb_: e.matmul(ps[cb_][:], lhsT=hT6[z][:, k, :], rhs=wgab[:, k, cb_ * 512:(cb_ + 1) * 512], start=(k == 0), stop=(k == 7)),
                  reads=[("hT6", z), ("wgab", k)], writes=[("ps", cb_)])
            A("act", lambda e, cb_=cb_: e.activation(out=sg[:, cb_ * 512:(cb_ + 1) * 512], in_=ps[cb_][:], func=AF.Sigmoid), reads=[("ps", cb_)], writes=[("sg", cb_)])
        for half in range(2):
            for (wt, wk, aT_, ak, bank0) in ((wbm, "wbm", yaT, "yaT", 4), (wbl, "wbl", ymT, "ymT", 5)):
                bank = bank0
                for c in range(4):
                    A("pe", lambda e, c=c, wt=wt, aT_=aT_, bank=bank: e.matmul(ps[bank][:], lhsT=aT_[:, c, tsl], rhs=wt[:, c, half * 512:(half + 1) * 512],
                                                                               start=(c == 0), stop=(c == 3)), reads=[ak, (wk, c)], writes=[("ps", bank)])
            hs = slice(half * 512, (half + 1) * 512)
            A("dve", lambda e: e.tensor_tensor(out=t1[:, hs], in0=ps[4][:], in1=sg[:, half * 512:(half + 1) * 512], op=ALU.mult), reads=[("ps", 4), ("sg", half)], writes=[("t1", half)])
            A("dve", lambda e: e.tensor_tensor(out=t2[:, hs], in0=ps[5][:], in1=sg[:, 1024 + half * 512:1024 + (half + 1) * 512], op=ALU.mult),
              reads=[("ps", 5), ("sg", 2 + half)], writes=[("t2", half)])
            A("pool", lambda e: e.tensor_tensor(out=mrg[:, hs], in0=t1[:, hs], in1=t2[:, hs], op=ALU.add), reads=[("t1", half), ("t2", half)], writes=[("mrg", half)])
        pb = psb(6)
        for k in range(8):
            A("pe", lambda e, k=k: e.transpose(out=pb[:, k * 128:(k + 1) * 128], in_=mrg[:, k * 128:(k + 1) * 128], identity=idb[:]),
              reads=[("mrg", k // 4), "idb"], writes=[("ps", 6)])
        A("act", lambda e: e.copy(out=mT[:], in_=pb.rearrange("p (k t) -> p k t", k=8)), reads=[("ps", 6)], writes=["mT"])
        for half in range(2):
            bank = 4 + half
            for k in range(8):
                A("pe", lambda e, k=k, bank=bank, half=half: e.matmul(ps[bank][:], lhsT=mT[:, k, :], rhs=wo[:, k, half * 512:(half + 1) * 512], start=(k == 0), stop=(k == 7)),
                  reads=["mT", ("wo", k)], writes=[("ps", bank)])
            A("dve", lambda e, bank=bank, half=half: e.tensor_tensor(out=x1t[z][:, half * 512:(half + 1) * 512], in0=ps[bank][:], in1=xt[sl][:, half * 512:(half + 1) * 512], op=ALU.add),
              reads=[("ps", bank), ("xt", sl)], writes=[("x1t", z)])
        A("sp", lambda e: e.dma_start(out=x1d[tsl, :], in_=x1t[z][:]), reads=[("x1t", z)], writes=["x1d"], dma=True)

    S.barrier(lambda e: e.memset(sst[:, 7:8], 0.0))
    al.regions = [[m_phase, SB_HI]]
    wup = R([128, 8, 4096], BF16)
    wdn = R([128, 32, 1024], BF16)
    gmlp = R([128, 1024], F32)
    gfin = R([128, 1024], F32)
    hTm = [R([128, 8, 256], BF16) for _ in range(2)]
    aT = R([128, 32, 256], BF16)
    rr = [R([128, 256], F32) for _ in range(2)]
    xres = [R([128, 1024], F32) for _ in range(2)]
    otile = xres
    A("sp", lambda e: e.dma_start(out=gmlp[:], in_=gmlpd), writes=["gmlp"], dma=True)
    A("sp", lambda e: e.dma_start(out=gfin[:], in_=gfind), writes=["gfin"], dma=True)
    load_w(wup, "wup", w_up, 8, 0, 4096)
    load_w(wdn, "wdn", w_down, 32, 0, 1024)
    for g in range(8):
        z = g % 2
        for i in range(2):
            r0 = g * 256 + i * 128
            xpipe(x1d[r0:r0 + 128, :], gmlp, "gmlp", hTm[z][:, :, i * 128:(i + 1) * 128], ("hTm", z), 7)
        for f in range(32):
            bank = f % 2
            for k in range(8):
                A("pe", lambda e, k=k, f=f, bank=bank: e.matmul(ps[bank][:, 0:256], lhsT=wup[:, k, f * 128:(f + 1) * 128], rhs=hTm[z][:, k, :], start=(k == 0), stop=(k == 7)),
                  reads=[("hTm", z), ("wup", k)], writes=[("ps", bank)])
            A("act", lambda e, bank=bank: e.activation(out=rr[bank][:], in_=ps[bank][:, 0:256], func=AF.Relu), reads=[("ps", bank)], writes=[("rr", bank)])
            A("dve", lambda e, f=f, bank=bank: e.tensor_tensor(out=aT[:, f, :], in0=rr[bank][:], in1=rr[bank][:], op=ALU.mult), reads=[("rr", bank)], writes=["aT"])
        for i in range(2):
            r0 = g * 256 + i * 128
            zz = (g * 2 + i) % 2
            A("sp", lambda e, r0=r0, zz=zz: e.dma_start(out=xres[zz][:], in_=x1d[r0:r0 + 128, :]), reads=["x1d"], writes=[("xres", zz)], dma=True)
            for half in range(2):
                bank = 2 + half
                for f in range(32):
                    A("pe", lambda e, f=f, bank=bank, half=half, i=i: e.matmul(ps[bank][:], lhsT=aT[:, f, i * 128:(i + 1) * 128], rhs=wdn[:, f, half * 512:(half + 1) * 512],
                                                                               start=(f == 0), stop=(f == 31)), reads=["aT", ("wdn", f)], writes=[("ps", bank)])
                A("dve", lambda e, bank=bank, half=half, zz=zz: e.tensor_tensor(out=xres[zz][:, half * 512:(half + 1) * 512], in0=ps[bank][:], in1=xres[zz][:, half * 512:(half + 1) * 512], op=ALU.add),
                  reads=[("ps", bank), ("xres", zz)], writes=[("xres", zz)])
            ssap = sst[:, 4 + zz:5 + zz]
            A("act", lambda e, zz=zz, ssap=ssap: e.activation(out=junk[:], in_=xres[zz][:], func=AF.Square, accum_out=ssap), reads=[("xres", zz)], writes=[("ssf", zz)])
            rstd_of(ssap, 1024, ("ssf", zz))
            A("dve", lambda e, zz=zz, ssap=ssap: e.scalar_tensor_tensor(out=otile[zz][:], in0=xres[zz][:], scalar=ssap, in1=gfin[:], op0=ALU.mult, op1=ALU.mult),
              reads=[("xres", zz), ("ssf", zz), "gfin"], writes=[("xres", zz)])
            A("sp", lambda e, r0=r0, zz=zz: e.dma_start(out=y[r0:r0 + 128, :], in_=otile[zz][:]), reads=[("xres", zz)], writes=[("yout", g * 2 + i)], dma=True)
    A("sp", lambda e: e.nop(), reads=[("yout", i_) for i_ in range(16)])
    S.emit(nc, st)
    st.close()
    return nc


def make_consts():
    c = np.zeros((128, 880), np.float32)
    r = np.arange(128)
    c[:, 0:128] = np.eye(128)
    c[:, 128:256] = (r[:, None] <= r[None, :])
    c[:, 256:384] = (r[:, None] >= r[None, :])
    c[:, 384:512] = (r[:, None] > r[None, :])
    c[:, 512:640] = (r[:, None] < r[None, :])
    c[:, 640:768] = 1.0
    for i in range(32):
        c[i, 784 + 64 + i] = 1.0
    return c


def bc(v, n=128):
    return np.ascontiguousarray(np.broadcast_to(np.asarray(v, np.float32).reshape(1, -1), (n, np.asarray(v).size)))


def prep_inputs(inp, core):
    b, j = divmod(core, 4)
    x = inp["x"][b]
    pos = inp["positions"][b]
    o0, o1 = NOWN * j, NOWN * (j + 1)
    xo = np.concatenate([x[:o0], x[o1:]], axis=0)
    xown = x[o0:o1]
    xh = np.zeros((128, 1024), np.float32)

    def row(n):
        return x[n] if 0 <= n < 8192 else np.zeros(1024, np.float32)

    for g in range(12):
        n0 = 512 * g if 512 * g < o0 else 512 * g + NOWN
        for q, n in enumerate((n0 - 2, n0 - 1, n0 + 512, n0 + 513)):
            xh[4 * g + q] = row(n)
    for g in range(4):
        n0 = o0 + 512 * g
        for q, n in enumerate((n0 - 2, n0 - 1, n0 + 512, n0 + 513)):
            xh[48 + 4 * g + q] = row(n)
    pos_all = np.concatenate([pos[:o0], pos[o1:], pos[o0:o1]])
    posT = np.concatenate([pos_all.reshape(64, 128).T, pos[o0:o1].reshape(16, 128).T], axis=1).astype(np.int32)
    tfv = (np.arange(48) < 16 * j).astype(np.float32)
    igb = inp["mlstm_igate_b"][0]
    fgb = inp["mlstm_fgate_b"][0]
    gb16 = np.concatenate([igb[0], igb[1], fgb[0], fgb[1]])
    cwv = inp["mlstm_conv_w"][0][:, 0, :]
    cw = np.ascontiguousarray(cwv.reshape(5, 8, 128).transpose(2, 1, 0)).reshape(128, 40)
    cb = np.ascontiguousarray(inp["mlstm_conv_b"][0].reshape(8, 128).T)
    d = {
        "xo": np.ascontiguousarray(xo), "xown": np.ascontiguousarray(xown), "xh": xh,
        "posT": np.ascontiguousarray(posT), "tf": bc(tfv), "cst": make_consts(),
        "gmix": bc(inp["norm_mix_g"][0]), "gmlp": bc(inp["norm_mlp_g"][0]), "gfin": bc(inp["norm_final_g"]),
        "gq": bc(inp["mla_q_norm_g"][0]), "gkv": bc(inp["mla_kv_norm_g"][0]), "gon": bc(inp["mlstm_out_norm_g"][0]),
        "gb": bc(np.tile(gb16, 4)), "cw": cw.astype(np.float32), "cb": cb.astype(np.float32),
        "w_in": inp["w_in"][0], "w_uq": inp["mla_w_uq"][0], "w_ukv": inp["mla_w_ukv"][0],
        "w_bm": inp["w_branch_mla"][0], "w_bl": inp["w_branch_mlstm"][0], "w_out": inp["w_out"][0],
        "w_up": inp["w_mlp_up"][0], "w_down": inp["w_mlp_down"][0],
    }
    return {k: np.ascontiguousarray(v) for k, v in d.items()}


def run(inputs, debug=None, cores=8):
    inp = {k: np.asarray(v) for k, v in inputs.items()}
    nc = build_program(debug)
    in_maps = [prep_inputs(inp, c) for c in range(cores)]
    res = run_bass_kernel_spmd(nc, in_maps, core_ids=list(range(cores)))
    return res


def kernel(**inputs):
    res = run(inputs)
    out = np.zeros((2, 8192, 1024), np.float32)
    for c in range(8):
        b, j = divmod(c, 4)
        out[b, NOWN * j:NOWN * (j + 1)] = res.results[c]["y"]
    return out
```

```python
import math
from contextlib import ExitStack
import numpy as np
import concourse.bass as bass
import concourse.mybir as mybir
from concourse.bass_utils import run_bass_kernel_spmd

F32 = mybir.dt.float32
BF16 = mybir.dt.bfloat16
I32 = mybir.dt.int32
AF = mybir.ActivationFunctionType
ALU = mybir.AluOpType

SEM_LIMIT = 20000
N_DSEM = 24
SB_LO = 16512
SB_HI = 229376
NOWN = 2048
NOTH = 6144
EPS = 1e-6
LNSC = -0.5 * math.log(128.0)
BIG = 30000.0


class Op:
    __slots__ = ("eng", "fn", "deps", "signal", "is_dma", "idx", "sem", "val", "dslot")

    def __init__(self, eng, fn, is_dma):
        self.eng = eng
        self.fn = fn
        self.deps = []
        self.signal = False
        self.is_dma = is_dma
        self.sem = None
        self.val = None
        self.dslot = None


class _Rec:
    def __init__(self):
        self.call = None

    def __getattr__(self, name):
        def f(*a, **k):
            self.call = (name, a, k)
            return self
        return f


class Sched:
    def __init__(self):
        self.ops = []
        self.last_w = {}
        self.readers = {}
        self.n_dma = 0
        self.dslot_last = {}
        self.fence_op = None

    def add(self, eng, fn, reads=(), writes=(), dma=False):
        rec = _Rec()
        fn(rec)
        call = rec.call
        op = Op(eng, call, dma)
        psk = [k for k in reads if isinstance(k, tuple) and k[0] == "ps"]
        if psk:
            reads = [k for k in reads if k not in psk]
            writes = list(writes) + psk
        deps = set()
        for k in reads:
            w = self.last_w.get(k)
            if w is not None:
                deps.add(w)
        for k in writes:
            w = self.last_w.get(k)
            if w is not None:
                deps.add(w)
            for r in self.readers.get(k, ()):
                deps.add(r)
        if self.fence_op is not None:
            deps.add(self.fence_op)
        if dma:
            slot = (eng, self.n_dma % N_DSEM)
            self.n_dma += 1
            prev = self.dslot_last.get(slot)
            if prev is not None:
                deps.add(prev)
            self.dslot_last[slot] = op
            op.dslot = slot
        for d in deps:
            if d is op:
                continue
            if (not d.is_dma) and d.eng == "pe" and eng == "pe" and not dma:
                continue
            d.signal = True
            op.deps.append(d)
        for k in reads:
            self.readers.setdefault(k, []).append(op)
        for k in writes:
            self.last_w[k] = op
            self.readers[k] = []
        self.ops.append(op)
        return op

    def barrier(self, fn):
        keys = set(self.last_w.keys()) | set(self.readers.keys())
        self.fence_op = None
        op = self.add("dve", fn, writes=list(keys))
        for o in self.dslot_last.values():
            if o is not op and o not in op.deps:
                o.signal = True
                op.deps.append(o)
        self.fence_op = op
        return op

    def emit(self, nc, stack):
        engs = ["pe", "act", "dve", "pool", "sp"]
        counts = {e: 0 for e in engs}
        dcount = {}
        for op in self.ops:
            if op.is_dma:
                c = dcount.get(op.dslot, 0) + 1
                dcount[op.dslot] = c
                op.sem = ("d", op.dslot)
                op.val = 16 * c
            elif op.signal:
                c = counts[op.eng]
                counts[op.eng] = c + 1
                op.sem = (op.eng, c // SEM_LIMIT)
                op.val = c % SEM_LIMIT + 1
        sems = {}
        for op in self.ops:
            if op.sem is not None and op.sem not in sems:
                sems[op.sem] = stack.enter_context(nc.semaphore("s_%d" % len(sems)))
        block = stack.enter_context(nc.Block())
        ops = self.ops

        def run(engname, e):
            waited = {}
            for op in ops:
                if op.eng != engname:
                    continue
                for d in op.deps:
                    key = d.sem
                    if waited.get(key, 0) >= d.val:
                        continue
                    e.wait_ge(sems[key], d.val)
                    waited[key] = d.val
                name, a_, k_ = op.fn
                ins = getattr(e, name)(*a_, **k_)
                if op.is_dma:
                    ins.then_inc(sems[op.sem], 16)
                elif op.signal:
                    ins.then_inc(sems[op.sem], 1)

        @block.tensor
        def _(e):
            run("pe", e)

        @block.scalar
        def _(e):
            run("act", e)

        @block.vector
        def _(e):
            run("dve", e)

        @block.gpsimd
        def _(e):
            run("pool", e)

        @block.sync
        def _(e):
            run("sp", e)


class Alloc:
    def __init__(self, nc):
        self.nc = nc
        self.base = SB_LO
        self.top = SB_LO
        self.n = 0

    def mark(self):
        return self.top

    def reset(self, m):
        self.top = m

    def set_regions(self, regions):
        self.regions = [list(r) for r in regions]

    def ralloc(self, shape, dt):
        nb = 1
        for s_ in shape[1:]:
            nb *= s_
        nb *= 2 if dt == BF16 else 4
        nb = (nb + 63) // 64 * 64
        for r in self.regions:
            if r[0] + nb <= r[1]:
                off = r[0]
                r[0] += nb
                self.n += 1
                return self.nc.alloc_sbuf_tensor_at("t%d" % self.n, list(shape), dt, offset=off)
        raise AssertionError(("SBUF overflow", shape, self.regions))

    def __call__(self, shape, dt):
        nb = 1
        for s in shape[1:]:
            nb *= s
        nb *= 2 if dt == BF16 else 4
        nb = (nb + 63) // 64 * 64
        off = self.top
        self.top += nb
        assert self.top <= SB_HI, ("SBUF overflow", self.top)
        self.n += 1
        return self.nc.alloc_sbuf_tensor_at("t%d" % self.n, list(shape), dt, offset=off)


def build_program(debug=None):
    nc = bass.Bass("TRN2", target_bir_lowering=False)

    def din(name, shape, dt=F32):
        return nc.dram_tensor(name, list(shape), dt, kind="ExternalInput").ap()

    xo = din("xo", [NOTH, 1024])
    xown = din("xown", [NOWN, 1024])
    xh = din("xh", [128, 1024])
    posT = din("posT", [128, 80], I32)
    tfd = din("tf", [128, 48])
    cstd = din("cst", [128, 880])
    gmixd = din("gmix", [128, 1024])
    gmlpd = din("gmlp", [128, 1024])
    gfind = din("gfin", [128, 1024])
    gqd = din("gq", [128, 384])
    gkvd = din("gkv", [128, 256])
    gond = din("gon", [128, 512])
    gbd = din("gb", [128, 64])
    cwd = din("cw", [128, 40])
    cbd = din("cb", [128, 8])
    w_in = din("w_in", [1024, 4784])
    w_uq = din("w_uq", [384, 768])
    w_ukv = din("w_ukv", [256, 1024])
    w_bm = din("w_bm", [512, 1024])
    w_bl = din("w_bl", [512, 1024])
    w_out = din("w_out", [1024, 1024])
    w_up = din("w_up", [1024, 4096])
    w_down = din("w_down", [4096, 1024])
    y = nc.dram_tensor("y", [NOWN, 1024], F32, kind="ExternalOutput").ap()
    x1d = nc.dram_tensor("x1d", [NOWN, 1024], F32).ap()
    dbg = None
    if debug:
        dbg = nc.dram_tensor("dbg", [NOWN, 1024], F32, kind="ExternalOutput").ap()

    S = Sched()
    A = S.add
    al = Alloc(nc)
    st = ExitStack()
    psbig = [st.enter_context(nc.psum_tensor("psb%d" % i, [128, 1024], F32)) for i in range(4)]
    ps = [psbig[i // 2][:, (i % 2) * 512:(i % 2 + 1) * 512] for i in range(8)]

    def psb(i):
        return ps[i][:].bitcast(BF16)

    cst = al([128, 880], F32)
    idb = al([128, 128], BF16)
    mLEb = al([128, 128], BF16)
    mGEb = al([128, 128], BF16)
    gmix = al([128, 1024], F32)
    xt = [al([128, 1024], F32) for _ in range(2)]
    junk = al([128, 1024], BF16)
    hb = [al([128, 1024], BF16) for _ in range(2)]
    sst = al([128, 8], F32)
    wst = [al([128, 1024], F32) for _ in range(3)]
    LNSCt = al([128, 1], F32)
    ONEt = al([128, 1], F32)
    EPSt = al([128, 1], F32)
    idf = cst[:, 0:128]
    mLE = cst[:, 128:256]
    mGE = cst[:, 256:384]
    mSU = cst[:, 384:512]
    mSL = cst[:, 512:640]
    ones = cst[:, 640:768]
    sel = cst[0:32, 784:880]

    A("sp", lambda e: e.dma_start(out=cst[:], in_=cstd), writes=["cst"], dma=True)
    A("sp", lambda e: e.dma_start(out=gmix[:], in_=gmixd), writes=["gmix"], dma=True)
    A("dve", lambda e: e.memset(ONEt[:], 1.0), writes=["onet"])
    A("dve", lambda e: e.memset(EPSt[:], EPS), writes=["epst"])
    A("dve", lambda e: e.tensor_copy(out=idb[:], in_=idf), reads=["cst"], writes=["idb"])
    A("dve", lambda e: e.tensor_copy(out=mLEb[:], in_=mLE), reads=["cst"], writes=["mLEb"])
    A("dve", lambda e: e.tensor_copy(out=mGEb[:], in_=mGE), reads=["cst"], writes=["mGEb"])

    wstate = {"i": 0}

    def load_w(dst, dkey, src, K, c_lo, c_hi, dcol=0, queue="sp"):
        for k in range(K):
            c0 = c_lo
            while c0 < c_hi:
                cc = min(1024, c_hi - c0)
                sl = wstate["i"] % 3
                wstate["i"] += 1
                A(queue, lambda e, sl=sl, k=k, c0=c0, cc=cc: e.dma_start(out=wst[sl][:, 0:cc], in_=src[k * 128:(k + 1) * 128, c0:c0 + cc]),
                  writes=[("wst", sl)], dma=True)
                d0 = dcol + (c0 - c_lo)
                ce = ("pool", "act", "dve")[wstate["i"] % 3] if cc >= 256 else "pool"
                if ce == "act":
                    A("act", lambda e, sl=sl, k=k, d0=d0, cc=cc: e.copy(out=dst[:, k, d0:d0 + cc], in_=wst[sl][:, 0:cc]),
                      reads=[("wst", sl)], writes=[(dkey, k)])
                else:
                    A(ce, lambda e, sl=sl, k=k, d0=d0, cc=cc: e.tensor_copy(out=dst[:, k, d0:d0 + cc], in_=wst[sl][:, 0:cc]),
                      reads=[("wst", sl)], writes=[(dkey, k)])
                c0 += cc

    xstate = {"i": 0}

    def rstd_of(sskey_ap, n, key):
        A("act", lambda e: e.activation(out=sskey_ap, in_=sskey_ap, func=AF.Ln, scale=1.0 / n, bias=EPSt[:, 0:1]), reads=[key, "epst"], writes=[key])
        A("act", lambda e: e.activation(out=sskey_ap, in_=sskey_ap, func=AF.Exp, scale=-0.5), reads=[key], writes=[key])

    def xpipe(src_rows, g_tile, gkey, hT_dst, hkey, tps, keep=False):
        i = xstate["i"]
        xstate["i"] += 1
        sl = i % 2
        A("sp", lambda e: e.dma_start(out=xt[sl][:], in_=src_rows), writes=[("xt", sl)], dma=True)
        ssap = sst[:, sl:sl + 1]
        A("act", lambda e: e.activation(out=junk[:], in_=xt[sl][:], func=AF.Square, accum_out=ssap), reads=[("xt", sl)], writes=[("ss", sl)])
        rstd_of(ssap, 1024, ("ss", sl))
        A("dve", lambda e: e.scalar_tensor_tensor(out=hb[sl][:], in0=xt[sl][:], scalar=ssap, in1=g_tile[:], op0=ALU.mult, op1=ALU.mult),
          reads=[("xt", sl), ("ss", sl), gkey], writes=[("hb", sl)])
        pb = psb(tps)
        for k in range(8):
            A("pe", lambda e, k=k: e.transpose(out=pb[:, k * 128:(k + 1) * 128], in_=hb[sl][:, k * 128:(k + 1) * 128], identity=idb[:]),
              reads=[("hb", sl), "idb"], writes=[("ps", tps)])
        A("act", lambda e: e.copy(out=hT_dst, in_=pb.rearrange("p (k t) -> p k t", k=8)), reads=[("ps", tps)], writes=[hkey])
        return sl

    m_phase = al.mark()
    LGe = [al([128, 8], F32) for _ in range(2)]
    Ie = [al([128, 8], F32) for _ in range(2)]
    exg = [al([128, 8], F32) for _ in range(2)]
    Wt = [al([128, 8], F32) for _ in range(2)]
    dect = [al([128, 4], F32) for _ in range(2)]
    Arun = al([128, 4], F32)
    ktil = [al([128, 128], BF16) for _ in range(4)]
    Cf = al([128, 4, 129], F32)
    Cb = al([128, 4, 129], F32)
    Cfb = al([128, 4, 129], BF16)
    Cbb = al([128, 4, 129], BF16)
    qT = al([128, 4, NOWN], BF16)
    kT = al([128, 4, NOWN], BF16)
    ktok = al([128, 16, 512], BF16)
    vaug = al([128, 16, 4, 129], BF16)
    sgo = al([128, 16, 512], BF16)
    Gown = al([128, 16, 16], F32)
    LFo = al([128, 16, 8], F32)
    m_alias = al.mark()
    wl = al([128, 8, 2064], BF16)
    tf = al([128, 48], F32)
    tb = al([128, 48], F32)
    pf = al([128, 48], F32)
    pbk = al([128, 48], F32)
    gbias = al([128, 64], F32)
    cw = al([128, 40], F32)
    cb = al([128, 8], F32)
    gon = al([128, 512], F32)
    haloT = al([128, 8, 128], F32)
    hTg = [al([128, 8, 512], BF16) for _ in range(2)]
    padb = [al([128, 516], F32) for _ in range(2)]
    acc = [al([128, 512], F32) for _ in range(2)]
    kTg1 = al([128, 4, 512], BF16)
    ktokg1 = al([128, 4, 512], BF16)
    vaugg1 = al([128, 4, 4, 129], BF16)
    kTg = [kTg1, kTg1]
    ktokg = [ktokg1, ktokg1]
    vaugg = [vaugg1, vaugg1]
    Gg = [al([128, 4, 16], F32) for _ in range(2)]
    LFg = [al([128, 4, 8], F32) for _ in range(2)]
    sgt = [al([128, 512], BF16) for _ in range(2)]
    m_p12 = al.mark()
    al.reset(m_alias)
    ymT = al([128, 4, NOWN], BF16)
    m_keep = al.mark()
    EBt = al([128, 16, 8], F32)
    ECt = al([128, 16, 8], F32)
    WSt = al([128, 16, 8], F32)
    DECt = al([128, 16, 8], F32)
    cmt = al([128, 8], F32)
    HB = al([128, 16, 512], F32)
    PT = [al([128, 4, 128], BF16) for _ in range(2)]
    den = al([128, 4], F32)
    scl = al([128, 4], F32)
    hsum = [al([128, 4, 128], F32) for _ in range(2)]
    ssq = al([128, 4], F32)
    ymt = [al([128, 4, 128], BF16) for _ in range(2)]
    assert al.mark() <= m_p12
    al.reset(m_p12)

    for (t_, d_, k_) in ((tf, tfd, "tf"), (gbias, gbd, "gbias"), (cw, cwd, "cw"), (cb, cbd, "cb"), (gon, gond, "gon")):
        A("sp", lambda e, t_=t_, d_=d_: e.dma_start(out=t_[:], in_=d_), writes=[k_], dma=True)
    A("dve", lambda e: e.tensor_scalar(out=tb[:], in0=tf[:], scalar1=-1.0, scalar2=1.0, op0=ALU.mult, op1=ALU.add), reads=["tf"], writes=["tb"])
    A("dve", lambda e: e.tensor_scalar(out=pf[:], in0=tf[:], scalar1=-1.0, scalar2=BIG, op0=ALU.add, op1=ALU.mult), reads=["tf"], writes=["pf"])
    A("dve", lambda e: e.tensor_scalar(out=pbk[:], in0=tf[:], scalar1=-BIG, scalar2=None, op0=ALU.mult), reads=["tf"], writes=["pbk"])
    for t_, k_ in ((Cf, "Cf"), (Cb, "Cb"), (Arun, "Arun")):
        A("dve", lambda e, t_=t_: e.memset(t_[:], 0.0), writes=[k_])
    A("pool", lambda e: e.memset(vaug[:, :, :, 128:129], 1.0), writes=["vaug1"])
    A("pool", lambda e: e.memset(vaugg1[:, :, :, 128:129], 1.0), writes=[("vaugg1", 0)])

    load_w(wl, "wl", w_in, 8, 672, 2720, 0)
    for (s0, d0) in ((2720, 2048), (2728, 2052), (2724, 2056), (2732, 2060)):
        load_w(wl, "wl", w_in, 8, s0, s0 + 4, d0)
    WLK = [("wl", k) for k in range(8)]

    TPSX, TPSK, CPS0, CPS1, MVPS, UPS0 = 0, 1, 2, 3, 4, 5

    hTh = hTg[1][:, :, 0:128]
    xpipe(xh, gmix, "gmix", hTh, ("hTg", 1), TPSX)
    for c in range(8):
        bank = CPS0 + (c % 2)
        for k in range(8):
            A("pe", lambda e, c=c, k=k, bank=bank: e.matmul(ps[bank][:, 0:128], lhsT=wl[:, k, c * 128:(c + 1) * 128], rhs=hTg[1][:, k, 0:128],
                                                           start=(k == 0), stop=(k == 7)),
              reads=[("hTg", 1), ("wl", k)], writes=[("ps", bank)])
        A("act", lambda e, c=c, bank=bank: e.copy(out=haloT[:, c, :], in_=ps[bank][:, 0:128]), reads=[("ps", bank)], writes=["haloT"])

    cstate = {"i": 0}

    def conv_chunk(par, wcol, chunk, hidx, dst_ap, dst_key):
        ci = cstate["i"]
        cstate["i"] += 1
        bank = CPS0 + (ci % 2)
        pz = ci % 2
        for k in range(8):
            A("pe", lambda e, k=k: e.matmul(ps[bank][:], lhsT=wl[:, k, wcol:wcol + 128], rhs=hTg[par][:, k, :], start=(k == 0), stop=(k == 7)),
              reads=[("hTg", par), ("wl", k)], writes=[("ps", bank)])
        A("act", lambda e: e.copy(out=padb[pz][:, 2:514], in_=ps[bank][:]), reads=[("ps", bank)], writes=[("padb", pz)])
        A("pool", lambda e: e.tensor_copy(out=padb[pz][:, 0:2], in_=haloT[:, chunk, hidx:hidx + 2]), reads=["haloT"], writes=[("padbL", pz)])
        A("pool", lambda e: e.tensor_copy(out=padb[pz][:, 514:516], in_=haloT[:, chunk, hidx + 2:hidx + 4]), reads=["haloT"], writes=[("padbR", pz)])
        rk = [("padb", pz), ("padbL", pz), ("padbR", pz), "cw", "cb"]
        A("dve", lambda e: e.tensor_scalar(out=acc[pz][:], in0=padb[pz][:, 0:512], scalar1=cw[:, chunk * 5:chunk * 5 + 1], scalar2=cb[:, chunk:chunk + 1],
                                           op0=ALU.mult, op1=ALU.add), reads=rk, writes=[("acc", pz)])
        for j in range(1, 5):
            A("dve", lambda e, j=j: e.scalar_tensor_tensor(out=acc[pz][:], in0=padb[pz][:, j:j + 512], scalar=cw[:, chunk * 5 + j:chunk * 5 + j + 1],
                                                           in1=acc[pz][:], op0=ALU.mult, op1=ALU.add), reads=rk + [("acc", pz)], writes=[("acc", pz)])
        A("act", lambda e: e.activation(out=dst_ap, in_=acc[pz][:], func=AF.Silu), reads=[("acc", pz)], writes=[dst_key])

    def mlstm_group(gi, own):
        par = gi % 2
        src = xown if own else xo
        g0 = (gi - 12) if own else gi
        for i in range(4):
            r0 = g0 * 512 + i * 128
            xpipe(src[r0:r0 + 128, :], gmix, "gmix", hTg[par][:, :, i * 128:(i + 1) * 128], ("hTg", par), TPSX)
        hidx = (48 + 4 * g0) if own else 4 * g0
        for h in range(4):
            if own:
                conv_chunk(par, h * 128, h, hidx, qT[:, h, g0 * 512:(g0 + 1) * 512], "qT")
                conv_chunk(par, 512 + h * 128, 4 + h, hidx, kT[:, h, g0 * 512:(g0 + 1) * 512], "kT")
            else:
                conv_chunk(par, 512 + h * 128, 4 + h, hidx, kTg[par][:, h, :], ("kTg", 0))
        for b2 in range(2):
            pb = psb(TPSK)
            for bb in range(2):
                blk = b2 * 2 + bb
                for h in range(4):
                    if own:
                        src_ap = kT[:, h, g0 * 512 + blk * 128:g0 * 512 + (blk + 1) * 128]
                        rkey = "kT"
                    else:
                        src_ap = kTg[par][:, h, blk * 128:(blk + 1) * 128]
                        rkey = ("kTg", 0)
                    o0 = (bb * 4 + h) * 128
                    A("pe", lambda e, src_ap=src_ap, o0=o0: e.transpose(out=pb[:, o0:o0 + 128], in_=src_ap, identity=idb[:]),
                      reads=[rkey, "idb"], writes=[("ps", TPSK)])
            if own:
                dst = ktok[:, g0 * 4 + b2 * 2:g0 * 4 + b2 * 2 + 2, :]
                dk = "ktok"
            else:
                dst = ktokg[par][:, b2 * 2:b2 * 2 + 2, :]
                dk = ("ktokg", 0)
            A("act", lambda e, dst=dst, pb=pb: e.copy(out=dst, in_=pb.rearrange("p (b c) -> p b c", b=2)), reads=[("ps", TPSK)], writes=[dk])
        for i in range(4):
            for k in range(8):
                A("pe", lambda e, i=i, k=k: e.matmul(ps[MVPS][:], lhsT=hTg[par][:, k, i * 128:(i + 1) * 128], rhs=wl[:, k, 1024:1536],
                                                     start=(k == 0), stop=(k == 7)), reads=[("hTg", par), ("wl", k)], writes=[("ps", MVPS)])
            if own:
                dst = vaug[:, g0 * 4 + i, :, 0:128]
                dk = "vaug"
            else:
                dst = vaugg[par][:, i, :, 0:128]
                dk = ("vaugg", 0)
            A("dve", lambda e, dst=dst: e.tensor_copy(out=dst, in_=ps[MVPS][:].rearrange("p (h d) -> p h d", h=4)), reads=[("ps", MVPS)], writes=[dk])
            if own:
                for k in range(8):
                    A("pe", lambda e, i=i, k=k: e.matmul(ps[MVPS][:], lhsT=hTg[par][:, k, i * 128:(i + 1) * 128], rhs=wl[:, k, 1536:2048],
                                                         start=(k == 0), stop=(k == 7)), reads=[("hTg", par), ("wl", k)], writes=[("ps", MVPS)])
                sp_ = i % 2
                A("act", lambda e, sp_=sp_: e.activation(out=sgt[sp_][:], in_=ps[MVPS][:], func=AF.Sigmoid), reads=[("ps", MVPS)], writes=[("sgt", sp_)])
                A("pool", lambda e, sp_=sp_, i=i: e.tensor_tensor(out=sgo[:, g0 * 4 + i, :], in0=sgt[sp_][:], in1=gon[:], op=ALU.mult),
                  reads=[("sgt", sp_), "gon"], writes=["sgo"])
        for i in range(4):
            for k in range(8):
                A("pe", lambda e, i=i, k=k: e.matmul(ps[MVPS][:, i * 16:(i + 1) * 16], lhsT=hTg[par][:, k, i * 128:(i + 1) * 128], rhs=wl[:, k, 2048:2064],
                                                     start=(k == 0), stop=(k == 7)), reads=[("hTg", par), ("wl", k)], writes=[("ps", MVPS)])
        if own:
            Gd = Gown[:, g0 * 4:g0 * 4 + 4, :]
            gk = "Gown"
            LFd = LFo[:, g0 * 4:g0 * 4 + 4, :]
            lk = "LFo"
        else:
            Gd = Gg[par][:]
            gk = ("Gg", par)
            LFd = LFg[par][:]
            lk = ("LFg", par)
        A("dve", lambda e: e.tensor_tensor(out=Gd, in0=ps[MVPS][:, 0:64].rearrange("p (b c) -> p b c", b=4),
                                           in1=gbias[:].rearrange("p (b c) -> p b c", b=4), op=ALU.add), reads=[("ps", MVPS), "gbias"], writes=[gk])
        A("act", lambda e: e.activation(out=LFd, in_=Gd[:, :, 8:16], func=AF.Exp, scale=-1.0), reads=[gk], writes=[lk])
        A("act", lambda e: e.activation(out=LFd, in_=LFd, func=AF.Ln, bias=ONEt[:, 0:1]), reads=[lk, "onet"], writes=[lk])
        if own:
            return
        for blk in range(4):
            gb_ = gi * 4 + blk
            z = blk % 2
            A("dve", lambda e: e.tensor_scalar(out=LGe[z][:, 0:4], in0=LFg[par][:, blk, 0:4], scalar1=tf[:, gb_:gb_ + 1], scalar2=-1.0, op0=ALU.mult, op1=ALU.mult),
              reads=[lk, "tf"], writes=[("LGe", z)])
            A("dve", lambda e: e.tensor_scalar(out=LGe[z][:, 4:8], in0=LFg[par][:, blk, 4:8], scalar1=tb[:, gb_:gb_ + 1], scalar2=-1.0, op0=ALU.mult, op1=ALU.mult),
              reads=[lk, "tb"], writes=[("LGe", z)])
            A("dve", lambda e: e.tensor_scalar(out=Ie[z][:, 0:4], in0=Gg[par][:, blk, 0:4], scalar1=pf[:, gb_:gb_ + 1], scalar2=None, op0=ALU.add),
              reads=[gk, "pf"], writes=[("Ie", z)])
            A("dve", lambda e: e.tensor_scalar(out=Ie[z][:, 4:8], in0=Gg[par][:, blk, 4:8], scalar1=pbk[:, gb_:gb_ + 1], scalar2=None, op0=ALU.add),
              reads=[gk, "pbk"], writes=[("Ie", z)])
            gp = UPS0 + 2
            A("pe", lambda e: e.matmul(ps[gp][:, 0:4], lhsT=mSU, rhs=LGe[z][:, 0:4], start=True, stop=True), reads=["cst", ("LGe", z)], writes=[("ps", gp)])
            A("pe", lambda e: e.matmul(ps[gp][:, 4:8], lhsT=mSL, rhs=LGe[z][:, 4:8], start=True, stop=True), reads=["cst", ("LGe", z)], writes=[("ps", gp)])
            A("pe", lambda e: e.matmul(ps[gp][:, 8:16], lhsT=ones, rhs=LGe[z][:, 0:8], start=True, stop=True), reads=["cst", ("LGe", z)], writes=[("ps", gp)])
            A("dve", lambda e: e.tensor_tensor(out=exg[z][:], in0=ps[gp][:, 0:8], in1=Ie[z][:], op=ALU.add), reads=[("ps", gp), ("Ie", z)], writes=[("exg", z)])
            A("dve", lambda e: e.tensor_tensor(out=exg[z][:, 4:8], in0=exg[z][:, 4:8], in1=Arun[:], op=ALU.add), reads=[("exg", z), "Arun"], writes=[("exg", z)])
            A("act", lambda e: e.activation(out=Wt[z][:], in_=exg[z][:], func=AF.Exp, bias=LNSCt[:, 0:1]), reads=[("exg", z), "lnsc"], writes=[("Wt", z)])
            A("act", lambda e: e.activation(out=dect[z][:], in_=ps[gp][:, 8:12], func=AF.Exp), reads=[("ps", gp)], writes=[("dect", z)])
            A("dve", lambda e: e.tensor_tensor(out=Arun[:], in0=Arun[:], in1=ps[gp][:, 12:16], op=ALU.add), reads=["Arun", ("ps", gp)], writes=["Arun"])
            for u in range(8):
                ch, h = divmod(u, 4)
                kz = u % 4
                A("dve", lambda e, u=u, h=h, kz=kz: e.tensor_scalar(out=ktil[kz][:], in0=ktokg[par][:, blk, h * 128:(h + 1) * 128],
                                                                    scalar1=Wt[z][:, u:u + 1], scalar2=None, op0=ALU.mult),
                  reads=[("ktokg", 0), ("Wt", z)], writes=[("ktil", kz)])
                bank = UPS0 + (u % 3)
                c0 = (u // 3) * 129 + (128 if bank == UPS0 + 2 else 0)
                A("pe", lambda e, h=h, kz=kz, bank=bank, c0=c0: e.matmul(ps[bank][:, c0:c0 + 129], lhsT=ktil[kz][:], rhs=vaugg[par][:, blk, h, :],
                                                                         start=True, stop=True),
                  reads=[("ktil", kz), ("vaugg", 0), ("vaugg1", 0)], writes=[("ps", bank)])
                if ch == 0:
                    A("dve", lambda e, h=h, bank=bank, c0=c0: e.scalar_tensor_tensor(out=Cf[:, h, :], in0=Cf[:, h, :], scalar=dect[z][:, h:h + 1],
                                                                                     in1=ps[bank][:, c0:c0 + 129], op0=ALU.mult, op1=ALU.add),
                      reads=["Cf", ("dect", z), ("ps", bank)], writes=["Cf"])
                else:
                    A("dve", lambda e, h=h, bank=bank, c0=c0: e.tensor_tensor(out=Cb[:, h, :], in0=Cb[:, h, :], in1=ps[bank][:, c0:c0 + 129], op=ALU.add),
                      reads=["Cb", ("ps", bank)], writes=["Cb"])

    A("dve", lambda e: e.memset(LNSCt[:], LNSC), writes=["lnsc"])

    for gi in range(12 if not (debug and (debug.endswith("_fast") or debug.startswith("p4"))) else 0):
        mlstm_group(gi, False)
    for gi in range(12, 16 if not (debug and debug.startswith("p4")) else 12):
        mlstm_group(gi, True)

    if debug and debug.split("_")[0] in ("qTe", "kTe"):
        src_t = qT if debug.startswith("qTe") else kT
        dt_ = al([128, 1024], F32)
        for h in range(4):
            for half in range(2):
                A("dve", lambda e, h=h, half=half: e.tensor_copy(out=dt_[:], in_=src_t[:, h, half * 1024:(half + 1) * 1024]), reads=["qT", "kT"], writes=["dt_"])
                r0 = (h * 2 + half) * 128
                A("sp", lambda e, r0=r0: e.dma_start(out=dbg[r0:r0 + 128, :], in_=dt_[:]), reads=["dt_"], writes=["dbgo"], dma=True)
        A("sp", lambda e: e.nop(), reads=["dbgo"])
        S.emit(nc, st)
        st.close()
        return nc
    p12keys = [("wl", k) for k in range(8)] + ["tf", "tb", "pf", "pbk", "gbias", "cw", "cb", "gon", "haloT", ("hTg", 0), ("hTg", 1),
               ("padb", 0), ("padb", 1), ("padbL", 0), ("padbL", 1), ("padbR", 0), ("padbR", 1), ("acc", 0), ("acc", 1),
               ("kTg", 0), ("ktokg", 0), ("vaugg", 0), ("vaugg1", 0), ("Gg", 0), ("Gg", 1), ("LFg", 0), ("LFg", 1), ("sgt", 0), ("sgt", 1)]
    p3keys = ["EBt", "ECt", "WSt", "DECt", "cmt", "HB", ("PT", 0), ("PT", 1), "den", "scl", ("hsum", 0), ("hsum", 1), "ssq", ("ymt", 0), ("ymt", 1), "ymT"]
    A("dve", lambda e: e.memset(cmt[:], 0.0), writes=p12keys + p3keys)
    A("pool", lambda e: e.tensor_copy(out=Cfb[:], in_=Cf[:]), reads=["Cf"], writes=["Cfb"])
    A("pool", lambda e: e.tensor_copy(out=Cbb[:], in_=Cb[:]), reads=["Cb"], writes=["Cbb"])
    GP = 7
    for blk in range(16 if not (debug and debug.startswith("p4")) else 0):
        z = blk % 2
        A("dve", lambda e, blk=blk: e.tensor_scalar(out=LGe[z][:], in0=LFo[:, blk, :], scalar1=-1.0, scalar2=None, op0=ALU.mult), reads=["LFo"], writes=[("LGe", z)])
        A("pe", lambda e: e.matmul(ps[GP][:, 0:4], lhsT=mLE, rhs=LGe[z][:, 0:4], start=True, stop=True), reads=["cst", ("LGe", z)], writes=[("ps", GP)])
        A("pe", lambda e: e.matmul(ps[GP][:, 4:8], lhsT=mGE, rhs=LGe[z][:, 4:8], start=True, stop=True), reads=["cst", ("LGe", z)], writes=[("ps", GP)])
        A("pe", lambda e: e.matmul(ps[GP][:, 8:16], lhsT=ones, rhs=LGe[z][:, 0:8], start=True, stop=True), reads=["cst", ("LGe", z)], writes=[("ps", GP)])
        A("act", lambda e, blk=blk: e.activation(out=EBt[:, blk, :], in_=ps[GP][:, 0:8], func=AF.Exp), reads=[("ps", GP)], writes=["EBt"])
        A("dve", lambda e, blk=blk: e.tensor_tensor(out=cmt[:], in0=Gown[:, blk, 0:8], in1=ps[GP][:, 0:8], op=ALU.subtract), reads=["Gown", ("ps", GP)], writes=["cmt"])
        A("act", lambda e, blk=blk: e.activation(out=ECt[:, blk, :], in_=cmt[:], func=AF.Exp, bias=LNSCt[:, 0:1]), reads=["cmt", "lnsc"], writes=["ECt"])
        A("act", lambda e, blk=blk: e.activation(out=DECt[:, blk, :], in_=ps[GP][:, 8:16], func=AF.Exp), reads=[("ps", GP)], writes=["DECt"])
        A("dve", lambda e, blk=blk: e.tensor_tensor(out=cmt[:], in0=cmt[:], in1=ps[GP][:, 8:16], op=ALU.add), reads=["cmt", ("ps", GP)], writes=["cmt"])
        A("act", lambda e, blk=blk: e.activation(out=WSt[:, blk, :], in_=cmt[:], func=AF.Exp, bias=LNSCt[:, 0:1]), reads=["cmt", "lnsc"], writes=["WSt"])

    SPS, ND0, ND1, UB0, UB1, TPY = 0, 1, 2, 3, 4, 5
    def dirpass(d):
        blocks = list(range(16)) if d == 0 else list(range(15, -1, -1))
        Cx, Cxb, ck, ckb = (Cf, Cfb, "Cf", "Cfb") if d == 0 else (Cb, Cbb, "Cb", "Cbb")
        maskb = mLEb if d == 0 else mGEb
        mk_ = "mLEb" if d == 0 else "mGEb"
        final = (d == 0)
        for bi, blk in enumerate(blocks):
            z = bi % 2
            tsl = slice(blk * 128, (blk + 1) * 128)
            for h in range(4):
                A("pe", lambda e, h=h: e.matmul(ps[SPS][:, h * 128:(h + 1) * 128], lhsT=kT[:, h, tsl], rhs=qT[:, h, tsl], start=True, stop=True),
                  reads=["kT", "qT"], writes=[("ps", SPS)])
            for h in range(4):
                A("dve", lambda e, h=h: e.scalar_tensor_tensor(out=PT[z][:, h, :], in0=ps[SPS][:, h * 128:(h + 1) * 128], scalar=ECt[:, blk, d * 4 + h:d * 4 + h + 1],
                                                               in1=maskb[:], op0=ALU.mult, op1=ALU.mult),
                  reads=[("ps", SPS), "ECt", mk_], writes=[("PT", z)])
            for h in range(4):
                bank = ND0 + h % 2
                c0 = (h // 2) * 129
                A("pe", lambda e, h=h, bank=bank, c0=c0: e.matmul(ps[bank][:, c0:c0 + 129], lhsT=PT[z][:, h, :], rhs=vaug[:, blk, h, :], start=True, stop=False),
                  reads=[("PT", z), "vaug", "vaug1"], writes=[("ps", bank)])
                A("pe", lambda e, h=h, bank=bank, c0=c0: e.matmul(ps[bank][:, c0:c0 + 129], lhsT=qT[:, h, tsl], rhs=Cxb[:, h, :], start=False, stop=True),
                  reads=["qT", ckb], writes=[("ps", bank)])
            for h in range(4):
                bank = ND0 + h % 2
                c0 = (h // 2) * 129
                A("dve", lambda e, h=h, bank=bank, c0=c0: e.tensor_tensor(out=den[:, h:h + 1], in0=ps[bank][:, c0 + 128:c0 + 129],
                                                                          in1=EBt[:, blk, d * 4 + h:d * 4 + h + 1], op=ALU.mult),
                  reads=[("ps", bank), "EBt"], writes=["den"])
            A("dve", lambda e: e.tensor_scalar(out=scl[:], in0=den[:], scalar1=-1.0, scalar2=None, op0=ALU.mult), reads=["den"], writes=["scl"])
            A("dve", lambda e: e.tensor_tensor(out=den[:], in0=den[:], in1=scl[:], op=ALU.max), reads=["den", "scl"], writes=["den"])
            A("dve", lambda e: e.tensor_scalar(out=den[:], in0=den[:], scalar1=1.0, scalar2=None, op0=ALU.max), reads=["den"], writes=["den"])
            A("dve", lambda e: e.reciprocal(out=den[:], in_=den[:]), reads=["den"], writes=["den"])
            A("dve", lambda e: e.tensor_tensor(out=scl[:], in0=den[:], in1=EBt[:, blk, d * 4:d * 4 + 4], op=ALU.mult), reads=["den", "EBt"], writes=["scl"])
            for h in range(4):
                bank = ND0 + h % 2
                c0 = (h // 2) * 129
                if not final:
                    A("dve", lambda e, h=h, bank=bank, c0=c0: e.tensor_scalar(out=HB[:, blk, h * 128:(h + 1) * 128], in0=ps[bank][:, c0:c0 + 128],
                                                                              scalar1=scl[:, h:h + 1], scalar2=None, op0=ALU.mult),
                      reads=[("ps", bank), "scl"], writes=["HB"])
                else:
                    A("dve", lambda e, h=h, bank=bank, c0=c0: e.scalar_tensor_tensor(out=hsum[z][:, h, :], in0=ps[bank][:, c0:c0 + 128], scalar=scl[:, h:h + 1],
                                                                                     in1=HB[:, blk, h * 128:(h + 1) * 128], op0=ALU.mult, op1=ALU.add),
                      reads=[("ps", bank), "scl", "HB"], writes=[("hsum", z)])
            if bi < 15:
                for h in range(4):
                    kz = h
                    bank = UB0 + h % 2
                    c0 = (h // 2) * 129
                    A("dve", lambda e, h=h, kz=kz: e.tensor_scalar(out=ktil[kz][:], in0=ktok[:, blk, h * 128:(h + 1) * 128],
                                                                   scalar1=WSt[:, blk, d * 4 + h:d * 4 + h + 1], scalar2=None, op0=ALU.mult),
                      reads=["ktok", "WSt"], writes=[("ktil", kz)])
                    A("pe", lambda e, h=h, kz=kz, bank=bank, c0=c0: e.matmul(ps[bank][:, c0:c0 + 129], lhsT=ktil[kz][:], rhs=vaug[:, blk, h, :], start=True, stop=True),
                      reads=[("ktil", kz), "vaug", "vaug1"], writes=[("ps", bank)])
                    A("dve", lambda e, h=h, bank=bank, c0=c0: e.scalar_tensor_tensor(out=Cx[:, h, :], in0=Cx[:, h, :], scalar=DECt[:, blk, d * 4 + h:d * 4 + h + 1],
                                                                                     in1=ps[bank][:, c0:c0 + 129], op0=ALU.mult, op1=ALU.add),
                      reads=[ck, "DECt", ("ps", bank)], writes=[ck])
                A("pool", lambda e: e.tensor_copy(out=Cxb[:], in_=Cx[:]), reads=[ck], writes=[ckb])
            if final:
                for h in range(4):
                    A("act", lambda e, h=h: e.activation(out=junk[:, 0:128], in_=hsum[z][:, h, :], func=AF.Square, accum_out=ssq[:, h:h + 1]),
                      reads=[("hsum", z)], writes=["ssq"])
                rstd_of(ssq[:], 128, "ssq")
                for h in range(4):
                    A("dve", lambda e, h=h: e.scalar_tensor_tensor(out=ymt[z][:, h, :], in0=hsum[z][:, h, :], scalar=ssq[:, h:h + 1],
                                                                   in1=sgo[:, blk, h * 128:(h + 1) * 128], op0=ALU.mult, op1=ALU.mult),
                      reads=[("hsum", z), "ssq", "sgo"], writes=[("ymt", z)])
                pb = psb(TPY)
                for h in range(4):
                    A("pe", lambda e, h=h: e.transpose(out=pb[:, h * 128:(h + 1) * 128], in_=ymt[z][:, h, :], identity=idb[:]),
                      reads=[("ymt", z), "idb"], writes=[("ps", TPY)])
                A("act", lambda e: e.copy(out=ymT[:, :, tsl], in_=pb[:, 0:512].rearrange("p (h t) -> p h t", h=4)), reads=[("ps", TPY)], writes=["ymT"])

    if not (debug and debug.startswith("p4")):
        dirpass(1)
    if debug and debug.startswith("HB"):
        for blk in range(16):
            A("sp", lambda e, blk=blk: e.dma_start(out=dbg[blk * 128:(blk + 1) * 128, 0:512], in_=HB[:, blk, :]), reads=["HB"], writes=["dbgo"], dma=True)
        A("sp", lambda e: e.nop(), reads=["dbgo"])
        S.emit(nc, st)
        st.close()
        return nc
    if not (debug and debug.startswith("p4")):
        dirpass(0)

    if debug and debug.split("_")[0] in ("mlstm", "qT", "kT"):
        debug = debug.split("_")[0]
        if debug == "qT":
            ymT = qT
        elif debug == "kT":
            ymT = kT
        ymk = {"mlstm": "ymT", "qT": "qT", "kT": "kT"}[debug]
        dt_ = al([128, 1024], F32)
        for h in range(4):
            for half in range(2):
                A("dve", lambda e, h=h, half=half: e.tensor_copy(out=dt_[:], in_=ymT[:, h, half * 1024:(half + 1) * 1024]), reads=[ymk], writes=["dt_"])
                r0 = (h * 2 + half) * 128
                A("sp", lambda e, r0=r0: e.dma_start(out=dbg[r0:r0 + 128, :], in_=dt_[:]), reads=["dt_"], writes=["dbgo"], dma=True)
        A("sp", lambda e: e.nop(), reads=["dbgo"])
        S.emit(nc, st)
        st.close()
        return nc

    S.barrier(lambda e: e.memset(sst[:, 7:8], 0.0))
    al.set_regions([(m_phase, m_alias), (m_keep, SB_HI)])
    R = al.ralloc
    ckvT = R([128, 2, 8192], BF16)
    kropeT = R([32, 8192], BF16)
    QT = R([96, 8, NOWN], BF16)
    yaT = R([128, 4, NOWN], BF16)
    reg_p4 = [list(r) for r in al.regions]
    wkv = R([128, 8, 288], BF16)
    wq = R([128, 8, 384], BF16)
    wuq = R([128, 3, 768], BF16)
    gq = R([128, 384], F32)
    gkv = R([128, 256], F32)
    hT4 = [R([128, 8, 128], BF16) for _ in range(2)]
    sinT = R([128, 80, 16], F32)
    cosT = R([128, 80, 16], F32)
    sinq = R([128, 16, 4, 16], F32)
    cosq = R([128, 16, 4, 16], F32)
    reg_tmp = [list(r) for r in al.regions]
    posi = R([128, 80], I32)
    posf = R([128, 80], F32)
    ang = R([128, 80, 16], F32)
    tq = R([128, 1280], F32)
    tk = R([128, 1280], I32)
    tkf = R([128, 1280], F32)

    load_w(wkv, "wkv", w_in, 8, 384, 672)
    load_w(wq, "wq", w_in, 8, 0, 384)
    load_w(wuq, "wuq", w_uq, 3, 0, 768)
    A("sp", lambda e: e.dma_start(out=gq[:], in_=gqd), writes=["gq"], dma=True)
    A("sp", lambda e: e.dma_start(out=gkv[:], in_=gkvd), writes=["gkv"], dma=True)
    A("sp", lambda e: e.dma_start(out=posi[:], in_=posT), writes=["posi"], dma=True)
    A("dve", lambda e: e.tensor_copy(out=posf[:], in_=posi[:]), reads=["posi"], writes=["posf"])
    invf = (np.float32(10000.0) ** (-np.arange(0, 32, 2, dtype=np.float32) / np.float32(32))).astype(np.float32)
    for f in range(16):
        A("dve", lambda e, f=f: e.tensor_scalar(out=ang[:, :, f], in0=posf[:], scalar1=float(invf[f]), scalar2=None, op0=ALU.mult), reads=["posf"], writes=["ang"])
    angf = ang[:].rearrange("p a b -> p (a b)")
    TWO_PI = 2.0 * math.pi
    for (dst, off) in ((sinT, 0.0), (cosT, 0.25)):
        dstf = dst[:].rearrange("p a b -> p (a b)")
        A("dve", lambda e, off=off: e.tensor_scalar(out=tq[:], in0=angf, scalar1=1.0 / TWO_PI, scalar2=off, op0=ALU.mult, op1=ALU.add), reads=["ang"], writes=["tq"])
        A("dve", lambda e: e.tensor_copy(out=tk[:], in_=tq[:]), reads=["tq"], writes=["tk"])
        A("dve", lambda e: e.tensor_copy(out=tkf[:], in_=tk[:]), reads=["tk"], writes=["tkf"])
        A("dve", lambda e: e.tensor_tensor(out=tq[:], in0=tq[:], in1=tkf[:], op=ALU.subtract), reads=["tq", "tkf"], writes=["tq"])
        A("dve", lambda e: e.tensor_scalar(out=tkf[:], in0=tq[:], scalar1=0.5, scalar2=None, op0=ALU.is_gt), reads=["tq"], writes=["tkf"])
        A("dve", lambda e: e.tensor_tensor(out=tq[:], in0=tq[:], in1=tkf[:], op=ALU.subtract), reads=["tq", "tkf"], writes=["tq"])
        A("dve", lambda e: e.tensor_scalar(out=tkf[:], in0=tq[:], scalar1=-0.5, scalar2=None, op0=ALU.is_lt), reads=["tq"], writes=["tkf"])
        A("dve", lambda e: e.tensor_tensor(out=tq[:], in0=tq[:], in1=tkf[:], op=ALU.add), reads=["tq", "tkf"], writes=["tq"])
        A("dve", lambda e: e.tensor_scalar(out=tq[:], in0=tq[:], scalar1=-0.4999, scalar2=0.4999, op0=ALU.max, op1=ALU.min), reads=["tq"], writes=["tq"])
        A("act", lambda e, dstf=dstf: e.activation(out=dstf, in_=tq[:], func=AF.Sin, scale=TWO_PI), reads=["tq"], writes=["sincos"])
    for hh in range(4):
        A("dve", lambda e, hh=hh: e.tensor_copy(out=sinq[:, :, hh, :], in_=sinT[:, 64:80, :]), reads=["sincos"], writes=["sinq"])
        A("dve", lambda e, hh=hh: e.tensor_copy(out=cosq[:, :, hh, :], in_=cosT[:, 64:80, :]), reads=["sincos"], writes=["cosq"])

    if debug == "p4a":
        A("sp", lambda e: e.dma_start(out=dbg[0:128, 0:1024], in_=sinT[:].rearrange("p a b -> p (a b)")[:, 0:1024]), reads=["sincos"], writes=["dbgo"], dma=True)
        A("sp", lambda e: e.nop(), reads=["dbgo"])
        S.emit(nc, st)
        st.close()
        return nc
    S.barrier(lambda e: e.memset(sst[:, 7:8], 0.0))
    al.regions = [list(r) for r in reg_tmp]
    cn = [R([128, 384], BF16) for _ in range(2)]
    krr = [R([128, 32], BF16) for _ in range(2)]
    rt = [R([128, 4, 16], F32) for _ in range(4)]
    cqnT = R([128, 3, 128], BF16)
    qtok = R([128, 8, 96], BF16)
    TPSX, LAT0, TPC, Q0, Q1, TPQ = 0, 1, 3, 4, 5, 6
    lstate = {"i": 0}

    def rope(x1, x2, cs, sn, o1, o2, rkeys, okey, shape4):
        ta, tb_, tc, td = [t[:] if shape4 else t[:, 0, :] for t in rt]
        A("dve", lambda e: e.tensor_tensor(out=ta, in0=x1, in1=cs, op=ALU.mult), reads=rkeys, writes=["rt0"])
        A("dve", lambda e: e.tensor_tensor(out=tb_, in0=x2, in1=sn, op=ALU.mult), reads=rkeys, writes=["rt1"])
        A("dve", lambda e: e.tensor_tensor(out=o1, in0=ta, in1=tb_, op=ALU.subtract), reads=["rt0", "rt1"], writes=[okey])
        A("dve", lambda e: e.tensor_tensor(out=tc, in0=x2, in1=cs, op=ALU.mult), reads=rkeys, writes=["rt2"])
        A("dve", lambda e: e.tensor_tensor(out=td, in0=x1, in1=sn, op=ALU.mult), reads=rkeys, writes=["rt3"])
        A("dve", lambda e: e.tensor_tensor(out=o2, in0=tc, in1=td, op=ALU.add), reads=["rt2", "rt3"], writes=[okey])

    for kt in range(64):
        z = kt % 2
        src = xo[kt * 128:(kt + 1) * 128, :] if kt < 48 else xown[(kt - 48) * 128:(kt - 47) * 128, :]
        xpipe(src, gmix, "gmix", hT4[z][:], ("hT4", z), TPSX)
        lat = LAT0 + z
        for k in range(8):
            A("pe", lambda e, k=k: e.matmul(ps[lat][:, 0:288], lhsT=hT4[z][:, k, :], rhs=wkv[:, k, :], start=(k == 0), stop=(k == 7)),
              reads=[("hT4", z), ("wkv", k)], writes=[("ps", lat)])
        ssap = sst[:, 2 + z:3 + z]
        A("act", lambda e: e.activation(out=junk[:, 0:256], in_=ps[lat][:, 0:256], func=AF.Square, accum_out=ssap), reads=[("ps", lat)], writes=[("ssl", z)])
        rstd_of(ssap, 256, ("ssl", z))
        A("dve", lambda e: e.scalar_tensor_tensor(out=cn[z][:, 0:256], in0=ps[lat][:, 0:256], scalar=ssap, in1=gkv[:], op0=ALU.mult, op1=ALU.mult),
          reads=[("ps", lat), ("ssl", z), "gkv"], writes=[("cn", z)])
        rope(ps[lat][:, 256:272], ps[lat][:, 272:288], cosT[:, kt, :], sinT[:, kt, :], krr[z][:, 0:16], krr[z][:, 16:32],
             [("ps", lat), "sincos"], ("krr", z), False)
        pb = psb(TPC)
        for c in range(2):
            A("pe", lambda e, c=c: e.transpose(out=pb[:, c * 128:(c + 1) * 128], in_=cn[z][:, c * 128:(c + 1) * 128], identity=idb[:]),
              reads=[("cn", z), "idb"], writes=[("ps", TPC)])
        A("pe", lambda e: e.transpose(out=pb[0:32, 256:384], in_=krr[z][:], identity=idb[:]), reads=[("krr", z), "idb"], writes=[("ps", TPC)])
        A("act", lambda e: e.copy(out=ckvT[:, :, kt * 128:(kt + 1) * 128], in_=pb[:, 0:256].rearrange("p (c t) -> p c t", c=2)), reads=[("ps", TPC)], writes=["ckvT"])
        A("act", lambda e: e.copy(out=kropeT[:, kt * 128:(kt + 1) * 128], in_=pb[0:32, 256:384]), reads=[("ps", TPC)], writes=["kropeT"])

    for ot in range(16 if debug != "p4b" else 0):
        z = ot % 2
        xpipe(xown[ot * 128:(ot + 1) * 128, :], gmix, "gmix", hT4[z][:], ("hT4", z), TPSX)
        lat = LAT0 + z
        for k in range(8):
            A("pe", lambda e, k=k: e.matmul(ps[lat][:, 0:384], lhsT=hT4[z][:, k, :], rhs=wq[:, k, :], start=(k == 0), stop=(k == 7)),
              reads=[("hT4", z), ("wq", k)], writes=[("ps", lat)])
        ssap = sst[:, 2 + z:3 + z]
        A("act", lambda e: e.activation(out=junk[:, 0:384], in_=ps[lat][:, 0:384], func=AF.Square, accum_out=ssap), reads=[("ps", lat)], writes=[("ssl", z)])
        rstd_of(ssap, 384, ("ssl", z))
        A("dve", lambda e: e.scalar_tensor_tensor(out=cn[z][:], in0=ps[lat][:, 0:384], scalar=ssap, in1=gq[:], op0=ALU.mult, op1=ALU.mult),
          reads=[("ps", lat), ("ssl", z), "gq"], writes=[("cn", z)])
        pb = psb(TPC)
        for c in range(3):
            A("pe", lambda e, c=c: e.transpose(out=pb[:, c * 128:(c + 1) * 128], in_=cn[z][:, c * 128:(c + 1) * 128], identity=idb[:]),
              reads=[("cn", z), "idb"], writes=[("ps", TPC)])
        A("act", lambda e: e.copy(out=cqnT[:], in_=pb[:, 0:384].rearrange("p (c t) -> p c t", c=3)), reads=[("ps", TPC)], writes=["cqnT"])
        qlvl = int(debug[3:]) if (debug and debug.startswith("p4q")) else 9
        if qlvl < 2:
            continue
        for x_ in range(2):
            qb = Q0 + x_
            for c in range(3):
                A("pe", lambda e, c=c: e.matmul(ps[qb][:, 0:384], lhsT=cqnT[:, c, :], rhs=wuq[:, c, x_ * 384:(x_ + 1) * 384], start=(c == 0), stop=(c == 2)),
                  reads=["cqnT", ("wuq", c)], writes=[("ps", qb)])
            V4 = ps[qb][:, 0:384].rearrange("p (h d) -> p h d", h=4)
            A("act", lambda e: e.copy(out=qtok[:, x_ * 4:x_ * 4 + 4, 0:64], in_=V4[:, :, 0:64]), reads=[("ps", qb)], writes=["qtokn"])
            if qlvl < 3:
                continue
            for hh in range(4):
                c0 = hh * 96
                rope(ps[qb][:, c0 + 64:c0 + 80], ps[qb][:, c0 + 80:c0 + 96], cosT[:, 64 + ot, :], sinT[:, 64 + ot, :],
                     qtok[:, x_ * 4 + hh, 64:80], qtok[:, x_ * 4 + hh, 80:96], [("ps", qb), "sincos"], "qtokr", False)
        if qlvl < 4:
            continue
        pq = psb(TPQ)
        for h in range(8):
            A("pe", lambda e, h=h: e.transpose(out=pq[0:96, h * 128:(h + 1) * 128], in_=qtok[:, h, :], identity=idb[:]),
              reads=["qtokn", "qtokr", "idb"], writes=[("ps", TPQ)])
        A("act", lambda e: e.copy(out=QT[:, :, ot * 128:(ot + 1) * 128], in_=pq[0:96, :].rearrange("p (h t) -> p h t", h=8)), reads=[("ps", TPQ)], writes=["QT"])

    if debug and debug.startswith("p4"):
        dtt = R([128, 1024], F32)
        for h in range(8):
            for half in range(2):
                A("dve", lambda e, h=h, half=half: e.tensor_copy(out=dtt[0:96, :], in_=QT[:, h, half * 1024:(half + 1) * 1024]), reads=["QT"], writes=["dtt"])
                r0 = (h * 2 + half) * 96
                A("sp", lambda e, r0=r0: e.dma_start(out=dbg[r0:r0 + 96, :], in_=dtt[0:96, :]), reads=["dtt"], writes=["dbgo"], dma=True)
        A("dve", lambda e: e.tensor_copy(out=dtt[0:32, :], in_=kropeT[:, 7168:8192]), reads=["kropeT"], writes=["dtt"])
        A("sp", lambda e: e.dma_start(out=dbg[1536:1568, :], in_=dtt[0:32, :]), reads=["dtt"], writes=["dbgo"], dma=True)
        A("dve", lambda e: e.tensor_copy(out=dtt[:], in_=ckvT[:, 1, 7168:8192]), reads=["ckvT"], writes=["dtt"])
        A("sp", lambda e: e.dma_start(out=dbg[1664:1792, :], in_=dtt[:]), reads=["dtt"], writes=["dbgo"], dma=True)
        A("sp", lambda e: e.nop(), reads=["dbgo"])
        S.emit(nc, st)
        st.close()
        return nc
    S.barrier(lambda e: e.memset(sst[:, 7:8], 0.0))
    al.regions = [list(r) for r in reg_p4]
    wkp = R([128, 2, 8, 96], BF16)
    wv = R([128, 2, 512], BF16)
    selb = R([32, 96], BF16)
    KT0 = R([96, 8192], BF16)
    KT = [KT0, KT0]
    VA = [R([128, 64, 65], BF16) for _ in range(2)]
    yattn = R([128, 16, 512], BF16)
    rden = R([128, 4], F32)
    A("pool", lambda e: e.memset(wkp[:], 0.0), writes=["wkp"])
    A("dve", lambda e: e.tensor_copy(out=selb[:], in_=sel), reads=["cst"], writes=["selb"])
    for b_ in range(2):
        A("pool", lambda e, b_=b_: e.memset(VA[b_][:, :, 64:65], 1.0), writes=[("VA1", b_)])
    for c in range(2):
        sl = wstate["i"] % 2
        wstate["i"] += 1
        A("sp", lambda e, c=c, sl=sl: e.dma_start(out=wst[sl][:], in_=w_ukv[c * 128:(c + 1) * 128, :]), writes=[("wst", sl)], dma=True)
        W3 = wst[sl][:].rearrange("p (h d) -> p h d", h=8)
        A("pool", lambda e, c=c: e.tensor_copy(out=wkp[:, c, :, 0:64], in_=W3[:, :, 0:64]), reads=[("wst", sl), "wkp"], writes=["wkp"])
        A("pool", lambda e, c=c: e.tensor_copy(out=wv[:, c, :].rearrange("p (h d) -> p h d", h=8), in_=W3[:, :, 64:128]), reads=[("wst", sl)], writes=["wv"])

    OB_, KB_, VB_ = (6, 7), (0, 1), 2
    SCALE = 96.0 ** -0.5
    Pb2 = [R([128, 1024], BF16) for _ in range(3)]
    kbs = {"i": 0}
    for h in range(8):
        bf = h % 2
        for grp in range(16):
            kb = KB_[kbs["i"] % 2]
            kbs["i"] += 1
            gs = slice(grp * 512, (grp + 1) * 512)
            A("pe", lambda e: e.matmul(ps[kb][0:96, :], lhsT=wkp[:, 0, h, :], rhs=ckvT[:, 0, gs], start=True, stop=False), reads=["wkp", "ckvT"], writes=[("ps", kb)])
            A("pe", lambda e: e.matmul(ps[kb][0:96, :], lhsT=wkp[:, 1, h, :], rhs=ckvT[:, 1, gs], start=False, stop=False), reads=["wkp", "ckvT"], writes=[("ps", kb)])
            A("pe", lambda e: e.matmul(ps[kb][0:96, :], lhsT=selb[:], rhs=kropeT[:, gs], start=False, stop=True), reads=["selb", "kropeT"], writes=[("ps", kb)])
            A("dve", lambda e: e.tensor_copy(out=KT[bf][:, gs], in_=ps[kb][0:96, :]), reads=[("ps", kb)], writes=[("KT", 0)])
        for tg in range(8):
            vb = 2 + (tg % 2)
            for tl in range(8):
                kt = tg * 8 + tl
                for c in range(2):
                    A("pe", lambda e, c=c, tl=tl, kt=kt: e.matmul(ps[vb][:, tl * 64:(tl + 1) * 64], lhsT=ckvT[:, c, kt * 128:(kt + 1) * 128],
                                                                 rhs=wv[:, c, h * 64:(h + 1) * 64], start=(c == 0), stop=(c == 1)),
                      reads=["ckvT", "wv"], writes=[("ps", vb)])
            A("act", lambda e: e.copy(out=VA[bf][:, tg * 8:(tg + 1) * 8, 0:64], in_=ps[vb][:].rearrange("p (t d) -> p t d", t=8)), reads=[("ps", vb)], writes=[("VA", bf)])
        for tt in range(4):
            ob = OB_[(h * 4 + tt) % 2]
            qs = slice(tt * 512, (tt + 1) * 512)

            def s_mm(sp_):
                j = sp_ % 3
                for half in range(2):
                    st_ = 2 * sp_ + half
                    A("pe", lambda e: e.matmul(ps[2 * j + half][:], lhsT=KT[bf][:, st_ * 128:(st_ + 1) * 128], rhs=QT[:, h, qs], start=True, stop=True),
                      reads=[("KT", 0), "QT"], writes=[("ps", 2 * j + half)])
                A("act", lambda e: e.activation(out=Pb2[j][:], in_=psbig[j][:], func=AF.Exp, scale=SCALE),
                  reads=[("ps", 2 * j), ("ps", 2 * j + 1)], writes=[("Pb", j)])

            s_mm(0)
            s_mm(1)
            for sp_ in range(32):
                j = sp_ % 3
                for half in range(2):
                    st_ = 2 * sp_ + half
                    for qq in range(4):
                        A("pe", lambda e, qq=qq: e.matmul(ps[ob][:, qq * 65:(qq + 1) * 65], lhsT=Pb2[j][:, half * 512 + qq * 128:half * 512 + (qq + 1) * 128],
                                                          rhs=VA[bf][:, st_, :], start=(st_ == 0 and qq == 0), stop=(st_ == 63), skip_group_check=True),
                          reads=[("Pb", j), ("VA", bf), ("VA1", bf)], writes=[("ps", ob)])
                if sp_ + 2 < 32:
                    s_mm(sp_ + 2)
            O3 = ps[ob][:, 0:260].rearrange("p (q d) -> p q d", q=4)
            A("dve", lambda e: e.reciprocal(out=rden[:], in_=O3[:, :, 64]), reads=[("ps", ob)], writes=["rden"])
            for qq in range(4):
                A("dve", lambda e, qq=qq: e.tensor_scalar(out=yattn[:, tt * 4 + qq, h * 64:(h + 1) * 64], in0=O3[:, qq, 0:64], scalar1=rden[:, qq:qq + 1],
                                                          scalar2=None, op0=ALU.mult), reads=[("ps", ob), "rden"], writes=["yattn"])
    for tile in range(16):
        pb = psb(VB_)
        for c in range(4):
            A("pe", lambda e, c=c: e.transpose(out=pb[:, c * 128:(c + 1) * 128], in_=yattn[:, tile, c * 128:(c + 1) * 128], identity=idb[:]),
              reads=["yattn", "idb"], writes=[("ps", VB_)])
        A("act", lambda e: e.copy(out=yaT[:, :, tile * 128:(tile + 1) * 128], in_=pb[:, 0:512].rearrange("p (c t) -> p c t", c=4)), reads=[("ps", VB_)], writes=["yaT"])

    S.barrier(lambda e: e.memset(sst[:, 7:8], 0.0))
    al.regions = [[m_phase, m_alias], [reg_p4[1][0], SB_HI]]
    wgab = R([128, 8, 2048], BF16)
    wbm = R([128, 4, 1024], BF16)
    wbl = R([128, 4, 1024], BF16)
    wo = R([128, 8, 1024], BF16)
    hT6 = [R([128, 8, 128], BF16) for _ in range(2)]
    sg = R([128, 2048], BF16)
    t1 = R([128, 1024], F32)
    t2 = R([128, 1024], F32)
    mrg = R([128, 1024], BF16)
    mT = R([128, 8, 128], BF16)
    x1t = [R([128, 1024], F32) for _ in range(2)]
    load_w(wgab, "wgab", w_in, 8, 2736, 4784)
    load_w(wbm, "wbm", w_bm, 4, 0, 1024)
    load_w(wbl, "wbl", w_bl, 4, 0, 1024)
    load_w(wo, "wo", w_out, 8, 0, 1024)
    for ot in range(16):
        z = ot % 2
        tsl = slice(ot * 128, (ot + 1) * 128)
        sl = xpipe(xown[tsl, :], gmix, "gmix", hT6[z][:], ("hT6", z), 7)
        for cb_ in range(4):
            for k in range(8):
                A("pe", lambda e, k=k, cb_=cb_: e.matmul(ps[cb_][:], lhsT=hT6[z][:, k, :], rhs=wgab[:, k, cb_ * 512:(cb_ + 1) * 512], start=(k == 0), stop=(k == 7)),
                  reads=[("hT6", z), ("wgab", k)], writes=[("ps", cb_)])
            A("act", lambda e, cb_=cb_: e.activation(out=sg[:, cb_ * 512:(cb_ + 1) * 512], in_=ps[cb_][:], func=AF.Sigmoid), reads=[("ps", cb_)], writes=[("sg", cb_)])
        for half in range(2):
            for (wt, wk, aT_, ak, bank0) in ((wbm, "wbm", yaT, "yaT", 4), (wbl, "wbl", ymT, "ymT", 5)):
                bank = bank0
                for c in range(4):
                    A("pe", lambda e, c=c, wt=wt, aT_=aT_, bank=bank: e.matmul(ps[bank][:], lhsT=aT_[:, c, tsl], rhs=wt[:, c, half * 512:(half + 1) * 512],
                                                                               start=(c == 0), stop=(c == 3)), reads=[ak, (wk, c)], writes=[("ps", bank)])
            hs = slice(half * 512, (half + 1) * 512)
            A("dve", lambda e: e.tensor_tensor(out=t1[:, hs], in0=ps[4][:], in1=sg[:, half * 512:(half + 1) * 512], op=ALU.mult), reads=[("ps", 4), ("sg", half)], writes=[("t1", half)])
            A("dve", lambda e: e.tensor_tensor(out=t2[:, hs], in0=ps[5][:], in1=sg[:, 1024 + half * 512:1024 + (half + 1) * 512], op=ALU.mult),
              reads=[("ps", 5), ("sg", 2 + half)], writes=[("t2", half)])
            A("pool", lambda e: e.tensor_tensor(out=mrg[:, hs], in0=t1[:, hs], in1=t2[:, hs], op=ALU.add), reads=[("t1", half), ("t2", half)], writes=[("mrg", half)])
        pb = psb(6)
        for k in range(8):
            A("pe", lambda e, k=k: e.transpose(out=pb[:, k * 128:(k + 1) * 128], in_=mrg[:, k * 128:(k + 1) * 128], identity=idb[:]),
              reads=[("mrg", k // 4), "idb"], writes=[("ps", 6)])
        A("act", lambda e: e.copy(out=mT[:], in_=pb.rearrange("p (k t) -> p k t", k=8)), reads=[("ps", 6)], writes=["mT"])
        for half in range(2):
            bank = 4 + half
            for k in range(8):
                A("pe", lambda e, k=k, bank=bank, half=half: e.matmul(ps[bank][:], lhsT=mT[:, k, :], rhs=wo[:, k, half * 512:(half + 1) * 512], start=(k == 0), stop=(k == 7)),
                  reads=["mT", ("wo", k)], writes=[("ps", bank)])
            A("dve", lambda e, bank=bank, half=half: e.tensor_tensor(out=x1t[z][:, half * 512:(half + 1) * 512], in0=ps[bank][:], in1=xt[sl][:, half * 512:(half + 1) * 512], op=ALU.add),
              reads=[("ps", bank), ("xt", sl)], writes=[("x1t", z)])
        A("sp", lambda e: e.dma_start(out=x1d[tsl, :], in_=x1t[z][:]), reads=[("x1t", z)], writes=["x1d"], dma=True)

    S.barrier(lambda e: e.memset(sst[:, 7:8], 0.0))
    al.regions = [[m_phase, SB_HI]]
    wup = R([128, 8, 4096], BF16)
    wdn = R([128, 32, 1024], BF16)
    gmlp = R([128, 1024], F32)
    gfin = R([128, 1024], F32)
    hTm = [R([128, 8, 256], BF16) for _ in range(2)]
    aT = R([128, 32, 256], BF16)
    rr = [R([128, 256], F32) for _ in range(2)]
    xres = [R([128, 1024], F32) for _ in range(2)]
    otile = xres
    A("sp", lambda e: e.dma_start(out=gmlp[:], in_=gmlpd), writes=["gmlp"], dma=True)
    A("sp", lambda e: e.dma_start(out=gfin[:], in_=gfind), writes=["gfin"], dma=True)
    load_w(wup, "wup", w_up, 8, 0, 4096)
    load_w(wdn, "wdn", w_down, 32, 0, 1024)
    for g in range(8):
        z = g % 2
        for i in range(2):
            r0 = g * 256 + i * 128
            xpipe(x1d[r0:r0 + 128, :], gmlp, "gmlp", hTm[z][:, :, i * 128:(i + 1) * 128], ("hTm", z), 7)
        for f in range(32):
            bank = f % 2
            for k in range(8):
                A("pe", lambda e, k=k, f=f, bank=bank: e.matmul(ps[bank][:, 0:256], lhsT=wup[:, k, f * 128:(f + 1) * 128], rhs=hTm[z][:, k, :], start=(k == 0), stop=(k == 7)),
                  reads=[("hTm", z), ("wup", k)], writes=[("ps", bank)])
            A("act", lambda e, bank=bank: e.activation(out=rr[bank][:], in_=ps[bank][:, 0:256], func=AF.Relu), reads=[("ps", bank)], writes=[("rr", bank)])
            A("dve", lambda e, f=f, bank=bank: e.tensor_tensor(out=aT[:, f, :], in0=rr[bank][:], in1=rr[bank][:], op=ALU.mult), reads=[("rr", bank)], writes=["aT"])
        for i in range(2):
            r0 = g * 256 + i * 128
            zz = (g * 2 + i) % 2
            A("sp", lambda e, r0=r0, zz=zz: e.dma_start(out=xres[zz][:], in_=x1d[r0:r0 + 128, :]), reads=["x1d"], writes=[("xres", zz)], dma=True)
            for half in range(2):
                bank = 2 + half
                for f in range(32):
                    A("pe", lambda e, f=f, bank=bank, half=half, i=i: e.matmul(ps[bank][:], lhsT=aT[:, f, i * 128:(i + 1) * 128], rhs=wdn[:, f, half * 512:(half + 1) * 512],
                                                                               start=(f == 0), stop=(f == 31)), reads=["aT", ("wdn", f)], writes=[("ps", bank)])
                A("dve", lambda e, bank=bank, half=half, zz=zz: e.tensor_tensor(out=xres[zz][:, half * 512:(half + 1) * 512], in0=ps[bank][:], in1=xres[zz][:, half * 512:(half + 1) * 512], op=ALU.add),
                  reads=[("ps", bank), ("xres", zz)], writes=[("xres", zz)])
            ssap = sst[:, 4 + zz:5 + zz]
            A("act", lambda e, zz=zz, ssap=ssap: e.activation(out=junk[:], in_=xres[zz][:], func=AF.Square, accum_out=ssap), reads=[("xres", zz)], writes=[("ssf", zz)])
            rstd_of(ssap, 1024, ("ssf", zz))
            A("dve", lambda e, zz=zz, ssap=ssap: e.scalar_tensor_tensor(out=otile[zz][:], in0=xres[zz][:], scalar=ssap, in1=gfin[:], op0=ALU.mult, op1=ALU.mult),
              reads=[("xres", zz), ("ssf", zz), "gfin"], writes=[("xres", zz)])
            A("sp", lambda e, r0=r0, zz=zz: e.dma_start(out=y[r0:r0 + 128, :], in_=otile[zz][:]), reads=[("xres", zz)], writes=[("yout", g * 2 + i)], dma=True)
    A("sp", lambda e: e.nop(), reads=[("yout", i_) for i_ in range(16)])
    S.emit(nc, st)
    st.close()
    return nc


def make_consts():
    c = np.zeros((128, 880), np.float32)
    r = np.arange(128)
    c[:, 0:128] = np.eye(128)
    c[:, 128:256] = (r[:, None] <= r[None, :])
    c[:, 256:384] = (r[:, None] >= r[None, :])
    c[:, 384:512] = (r[:, None] > r[None, :])
    c[:, 512:640] = (r[:, None] < r[None, :])
    c[:, 640:768] = 1.0
    for i in range(32):
        c[i, 784 + 64 + i] = 1.0
    return c


def bc(v, n=128):
    return np.ascontiguousarray(np.broadcast_to(np.asarray(v, np.float32).reshape(1, -1), (n, np.asarray(v).size)))


def prep_inputs(inp, core):
    b, j = divmod(core, 4)
    x = inp["x"][b]
    pos = inp["positions"][b]
    o0, o1 = NOWN * j, NOWN * (j + 1)
    xo = np.concatenate([x[:o0], x[o1:]], axis=0)
    xown = x[o0:o1]
    xh = np.zeros((128, 1024), np.float32)

    def row(n):
        return x[n] if 0 <= n < 8192 else np.zeros(1024, np.float32)

    for g in range(12):
        n0 = 512 * g if 512 * g < o0 else 512 * g + NOWN
        for q, n in enumerate((n0 - 2, n0 - 1, n0 + 512, n0 + 513)):
            xh[4 * g + q] = row(n)
    for g in range(4):
        n0 = o0 + 512 * g
        for q, n in enumerate((n0 - 2, n0 - 1, n0 + 512, n0 + 513)):
            xh[48 + 4 * g + q] = row(n)
    pos_all = np.concatenate([pos[:o0], pos[o1:], pos[o0:o1]])
    posT = np.concatenate([pos_all.reshape(64, 128).T, pos[o0:o1].reshape(16, 128).T], axis=1).astype(np.int32)
    tfv = (np.arange(48) < 16 * j).astype(np.float32)
    igb = inp["mlstm_igate_b"][0]
    fgb = inp["mlstm_fgate_b"][0]
    gb16 = np.concatenate([igb[0], igb[1], fgb[0], fgb[1]])
    cwv = inp["mlstm_conv_w"][0][:, 0, :]
    cw = np.ascontiguousarray(cwv.reshape(5, 8, 128).transpose(2, 1, 0)).reshape(128, 40)
    cb = np.ascontiguousarray(inp["mlstm_conv_b"][0].reshape(8, 128).T)
    d = {
        "xo": np.ascontiguousarray(xo), "xown": np.ascontiguousarray(xown), "xh": xh,
        "posT": np.ascontiguousarray(posT), "tf": bc(tfv), "cst": make_consts(),
        "gmix": bc(inp["norm_mix_g"][0]), "gmlp": bc(inp["norm_mlp_g"][0]), "gfin": bc(inp["norm_final_g"]),
        "gq": bc(inp["mla_q_norm_g"][0]), "gkv": bc(inp["mla_kv_norm_g"][0]), "gon": bc(inp["mlstm_out_norm_g"][0]),
        "gb": bc(np.tile(gb16, 4)), "cw": cw.astype(np.float32), "cb": cb.astype(np.float32),
        "w_in": inp["w_in"][0], "w_uq": inp["mla_w_uq"][0], "w_ukv": inp["mla_w_ukv"][0],
        "w_bm": inp["w_branch_mla"][0], "w_bl": inp["w_branch_mlstm"][0], "w_out": inp["w_out"][0],
        "w_up": inp["w_mlp_up"][0], "w_down": inp["w_mlp_down"][0],
    }
    return {k: np.ascontiguousarray(v) for k, v in d.items()}


def run(inputs, debug=None, cores=8):
    inp = {k: np.asarray(v) for k, v in inputs.items()}
    nc = build_program(debug)
    in_maps = [prep_inputs(inp, c) for c in range(cores)]
    res = run_bass_kernel_spmd(nc, in_maps, core_ids=list(range(cores)))
    return res


def kernel(**inputs):
    res = run(inputs)
    out = np.zeros((2, 8192, 1024), np.float32)
    for c in range(8):
        b, j = divmod(c, 4)
        out[b, NOWN * j:NOWN * (j + 1)] = res.results[c]["y"]
    return out
```

```python
import math
from contextlib import ExitStack
import numpy as np
import concourse.bass as bass
import concourse.mybir as mybir
from concourse.bass_utils import run_bass_kernel_spmd

F32 = mybir.dt.float32
BF16 = mybir.dt.bfloat16
I32 = mybir.dt.int32
AF = mybir.ActivationFunctionType
ALU = mybir.AluOpType

SEM_LIMIT = 20000
N_DSEM = 24
SB_LO = 16512
SB_HI = 229376
NOWN = 2048
NOTH = 6144
EPS = 1e-6
LNSC = -0.5 * math.log(128.0)
BIG = 30000.0


class Op:
    __slots__ = ("eng", "fn", "deps", "signal", "is_dma", "idx", "sem", "val", "dslot")

    def __init__(self, eng, fn, is_dma):
        self.eng = eng
        self.fn = fn
        self.deps = []
        self.signal = False
        self.is_dma = is_dma
        self.sem = None
        self.val = None
        self.dslot = None


class _Rec:
    def __init__(self):
        self.call = None

    def __getattr__(self, name):
        def f(*a, **k):
            self.call = (name, a, k)
            return self
        return f


class Sched:
    def __init__(self):
        self.ops = []
        self.last_w = {}
        self.readers = {}
        self.n_dma = 0
        self.dslot_last = {}
        self.fence_op = None

    def add(self, eng, fn, reads=(), writes=(), dma=False):
        rec = _Rec()
        fn(rec)
        call = rec.call
        op = Op(eng, call, dma)
        psk = [k for k in reads if isinstance(k, tuple) and k[0] == "ps"]
        if psk:
            reads = [k for k in reads if k not in psk]
            writes = list(writes) + psk
        deps = set()
        for k in reads:
            w = self.last_w.get(k)
            if w is not None:
                deps.add(w)
        for k in writes:
            w = self.last_w.get(k)
            if w is not None:
                deps.add(w)
            for r in self.readers.get(k, ()):
                deps.add(r)
        if self.fence_op is not None:
            deps.add(self.fence_op)
        if dma:
            slot = (eng, self.n_dma % N_DSEM)
            self.n_dma += 1
            prev = self.dslot_last.get(slot)
            if prev is not None:
                deps.add(prev)
            self.dslot_last[slot] = op
            op.dslot = slot
        for d in deps:
            if d is op:
                continue
            if (not d.is_dma) and d.eng == "pe" and eng == "pe" and not dma:
                continue
            d.signal = True
            op.deps.append(d)
        for k in reads:
            self.readers.setdefault(k, []).append(op)
        for k in writes:
            self.last_w[k] = op
            self.readers[k] = []
        self.ops.append(op)
        return op

    def barrier(self, fn):
        keys = set(self.last_w.keys()) | set(self.readers.keys())
        self.fence_op = None
        op = self.add("dve", fn, writes=list(keys))
        for o in self.dslot_last.values():
            if o is not op and o not in op.deps:
                o.signal = True
                op.deps.append(o)
        self.fence_op = op
        return op

    def emit(self, nc, stack):
        engs = ["pe", "act", "dve", "pool", "sp"]
        counts = {e: 0 for e in engs}
        dcount = {}
        for op in self.ops:
            if op.is_dma:
                c = dcount.get(op.dslot, 0) + 1
                dcount[op.dslot] = c
                op.sem = ("d", op.dslot)
                op.val = 16 * c
            elif op.signal:
                c = counts[op.eng]
                counts[op.eng] = c + 1
                op.sem = (op.eng, c // SEM_LIMIT)
                op.val = c % SEM_LIMIT + 1
        sems = {}
        for op in self.ops:
            if op.sem is not None and op.sem not in sems:
                sems[op.sem] = stack.enter_context(nc.semaphore("s_%d" % len(sems)))
        block = stack.enter_context(nc.Block())
        ops = self.ops

        def run(engname, e):
            waited = {}
            for op in ops:
                if op.eng != engname:
                    continue
                for d in op.deps:
                    key = d.sem
                    if waited.get(key, 0) >= d.val:
                        continue
                    e.wait_ge(sems[key], d.val)
                    waited[key] = d.val
                name, a_, k_ = op.fn
                ins = getattr(e, name)(*a_, **k_)
                if op.is_dma:
                    ins.then_inc(sems[op.sem], 16)
                elif op.signal:
                    ins.then_inc(sems[op.sem], 1)

        @block.tensor
        def _(e):
            run("pe", e)

        @block.scalar
        def _(e):
            run("act", e)

        @block.vector
        def _(e):
            run("dve", e)

        @block.gpsimd
        def _(e):
            run("pool", e)

        @block.sync
        def _(e):
            run("sp", e)


class Alloc:
    def __init__(self, nc):
        self.nc = nc
        self.base = SB_LO
        self.top = SB_LO
        self.n = 0

    def mark(self):
        return self.top

    def reset(self, m):
        self.top = m

    def set_regions(self, regions):
        self.regions = [list(r) for r in regions]

    def ralloc(self, shape, dt):
        nb = 1
        for s_ in shape[1:]:
            nb *= s_
        nb *= 2 if dt == BF16 else 4
        nb = (nb + 63) // 64 * 64
        for r in self.regions:
            if r[0] + nb <= r[1]:
                off = r[0]
                r[0] += nb
                self.n += 1
                return self.nc.alloc_sbuf_tensor_at("t%d" % self.n, list(shape), dt, offset=off)
        raise AssertionError(("SBUF overflow", shape, self.regions))

    def __call__(self, shape, dt):
        nb = 1
        for s in shape[1:]:
            nb *= s
        nb *= 2 if dt == BF16 else 4
        nb = (nb + 63) // 64 * 64
        off = self.top
        self.top += nb
        assert self.top <= SB_HI, ("SBUF overflow", self.top)
        self.n += 1
        return self.nc.alloc_sbuf_tensor_at("t%d" % self.n, list(shape), dt, offset=off)


def build_program(debug=None):
    nc = bass.Bass("TRN2", target_bir_lowering=False)

    def din(name, shape, dt=F32):
        return nc.dram_tensor(name, list(shape), dt, kind="ExternalInput").ap()

    xo = din("xo", [NOTH, 1024])
    xown = din("xown", [NOWN, 1024])
    xh = din("xh", [128, 1024])
    posT = din("posT", [128, 80], I32)
    tfd = din("tf", [128, 48])
    cstd = din("cst", [128, 880])
    gmixd = din("gmix", [128, 1024])
    gmlpd = din("gmlp", [128, 1024])
    gfind = din("gfin", [128, 1024])
    gqd = din("gq", [128, 384])
    gkvd = din("gkv", [128, 256])
    gond = din("gon", [128, 512])
    gbd = din("gb", [128, 64])
    cwd = din("cw", [128, 40])
    cbd = din("cb", [128, 8])
    w_in = din("w_in", [1024, 4784])
    w_uq = din("w_uq", [384, 768])
    w_ukv = din("w_ukv", [256, 1024])
    w_bm = din("w_bm", [512, 1024])
    w_bl = din("w_bl", [512, 1024])
    w_out = din("w_out", [1024, 1024])
    w_up = din("w_up", [1024, 4096])
    w_down = din("w_down", [4096, 1024])
    y = nc.dram_tensor("y", [NOWN, 1024], F32, kind="ExternalOutput").ap()
    x1d = nc.dram_tensor("x1d", [NOWN, 1024], F32).ap()
    dbg = None
    if debug:
        dbg = nc.dram_tensor("dbg", [NOWN, 1024], F32, kind="ExternalOutput").ap()

    S = Sched()
    A = S.add
    al = Alloc(nc)
    st = ExitStack()
    psbig = [st.enter_context(nc.psum_tensor("psb%d" % i, [128, 1024], F32)) for i in range(4)]
    ps = [psbig[i // 2][:, (i % 2) * 512:(i % 2 + 1) * 512] for i in range(8)]

    def psb(i):
        return ps[i][:].bitcast(BF16)

    cst = al([128, 880], F32)
    idb = al([128, 128], BF16)
    mLEb = al([128, 128], BF16)
    mGEb = al([128, 128], BF16)
    gmix = al([128, 1024], F32)
    xt = [al([128, 1024], F32) for _ in range(2)]
    junk = al([128, 1024], BF16)
    hb = [al([128, 1024], BF16) for _ in range(2)]
    sst = al([128, 8], F32)
    wst = [al([128, 1024], F32) for _ in range(3)]
    LNSCt = al([128, 1], F32)
    ONEt = al([128, 1], F32)
    EPSt = al([128, 1], F32)
    idf = cst[:, 0:128]
    mLE = cst[:, 128:256]
    mGE = cst[:, 256:384]
    mSU = cst[:, 384:512]
    mSL = cst[:, 512:640]
    ones = cst[:, 640:768]
    sel = cst[0:32, 784:880]

    A("sp", lambda e: e.dma_start(out=cst[:], in_=cstd), writes=["cst"], dma=True)
    A("sp", lambda e: e.dma_start(out=gmix[:], in_=gmixd), writes=["gmix"], dma=True)
    A("dve", lambda e: e.memset(ONEt[:], 1.0), writes=["onet"])
    A("dve", lambda e: e.memset(EPSt[:], EPS), writes=["epst"])
    A("dve", lambda e: e.tensor_copy(out=idb[:], in_=idf), reads=["cst"], writes=["idb"])
    A("dve", lambda e: e.tensor_copy(out=mLEb[:], in_=mLE), reads=["cst"], writes=["mLEb"])
    A("dve", lambda e: e.tensor_copy(out=mGEb[:], in_=mGE), reads=["cst"], writes=["mGEb"])

    wstate = {"i": 0}

    def load_w(dst, dkey, src, K, c_lo, c_hi, dcol=0, queue="sp"):
        for k in range(K):
            c0 = c_lo
            while c0 < c_hi:
                cc = min(1024, c_hi - c0)
                sl = wstate["i"] % 3
                wstate["i"] += 1
                A(queue, lambda e, sl=sl, k=k, c0=c0, cc=cc: e.dma_start(out=wst[sl][:, 0:cc], in_=src[k * 128:(k + 1) * 128, c0:c0 + cc]),
                  writes=[("wst", sl)], dma=True)
                d0 = dcol + (c0 - c_lo)
                ce = ("pool", "act", "dve")[wstate["i"] % 3] if cc >= 256 else "pool"
                if ce == "act":
                    A("act", lambda e, sl=sl, k=k, d0=d0, cc=cc: e.copy(out=dst[:, k, d0:d0 + cc], in_=wst[sl][:, 0:cc]),
                      reads=[("wst", sl)], writes=[(dkey, k)])
                else:
                    A(ce, lambda e, sl=sl, k=k, d0=d0, cc=cc: e.tensor_copy(out=dst[:, k, d0:d0 + cc], in_=wst[sl][:, 0:cc]),
                      reads=[("wst", sl)], writes=[(dkey, k)])
                c0 += cc

    xstate = {"i": 0}

    def rstd_of(sskey_ap, n, key):
        A("act", lambda e: e.activation(out=sskey_ap, in_=sskey_ap, func=AF.Ln, scale=1.0 / n, bias=EPSt[:, 0:1]), reads=[key, "epst"], writes=[key])
        A("act", lambda e: e.activation(out=sskey_ap, in_=sskey_ap, func=AF.Exp, scale=-0.5), reads=[key], writes=[key])

    def xpipe(src_rows, g_tile, gkey, hT_dst, hkey, tps, keep=False):
        i = xstate["i"]
        xstate["i"] += 1
        sl = i % 2
        A("sp", lambda e: e.dma_start(out=xt[sl][:], in_=src_rows), writes=[("xt", sl)], dma=True)
        ssap = sst[:, sl:sl + 1]
        A("act", lambda e: e.activation(out=junk[:], in_=xt[sl][:], func=AF.Square, accum_out=ssap), reads=[("xt", sl)], writes=[("ss", sl)])
        rstd_of(ssap, 1024, ("ss", sl))
        A("dve", lambda e: e.scalar_tensor_tensor(out=hb[sl][:], in0=xt[sl][:], scalar=ssap, in1=g_tile[:], op0=ALU.mult, op1=ALU.mult),
          reads=[("xt", sl), ("ss", sl), gkey], writes=[("hb", sl)])
        pb = psb(tps)
        for k in range(8):
            A("pe", lambda e, k=k: e.transpose(out=pb[:, k * 128:(k + 1) * 128], in_=hb[sl][:, k * 128:(k + 1) * 128], identity=idb[:]),
              reads=[("hb", sl), "idb"], writes=[("ps", tps)])
        A("act", lambda e: e.copy(out=hT_dst, in_=pb.rearrange("p (k t) -> p k t", k=8)), reads=[("ps", tps)], writes=[hkey])
        return sl

    m_phase = al.mark()
    LGe = [al([128, 8], F32) for _ in range(2)]
    Ie = [al([128, 8], F32) for _ in range(2)]
    exg = [al([128, 8], F32) for _ in range(2)]
    Wt = [al([128, 8], F32) for _ in range(2)]
    dect = [al([128, 4], F32) for _ in range(2)]
    Arun = al([128, 4], F32)
    ktil = [al([128, 128], BF16) for _ in range(4)]
    Cf = al([128, 4, 129], F32)
    Cb = al([128, 4, 129], F32)
    Cfb = al([128, 4, 129], BF16)
    Cbb = al([128, 4, 129], BF16)
    qT = al([128, 4, NOWN], BF16)
    kT = al([128, 4, NOWN], BF16)
    ktok = al([128, 16, 512], BF16)
    vaug = al([128, 16, 4, 129], BF16)
    sgo = al([128, 16, 512], BF16)
    Gown = al([128, 16, 16], F32)
    LFo = al([128, 16, 8], F32)
    m_alias = al.mark()
    wl = al([128, 8, 2064], BF16)
    tf = al([128, 48], F32)
    tb = al([128, 48], F32)
    pf = al([128, 48], F32)
    pbk = al([128, 48], F32)
    gbias = al([128, 64], F32)
    cw = al([128, 40], F32)
    cb = al([128, 8], F32)
    gon = al([128, 512], F32)
    haloT = al([128, 8, 128], F32)
    hTg = [al([128, 8, 512], BF16) for _ in range(2)]
    padb = [al([128, 516], F32) for _ in range(2)]
    acc = [al([128, 512], F32) for _ in range(2)]
    kTg1 = al([128, 4, 512], BF16)
    ktokg1 = al([128, 4, 512], BF16)
    vaugg1 = al([128, 4, 4, 129], BF16)
    kTg = [kTg1, kTg1]
    ktokg = [ktokg1, ktokg1]
    vaugg = [vaugg1, vaugg1]
    Gg = [al([128, 4, 16], F32) for _ in range(2)]
    LFg = [al([128, 4, 8], F32) for _ in range(2)]
    sgt = [al([128, 512], BF16) for _ in range(2)]
    m_p12 = al.mark()
    al.reset(m_alias)
    ymT = al([128, 4, NOWN], BF16)
    m_keep = al.mark()
    EBt = al([128, 16, 8], F32)
    ECt = al([128, 16, 8], F32)
    WSt = al([128, 16, 8], F32)
    DECt = al([128, 16, 8], F32)
    cmt = al([128, 8], F32)
    HB = al([128, 16, 512], F32)
    PT = [al([128, 4, 128], BF16) for _ in range(2)]
    den = al([128, 4], F32)
    scl = al([128, 4], F32)
    hsum = [al([128, 4, 128], F32) for _ in range(2)]
    ssq = al([128, 4], F32)
    ymt = [al([128, 4, 128], BF16) for _ in range(2)]
    assert al.mark() <= m_p12
    al.reset(m_p12)

    for (t_, d_, k_) in ((tf, tfd, "tf"), (gbias, gbd, "gbias"), (cw, cwd, "cw"), (cb, cbd, "cb"), (gon, gond, "gon")):
        A("sp", lambda e, t_=t_, d_=d_: e.dma_start(out=t_[:], in_=d_), writes=[k_], dma=True)
    A("dve", lambda e: e.tensor_scalar(out=tb[:], in0=tf[:], scalar1=-1.0, scalar2=1.0, op0=ALU.mult, op1=ALU.add), reads=["tf"], writes=["tb"])
    A("dve", lambda e: e.tensor_scalar(out=pf[:], in0=tf[:], scalar1=-1.0, scalar2=BIG, op0=ALU.add, op1=ALU.mult), reads=["tf"], writes=["pf"])
    A("dve", lambda e: e.tensor_scalar(out=pbk[:], in0=tf[:], scalar1=-BIG, scalar2=None, op0=ALU.mult), reads=["tf"], writes=["pbk"])
    for t_, k_ in ((Cf, "Cf"), (Cb, "Cb"), (Arun, "Arun")):
        A("dve", lambda e, t_=t_: e.memset(t_[:], 0.0), writes=[k_])
    A("pool", lambda e: e.memset(vaug[:, :, :, 128:129], 1.0), writes=["vaug1"])
    A("pool", lambda e: e.memset(vaugg1[:, :, :, 128:129], 1.0), writes=[("vaugg1", 0)])

    load_w(wl, "wl", w_in, 8, 672, 2720, 0)
    for (s0, d0) in ((2720, 2048), (2728, 2052), (2724, 2056), (2732, 2060)):
        load_w(wl, "wl", w_in, 8, s0, s0 + 4, d0)
    WLK = [("wl", k) for k in range(8)]

    TPSX, TPSK, CPS0, CPS1, MVPS, UPS0 = 0, 1, 2, 3, 4, 5

    hTh = hTg[1][:, :, 0:128]
    xpipe(xh, gmix, "gmix", hTh, ("hTg", 1), TPSX)
    for c in range(8):
        bank = CPS0 + (c % 2)
        for k in range(8):
            A("pe", lambda e, c=c, k=k, bank=bank: e.matmul(ps[bank][:, 0:128], lhsT=wl[:, k, c * 128:(c + 1) * 128], rhs=hTg[1][:, k, 0:128],
                                                           start=(k == 0), stop=(k == 7)),
              reads=[("hTg", 1), ("wl", k)], writes=[("ps", bank)])
        A("act", lambda e, c=c, bank=bank: e.copy(out=haloT[:, c, :], in_=ps[bank][:, 0:128]), reads=[("ps", bank)], writes=["haloT"])

    cstate = {"i": 0}

    def conv_chunk(par, wcol, chunk, hidx, dst_ap, dst_key):
        ci = cstate["i"]
        cstate["i"] += 1
        bank = CPS0 + (ci % 2)
        pz = ci % 2
        for k in range(8):
            A("pe", lambda e, k=k: e.matmul(ps[bank][:], lhsT=wl[:, k, wcol:wcol + 128], rhs=hTg[par][:, k, :], start=(k == 0), stop=(k == 7)),
              reads=[("hTg", par), ("wl", k)], writes=[("ps", bank)])
        A("act", lambda e: e.copy(out=padb[pz][:, 2:514], in_=ps[bank][:]), reads=[("ps", bank)], writes=[("padb", pz)])
        A("pool", lambda e: e.tensor_copy(out=padb[pz][:, 0:2], in_=haloT[:, chunk, hidx:hidx + 2]), reads=["haloT"], writes=[("padbL", pz)])
        A("pool", lambda e: e.tensor_copy(out=padb[pz][:, 514:516], in_=haloT[:, chunk, hidx + 2:hidx + 4]), reads=["haloT"], writes=[("padbR", pz)])
        rk = [("padb", pz), ("padbL", pz), ("padbR", pz), "cw", "cb"]
        A("dve", lambda e: e.tensor_scalar(out=acc[pz][:], in0=padb[pz][:, 0:512], scalar1=cw[:, chunk * 5:chunk * 5 + 1], scalar2=cb[:, chunk:chunk + 1],
                                           op0=ALU.mult, op1=ALU.add), reads=rk, writes=[("acc", pz)])
        for j in range(1, 5):
            A("dve", lambda e, j=j: e.scalar_tensor_tensor(out=acc[pz][:], in0=padb[pz][:, j:j + 512], scalar=cw[:, chunk * 5 + j:chunk * 5 + j + 1],
                                                           in1=acc[pz][:], op0=ALU.mult, op1=ALU.add), reads=rk + [("acc", pz)], writes=[("acc", pz)])
        A("act", lambda e: e.activation(out=dst_ap, in_=acc[pz][:], func=AF.Silu), reads=[("acc", pz)], writes=[dst_key])

    def mlstm_group(gi, own):
        par = gi % 2
        src = xown if own else xo
        g0 = (gi - 12) if own else gi
        for i in range(4):
            r0 = g0 * 512 + i * 128
            xpipe(src[r0:r0 + 128, :], gmix, "gmix", hTg[par][:, :, i * 128:(i + 1) * 128], ("hTg", par), TPSX)
        hidx = (48 + 4 * g0) if own else 4 * g0
        for h in range(4):
            if own:
                conv_chunk(par, h * 128, h, hidx, qT[:, h, g0 * 512:(g0 + 1) * 512], "qT")
                conv_chunk(par, 512 + h * 128, 4 + h, hidx, kT[:, h, g0 * 512:(g0 + 1) * 512], "kT")
            else:
                conv_chunk(par, 512 + h * 128, 4 + h, hidx, kTg[par][:, h, :], ("kTg", 0))
        for b2 in range(2):
            pb = psb(TPSK)
            for bb in range(2):
                blk = b2 * 2 + bb
                for h in range(4):
                    if own:
                        src_ap = kT[:, h, g0 * 512 + blk * 128:g0 * 512 + (blk + 1) * 128]
                        rkey = "kT"
                    else:
                        src_ap = kTg[par][:, h, blk * 128:(blk + 1) * 128]
                        rkey = ("kTg", 0)
                    o0 = (bb * 4 + h) * 128
                    A("pe", lambda e, src_ap=src_ap, o0=o0: e.transpose(out=pb[:, o0:o0 + 128], in_=src_ap, identity=idb[:]),
                      reads=[rkey, "idb"], writes=[("ps", TPSK)])
            if own:
                dst = ktok[:, g0 * 4 + b2 * 2:g0 * 4 + b2 * 2 + 2, :]
                dk = "ktok"
            else:
                dst = ktokg[par][:, b2 * 2:b2 * 2 + 2, :]
                dk = ("ktokg", 0)
            A("act", lambda e, dst=dst, pb=pb: e.copy(out=dst, in_=pb.rearrange("p (b c) -> p b c", b=2)), reads=[("ps", TPSK)], writes=[dk])
        for i in range(4):
            mvb = (4, 5)[i % 2]
            for k in range(8):
                A("pe", lambda e, i=i, k=k: e.matmul(ps[mvb][:], lhsT=hTg[par][:, k, i * 128:(i + 1) * 128], rhs=wl[:, k, 1024:1536],
                                                     start=(k == 0), stop=(k == 7)), reads=[("hTg", par), ("wl", k)], writes=[("ps", mvb)])
            if own:
                dst = vaug[:, g0 * 4 + i, :, 0:128]
                dk = "vaug"
            else:
                dst = vaugg[par][:, i, :, 0:128]
                dk = ("vaugg", 0)
            A("dve", lambda e, dst=dst: e.tensor_copy(out=dst, in_=ps[mvb][:].rearrange("p (h d) -> p h d", h=4)), reads=[("ps", mvb)], writes=[dk])
            if own:
                mob = (5, 4)[i % 2]
                for k in range(8):
                    A("pe", lambda e, i=i, k=k: e.matmul(ps[mob][:], lhsT=hTg[par][:, k, i * 128:(i + 1) * 128], rhs=wl[:, k, 1536:2048],
                                                         start=(k == 0), stop=(k == 7)), reads=[("hTg", par), ("wl", k)], writes=[("ps", mob)])
                sp_ = i % 2
                A("act", lambda e, sp_=sp_: e.activation(out=sgt[sp_][:], in_=ps[mob][:], func=AF.Sigmoid), reads=[("ps", mob)], writes=[("sgt", sp_)])
                A("pool", lambda e, sp_=sp_, i=i: e.tensor_tensor(out=sgo[:, g0 * 4 + i, :], in0=sgt[sp_][:], in1=gon[:], op=ALU.mult),
                  reads=[("sgt", sp_), "gon"], writes=["sgo"])
        for i in range(4):
            for k in range(8):
                A("pe", lambda e, i=i, k=k: e.matmul(ps[MVPS][:, i * 16:(i + 1) * 16], lhsT=hTg[par][:, k, i * 128:(i + 1) * 128], rhs=wl[:, k, 2048:2064],
                                                     start=(k == 0), stop=(k == 7)), reads=[("hTg", par), ("wl", k)], writes=[("ps", MVPS)])
        if own:
            Gd = Gown[:, g0 * 4:g0 * 4 + 4, :]
            gk = "Gown"
            LFd = LFo[:, g0 * 4:g0 * 4 + 4, :]
            lk = "LFo"
        else:
            Gd = Gg[par][:]
            gk = ("Gg", par)
            LFd = LFg[par][:]
            lk = ("LFg", par)
        A("dve", lambda e: e.tensor_tensor(out=Gd, in0=ps[MVPS][:, 0:64].rearrange("p (b c) -> p b c", b=4),
                                           in1=gbias[:].rearrange("p (b c) -> p b c", b=4), op=ALU.add), reads=[("ps", MVPS), "gbias"], writes=[gk])
        A("act", lambda e: e.activation(out=LFd, in_=Gd[:, :, 8:16], func=AF.Exp, scale=-1.0), reads=[gk], writes=[lk])
        A("act", lambda e: e.activation(out=LFd, in_=LFd, func=AF.Ln, bias=ONEt[:, 0:1]), reads=[lk, "onet"], writes=[lk])
        if own:
            return
        for blk in range(4):
            gb_ = gi * 4 + blk
            z = blk % 2
            A("dve", lambda e: e.tensor_scalar(out=LGe[z][:, 0:4], in0=LFg[par][:, blk, 0:4], scalar1=tf[:, gb_:gb_ + 1], scalar2=-1.0, op0=ALU.mult, op1=ALU.mult),
              reads=[lk, "tf"], writes=[("LGe", z)])
            A("dve", lambda e: e.tensor_scalar(out=LGe[z][:, 4:8], in0=LFg[par][:, blk, 4:8], scalar1=tb[:, gb_:gb_ + 1], scalar2=-1.0, op0=ALU.mult, op1=ALU.mult),
              reads=[lk, "tb"], writes=[("LGe", z)])
            A("dve", lambda e: e.tensor_scalar(out=Ie[z][:, 0:4], in0=Gg[par][:, blk, 0:4], scalar1=pf[:, gb_:gb_ + 1], scalar2=None, op0=ALU.add),
              reads=[gk, "pf"], writes=[("Ie", z)])
            A("dve", lambda e: e.tensor_scalar(out=Ie[z][:, 4:8], in0=Gg[par][:, blk, 4:8], scalar1=pbk[:, gb_:gb_ + 1], scalar2=None, op0=ALU.add),
              reads=[gk, "pbk"], writes=[("Ie", z)])
            gp = UPS0 + 2
            A("pe", lambda e: e.matmul(ps[gp][:, 0:4], lhsT=mSU, rhs=LGe[z][:, 0:4], start=True, stop=True), reads=["cst", ("LGe", z)], writes=[("ps", gp)])
            A("pe", lambda e: e.matmul(ps[gp][:, 4:8], lhsT=mSL, rhs=LGe[z][:, 4:8], start=True, stop=True), reads=["cst", ("LGe", z)], writes=[("ps", gp)])
            A("pe", lambda e: e.matmul(ps[gp][:, 8:16], lhsT=ones, rhs=LGe[z][:, 0:8], start=True, stop=True), reads=["cst", ("LGe", z)], writes=[("ps", gp)])
            A("dve", lambda e: e.tensor_tensor(out=exg[z][:], in0=ps[gp][:, 0:8], in1=Ie[z][:], op=ALU.add), reads=[("ps", gp), ("Ie", z)], writes=[("exg", z)])
            A("dve", lambda e: e.tensor_tensor(out=exg[z][:, 4:8], in0=exg[z][:, 4:8], in1=Arun[:], op=ALU.add), reads=[("exg", z), "Arun"], writes=[("exg", z)])
            A("act", lambda e: e.activation(out=Wt[z][:], in_=exg[z][:], func=AF.Exp, bias=LNSCt[:, 0:1]), reads=[("exg", z), "lnsc"], writes=[("Wt", z)])
            A("act", lambda e: e.activation(out=dect[z][:], in_=ps[gp][:, 8:12], func=AF.Exp), reads=[("ps", gp)], writes=[("dect", z)])
            A("dve", lambda e: e.tensor_tensor(out=Arun[:], in0=Arun[:], in1=ps[gp][:, 12:16], op=ALU.add), reads=["Arun", ("ps", gp)], writes=["Arun"])
            for u in range(8):
                ch, h = divmod(u, 4)
                kz = u % 4
                A("dve", lambda e, u=u, h=h, kz=kz: e.tensor_scalar(out=ktil[kz][:], in0=ktokg[par][:, blk, h * 128:(h + 1) * 128],
                                                                    scalar1=Wt[z][:, u:u + 1], scalar2=None, op0=ALU.mult),
                  reads=[("ktokg", 0), ("Wt", z)], writes=[("ktil", kz)])
                bank = UPS0 + (u % 3)
                c0 = (u // 3) * 129 + (128 if bank == UPS0 + 2 else 0)
                A("pe", lambda e, h=h, kz=kz, bank=bank, c0=c0: e.matmul(ps[bank][:, c0:c0 + 129], lhsT=ktil[kz][:], rhs=vaugg[par][:, blk, h, :],
                                                                         start=True, stop=True),
                  reads=[("ktil", kz), ("vaugg", 0), ("vaugg1", 0)], writes=[("ps", bank)])
                if ch == 0:
                    A("dve", lambda e, h=h, bank=bank, c0=c0: e.scalar_tensor_tensor(out=Cf[:, h, :], in0=Cf[:, h, :], scalar=dect[z][:, h:h + 1],
                                                                                     in1=ps[bank][:, c0:c0 + 129], op0=ALU.mult, op1=ALU.add),
                      reads=["Cf", ("dect", z), ("ps", bank)], writes=["Cf"])
                else:
                    A("dve", lambda e, h=h, bank=bank, c0=c0: e.tensor_tensor(out=Cb[:, h, :], in0=Cb[:, h, :], in1=ps[bank][:, c0:c0 + 129], op=ALU.add),
                      reads=["Cb", ("ps", bank)], writes=["Cb"])

    A("dve", lambda e: e.memset(LNSCt[:], LNSC), writes=["lnsc"])

    for gi in range(12 if not (debug and (debug.endswith("_fast") or debug.startswith("p4"))) else 0):
        mlstm_group(gi, False)
    for gi in range(12, 16 if not (debug and debug.startswith("p4")) else 12):
        mlstm_group(gi, True)

    if debug and debug.split("_")[0] in ("qTe", "kTe"):
        src_t = qT if debug.startswith("qTe") else kT
        dt_ = al([128, 1024], F32)
        for h in range(4):
            for half in range(2):
                A("dve", lambda e, h=h, half=half: e.tensor_copy(out=dt_[:], in_=src_t[:, h, half * 1024:(half + 1) * 1024]), reads=["qT", "kT"], writes=["dt_"])
                r0 = (h * 2 + half) * 128
                A("sp", lambda e, r0=r0: e.dma_start(out=dbg[r0:r0 + 128, :], in_=dt_[:]), reads=["dt_"], writes=["dbgo"], dma=True)
        A("sp", lambda e: e.nop(), reads=["dbgo"])
        S.emit(nc, st)
        st.close()
        return nc
    p12keys = [("wl", k) for k in range(8)] + ["tf", "tb", "pf", "pbk", "gbias", "cw", "cb", "gon", "haloT", ("hTg", 0), ("hTg", 1),
               ("padb", 0), ("padb", 1), ("padbL", 0), ("padbL", 1), ("padbR", 0), ("padbR", 1), ("acc", 0), ("acc", 1),
               ("kTg", 0), ("ktokg", 0), ("vaugg", 0), ("vaugg1", 0), ("Gg", 0), ("Gg", 1), ("LFg", 0), ("LFg", 1), ("sgt", 0), ("sgt", 1)]
    p3keys = ["EBt", "ECt", "WSt", "DECt", "cmt", "HB", ("PT", 0), ("PT", 1), "den", "scl", ("hsum", 0), ("hsum", 1), "ssq", ("ymt", 0), ("ymt", 1), "ymT"]
    A("dve", lambda e: e.memset(cmt[:], 0.0), writes=p12keys + p3keys)
    A("pool", lambda e: e.tensor_copy(out=Cfb[:], in_=Cf[:]), reads=["Cf"], writes=["Cfb"])
    A("pool", lambda e: e.tensor_copy(out=Cbb[:], in_=Cb[:]), reads=["Cb"], writes=["Cbb"])
    GP = 7
    for blk in range(16 if not (debug and debug.startswith("p4")) else 0):
        z = blk % 2
        A("dve", lambda e, blk=blk: e.tensor_scalar(out=LGe[z][:], in0=LFo[:, blk, :], scalar1=-1.0, scalar2=None, op0=ALU.mult), reads=["LFo"], writes=[("LGe", z)])
        A("pe", lambda e: e.matmul(ps[GP][:, 0:4], lhsT=mLE, rhs=LGe[z][:, 0:4], start=True, stop=True), reads=["cst", ("LGe", z)], writes=[("ps", GP)])
        A("pe", lambda e: e.matmul(ps[GP][:, 4:8], lhsT=mGE, rhs=LGe[z][:, 4:8], start=True, stop=True), reads=["cst", ("LGe", z)], writes=[("ps", GP)])
        A("pe", lambda e: e.matmul(ps[GP][:, 8:16], lhsT=ones, rhs=LGe[z][:, 0:8], start=True, stop=True), reads=["cst", ("LGe", z)], writes=[("ps", GP)])
        A("act", lambda e, blk=blk: e.activation(out=EBt[:, blk, :], in_=ps[GP][:, 0:8], func=AF.Exp), reads=[("ps", GP)], writes=["EBt"])
        A("dve", lambda e, blk=blk: e.tensor_tensor(out=cmt[:], in0=Gown[:, blk, 0:8], in1=ps[GP][:, 0:8], op=ALU.subtract), reads=["Gown", ("ps", GP)], writes=["cmt"])
        A("act", lambda e, blk=blk: e.activation(out=ECt[:, blk, :], in_=cmt[:], func=AF.Exp, bias=LNSCt[:, 0:1]), reads=["cmt", "lnsc"], writes=["ECt"])
        A("act", lambda e, blk=blk: e.activation(out=DECt[:, blk, :], in_=ps[GP][:, 8:16], func=AF.Exp), reads=[("ps", GP)], writes=["DECt"])
        A("dve", lambda e, blk=blk: e.tensor_tensor(out=cmt[:], in0=cmt[:], in1=ps[GP][:, 8:16], op=ALU.add), reads=["cmt", ("ps", GP)], writes=["cmt"])
        A("act", lambda e, blk=blk: e.activation(out=WSt[:, blk, :], in_=cmt[:], func=AF.Exp, bias=LNSCt[:, 0:1]), reads=["cmt", "lnsc"], writes=["WSt"])

    SPS, ND0, ND1, UB0, UB1, TPY = 0, 1, 2, 3, 4, 5
    def dirpass(d):
        blocks = list(range(16)) if d == 0 else list(range(15, -1, -1))
        Cx, Cxb, ck, ckb = (Cf, Cfb, "Cf", "Cfb") if d == 0 else (Cb, Cbb, "Cb", "Cbb")
        maskb = mLEb if d == 0 else mGEb
        mk_ = "mLEb" if d == 0 else "mGEb"
        final = (d == 0)
        for bi, blk in enumerate(blocks):
            z = bi % 2
            tsl = slice(blk * 128, (blk + 1) * 128)
            for h in range(4):
                A("pe", lambda e, h=h: e.matmul(ps[SPS][:, h * 128:(h + 1) * 128], lhsT=kT[:, h, tsl], rhs=qT[:, h, tsl], start=True, stop=True),
                  reads=["kT", "qT"], writes=[("ps", SPS)])
            for h in range(4):
                A("dve", lambda e, h=h: e.scalar_tensor_tensor(out=PT[z][:, h, :], in0=ps[SPS][:, h * 128:(h + 1) * 128], scalar=ECt[:, blk, d * 4 + h:d * 4 + h + 1],
                                                               in1=maskb[:], op0=ALU.mult, op1=ALU.mult),
                  reads=[("ps", SPS), "ECt", mk_], writes=[("PT", z)])
            for h in range(4):
                bank = ND0 + h % 2
                c0 = (h // 2) * 129
                A("pe", lambda e, h=h, bank=bank, c0=c0: e.matmul(ps[bank][:, c0:c0 + 129], lhsT=PT[z][:, h, :], rhs=vaug[:, blk, h, :], start=True, stop=False),
                  reads=[("PT", z), "vaug", "vaug1"], writes=[("ps", bank)])
                A("pe", lambda e, h=h, bank=bank, c0=c0: e.matmul(ps[bank][:, c0:c0 + 129], lhsT=qT[:, h, tsl], rhs=Cxb[:, h, :], start=False, stop=True),
                  reads=["qT", ckb], writes=[("ps", bank)])
            for h in range(4):
                bank = ND0 + h % 2
                c0 = (h // 2) * 129
                A("dve", lambda e, h=h, bank=bank, c0=c0: e.tensor_tensor(out=den[:, h:h + 1], in0=ps[bank][:, c0 + 128:c0 + 129],
                                                                          in1=EBt[:, blk, d * 4 + h:d * 4 + h + 1], op=ALU.mult),
                  reads=[("ps", bank), "EBt"], writes=["den"])
            A("dve", lambda e: e.tensor_scalar(out=scl[:], in0=den[:], scalar1=-1.0, scalar2=None, op0=ALU.mult), reads=["den"], writes=["scl"])
            A("dve", lambda e: e.tensor_tensor(out=den[:], in0=den[:], in1=scl[:], op=ALU.max), reads=["den", "scl"], writes=["den"])
            A("dve", lambda e: e.tensor_scalar(out=den[:], in0=den[:], scalar1=1.0, scalar2=None, op0=ALU.max), reads=["den"], writes=["den"])
            A("dve", lambda e: e.reciprocal(out=den[:], in_=den[:]), reads=["den"], writes=["den"])
            A("dve", lambda e: e.tensor_tensor(out=scl[:], in0=den[:], in1=EBt[:, blk, d * 4:d * 4 + 4], op=ALU.mult), reads=["den", "EBt"], writes=["scl"])
            for h in range(4):
                bank = ND0 + h % 2
                c0 = (h // 2) * 129
                if not final:
                    A("dve", lambda e, h=h, bank=bank, c0=c0: e.tensor_scalar(out=HB[:, blk, h * 128:(h + 1) * 128], in0=ps[bank][:, c0:c0 + 128],
                                                                              scalar1=scl[:, h:h + 1], scalar2=None, op0=ALU.mult),
                      reads=[("ps", bank), "scl"], writes=["HB"])
                else:
                    A("dve", lambda e, h=h, bank=bank, c0=c0: e.scalar_tensor_tensor(out=hsum[z][:, h, :], in0=ps[bank][:, c0:c0 + 128], scalar=scl[:, h:h + 1],
                                                                                     in1=HB[:, blk, h * 128:(h + 1) * 128], op0=ALU.mult, op1=ALU.add),
                      reads=[("ps", bank), "scl", "HB"], writes=[("hsum", z)])
            if bi < 15:
                for h in range(4):
                    kz = h
                    bank = UB0 + h % 2
                    c0 = (h // 2) * 129
                    A("dve", lambda e, h=h, kz=kz: e.tensor_scalar(out=ktil[kz][:], in0=ktok[:, blk, h * 128:(h + 1) * 128],
                                                                   scalar1=WSt[:, blk, d * 4 + h:d * 4 + h + 1], scalar2=None, op0=ALU.mult),
                      reads=["ktok", "WSt"], writes=[("ktil", kz)])
                    A("pe", lambda e, h=h, kz=kz, bank=bank, c0=c0: e.matmul(ps[bank][:, c0:c0 + 129], lhsT=ktil[kz][:], rhs=vaug[:, blk, h, :], start=True, stop=True),
                      reads=[("ktil", kz), "vaug", "vaug1"], writes=[("ps", bank)])
                    A("dve", lambda e, h=h, bank=bank, c0=c0: e.scalar_tensor_tensor(out=Cx[:, h, :], in0=Cx[:, h, :], scalar=DECt[:, blk, d * 4 + h:d * 4 + h + 1],
                                                                                     in1=ps[bank][:, c0:c0 + 129], op0=ALU.mult, op1=ALU.add),
                      reads=[ck, "DECt", ("ps", bank)], writes=[ck])
                A("pool", lambda e: e.tensor_copy(out=Cxb[:], in_=Cx[:]), reads=[ck], writes=[ckb])
            if final:
                for h in range(4):
                    A("act", lambda e, h=h: e.activation(out=junk[:, 0:128], in_=hsum[z][:, h, :], func=AF.Square, accum_out=ssq[:, h:h + 1]),
                      reads=[("hsum", z)], writes=["ssq"])
                rstd_of(ssq[:], 128, "ssq")
                for h in range(4):
                    A("dve", lambda e, h=h: e.scalar_tensor_tensor(out=ymt[z][:, h, :], in0=hsum[z][:, h, :], scalar=ssq[:, h:h + 1],
                                                                   in1=sgo[:, blk, h * 128:(h + 1) * 128], op0=ALU.mult, op1=ALU.mult),
                      reads=[("hsum", z), "ssq", "sgo"], writes=[("ymt", z)])
                pb = psb(TPY)
                for h in range(4):
                    A("pe", lambda e, h=h: e.transpose(out=pb[:, h * 128:(h + 1) * 128], in_=ymt[z][:, h, :], identity=idb[:]),
                      reads=[("ymt", z), "idb"], writes=[("ps", TPY)])
                A("act", lambda e: e.copy(out=ymT[:, :, tsl], in_=pb[:, 0:512].rearrange("p (h t) -> p h t", h=4)), reads=[("ps", TPY)], writes=["ymT"])

    if not (debug and debug.startswith("p4")):
        dirpass(1)
    if debug and debug.startswith("HB"):
        for blk in range(16):
            A("sp", lambda e, blk=blk: e.dma_start(out=dbg[blk * 128:(blk + 1) * 128, 0:512], in_=HB[:, blk, :]), reads=["HB"], writes=["dbgo"], dma=True)
        A("sp", lambda e: e.nop(), reads=["dbgo"])
        S.emit(nc, st)
        st.close()
        return nc
    if not (debug and debug.startswith("p4")):
        dirpass(0)

    if debug and debug.split("_")[0] in ("mlstm", "qT", "kT"):
        debug = debug.split("_")[0]
        if debug == "qT":
            ymT = qT
        elif debug == "kT":
            ymT = kT
        ymk = {"mlstm": "ymT", "qT": "qT", "kT": "kT"}[debug]
        dt_ = al([128, 1024], F32)
        for h in range(4):
            for half in range(2):
                A("dve", lambda e, h=h, half=half: e.tensor_copy(out=dt_[:], in_=ymT[:, h, half * 1024:(half + 1) * 1024]), reads=[ymk], writes=["dt_"])
                r0 = (h * 2 + half) * 128
                A("sp", lambda e, r0=r0: e.dma_start(out=dbg[r0:r0 + 128, :], in_=dt_[:]), reads=["dt_"], writes=["dbgo"], dma=True)
        A("sp", lambda e: e.nop(), reads=["dbgo"])
        S.emit(nc, st)
        st.close()
        return nc

    S.barrier(lambda e: e.memset(sst[:, 7:8], 0.0))
    al.set_regions([(m_phase, m_alias), (m_keep, SB_HI)])
    R = al.ralloc
    ckvT = R([128, 2, 8192], BF16)
    kropeT = R([32, 8192], BF16)
    QT = R([96, 8, NOWN], BF16)
    yaT = R([128, 4, NOWN], BF16)
    reg_p4 = [list(r) for r in al.regions]
    wkv = R([128, 8, 288], BF16)
    wq = R([128, 8, 384], BF16)
    wuq = R([128, 3, 768], BF16)
    gq = R([128, 384], F32)
    gkv = R([128, 256], F32)
    hT4 = [R([128, 8, 128], BF16) for _ in range(2)]
    sinT = R([128, 80, 16], F32)
    cosT = R([128, 80, 16], F32)
    sinq = R([128, 16, 4, 16], F32)
    cosq = R([128, 16, 4, 16], F32)
    reg_tmp = [list(r) for r in al.regions]
    posi = R([128, 80], I32)
    posf = R([128, 80], F32)
    ang = R([128, 80, 16], F32)
    tq = R([128, 1280], F32)
    tk = R([128, 1280], I32)
    tkf = R([128, 1280], F32)

    load_w(wkv, "wkv", w_in, 8, 384, 672)
    load_w(wq, "wq", w_in, 8, 0, 384)
    load_w(wuq, "wuq", w_uq, 3, 0, 768)
    A("sp", lambda e: e.dma_start(out=gq[:], in_=gqd), writes=["gq"], dma=True)
    A("sp", lambda e: e.dma_start(out=gkv[:], in_=gkvd), writes=["gkv"], dma=True)
    A("sp", lambda e: e.dma_start(out=posi[:], in_=posT), writes=["posi"], dma=True)
    A("dve", lambda e: e.tensor_copy(out=posf[:], in_=posi[:]), reads=["posi"], writes=["posf"])
    invf = (np.float32(10000.0) ** (-np.arange(0, 32, 2, dtype=np.float32) / np.float32(32))).astype(np.float32)
    for f in range(16):
        A("dve", lambda e, f=f: e.tensor_scalar(out=ang[:, :, f], in0=posf[:], scalar1=float(invf[f]), scalar2=None, op0=ALU.mult), reads=["posf"], writes=["ang"])
    angf = ang[:].rearrange("p a b -> p (a b)")
    TWO_PI = 2.0 * math.pi
    for (dst, off) in ((sinT, 0.0), (cosT, 0.25)):
        dstf = dst[:].rearrange("p a b -> p (a b)")
        A("dve", lambda e, off=off: e.tensor_scalar(out=tq[:], in0=angf, scalar1=1.0 / TWO_PI, scalar2=off, op0=ALU.mult, op1=ALU.add), reads=["ang"], writes=["tq"])
        A("dve", lambda e: e.tensor_copy(out=tk[:], in_=tq[:]), reads=["tq"], writes=["tk"])
        A("dve", lambda e: e.tensor_copy(out=tkf[:], in_=tk[:]), reads=["tk"], writes=["tkf"])
        A("dve", lambda e: e.tensor_tensor(out=tq[:], in0=tq[:], in1=tkf[:], op=ALU.subtract), reads=["tq", "tkf"], writes=["tq"])
        A("dve", lambda e: e.tensor_scalar(out=tkf[:], in0=tq[:], scalar1=0.5, scalar2=None, op0=ALU.is_gt), reads=["tq"], writes=["tkf"])
        A("dve", lambda e: e.tensor_tensor(out=tq[:], in0=tq[:], in1=tkf[:], op=ALU.subtract), reads=["tq", "tkf"], writes=["tq"])
        A("dve", lambda e: e.tensor_scalar(out=tkf[:], in0=tq[:], scalar1=-0.5, scalar2=None, op0=ALU.is_lt), reads=["tq"], writes=["tkf"])
        A("dve", lambda e: e.tensor_tensor(out=tq[:], in0=tq[:], in1=tkf[:], op=ALU.add), reads=["tq", "tkf"], writes=["tq"])
        A("dve", lambda e: e.tensor_scalar(out=tq[:], in0=tq[:], scalar1=-0.4999, scalar2=0.4999, op0=ALU.max, op1=ALU.min), reads=["tq"], writes=["tq"])
        A("act", lambda e, dstf=dstf: e.activation(out=dstf, in_=tq[:], func=AF.Sin, scale=TWO_PI), reads=["tq"], writes=["sincos"])
    for hh in range(4):
        A("dve", lambda e, hh=hh: e.tensor_copy(out=sinq[:, :, hh, :], in_=sinT[:, 64:80, :]), reads=["sincos"], writes=["sinq"])
        A("dve", lambda e, hh=hh: e.tensor_copy(out=cosq[:, :, hh, :], in_=cosT[:, 64:80, :]), reads=["sincos"], writes=["cosq"])

    if debug == "p4a":
        A("sp", lambda e: e.dma_start(out=dbg[0:128, 0:1024], in_=sinT[:].rearrange("p a b -> p (a b)")[:, 0:1024]), reads=["sincos"], writes=["dbgo"], dma=True)
        A("sp", lambda e: e.nop(), reads=["dbgo"])
        S.emit(nc, st)
        st.close()
        return nc
    S.barrier(lambda e: e.memset(sst[:, 7:8], 0.0))
    al.regions = [list(r) for r in reg_tmp]
    cn = [R([128, 384], BF16) for _ in range(2)]
    krr = [R([128, 32], BF16) for _ in range(2)]
    rt = [R([128, 4, 16], F32) for _ in range(8)]
    cqnT = R([128, 3, 128], BF16)
    qtok = R([128, 8, 96], BF16)
    TPSX, LAT0, TPC, Q0, Q1, TPQ = 0, 1, 3, 4, 5, 6
    lstate = {"i": 0}

    def rope(x1, x2, cs, sn, o1, o2, rkeys, okey, shape4):
        rs_ = lstate["i"] % 2
        lstate["i"] += 1
        ta, tb_, tc, td = [t[:] if shape4 else t[:, 0, :] for t in rt[rs_ * 4:rs_ * 4 + 4]]
        k0, k1, k2, k3 = [("rt", rs_, q_) for q_ in range(4)]
        A("dve", lambda e: e.tensor_tensor(out=ta, in0=x1, in1=cs, op=ALU.mult), reads=rkeys, writes=[k0])
        A("dve", lambda e: e.tensor_tensor(out=tb_, in0=x2, in1=sn, op=ALU.mult), reads=rkeys, writes=[k1])
        A("dve", lambda e: e.tensor_tensor(out=o1, in0=ta, in1=tb_, op=ALU.subtract), reads=[k0, k1], writes=[okey])
        A("dve", lambda e: e.tensor_tensor(out=tc, in0=x2, in1=cs, op=ALU.mult), reads=rkeys, writes=[k2])
        A("dve", lambda e: e.tensor_tensor(out=td, in0=x1, in1=sn, op=ALU.mult), reads=rkeys, writes=[k3])
        A("dve", lambda e: e.tensor_tensor(out=o2, in0=tc, in1=td, op=ALU.add), reads=[k2, k3], writes=[okey])

    for kt in range(64):
        z = kt % 2
        src = xo[kt * 128:(kt + 1) * 128, :] if kt < 48 else xown[(kt - 48) * 128:(kt - 47) * 128, :]
        xpipe(src, gmix, "gmix", hT4[z][:], ("hT4", z), TPSX)
        lat = LAT0 + z
        for k in range(8):
            A("pe", lambda e, k=k: e.matmul(ps[lat][:, 0:288], lhsT=hT4[z][:, k, :], rhs=wkv[:, k, :], start=(k == 0), stop=(k == 7)),
              reads=[("hT4", z), ("wkv", k)], writes=[("ps", lat)])
        ssap = sst[:, 2 + z:3 + z]
        A("act", lambda e: e.activation(out=junk[:, 0:256], in_=ps[lat][:, 0:256], func=AF.Square, accum_out=ssap), reads=[("ps", lat)], writes=[("ssl", z)])
        rstd_of(ssap, 256, ("ssl", z))
        A("dve", lambda e: e.scalar_tensor_tensor(out=cn[z][:, 0:256], in0=ps[lat][:, 0:256], scalar=ssap, in1=gkv[:], op0=ALU.mult, op1=ALU.mult),
          reads=[("ps", lat), ("ssl", z), "gkv"], writes=[("cn", z)])
        rope(ps[lat][:, 256:272], ps[lat][:, 272:288], cosT[:, kt, :], sinT[:, kt, :], krr[z][:, 0:16], krr[z][:, 16:32],
             [("ps", lat), "sincos"], ("krr", z), False)
        tpc = (3, 7)[z]
        pb = psb(tpc)
        for c in range(2):
            A("pe", lambda e, c=c: e.transpose(out=pb[:, c * 128:(c + 1) * 128], in_=cn[z][:, c * 128:(c + 1) * 128], identity=idb[:]),
              reads=[("cn", z), "idb"], writes=[("ps", tpc)])
        A("pe", lambda e: e.transpose(out=pb[0:32, 256:384], in_=krr[z][:], identity=idb[:]), reads=[("krr", z), "idb"], writes=[("ps", tpc)])
        A("act", lambda e: e.copy(out=ckvT[:, :, kt * 128:(kt + 1) * 128], in_=pb[:, 0:256].rearrange("p (c t) -> p c t", c=2)), reads=[("ps", tpc)], writes=["ckvT"])
        A("act", lambda e: e.copy(out=kropeT[:, kt * 128:(kt + 1) * 128], in_=pb[0:32, 256:384]), reads=[("ps", tpc)], writes=["kropeT"])

    for ot in range(16 if debug != "p4b" else 0):
        z = ot % 2
        xpipe(xown[ot * 128:(ot + 1) * 128, :], gmix, "gmix", hT4[z][:], ("hT4", z), TPSX)
        lat = LAT0 + z
        for k in range(8):
            A("pe", lambda e, k=k: e.matmul(ps[lat][:, 0:384], lhsT=hT4[z][:, k, :], rhs=wq[:, k, :], start=(k == 0), stop=(k == 7)),
              reads=[("hT4", z), ("wq", k)], writes=[("ps", lat)])
        ssap = sst[:, 2 + z:3 + z]
        A("act", lambda e: e.activation(out=junk[:, 0:384], in_=ps[lat][:, 0:384], func=AF.Square, accum_out=ssap), reads=[("ps", lat)], writes=[("ssl", z)])
        rstd_of(ssap, 384, ("ssl", z))
        A("dve", lambda e: e.scalar_tensor_tensor(out=cn[z][:], in0=ps[lat][:, 0:384], scalar=ssap, in1=gq[:], op0=ALU.mult, op1=ALU.mult),
          reads=[("ps", lat), ("ssl", z), "gq"], writes=[("cn", z)])
        pb = psb(TPC)
        for c in range(3):
            A("pe", lambda e, c=c: e.transpose(out=pb[:, c * 128:(c + 1) * 128], in_=cn[z][:, c * 128:(c + 1) * 128], identity=idb[:]),
              reads=[("cn", z), "idb"], writes=[("ps", TPC)])
        A("act", lambda e: e.copy(out=cqnT[:], in_=pb[:, 0:384].rearrange("p (c t) -> p c t", c=3)), reads=[("ps", TPC)], writes=["cqnT"])
        qlvl = int(debug[3:]) if (debug and debug.startswith("p4q")) else 9
        if qlvl < 2:
            continue
        for x_ in range(2):
            qb = Q0 + x_
            for c in range(3):
                A("pe", lambda e, c=c: e.matmul(ps[qb][:, 0:384], lhsT=cqnT[:, c, :], rhs=wuq[:, c, x_ * 384:(x_ + 1) * 384], start=(c == 0), stop=(c == 2)),
                  reads=["cqnT", ("wuq", c)], writes=[("ps", qb)])
            V4 = ps[qb][:, 0:384].rearrange("p (h d) -> p h d", h=4)
            A("act", lambda e: e.copy(out=qtok[:, x_ * 4:x_ * 4 + 4, 0:64], in_=V4[:, :, 0:64]), reads=[("ps", qb)], writes=["qtokn"])
            if qlvl < 3:
                continue
            for hh in range(4):
                c0 = hh * 96
                rope(ps[qb][:, c0 + 64:c0 + 80], ps[qb][:, c0 + 80:c0 + 96], cosT[:, 64 + ot, :], sinT[:, 64 + ot, :],
                     qtok[:, x_ * 4 + hh, 64:80], qtok[:, x_ * 4 + hh, 80:96], [("ps", qb), "sincos"], "qtokr", False)
        if qlvl < 4:
            continue
        pq = psb(TPQ)
        for h in range(8):
            A("pe", lambda e, h=h: e.transpose(out=pq[0:96, h * 128:(h + 1) * 128], in_=qtok[:, h, :], identity=idb[:]),
              reads=["qtokn", "qtokr", "idb"], writes=[("ps", TPQ)])
        A("act", lambda e: e.copy(out=QT[:, :, ot * 128:(ot + 1) * 128], in_=pq[0:96, :].rearrange("p (h t) -> p h t", h=8)), reads=[("ps", TPQ)], writes=["QT"])

    if debug and debug.startswith("p4"):
        dtt = R([128, 1024], F32)
        for h in range(8):
            for half in range(2):
                A("dve", lambda e, h=h, half=half: e.tensor_copy(out=dtt[0:96, :], in_=QT[:, h, half * 1024:(half + 1) * 1024]), reads=["QT"], writes=["dtt"])
                r0 = (h * 2 + half) * 96
                A("sp", lambda e, r0=r0: e.dma_start(out=dbg[r0:r0 + 96, :], in_=dtt[0:96, :]), reads=["dtt"], writes=["dbgo"], dma=True)
        A("dve", lambda e: e.tensor_copy(out=dtt[0:32, :], in_=kropeT[:, 7168:8192]), reads=["kropeT"], writes=["dtt"])
        A("sp", lambda e: e.dma_start(out=dbg[1536:1568, :], in_=dtt[0:32, :]), reads=["dtt"], writes=["dbgo"], dma=True)
        A("dve", lambda e: e.tensor_copy(out=dtt[:], in_=ckvT[:, 1, 7168:8192]), reads=["ckvT"], writes=["dtt"])
        A("sp", lambda e: e.dma_start(out=dbg[1664:1792, :], in_=dtt[:]), reads=["dtt"], writes=["dbgo"], dma=True)
        A("sp", lambda e: e.nop(), reads=["dbgo"])
        S.emit(nc, st)
        st.close()
        return nc
    S.barrier(lambda e: e.memset(sst[:, 7:8], 0.0))
    al.regions = [list(r) for r in reg_p4]
    wkp = R([128, 2, 8, 96], BF16)
    wv = R([128, 2, 512], BF16)
    selb = R([32, 96], BF16)
    KT0 = R([96, 8192], BF16)
    KT = [KT0, KT0]
    VA = [R([128, 64, 65], BF16) for _ in range(2)]
    yattn = R([128, 16, 512], BF16)
    rden = R([128, 4], F32)
    A("pool", lambda e: e.memset(wkp[:], 0.0), writes=["wkp"])
    A("dve", lambda e: e.tensor_copy(out=selb[:], in_=sel), reads=["cst"], writes=["selb"])
    for b_ in range(2):
        A("pool", lambda e, b_=b_: e.memset(VA[b_][:, :, 64:65], 1.0), writes=[("VA1", b_)])
    for c in range(2):
        sl = wstate["i"] % 2
        wstate["i"] += 1
        A("sp", lambda e, c=c, sl=sl: e.dma_start(out=wst[sl][:], in_=w_ukv[c * 128:(c + 1) * 128, :]), writes=[("wst", sl)], dma=True)
        W3 = wst[sl][:].rearrange("p (h d) -> p h d", h=8)
        A("pool", lambda e, c=c: e.tensor_copy(out=wkp[:, c, :, 0:64], in_=W3[:, :, 0:64]), reads=[("wst", sl), "wkp"], writes=["wkp"])
        A("pool", lambda e, c=c: e.tensor_copy(out=wv[:, c, :].rearrange("p (h d) -> p h d", h=8), in_=W3[:, :, 64:128]), reads=[("wst", sl)], writes=["wv"])

    OB_, KB_, VB_ = (6, 7), (0, 1), 2
    SCALE = 96.0 ** -0.5
    Pb2 = [R([128, 1024], BF16) for _ in range(3)]
    kbs = {"i": 0}
    for h in range(8):
        bf = h % 2
        for grp in range(16):
            kb = KB_[kbs["i"] % 2]
            kbs["i"] += 1
            gs = slice(grp * 512, (grp + 1) * 512)
            A("pe", lambda e: e.matmul(ps[kb][0:96, :], lhsT=wkp[:, 0, h, :], rhs=ckvT[:, 0, gs], start=True, stop=False), reads=["wkp", "ckvT"], writes=[("ps", kb)])
            A("pe", lambda e: e.matmul(ps[kb][0:96, :], lhsT=wkp[:, 1, h, :], rhs=ckvT[:, 1, gs], start=False, stop=False), reads=["wkp", "ckvT"], writes=[("ps", kb)])
            A("pe", lambda e: e.matmul(ps[kb][0:96, :], lhsT=selb[:], rhs=kropeT[:, gs], start=False, stop=True), reads=["selb", "kropeT"], writes=[("ps", kb)])
            A("dve", lambda e: e.tensor_copy(out=KT[bf][:, gs], in_=ps[kb][0:96, :]), reads=[("ps", kb)], writes=[("KT", 0)])
        for tg in range(8):
            vb = 2 + (tg % 2)
            for tl in range(8):
                kt = tg * 8 + tl
                for c in range(2):
                    A("pe", lambda e, c=c, tl=tl, kt=kt: e.matmul(ps[vb][:, tl * 64:(tl + 1) * 64], lhsT=ckvT[:, c, kt * 128:(kt + 1) * 128],
                                                                 rhs=wv[:, c, h * 64:(h + 1) * 64], start=(c == 0), stop=(c == 1)),
                      reads=["ckvT", "wv"], writes=[("ps", vb)])
            A("act", lambda e: e.copy(out=VA[bf][:, tg * 8:(tg + 1) * 8, 0:64], in_=ps[vb][:].rearrange("p (t d) -> p t d", t=8)), reads=[("ps", vb)], writes=[("VA", bf)])
        for tt in range(4):
            ob = OB_[(h * 4 + tt) % 2]
            qs = slice(tt * 512, (tt + 1) * 512)

            def s_mm(sp_):
                j = sp_ % 3
                for half in range(2):
                    st_ = 2 * sp_ + half
                    A("pe", lambda e: e.matmul(ps[2 * j + half][:], lhsT=KT[bf][:, st_ * 128:(st_ + 1) * 128], rhs=QT[:, h, qs], start=True, stop=True),
                      reads=[("KT", 0), "QT"], writes=[("ps", 2 * j + half)])
                A("act", lambda e: e.activation(out=Pb2[j][:], in_=psbig[j][:], func=AF.Exp, scale=SCALE),
                  reads=[("ps", 2 * j), ("ps", 2 * j + 1)], writes=[("Pb", j)])

            s_mm(0)
            s_mm(1)
            for sp_ in range(32):
                j = sp_ % 3
                for half in range(2):
                    st_ = 2 * sp_ + half
                    for qq in range(4):
                        A("pe", lambda e, qq=qq: e.matmul(ps[ob][:, qq * 65:(qq + 1) * 65], lhsT=Pb2[j][:, half * 512 + qq * 128:half * 512 + (qq + 1) * 128],
                                                          rhs=VA[bf][:, st_, :], start=(st_ == 0 and qq == 0), stop=(st_ == 63), skip_group_check=True),
                          reads=[("Pb", j), ("VA", bf), ("VA1", bf)], writes=[("ps", ob)])
                if sp_ + 2 < 32:
                    s_mm(sp_ + 2)
            O3 = ps[ob][:, 0:260].rearrange("p (q d) -> p q d", q=4)
            A("dve", lambda e: e.reciprocal(out=rden[:], in_=O3[:, :, 64]), reads=[("ps", ob)], writes=["rden"])
            for qq in range(4):
                A("dve", lambda e, qq=qq: e.tensor_scalar(out=yattn[:, tt * 4 + qq, h * 64:(h + 1) * 64], in0=O3[:, qq, 0:64], scalar1=rden[:, qq:qq + 1],
                                                          scalar2=None, op0=ALU.mult), reads=[("ps", ob), "rden"], writes=["yattn"])
    for tile in range(16):
        pb = psb(VB_)
        for c in range(4):
            A("pe", lambda e, c=c: e.transpose(out=pb[:, c * 128:(c + 1) * 128], in_=yattn[:, tile, c * 128:(c + 1) * 128], identity=idb[:]),
              reads=["yattn", "idb"], writes=[("ps", VB_)])
        A("act", lambda e: e.copy(out=yaT[:, :, tile * 128:(tile + 1) * 128], in_=pb[:, 0:512].rearrange("p (c t) -> p c t", c=4)), reads=[("ps", VB_)], writes=["yaT"])

    S.barrier(lambda e: e.memset(sst[:, 7:8], 0.0))
    al.regions = [[m_phase, m_alias], [reg_p4[1][0], SB_HI]]
    wgab = R([128, 8, 2048], BF16)
    wbm = R([128, 4, 1024], BF16)
    wbl = R([128, 4, 1024], BF16)
    wo = R([128, 8, 1024], BF16)
    hT6 = [R([128, 8, 128], BF16) for _ in range(2)]
    sg = R([128, 2048], BF16)
    t1 = R([128, 1024], F32)
    t2 = R([128, 1024], F32)
    mrg = R([128, 1024], BF16)
    mT = R([128, 8, 128], BF16)
    x1t = [R([128, 1024], F32) for _ in range(2)]
    load_w(wgab, "wgab", w_in, 8, 2736, 4784)
    load_w(wbm, "wbm", w_bm, 4, 0, 1024)
    load_w(wbl, "wbl", w_bl, 4, 0, 1024)
    load_w(wo, "wo", w_out, 8, 0, 1024)
    for ot in range(16):
        z = ot % 2
        tsl = slice(ot * 128, (ot + 1) * 128)
        sl = xpipe(xown[tsl, :], gmix, "gmix", hT6[z][:], ("hT6", z), 7)
        for cb_ in range(4):
            for k in range(8):
                A("pe", lambda e, k=k, cb_=cb_: e.matmul(ps[cb_][:], lhsT=hT6[z][:, k, :], rhs=wgab[:, k, cb_ * 512:(cb_ + 1) * 512], start=(k == 0), stop=(k == 7)),
                  reads=[("hT6", z), ("wgab", k)], writes=[("ps", cb_)])
            A("act", lambda e, cb_=cb_: e.activation(out=sg[:, cb_ * 512:(cb_ + 1) * 512], in_=ps[cb_][:], func=AF.Sigmoid), reads=[("ps", cb_)], writes=[("sg", cb_)])
        for half in range(2):
            for (wt, wk, aT_, ak, bank0) in ((wbm, "wbm", yaT, "yaT", 4), (wbl, "wbl", ymT, "ymT", 5)):
                bank = bank0
                for c in range(4):
                    A("pe", lambda e, c=c, wt=wt, aT_=aT_, bank=bank: e.matmul(ps[bank][:], lhsT=aT_[:, c, tsl], rhs=wt[:, c, half * 512:(half + 1) * 512],
                                                                               start=(c == 0), stop=(c == 3)), reads=[ak, (wk, c)], writes=[("ps", bank)])
            hs = slice(half * 512, (half + 1) * 512)
            A("dve", lambda e: e.tensor_tensor(out=t1[:, hs], in0=ps[4][:], in1=sg[:, half * 512:(half + 1) * 512], op=ALU.mult), reads=[("ps", 4), ("sg", half)], writes=[("t1", half)])
            A("dve", lambda e: e.tensor_tensor(out=t2[:, hs], in0=ps[5][:], in1=sg[:, 1024 + half * 512:1024 + (half + 1) * 512], op=ALU.mult),
              reads=[("ps", 5), ("sg", 2 + half)], writes=[("t2", half)])
            A("pool", lambda e: e.tensor_tensor(out=mrg[:, hs], in0=t1[:, hs], in1=t2[:, hs], op=ALU.add), reads=[("t1", half), ("t2", half)], writes=[("mrg", half)])
        pb = psb(6)
        for k in range(8):
            A("pe", lambda e, k=k: e.transpose(out=pb[:, k * 128:(k + 1) * 128], in_=mrg[:, k * 128:(k + 1) * 128], identity=idb[:]),
              reads=[("mrg", k // 4), "idb"], writes=[("ps", 6)])
        A("act", lambda e: e.copy(out=mT[:], in_=pb.rearrange("p (k t) -> p k t", k=8)), reads=[("ps", 6)], writes=["mT"])
        for half in range(2):
            bank = 4 + half
            for k in range(8):
                A("pe", lambda e, k=k, bank=bank, half=half: e.matmul(ps[bank][:], lhsT=mT[:, k, :], rhs=wo[:, k, half * 512:(half + 1) * 512], start=(k == 0), stop=(k == 7)),
                  reads=["mT", ("wo", k)], writes=[("ps", bank)])
            A("dve", lambda e, bank=bank, half=half: e.tensor_tensor(out=x1t[z][:, half * 512:(half + 1) * 512], in0=ps[bank][:], in1=xt[sl][:, half * 512:(half + 1) * 512], op=ALU.add),
              reads=[("ps", bank), ("xt", sl)], writes=[("x1t", z)])
        A("sp", lambda e: e.dma_start(out=x1d[tsl, :], in_=x1t[z][:]), reads=[("x1t", z)], writes=["x1d"], dma=True)

    S.barrier(lambda e: e.memset(sst[:, 7:8], 0.0))
    al.regions = [[m_phase, SB_HI]]
    wup = R([128, 8, 4096], BF16)
    wdn = R([128, 32, 1024], BF16)
    gmlp = R([128, 1024], F32)
    gfin = R([128, 1024], F32)
    hTm = [R([128, 8, 256], BF16) for _ in range(2)]
    aT = R([128, 32, 256], BF16)
    rr = [R([128, 256], F32) for _ in range(2)]
    xres = [R([128, 1024], F32) for _ in range(2)]
    otile = xres
    A("sp", lambda e: e.dma_start(out=gmlp[:], in_=gmlpd), writes=["gmlp"], dma=True)
    A("sp", lambda e: e.dma_start(out=gfin[:], in_=gfind), writes=["gfin"], dma=True)
    load_w(wup, "wup", w_up, 8, 0, 4096)
    load_w(wdn, "wdn", w_down, 32, 0, 1024)
    for g in range(8):
        z = g % 2
        for i in range(2):
            r0 = g * 256 + i * 128
            xpipe(x1d[r0:r0 + 128, :], gmlp, "gmlp", hTm[z][:, :, i * 128:(i + 1) * 128], ("hTm", z), 7)
        for f in range(32):
            bank = f % 2
            for k in range(8):
                A("pe", lambda e, k=k, f=f, bank=bank: e.matmul(ps[bank][:, 0:256], lhsT=wup[:, k, f * 128:(f + 1) * 128], rhs=hTm[z][:, k, :], start=(k == 0), stop=(k == 7)),
                  reads=[("hTm", z), ("wup", k)], writes=[("ps", bank)])
            A("act", lambda e, bank=bank: e.activation(out=rr[bank][:], in_=ps[bank][:, 0:256], func=AF.Relu), reads=[("ps", bank)], writes=[("rr", bank)])
            A("dve", lambda e, f=f, bank=bank: e.tensor_tensor(out=aT[:, f, :], in0=rr[bank][:], in1=rr[bank][:], op=ALU.mult), reads=[("rr", bank)], writes=["aT"])
        for i in range(2):
            r0 = g * 256 + i * 128
            zz = (g * 2 + i) % 2
            A("sp", lambda e, r0=r0, zz=zz: e.dma_start(out=xres[zz][:], in_=x1d[r0:r0 + 128, :]), reads=["x1d"], writes=[("xres", zz)], dma=True)
            for half in range(2):
                bank = 2 + half
                for f in range(32):
                    A("pe", lambda e, f=f, bank=bank, half=half, i=i: e.matmul(ps[bank][:], lhsT=aT[:, f, i * 128:(i + 1) * 128], rhs=wdn[:, f, half * 512:(half + 1) * 512],
                                                                               start=(f == 0), stop=(f == 31)), reads=["aT", ("wdn", f)], writes=[("ps", bank)])
                A("dve", lambda e, bank=bank, half=half, zz=zz: e.tensor_tensor(out=xres[zz][:, half * 512:(half + 1) * 512], in0=ps[bank][:], in1=xres[zz][:, half * 512:(half + 1) * 512], op=ALU.add),
                  reads=[("ps", bank), ("xres", zz)], writes=[("xres", zz)])
            ssap = sst[:, 4 + zz:5 + zz]
            A("act", lambda e, zz=zz, ssap=ssap: e.activation(out=junk[:], in_=xres[zz][:], func=AF.Square, accum_out=ssap), reads=[("xres", zz)], writes=[("ssf", zz)])
            rstd_of(ssap, 1024, ("ssf", zz))
            A("dve", lambda e, zz=zz, ssap=ssap: e.scalar_tensor_tensor(out=otile[zz][:], in0=xres[zz][:], scalar=ssap, in1=gfin[:], op0=ALU.mult, op1=ALU.mult),
              reads=[("xres", zz), ("ssf", zz), "gfin"], writes=[("xres", zz)])
            A("sp", lambda e, r0=r0, zz=zz: e.dma_start(out=y[r0:r0 + 128, :], in_=otile[zz][:]), reads=[("xres", zz)], writes=[("yout", g * 2 + i)], dma=True)
    A("sp", lambda e: e.nop(), reads=[("yout", i_) for i_ in range(16)])
    S.emit(nc, st)
    st.close()
    return nc


def make_consts():
    c = np.zeros((128, 880), np.float32)
    r = np.arange(128)
    c[:, 0:128] = np.eye(128)
    c[:, 128:256] = (r[:, None] <= r[None, :])
    c[:, 256:384] = (r[:, None] >= r[None, :])
    c[:, 384:512] = (r[:, None] > r[None, :])
    c[:, 512:640] = (r[:, None] < r[None, :])
    c[:, 640:768] = 1.0
    for i in range(32):
        c[i, 784 + 64 + i] = 1.0
    return c


def bc(v, n=128):
    return np.ascontiguousarray(np.broadcast_to(np.asarray(v, np.float32).reshape(1, -1), (n, np.asarray(v).size)))


def prep_inputs(inp, core):
    b, j = divmod(core, 4)
    x = inp["x"][b]
    pos = inp["positions"][b]
    o0, o1 = NOWN * j, NOWN * (j + 1)
    xo = np.concatenate([x[:o0], x[o1:]], axis=0)
    xown = x[o0:o1]
    xh = np.zeros((128, 1024), np.float32)

    def row(n):
        return x[n] if 0 <= n < 8192 else np.zeros(1024, np.float32)

    for g in range(12):
        n0 = 512 * g if 512 * g < o0 else 512 * g + NOWN
        for q, n in enumerate((n0 - 2, n0 - 1, n0 + 512, n0 + 513)):
            xh[4 * g + q] = row(n)
    for g in range(4):
        n0 = o0 + 512 * g
        for q, n in enumerate((n0 - 2, n0 - 1, n0 + 512, n0 + 513)):
            xh[48 + 4 * g + q] = row(n)
    pos_all = np.concatenate([pos[:o0], pos[o1:], pos[o0:o1]])
    posT = np.concatenate([pos_all.reshape(64, 128).T, pos[o0:o1].reshape(16, 128).T], axis=1).astype(np.int32)
    tfv = (np.arange(48) < 16 * j).astype(np.float32)
    igb = inp["mlstm_igate_b"][0]
    fgb = inp["mlstm_fgate_b"][0]
    gb16 = np.concatenate([igb[0], igb[1], fgb[0], fgb[1]])
    cwv = inp["mlstm_conv_w"][0][:, 0, :]
    cw = np.ascontiguousarray(cwv.reshape(5, 8, 128).transpose(2, 1, 0)).reshape(128, 40)
    cb = np.ascontiguousarray(inp["mlstm_conv_b"][0].reshape(8, 128).T)
    d = {
        "xo": np.ascontiguousarray(xo), "xown": np.ascontiguousarray(xown), "xh": xh,
        "posT": np.ascontiguousarray(posT), "tf": bc(tfv), "cst": make_consts(),
        "gmix": bc(inp["norm_mix_g"][0]), "gmlp": bc(inp["norm_mlp_g"][0]), "gfin": bc(inp["norm_final_g"]),
        "gq": bc(inp["mla_q_norm_g"][0]), "gkv": bc(inp["mla_kv_norm_g"][0]), "gon": bc(inp["mlstm_out_norm_g"][0]),
        "gb": bc(np.tile(gb16, 4)), "cw": cw.astype(np.float32), "cb": cb.astype(np.float32),
        "w_in": inp["w_in"][0], "w_uq": inp["mla_w_uq"][0], "w_ukv": inp["mla_w_ukv"][0],
        "w_bm": inp["w_branch_mla"][0], "w_bl": inp["w_branch_mlstm"][0], "w_out": inp["w_out"][0],
        "w_up": inp["w_mlp_up"][0], "w_down": inp["w_mlp_down"][0],
    }
    return {k: np.ascontiguousarray(v) for k, v in d.items()}


def run(inputs, debug=None, cores=8):
    inp = {k: np.asarray(v) for k, v in inputs.items()}
    nc = build_program(debug)
    in_maps = [prep_inputs(inp, c) for c in range(cores)]
    res = run_bass_kernel_spmd(nc, in_maps, core_ids=list(range(cores)))
    return res


def kernel(**inputs):
    res = run(inputs)
    out = np.zeros((2, 8192, 1024), np.float32)
    for c in range(8):
        b, j = divmod(c, 4)
        out[b, NOWN * j:NOWN * (j + 1)] = res.results[c]["y"]
    return out
```

```python
import math
from contextlib import ExitStack
import numpy as np
import concourse.bass as bass
import concourse.mybir as mybir
from concourse.bass_utils import run_bass_kernel_spmd

F32 = mybir.dt.float32
BF16 = mybir.dt.bfloat16
I32 = mybir.dt.int32
AF = mybir.ActivationFunctionType
ALU = mybir.AluOpType

SEM_LIMIT = 20000
N_DSEM = 24
SB_LO = 16512
SB_HI = 229376
NOWN = 2048
NOTH = 6144
EPS = 1e-6
LNSC = -0.5 * math.log(128.0)
BIG = 30000.0


class Op:
    __slots__ = ("eng", "fn", "deps", "signal", "is_dma", "idx", "sem", "val", "dslot")

    def __init__(self, eng, fn, is_dma):
        self.eng = eng
        self.fn = fn
        self.deps = []
        self.signal = False
        self.is_dma = is_dma
        self.sem = None
        self.val = None
        self.dslot = None


class _Rec:
    def __init__(self):
        self.call = None

    def __getattr__(self, name):
        def f(*a, **k):
            self.call = (name, a, k)
            return self
        return f


class Sched:
    def __init__(self):
        self.ops = []
        self.last_w = {}
        self.readers = {}
        self.n_dma = 0
        self.dslot_last = {}
        self.fence_op = None

    def add(self, eng, fn, reads=(), writes=(), dma=False):
        rec = _Rec()
        fn(rec)
        call = rec.call
        op = Op(eng, call, dma)
        psk = [k for k in reads if isinstance(k, tuple) and k[0] == "ps"]
        if psk:
            reads = [k for k in reads if k not in psk]
            writes = list(writes) + psk
        deps = set()
        for k in reads:
            w = self.last_w.get(k)
            if w is not None:
                deps.add(w)
        for k in writes:
            w = self.last_w.get(k)
            if w is not None:
                deps.add(w)
            for r in self.readers.get(k, ()):
                deps.add(r)
        if self.fence_op is not None:
            deps.add(self.fence_op)
        if dma:
            slot = (eng, self.n_dma % N_DSEM)
            self.n_dma += 1
            prev = self.dslot_last.get(slot)
            if prev is not None:
                deps.add(prev)
            self.dslot_last[slot] = op
            op.dslot = slot
        for d in deps:
            if d is op:
                continue
            if (not d.is_dma) and d.eng == "pe" and eng == "pe" and not dma:
                continue
            d.signal = True
            op.deps.append(d)
        for k in reads:
            self.readers.setdefault(k, []).append(op)
        for k in writes:
            self.last_w[k] = op
            self.readers[k] = []
        self.ops.append(op)
        return op

    def barrier(self, fn):
        keys = set(self.last_w.keys()) | set(self.readers.keys())
        self.fence_op = None
        op = self.add("dve", fn, writes=list(keys))
        for o in self.dslot_last.values():
            if o is not op and o not in op.deps:
                o.signal = True
                op.deps.append(o)
        self.fence_op = op
        return op

    def emit(self, nc, stack):
        engs = ["pe", "act", "dve", "pool", "sp"]
        counts = {e: 0 for e in engs}
        dcount = {}
        for op in self.ops:
            if op.is_dma:
                c = dcount.get(op.dslot, 0) + 1
                dcount[op.dslot] = c
                op.sem = ("d", op.dslot)
                op.val = 16 * c
            elif op.signal:
                c = counts[op.eng]
                counts[op.eng] = c + 1
                op.sem = (op.eng, c // SEM_LIMIT)
                op.val = c % SEM_LIMIT + 1
        sems = {}
        for op in self.ops:
            if op.sem is not None and op.sem not in sems:
                sems[op.sem] = stack.enter_context(nc.semaphore("s_%d" % len(sems)))
        block = stack.enter_context(nc.Block())
        ops = self.ops

        def run(engname, e):
            waited = {}
            for op in ops:
                if op.eng != engname:
                    continue
                for d in op.deps:
                    key = d.sem
                    if waited.get(key, 0) >= d.val:
                        continue
                    e.wait_ge(sems[key], d.val)
                    waited[key] = d.val
                name, a_, k_ = op.fn
                ins = getattr(e, name)(*a_, **k_)
                if op.is_dma:
                    ins.then_inc(sems[op.sem], 16)
                elif op.signal:
                    ins.then_inc(sems[op.sem], 1)

        @block.tensor
        def _(e):
            run("pe", e)

        @block.scalar
        def _(e):
            run("act", e)

        @block.vector
        def _(e):
            run("dve", e)

        @block.gpsimd
        def _(e):
            run("pool", e)

        @block.sync
        def _(e):
            run("sp", e)


class Alloc:
    def __init__(self, nc):
        self.nc = nc
        self.base = SB_LO
        self.top = SB_LO
        self.n = 0

    def mark(self):
        return self.top

    def reset(self, m):
        self.top = m

    def set_regions(self, regions):
        self.regions = [list(r) for r in regions]

    def ralloc(self, shape, dt):
        nb = 1
        for s_ in shape[1:]:
            nb *= s_
        nb *= 2 if dt == BF16 else 4
        nb = (nb + 63) // 64 * 64
        for r in self.regions:
            if r[0] + nb <= r[1]:
                off = r[0]
                r[0] += nb
                self.n += 1
                return self.nc.alloc_sbuf_tensor_at("t%d" % self.n, list(shape), dt, offset=off)
        raise AssertionError(("SBUF overflow", shape, self.regions))

    def __call__(self, shape, dt):
        nb = 1
        for s in shape[1:]:
            nb *= s
        nb *= 2 if dt == BF16 else 4
        nb = (nb + 63) // 64 * 64
        off = self.top
        self.top += nb
        assert self.top <= SB_HI, ("SBUF overflow", self.top)
        self.n += 1
        return self.nc.alloc_sbuf_tensor_at("t%d" % self.n, list(shape), dt, offset=off)


def build_program(debug=None):
    nc = bass.Bass("TRN2", target_bir_lowering=False)

    def din(name, shape, dt=F32):
        return nc.dram_tensor(name, list(shape), dt, kind="ExternalInput").ap()

    xo = din("xo", [NOTH, 1024])
    xown = din("xown", [NOWN, 1024])
    xh = din("xh", [128, 1024])
    posT = din("posT", [128, 80], I32)
    tfd = din("tf", [128, 48])
    cstd = din("cst", [128, 880])
    gmixd = din("gmix", [128, 1024])
    gmlpd = din("gmlp", [128, 1024])
    gfind = din("gfin", [128, 1024])
    gqd = din("gq", [128, 384])
    gkvd = din("gkv", [128, 256])
    gond = din("gon", [128, 512])
    gbd = din("gb", [128, 64])
    cwd = din("cw", [128, 40])
    cbd = din("cb", [128, 8])
    w_in = din("w_in", [1024, 4784])
    w_uq = din("w_uq", [384, 768])
    w_ukv = din("w_ukv", [256, 1024])
    w_bm = din("w_bm", [512, 1024])
    w_bl = din("w_bl", [512, 1024])
    w_out = din("w_out", [1024, 1024])
    w_up = din("w_up", [1024, 4096])
    w_down = din("w_down", [4096, 1024])
    y = nc.dram_tensor("y", [NOWN, 1024], F32, kind="ExternalOutput").ap()
    x1d = nc.dram_tensor("x1d", [NOWN, 1024], F32).ap()
    dbg = None
    if debug:
        dbg = nc.dram_tensor("dbg", [NOWN, 1024], F32, kind="ExternalOutput").ap()

    S = Sched()
    A = S.add
    al = Alloc(nc)
    st = ExitStack()
    psbig = [st.enter_context(nc.psum_tensor("psb%d" % i, [128, 1024], F32)) for i in range(4)]
    ps = [psbig[i // 2][:, (i % 2) * 512:(i % 2 + 1) * 512] for i in range(8)]

    def psb(i):
        return ps[i][:].bitcast(BF16)

    cst = al([128, 880], F32)
    idb = al([128, 128], BF16)
    mLEb = al([128, 128], BF16)
    mGEb = al([128, 128], BF16)
    gmix = al([128, 1024], F32)
    xt = [al([128, 1024], F32) for _ in range(2)]
    junk = al([128, 1024], BF16)
    hb = [al([128, 1024], BF16) for _ in range(2)]
    sst = al([128, 8], F32)
    wst = [al([128, 1024], F32) for _ in range(3)]
    LNSCt = al([128, 1], F32)
    ONEt = al([128, 1], F32)
    EPSt = al([128, 1], F32)
    idf = cst[:, 0:128]
    mLE = cst[:, 128:256]
    mGE = cst[:, 256:384]
    mSU = cst[:, 384:512]
    mSL = cst[:, 512:640]
    ones = cst[:, 640:768]
    sel = cst[0:32, 784:880]

    A("sp", lambda e: e.dma_start(out=cst[:], in_=cstd), writes=["cst"], dma=True)
    A("sp", lambda e: e.dma_start(out=gmix[:], in_=gmixd), writes=["gmix"], dma=True)
    A("dve", lambda e: e.memset(ONEt[:], 1.0), writes=["onet"])
    A("dve", lambda e: e.memset(EPSt[:], EPS), writes=["epst"])
    A("dve", lambda e: e.tensor_copy(out=idb[:], in_=idf), reads=["cst"], writes=["idb"])
    A("dve", lambda e: e.tensor_copy(out=mLEb[:], in_=mLE), reads=["cst"], writes=["mLEb"])
    A("dve", lambda e: e.tensor_copy(out=mGEb[:], in_=mGE), reads=["cst"], writes=["mGEb"])

    wstate = {"i": 0}

    def load_w(dst, dkey, src, K, c_lo, c_hi, dcol=0, queue="sp"):
        for k in range(K):
            c0 = c_lo
            while c0 < c_hi:
                cc = min(1024, c_hi - c0)
                sl = wstate["i"] % 3
                wstate["i"] += 1
                A(queue, lambda e, sl=sl, k=k, c0=c0, cc=cc: e.dma_start(out=wst[sl][:, 0:cc], in_=src[k * 128:(k + 1) * 128, c0:c0 + cc]),
                  writes=[("wst", sl)], dma=True)
                d0 = dcol + (c0 - c_lo)
                ce = ("pool", "act", "dve")[wstate["i"] % 3] if cc >= 256 else "pool"
                if ce == "act":
                    A("act", lambda e, sl=sl, k=k, d0=d0, cc=cc: e.copy(out=dst[:, k, d0:d0 + cc], in_=wst[sl][:, 0:cc]),
                      reads=[("wst", sl)], writes=[(dkey, k)])
                else:
                    A(ce, lambda e, sl=sl, k=k, d0=d0, cc=cc: e.tensor_copy(out=dst[:, k, d0:d0 + cc], in_=wst[sl][:, 0:cc]),
                      reads=[("wst", sl)], writes=[(dkey, k)])
                c0 += cc

    xstate = {"i": 0}

    def rstd_of(sskey_ap, n, key):
        A("act", lambda e: e.activation(out=sskey_ap, in_=sskey_ap, func=AF.Ln, scale=1.0 / n, bias=EPSt[:, 0:1]), reads=[key, "epst"], writes=[key])
        A("act", lambda e: e.activation(out=sskey_ap, in_=sskey_ap, func=AF.Exp, scale=-0.5), reads=[key], writes=[key])

    def xpipe(src_rows, g_tile, gkey, hT_dst, hkey, tps, keep=False):
        i = xstate["i"]
        xstate["i"] += 1
        sl = i % 2
        A("sp", lambda e: e.dma_start(out=xt[sl][:], in_=src_rows), writes=[("xt", sl)], dma=True)
        ssap = sst[:, sl:sl + 1]
        A("act", lambda e: e.activation(out=junk[:], in_=xt[sl][:], func=AF.Square, accum_out=ssap), reads=[("xt", sl)], writes=[("ss", sl)])
        rstd_of(ssap, 1024, ("ss", sl))
        A("dve", lambda e: e.scalar_tensor_tensor(out=hb[sl][:], in0=xt[sl][:], scalar=ssap, in1=g_tile[:], op0=ALU.mult, op1=ALU.mult),
          reads=[("xt", sl), ("ss", sl), gkey], writes=[("hb", sl)])
        pb = psb(tps)
        for k in range(8):
            A("pe", lambda e, k=k: e.transpose(out=pb[:, k * 128:(k + 1) * 128], in_=hb[sl][:, k * 128:(k + 1) * 128], identity=idb[:]),
              reads=[("hb", sl), "idb"], writes=[("ps", tps)])
        A("act", lambda e: e.copy(out=hT_dst, in_=pb.rearrange("p (k t) -> p k t", k=8)), reads=[("ps", tps)], writes=[hkey])
        return sl

    m_phase = al.mark()
    LGe = [al([128, 8], F32) for _ in range(2)]
    Ie = [al([128, 8], F32) for _ in range(2)]
    exg = [al([128, 8], F32) for _ in range(2)]
    Wt = [al([128, 8], F32) for _ in range(2)]
    dect = [al([128, 4], F32) for _ in range(2)]
    Arun = al([128, 4], F32)
    ktil = [al([128, 128], BF16) for _ in range(4)]
    Cf = al([128, 4, 129], F32)
    Cb = al([128, 4, 129], F32)
    Cfb = al([128, 4, 129], BF16)
    Cbb = al([128, 4, 129], BF16)
    qT = al([128, 4, NOWN], BF16)
    kT = al([128, 4, NOWN], BF16)
    ktok = al([128, 16, 512], BF16)
    vaug = al([128, 16, 4, 129], BF16)
    sgo = al([128, 16, 512], BF16)
    Gown = al([128, 16, 16], F32)
    LFo = al([128, 16, 8], F32)
    m_alias = al.mark()
    wl = al([128, 8, 2064], BF16)
    tf = al([128, 48], F32)
    tb = al([128, 48], F32)
    pf = al([128, 48], F32)
    pbk = al([128, 48], F32)
    gbias = al([128, 64], F32)
    cw = al([128, 40], F32)
    cb = al([128, 8], F32)
    gon = al([128, 512], F32)
    haloT = al([128, 8, 128], F32)
    hTg = [al([128, 8, 512], BF16) for _ in range(2)]
    padb = [al([128, 516], F32) for _ in range(2)]
    acc = [al([128, 512], F32) for _ in range(2)]
    kTg1 = al([128, 4, 512], BF16)
    ktokg1 = al([128, 4, 512], BF16)
    vaugg1 = al([128, 4, 4, 129], BF16)
    kTg = [kTg1, kTg1]
    ktokg = [ktokg1, ktokg1]
    vaugg = [vaugg1, vaugg1]
    Gg = [al([128, 4, 16], F32) for _ in range(2)]
    LFg = [al([128, 4, 8], F32) for _ in range(2)]
    sgt = [al([128, 512], BF16) for _ in range(2)]
    m_p12 = al.mark()
    al.reset(m_alias)
    ymT = al([128, 4, NOWN], BF16)
    m_keep = al.mark()
    EBt = al([128, 16, 8], F32)
    ECt = al([128, 16, 8], F32)
    WSt = al([128, 16, 8], F32)
    DECt = al([128, 16, 8], F32)
    cmt = al([128, 8], F32)
    HB = al([128, 16, 512], F32)
    PT = [al([128, 4, 128], BF16) for _ in range(2)]
    den = al([128, 4], F32)
    scl = al([128, 4], F32)
    hsum = [al([128, 4, 128], F32) for _ in range(2)]
    ssq = al([128, 4], F32)
    ymt = [al([128, 4, 128], BF16) for _ in range(2)]
    assert al.mark() <= m_p12
    al.reset(m_p12)

    for (t_, d_, k_) in ((tf, tfd, "tf"), (gbias, gbd, "gbias"), (cw, cwd, "cw"), (cb, cbd, "cb"), (gon, gond, "gon")):
        A("sp", lambda e, t_=t_, d_=d_: e.dma_start(out=t_[:], in_=d_), writes=[k_], dma=True)
    A("dve", lambda e: e.tensor_scalar(out=tb[:], in0=tf[:], scalar1=-1.0, scalar2=1.0, op0=ALU.mult, op1=ALU.add), reads=["tf"], writes=["tb"])
    A("dve", lambda e: e.tensor_scalar(out=pf[:], in0=tf[:], scalar1=-1.0, scalar2=BIG, op0=ALU.add, op1=ALU.mult), reads=["tf"], writes=["pf"])
    A("dve", lambda e: e.tensor_scalar(out=pbk[:], in0=tf[:], scalar1=-BIG, scalar2=None, op0=ALU.mult), reads=["tf"], writes=["pbk"])
    for t_, k_ in ((Cf, "Cf"), (Cb, "Cb"), (Arun, "Arun")):
        A("dve", lambda e, t_=t_: e.memset(t_[:], 0.0), writes=[k_])
    A("pool", lambda e: e.memset(vaug[:, :, :, 128:129], 1.0), writes=["vaug1"])
    A("pool", lambda e: e.memset(vaugg1[:, :, :, 128:129], 1.0), writes=[("vaugg1", 0)])

    load_w(wl, "wl", w_in, 8, 672, 2720, 0)
    for (s0, d0) in ((2720, 2048), (2728, 2052), (2724, 2056), (2732, 2060)):
        load_w(wl, "wl", w_in, 8, s0, s0 + 4, d0)
    WLK = [("wl", k) for k in range(8)]

    TPSX, TPSK, CPS0, CPS1, MVPS, UPS0 = 0, 1, 2, 3, 4, 5

    hTh = hTg[1][:, :, 0:128]
    xpipe(xh, gmix, "gmix", hTh, ("hTg", 1), TPSX)
    for c in range(8):
        bank = CPS0 + (c % 2)
        for k in range(8):
            A("pe", lambda e, c=c, k=k, bank=bank: e.matmul(ps[bank][:, 0:128], lhsT=wl[:, k, c * 128:(c + 1) * 128], rhs=hTg[1][:, k, 0:128],
                                                           start=(k == 0), stop=(k == 7)),
              reads=[("hTg", 1), ("wl", k)], writes=[("ps", bank)])
        A("act", lambda e, c=c, bank=bank: e.copy(out=haloT[:, c, :], in_=ps[bank][:, 0:128]), reads=[("ps", bank)], writes=["haloT"])

    cstate = {"i": 0}

    def conv_chunk(par, wcol, chunk, hidx, dst_ap, dst_key):
        ci = cstate["i"]
        cstate["i"] += 1
        bank = CPS0 + (ci % 2)
        pz = ci % 2
        for k in range(8):
            A("pe", lambda e, k=k: e.matmul(ps[bank][:], lhsT=wl[:, k, wcol:wcol + 128], rhs=hTg[par][:, k, :], start=(k == 0), stop=(k == 7)),
              reads=[("hTg", par), ("wl", k)], writes=[("ps", bank)])
        A("act", lambda e: e.copy(out=padb[pz][:, 2:514], in_=ps[bank][:]), reads=[("ps", bank)], writes=[("padb", pz)])
        A("pool", lambda e: e.tensor_copy(out=padb[pz][:, 0:2], in_=haloT[:, chunk, hidx:hidx + 2]), reads=["haloT"], writes=[("padbL", pz)])
        A("pool", lambda e: e.tensor_copy(out=padb[pz][:, 514:516], in_=haloT[:, chunk, hidx + 2:hidx + 4]), reads=["haloT"], writes=[("padbR", pz)])
        rk = [("padb", pz), ("padbL", pz), ("padbR", pz), "cw", "cb"]
        A("dve", lambda e: e.tensor_scalar(out=acc[pz][:], in0=padb[pz][:, 0:512], scalar1=cw[:, chunk * 5:chunk * 5 + 1], scalar2=cb[:, chunk:chunk + 1],
                                           op0=ALU.mult, op1=ALU.add), reads=rk, writes=[("acc", pz)])
        for j in range(1, 5):
            A("dve", lambda e, j=j: e.scalar_tensor_tensor(out=acc[pz][:], in0=padb[pz][:, j:j + 512], scalar=cw[:, chunk * 5 + j:chunk * 5 + j + 1],
                                                           in1=acc[pz][:], op0=ALU.mult, op1=ALU.add), reads=rk + [("acc", pz)], writes=[("acc", pz)])
        A("act", lambda e: e.activation(out=dst_ap, in_=acc[pz][:], func=AF.Silu), reads=[("acc", pz)], writes=[dst_key])

    def mlstm_pre(gi):
        own = gi >= 12
        par = gi % 2
        src = xown if own else xo
        g0 = (gi - 12) if own else gi
        for i in range(4):
            r0 = g0 * 512 + i * 128
            xpipe(src[r0:r0 + 128, :], gmix, "gmix", hTg[par][:, :, i * 128:(i + 1) * 128], ("hTg", par), TPSX)

    def mlstm_group(gi, own, mid=None):
        par = gi % 2
        g0 = (gi - 12) if own else gi
        hidx = (48 + 4 * g0) if own else 4 * g0
        for h in range(4):
            if own:
                conv_chunk(par, h * 128, h, hidx, qT[:, h, g0 * 512:(g0 + 1) * 512], "qT")
                conv_chunk(par, 512 + h * 128, 4 + h, hidx, kT[:, h, g0 * 512:(g0 + 1) * 512], "kT")
            else:
                conv_chunk(par, 512 + h * 128, 4 + h, hidx, kTg[par][:, h, :], ("kTg", 0))
        if mid is not None:
            mid()
        for b2 in range(2):
            pb = psb(TPSK)
            for bb in range(2):
                blk = b2 * 2 + bb
                for h in range(4):
                    if own:
                        src_ap = kT[:, h, g0 * 512 + blk * 128:g0 * 512 + (blk + 1) * 128]
                        rkey = "kT"
                    else:
                        src_ap = kTg[par][:, h, blk * 128:(blk + 1) * 128]
                        rkey = ("kTg", 0)
                    o0 = (bb * 4 + h) * 128
                    A("pe", lambda e, src_ap=src_ap, o0=o0: e.transpose(out=pb[:, o0:o0 + 128], in_=src_ap, identity=idb[:]),
                      reads=[rkey, "idb"], writes=[("ps", TPSK)])
            if own:
                dst = ktok[:, g0 * 4 + b2 * 2:g0 * 4 + b2 * 2 + 2, :]
                dk = "ktok"
            else:
                dst = ktokg[par][:, b2 * 2:b2 * 2 + 2, :]
                dk = ("ktokg", 0)
            A("act", lambda e, dst=dst, pb=pb: e.copy(out=dst, in_=pb.rearrange("p (b c) -> p b c", b=2)), reads=[("ps", TPSK)], writes=[dk])
        for i in range(4):
            mvb = (4, 5)[i % 2]
            for k in range(8):
                A("pe", lambda e, i=i, k=k: e.matmul(ps[mvb][:], lhsT=hTg[par][:, k, i * 128:(i + 1) * 128], rhs=wl[:, k, 1024:1536],
                                                     start=(k == 0), stop=(k == 7)), reads=[("hTg", par), ("wl", k)], writes=[("ps", mvb)])
            if own:
                dst = vaug[:, g0 * 4 + i, :, 0:128]
                dk = "vaug"
            else:
                dst = vaugg[par][:, i, :, 0:128]
                dk = ("vaugg", 0)
            A("dve", lambda e, dst=dst: e.tensor_copy(out=dst, in_=ps[mvb][:].rearrange("p (h d) -> p h d", h=4)), reads=[("ps", mvb)], writes=[dk])
            if own:
                mob = (5, 4)[i % 2]
                for k in range(8):
                    A("pe", lambda e, i=i, k=k: e.matmul(ps[mob][:], lhsT=hTg[par][:, k, i * 128:(i + 1) * 128], rhs=wl[:, k, 1536:2048],
                                                         start=(k == 0), stop=(k == 7)), reads=[("hTg", par), ("wl", k)], writes=[("ps", mob)])
                sp_ = i % 2
                A("act", lambda e, sp_=sp_: e.activation(out=sgt[sp_][:], in_=ps[mob][:], func=AF.Sigmoid), reads=[("ps", mob)], writes=[("sgt", sp_)])
                A("pool", lambda e, sp_=sp_, i=i: e.tensor_tensor(out=sgo[:, g0 * 4 + i, :], in0=sgt[sp_][:], in1=gon[:], op=ALU.mult),
                  reads=[("sgt", sp_), "gon"], writes=["sgo"])
        for i in range(4):
            for k in range(8):
                A("pe", lambda e, i=i, k=k: e.matmul(ps[MVPS][:, i * 16:(i + 1) * 16], lhsT=hTg[par][:, k, i * 128:(i + 1) * 128], rhs=wl[:, k, 2048:2064],
                                                     start=(k == 0), stop=(k == 7)), reads=[("hTg", par), ("wl", k)], writes=[("ps", MVPS)])
        if own:
            Gd = Gown[:, g0 * 4:g0 * 4 + 4, :]
            gk = "Gown"
            LFd = LFo[:, g0 * 4:g0 * 4 + 4, :]
            lk = "LFo"
        else:
            Gd = Gg[par][:]
            gk = ("Gg", par)
            LFd = LFg[par][:]
            lk = ("LFg", par)
        A("dve", lambda e: e.tensor_tensor(out=Gd, in0=ps[MVPS][:, 0:64].rearrange("p (b c) -> p b c", b=4),
                                           in1=gbias[:].rearrange("p (b c) -> p b c", b=4), op=ALU.add), reads=[("ps", MVPS), "gbias"], writes=[gk])
        A("act", lambda e: e.activation(out=LFd, in_=Gd[:, :, 8:16], func=AF.Exp, scale=-1.0), reads=[gk], writes=[lk])
        A("act", lambda e: e.activation(out=LFd, in_=LFd, func=AF.Ln, bias=ONEt[:, 0:1]), reads=[lk, "onet"], writes=[lk])
        if own:
            return
        for blk in range(4):
            gb_ = gi * 4 + blk
            z = blk % 2
            A("dve", lambda e: e.tensor_scalar(out=LGe[z][:, 0:4], in0=LFg[par][:, blk, 0:4], scalar1=tf[:, gb_:gb_ + 1], scalar2=-1.0, op0=ALU.mult, op1=ALU.mult),
              reads=[lk, "tf"], writes=[("LGe", z)])
            A("dve", lambda e: e.tensor_scalar(out=LGe[z][:, 4:8], in0=LFg[par][:, blk, 4:8], scalar1=tb[:, gb_:gb_ + 1], scalar2=-1.0, op0=ALU.mult, op1=ALU.mult),
              reads=[lk, "tb"], writes=[("LGe", z)])
            A("dve", lambda e: e.tensor_scalar(out=Ie[z][:, 0:4], in0=Gg[par][:, blk, 0:4], scalar1=pf[:, gb_:gb_ + 1], scalar2=None, op0=ALU.add),
              reads=[gk, "pf"], writes=[("Ie", z)])
            A("dve", lambda e: e.tensor_scalar(out=Ie[z][:, 4:8], in0=Gg[par][:, blk, 4:8], scalar1=pbk[:, gb_:gb_ + 1], scalar2=None, op0=ALU.add),
              reads=[gk, "pbk"], writes=[("Ie", z)])
            gp = UPS0 + 2
            A("pe", lambda e: e.matmul(ps[gp][:, 0:4], lhsT=mSU, rhs=LGe[z][:, 0:4], start=True, stop=True), reads=["cst", ("LGe", z)], writes=[("ps", gp)])
            A("pe", lambda e: e.matmul(ps[gp][:, 4:8], lhsT=mSL, rhs=LGe[z][:, 4:8], start=True, stop=True), reads=["cst", ("LGe", z)], writes=[("ps", gp)])
            A("pe", lambda e: e.matmul(ps[gp][:, 8:16], lhsT=ones, rhs=LGe[z][:, 0:8], start=True, stop=True), reads=["cst", ("LGe", z)], writes=[("ps", gp)])
            A("dve", lambda e: e.tensor_tensor(out=exg[z][:], in0=ps[gp][:, 0:8], in1=Ie[z][:], op=ALU.add), reads=[("ps", gp), ("Ie", z)], writes=[("exg", z)])
            A("dve", lambda e: e.tensor_tensor(out=exg[z][:, 4:8], in0=exg[z][:, 4:8], in1=Arun[:], op=ALU.add), reads=[("exg", z), "Arun"], writes=[("exg", z)])
            A("act", lambda e: e.activation(out=Wt[z][:], in_=exg[z][:], func=AF.Exp, bias=LNSCt[:, 0:1]), reads=[("exg", z), "lnsc"], writes=[("Wt", z)])
            A("act", lambda e: e.activation(out=dect[z][:], in_=ps[gp][:, 8:12], func=AF.Exp), reads=[("ps", gp)], writes=[("dect", z)])
            A("dve", lambda e: e.tensor_tensor(out=Arun[:], in0=Arun[:], in1=ps[gp][:, 12:16], op=ALU.add), reads=["Arun", ("ps", gp)], writes=["Arun"])
            for u in range(8):
                ch, h = divmod(u, 4)
                kz = u % 4
                A("dve", lambda e, u=u, h=h, kz=kz: e.tensor_scalar(out=ktil[kz][:], in0=ktokg[par][:, blk, h * 128:(h + 1) * 128],
                                                                    scalar1=Wt[z][:, u:u + 1], scalar2=None, op0=ALU.mult),
                  reads=[("ktokg", 0), ("Wt", z)], writes=[("ktil", kz)])
                bank = UPS0 + (u % 3)
                c0 = (u // 3) * 129 + (128 if bank == UPS0 + 2 else 0)
                A("pe", lambda e, h=h, kz=kz, bank=bank, c0=c0: e.matmul(ps[bank][:, c0:c0 + 129], lhsT=ktil[kz][:], rhs=vaugg[par][:, blk, h, :],
                                                                         start=True, stop=True),
                  reads=[("ktil", kz), ("vaugg", 0), ("vaugg1", 0)], writes=[("ps", bank)])
                if ch == 0:
                    A("dve", lambda e, h=h, bank=bank, c0=c0: e.scalar_tensor_tensor(out=Cf[:, h, :], in0=Cf[:, h, :], scalar=dect[z][:, h:h + 1],
                                                                                     in1=ps[bank][:, c0:c0 + 129], op0=ALU.mult, op1=ALU.add),
                      reads=["Cf", ("dect", z), ("ps", bank)], writes=["Cf"])
                else:
                    A("dve", lambda e, h=h, bank=bank, c0=c0: e.tensor_tensor(out=Cb[:, h, :], in0=Cb[:, h, :], in1=ps[bank][:, c0:c0 + 129], op=ALU.add),
                      reads=["Cb", ("ps", bank)], writes=["Cb"])

    A("dve", lambda e: e.memset(LNSCt[:], LNSC), writes=["lnsc"])

    gseq = list(range(12 if not (debug and (debug.endswith("_fast") or debug.startswith("p4"))) else 0))
    gseq += list(range(12, 16 if not (debug and debug.startswith("p4")) else 12))
    if gseq:
        mlstm_pre(gseq[0])
    for gidx, gi in enumerate(gseq):
        nxt = gseq[gidx + 1] if gidx + 1 < len(gseq) else None
        mlstm_group(gi, gi >= 12, mid=(lambda nxt=nxt: mlstm_pre(nxt)) if nxt is not None else None)

    if debug and debug.split("_")[0] in ("qTe", "kTe"):
        src_t = qT if debug.startswith("qTe") else kT
        dt_ = al([128, 1024], F32)
        for h in range(4):
            for half in range(2):
                A("dve", lambda e, h=h, half=half: e.tensor_copy(out=dt_[:], in_=src_t[:, h, half * 1024:(half + 1) * 1024]), reads=["qT", "kT"], writes=["dt_"])
                r0 = (h * 2 + half) * 128
                A("sp", lambda e, r0=r0: e.dma_start(out=dbg[r0:r0 + 128, :], in_=dt_[:]), reads=["dt_"], writes=["dbgo"], dma=True)
        A("sp", lambda e: e.nop(), reads=["dbgo"])
        S.emit(nc, st)
        st.close()
        return nc
    p12keys = [("wl", k) for k in range(8)] + ["tf", "tb", "pf", "pbk", "gbias", "cw", "cb", "gon", "haloT", ("hTg", 0), ("hTg", 1),
               ("padb", 0), ("padb", 1), ("padbL", 0), ("padbL", 1), ("padbR", 0), ("padbR", 1), ("acc", 0), ("acc", 1),
               ("kTg", 0), ("ktokg", 0), ("vaugg", 0), ("vaugg1", 0), ("Gg", 0), ("Gg", 1), ("LFg", 0), ("LFg", 1), ("sgt", 0), ("sgt", 1)]
    p3keys = ["EBt", "ECt", "WSt", "DECt", "cmt", "HB", ("PT", 0), ("PT", 1), "den", "scl", ("hsum", 0), ("hsum", 1), "ssq", ("ymt", 0), ("ymt", 1), "ymT"]
    A("dve", lambda e: e.memset(cmt[:], 0.0), writes=p12keys + p3keys)
    A("pool", lambda e: e.tensor_copy(out=Cfb[:], in_=Cf[:]), reads=["Cf"], writes=["Cfb"])
    A("pool", lambda e: e.tensor_copy(out=Cbb[:], in_=Cb[:]), reads=["Cb"], writes=["Cbb"])
    GP = 7
    for blk in range(16 if not (debug and debug.startswith("p4")) else 0):
        z = blk % 2
        A("dve", lambda e, blk=blk: e.tensor_scalar(out=LGe[z][:], in0=LFo[:, blk, :], scalar1=-1.0, scalar2=None, op0=ALU.mult), reads=["LFo"], writes=[("LGe", z)])
        A("pe", lambda e: e.matmul(ps[GP][:, 0:4], lhsT=mLE, rhs=LGe[z][:, 0:4], start=True, stop=True), reads=["cst", ("LGe", z)], writes=[("ps", GP)])
        A("pe", lambda e: e.matmul(ps[GP][:, 4:8], lhsT=mGE, rhs=LGe[z][:, 4:8], start=True, stop=True), reads=["cst", ("LGe", z)], writes=[("ps", GP)])
        A("pe", lambda e: e.matmul(ps[GP][:, 8:16], lhsT=ones, rhs=LGe[z][:, 0:8], start=True, stop=True), reads=["cst", ("LGe", z)], writes=[("ps", GP)])
        A("act", lambda e, blk=blk: e.activation(out=EBt[:, blk, :], in_=ps[GP][:, 0:8], func=AF.Exp), reads=[("ps", GP)], writes=["EBt"])
        A("dve", lambda e, blk=blk: e.tensor_tensor(out=cmt[:], in0=Gown[:, blk, 0:8], in1=ps[GP][:, 0:8], op=ALU.subtract), reads=["Gown", ("ps", GP)], writes=["cmt"])
        A("act", lambda e, blk=blk: e.activation(out=ECt[:, blk, :], in_=cmt[:], func=AF.Exp, bias=LNSCt[:, 0:1]), reads=["cmt", "lnsc"], writes=["ECt"])
        A("act", lambda e, blk=blk: e.activation(out=DECt[:, blk, :], in_=ps[GP][:, 8:16], func=AF.Exp), reads=[("ps", GP)], writes=["DECt"])
        A("dve", lambda e, blk=blk: e.tensor_tensor(out=cmt[:], in0=cmt[:], in1=ps[GP][:, 8:16], op=ALU.add), reads=["cmt", ("ps", GP)], writes=["cmt"])
        A("act", lambda e, blk=blk: e.activation(out=WSt[:, blk, :], in_=cmt[:], func=AF.Exp, bias=LNSCt[:, 0:1]), reads=["cmt", "lnsc"], writes=["WSt"])

    SPS, ND0, ND1, UB0, UB1, TPY = 0, 1, 2, 3, 4, 5
    def dirpass(d):
        blocks = list(range(16)) if d == 0 else list(range(15, -1, -1))
        Cx, Cxb, ck, ckb = (Cf, Cfb, "Cf", "Cfb") if d == 0 else (Cb, Cbb, "Cb", "Cbb")
        maskb = mLEb if d == 0 else mGEb
        mk_ = "mLEb" if d == 0 else "mGEb"
        final = (d == 0)
        for bi, blk in enumerate(blocks):
            z = bi % 2
            tsl = slice(blk * 128, (blk + 1) * 128)
            for h in range(4):
                A("pe", lambda e, h=h: e.matmul(ps[SPS][:, h * 128:(h + 1) * 128], lhsT=kT[:, h, tsl], rhs=qT[:, h, tsl], start=True, stop=True),
                  reads=["kT", "qT"], writes=[("ps", SPS)])
            for h in range(4):
                A("dve", lambda e, h=h: e.scalar_tensor_tensor(out=PT[z][:, h, :], in0=ps[SPS][:, h * 128:(h + 1) * 128], scalar=ECt[:, blk, d * 4 + h:d * 4 + h + 1],
                                                               in1=maskb[:], op0=ALU.mult, op1=ALU.mult),
                  reads=[("ps", SPS), "ECt", mk_], writes=[("PT", z)])
            for h in range(4):
                bank = ND0 + h % 2
                c0 = (h // 2) * 129
                A("pe", lambda e, h=h, bank=bank, c0=c0: e.matmul(ps[bank][:, c0:c0 + 129], lhsT=PT[z][:, h, :], rhs=vaug[:, blk, h, :], start=True, stop=False),
                  reads=[("PT", z), "vaug", "vaug1"], writes=[("ps", bank)])
                A("pe", lambda e, h=h, bank=bank, c0=c0: e.matmul(ps[bank][:, c0:c0 + 129], lhsT=qT[:, h, tsl], rhs=Cxb[:, h, :], start=False, stop=True),
                  reads=["qT", ckb], writes=[("ps", bank)])
            for h in range(4):
                bank = ND0 + h % 2
                c0 = (h // 2) * 129
                A("dve", lambda e, h=h, bank=bank, c0=c0: e.tensor_tensor(out=den[:, h:h + 1], in0=ps[bank][:, c0 + 128:c0 + 129],
                                                                          in1=EBt[:, blk, d * 4 + h:d * 4 + h + 1], op=ALU.mult),
                  reads=[("ps", bank), "EBt"], writes=["den"])
            A("dve", lambda e: e.tensor_scalar(out=scl[:], in0=den[:], scalar1=-1.0, scalar2=None, op0=ALU.mult), reads=["den"], writes=["scl"])
            A("dve", lambda e: e.tensor_tensor(out=den[:], in0=den[:], in1=scl[:], op=ALU.max), reads=["den", "scl"], writes=["den"])
            A("dve", lambda e: e.tensor_scalar(out=den[:], in0=den[:], scalar1=1.0, scalar2=None, op0=ALU.max), reads=["den"], writes=["den"])
            A("dve", lambda e: e.reciprocal(out=den[:], in_=den[:]), reads=["den"], writes=["den"])
            A("dve", lambda e: e.tensor_tensor(out=scl[:], in0=den[:], in1=EBt[:, blk, d * 4:d * 4 + 4], op=ALU.mult), reads=["den", "EBt"], writes=["scl"])
            for h in range(4):
                bank = ND0 + h % 2
                c0 = (h // 2) * 129
                if not final:
                    A("dve", lambda e, h=h, bank=bank, c0=c0: e.tensor_scalar(out=HB[:, blk, h * 128:(h + 1) * 128], in0=ps[bank][:, c0:c0 + 128],
                                                                              scalar1=scl[:, h:h + 1], scalar2=None, op0=ALU.mult),
                      reads=[("ps", bank), "scl"], writes=["HB"])
                else:
                    A("dve", lambda e, h=h, bank=bank, c0=c0: e.scalar_tensor_tensor(out=hsum[z][:, h, :], in0=ps[bank][:, c0:c0 + 128], scalar=scl[:, h:h + 1],
                                                                                     in1=HB[:, blk, h * 128:(h + 1) * 128], op0=ALU.mult, op1=ALU.add),
                      reads=[("ps", bank), "scl", "HB"], writes=[("hsum", z)])
            if bi < 15:
                for h in range(4):
                    kz = h
                    bank = UB0 + h % 2
                    c0 = (h // 2) * 129
                    A("dve", lambda e, h=h, kz=kz: e.tensor_scalar(out=ktil[kz][:], in0=ktok[:, blk, h * 128:(h + 1) * 128],
                                                                   scalar1=WSt[:, blk, d * 4 + h:d * 4 + h + 1], scalar2=None, op0=ALU.mult),
                      reads=["ktok", "WSt"], writes=[("ktil", kz)])
                    A("pe", lambda e, h=h, kz=kz, bank=bank, c0=c0: e.matmul(ps[bank][:, c0:c0 + 129], lhsT=ktil[kz][:], rhs=vaug[:, blk, h, :], start=True, stop=True),
                      reads=[("ktil", kz), "vaug", "vaug1"], writes=[("ps", bank)])
                    A("dve", lambda e, h=h, bank=bank, c0=c0: e.scalar_tensor_tensor(out=Cx[:, h, :], in0=Cx[:, h, :], scalar=DECt[:, blk, d * 4 + h:d * 4 + h + 1],
                                                                                     in1=ps[bank][:, c0:c0 + 129], op0=ALU.mult, op1=ALU.add),
                      reads=[ck, "DECt", ("ps", bank)], writes=[ck])
                A("pool", lambda e: e.tensor_copy(out=Cxb[:], in_=Cx[:]), reads=[ck], writes=[ckb])
            if final:
                for h in range(4):
                    A("act", lambda e, h=h: e.activation(out=junk[:, 0:128], in_=hsum[z][:, h, :], func=AF.Square, accum_out=ssq[:, h:h + 1]),
                      reads=[("hsum", z)], writes=["ssq"])
                rstd_of(ssq[:], 128, "ssq")
                for h in range(4):
                    A("dve", lambda e, h=h: e.scalar_tensor_tensor(out=ymt[z][:, h, :], in0=hsum[z][:, h, :], scalar=ssq[:, h:h + 1],
                                                                   in1=sgo[:, blk, h * 128:(h + 1) * 128], op0=ALU.mult, op1=ALU.mult),
                      reads=[("hsum", z), "ssq", "sgo"], writes=[("ymt", z)])
                pb = psb(TPY)
                for h in range(4):
                    A("pe", lambda e, h=h: e.transpose(out=pb[:, h * 128:(h + 1) * 128], in_=ymt[z][:, h, :], identity=idb[:]),
                      reads=[("ymt", z), "idb"], writes=[("ps", TPY)])
                A("act", lambda e: e.copy(out=ymT[:, :, tsl], in_=pb[:, 0:512].rearrange("p (h t) -> p h t", h=4)), reads=[("ps", TPY)], writes=["ymT"])

    if not (debug and debug.startswith("p4")):
        dirpass(1)
    if debug and debug.startswith("HB"):
        for blk in range(16):
            A("sp", lambda e, blk=blk: e.dma_start(out=dbg[blk * 128:(blk + 1) * 128, 0:512], in_=HB[:, blk, :]), reads=["HB"], writes=["dbgo"], dma=True)
        A("sp", lambda e: e.nop(), reads=["dbgo"])
        S.emit(nc, st)
        st.close()
        return nc
    if not (debug and debug.startswith("p4")):
        dirpass(0)

    if debug and debug.split("_")[0] in ("mlstm", "qT", "kT"):
        debug = debug.split("_")[0]
        if debug == "qT":
            ymT = qT
        elif debug == "kT":
            ymT = kT
        ymk = {"mlstm": "ymT", "qT": "qT", "kT": "kT"}[debug]
        dt_ = al([128, 1024], F32)
        for h in range(4):
            for half in range(2):
                A("dve", lambda e, h=h, half=half: e.tensor_copy(out=dt_[:], in_=ymT[:, h, half * 1024:(half + 1) * 1024]), reads=[ymk], writes=["dt_"])
                r0 = (h * 2 + half) * 128
                A("sp", lambda e, r0=r0: e.dma_start(out=dbg[r0:r0 + 128, :], in_=dt_[:]), reads=["dt_"], writes=["dbgo"], dma=True)
        A("sp", lambda e: e.nop(), reads=["dbgo"])
        S.emit(nc, st)
        st.close()
        return nc

    S.barrier(lambda e: e.memset(sst[:, 7:8], 0.0))
    al.set_regions([(m_phase, m_alias), (m_keep, SB_HI)])
    R = al.ralloc
    ckvT = R([128, 2, 8192], BF16)
    kropeT = R([32, 8192], BF16)
    QT = R([96, 8, NOWN], BF16)
    yaT = R([128, 4, NOWN], BF16)
    reg_p4 = [list(r) for r in al.regions]
    wkv = R([128, 8, 288], BF16)
    wq = R([128, 8, 384], BF16)
    wuq = R([128, 3, 768], BF16)
    gq = R([128, 384], F32)
    gkv = R([128, 256], F32)
    hT4 = [R([128, 8, 128], BF16) for _ in range(2)]
    sinT = R([128, 80, 16], F32)
    cosT = R([128, 80, 16], F32)
    sinq = R([128, 16, 4, 16], F32)
    cosq = R([128, 16, 4, 16], F32)
    reg_tmp = [list(r) for r in al.regions]
    posi = R([128, 80], I32)
    posf = R([128, 80], F32)
    ang = R([128, 80, 16], F32)
    tq = R([128, 1280], F32)
    tk = R([128, 1280], I32)
    tkf = R([128, 1280], F32)

    load_w(wkv, "wkv", w_in, 8, 384, 672)
    load_w(wq, "wq", w_in, 8, 0, 384)
    load_w(wuq, "wuq", w_uq, 3, 0, 768)
    A("sp", lambda e: e.dma_start(out=gq[:], in_=gqd), writes=["gq"], dma=True)
    A("sp", lambda e: e.dma_start(out=gkv[:], in_=gkvd), writes=["gkv"], dma=True)
    A("sp", lambda e: e.dma_start(out=posi[:], in_=posT), writes=["posi"], dma=True)
    A("dve", lambda e: e.tensor_copy(out=posf[:], in_=posi[:]), reads=["posi"], writes=["posf"])
    invf = (np.float32(10000.0) ** (-np.arange(0, 32, 2, dtype=np.float32) / np.float32(32))).astype(np.float32)
    for f in range(16):
        A("dve", lambda e, f=f: e.tensor_scalar(out=ang[:, :, f], in0=posf[:], scalar1=float(invf[f]), scalar2=None, op0=ALU.mult), reads=["posf"], writes=["ang"])
    angf = ang[:].rearrange("p a b -> p (a b)")
    TWO_PI = 2.0 * math.pi
    for (dst, off) in ((sinT, 0.0), (cosT, 0.25)):
        dstf = dst[:].rearrange("p a b -> p (a b)")
        A("dve", lambda e, off=off: e.tensor_scalar(out=tq[:], in0=angf, scalar1=1.0 / TWO_PI, scalar2=off, op0=ALU.mult, op1=ALU.add), reads=["ang"], writes=["tq"])
        A("dve", lambda e: e.tensor_copy(out=tk[:], in_=tq[:]), reads=["tq"], writes=["tk"])
        A("dve", lambda e: e.tensor_copy(out=tkf[:], in_=tk[:]), reads=["tk"], writes=["tkf"])
        A("dve", lambda e: e.tensor_tensor(out=tq[:], in0=tq[:], in1=tkf[:], op=ALU.subtract), reads=["tq", "tkf"], writes=["tq"])
        A("dve", lambda e: e.tensor_scalar(out=tkf[:], in0=tq[:], scalar1=0.5, scalar2=None, op0=ALU.is_gt), reads=["tq"], writes=["tkf"])
        A("dve", lambda e: e.tensor_tensor(out=tq[:], in0=tq[:], in1=tkf[:], op=ALU.subtract), reads=["tq", "tkf"], writes=["tq"])
        A("dve", lambda e: e.tensor_scalar(out=tkf[:], in0=tq[:], scalar1=-0.5, scalar2=None, op0=ALU.is_lt), reads=["tq"], writes=["tkf"])
        A("dve", lambda e: e.tensor_tensor(out=tq[:], in0=tq[:], in1=tkf[:], op=ALU.add), reads=["tq", "tkf"], writes=["tq"])
        A("dve", lambda e: e.tensor_scalar(out=tq[:], in0=tq[:], scalar1=-0.4999, scalar2=0.4999, op0=ALU.max, op1=ALU.min), reads=["tq"], writes=["tq"])
        A("act", lambda e, dstf=dstf: e.activation(out=dstf, in_=tq[:], func=AF.Sin, scale=TWO_PI), reads=["tq"], writes=["sincos"])
    for hh in range(4):
        A("dve", lambda e, hh=hh: e.tensor_copy(out=sinq[:, :, hh, :], in_=sinT[:, 64:80, :]), reads=["sincos"], writes=["sinq"])
        A("dve", lambda e, hh=hh: e.tensor_copy(out=cosq[:, :, hh, :], in_=cosT[:, 64:80, :]), reads=["sincos"], writes=["cosq"])

    if debug == "p4a":
        A("sp", lambda e: e.dma_start(out=dbg[0:128, 0:1024], in_=sinT[:].rearrange("p a b -> p (a b)")[:, 0:1024]), reads=["sincos"], writes=["dbgo"], dma=True)
        A("sp", lambda e: e.nop(), reads=["dbgo"])
        S.emit(nc, st)
        st.close()
        return nc
    S.barrier(lambda e: e.memset(sst[:, 7:8], 0.0))
    al.regions = [list(r) for r in reg_tmp]
    cn = [R([128, 384], BF16) for _ in range(2)]
    krr = [R([128, 32], BF16) for _ in range(2)]
    rt = [R([128, 4, 16], F32) for _ in range(8)]
    cqnT = R([128, 3, 128], BF16)
    qtok = R([128, 8, 96], BF16)
    TPSX, LAT0, TPC, Q0, Q1, TPQ = 0, 1, 3, 4, 5, 6
    lstate = {"i": 0}

    def rope(x1, x2, cs, sn, o1, o2, rkeys, okey, shape4):
        rs_ = lstate["i"] % 2
        lstate["i"] += 1
        ta, tb_, tc, td = [t[:] if shape4 else t[:, 0, :] for t in rt[rs_ * 4:rs_ * 4 + 4]]
        k0, k1, k2, k3 = [("rt", rs_, q_) for q_ in range(4)]
        A("dve", lambda e: e.tensor_tensor(out=ta, in0=x1, in1=cs, op=ALU.mult), reads=rkeys, writes=[k0])
        A("dve", lambda e: e.tensor_tensor(out=tb_, in0=x2, in1=sn, op=ALU.mult), reads=rkeys, writes=[k1])
        A("dve", lambda e: e.tensor_tensor(out=o1, in0=ta, in1=tb_, op=ALU.subtract), reads=[k0, k1], writes=[okey])
        A("dve", lambda e: e.tensor_tensor(out=tc, in0=x2, in1=cs, op=ALU.mult), reads=rkeys, writes=[k2])
        A("dve", lambda e: e.tensor_tensor(out=td, in0=x1, in1=sn, op=ALU.mult), reads=rkeys, writes=[k3])
        A("dve", lambda e: e.tensor_tensor(out=o2, in0=tc, in1=td, op=ALU.add), reads=[k2, k3], writes=[okey])

    def kv_pre(kt):
        z = kt % 2
        src = xo[kt * 128:(kt + 1) * 128, :] if kt < 48 else xown[(kt - 48) * 128:(kt - 47) * 128, :]
        xpipe(src, gmix, "gmix", hT4[z][:], ("hT4", z), TPSX)

    kv_pre(0)
    for kt in range(64):
        z = kt % 2
        if kt + 1 < 64:
            kv_pre(kt + 1)
        lat = LAT0 + z
        for k in range(8):
            A("pe", lambda e, k=k: e.matmul(ps[lat][:, 0:288], lhsT=hT4[z][:, k, :], rhs=wkv[:, k, :], start=(k == 0), stop=(k == 7)),
              reads=[("hT4", z), ("wkv", k)], writes=[("ps", lat)])
        ssap = sst[:, 2 + z:3 + z]
        A("act", lambda e: e.activation(out=junk[:, 0:256], in_=ps[lat][:, 0:256], func=AF.Square, accum_out=ssap), reads=[("ps", lat)], writes=[("ssl", z)])
        rstd_of(ssap, 256, ("ssl", z))
        A("dve", lambda e: e.scalar_tensor_tensor(out=cn[z][:, 0:256], in0=ps[lat][:, 0:256], scalar=ssap, in1=gkv[:], op0=ALU.mult, op1=ALU.mult),
          reads=[("ps", lat), ("ssl", z), "gkv"], writes=[("cn", z)])
        rope(ps[lat][:, 256:272], ps[lat][:, 272:288], cosT[:, kt, :], sinT[:, kt, :], krr[z][:, 0:16], krr[z][:, 16:32],
             [("ps", lat), "sincos"], ("krr", z), False)
        tpc = (3, 7)[z]
        pb = psb(tpc)
        for c in range(2):
            A("pe", lambda e, c=c: e.transpose(out=pb[:, c * 128:(c + 1) * 128], in_=cn[z][:, c * 128:(c + 1) * 128], identity=idb[:]),
              reads=[("cn", z), "idb"], writes=[("ps", tpc)])
        A("pe", lambda e: e.transpose(out=pb[0:32, 256:384], in_=krr[z][:], identity=idb[:]), reads=[("krr", z), "idb"], writes=[("ps", tpc)])
        A("act", lambda e: e.copy(out=ckvT[:, :, kt * 128:(kt + 1) * 128], in_=pb[:, 0:256].rearrange("p (c t) -> p c t", c=2)), reads=[("ps", tpc)], writes=["ckvT"])
        A("act", lambda e: e.copy(out=kropeT[:, kt * 128:(kt + 1) * 128], in_=pb[0:32, 256:384]), reads=[("ps", tpc)], writes=["kropeT"])

    def q_pre(ot):
        z = ot % 2
        xpipe(xown[ot * 128:(ot + 1) * 128, :], gmix, "gmix", hT4[z][:], ("hT4", z), TPSX)

    n_ot = 16 if debug != "p4b" else 0
    if n_ot:
        q_pre(0)
    for ot in range(n_ot):
        z = ot % 2
        if ot + 1 < n_ot:
            q_pre(ot + 1)
        lat = LAT0 + z
        for k in range(8):
            A("pe", lambda e, k=k: e.matmul(ps[lat][:, 0:384], lhsT=hT4[z][:, k, :], rhs=wq[:, k, :], start=(k == 0), stop=(k == 7)),
              reads=[("hT4", z), ("wq", k)], writes=[("ps", lat)])
        ssap = sst[:, 2 + z:3 + z]
        A("act", lambda e: e.activation(out=junk[:, 0:384], in_=ps[lat][:, 0:384], func=AF.Square, accum_out=ssap), reads=[("ps", lat)], writes=[("ssl", z)])
        rstd_of(ssap, 384, ("ssl", z))
        A("dve", lambda e: e.scalar_tensor_tensor(out=cn[z][:], in0=ps[lat][:, 0:384], scalar=ssap, in1=gq[:], op0=ALU.mult, op1=ALU.mult),
          reads=[("ps", lat), ("ssl", z), "gq"], writes=[("cn", z)])
        pb = psb(TPC)
        for c in range(3):
            A("pe", lambda e, c=c: e.transpose(out=pb[:, c * 128:(c + 1) * 128], in_=cn[z][:, c * 128:(c + 1) * 128], identity=idb[:]),
              reads=[("cn", z), "idb"], writes=[("ps", TPC)])
        A("act", lambda e: e.copy(out=cqnT[:], in_=pb[:, 0:384].rearrange("p (c t) -> p c t", c=3)), reads=[("ps", TPC)], writes=["cqnT"])
        qlvl = int(debug[3:]) if (debug and debug.startswith("p4q")) else 9
        if qlvl < 2:
            continue
        for x_ in range(2):
            qb = Q0 + x_
            for c in range(3):
                A("pe", lambda e, c=c: e.matmul(ps[qb][:, 0:384], lhsT=cqnT[:, c, :], rhs=wuq[:, c, x_ * 384:(x_ + 1) * 384], start=(c == 0), stop=(c == 2)),
                  reads=["cqnT", ("wuq", c)], writes=[("ps", qb)])
            V4 = ps[qb][:, 0:384].rearrange("p (h d) -> p h d", h=4)
            A("act", lambda e: e.copy(out=qtok[:, x_ * 4:x_ * 4 + 4, 0:64], in_=V4[:, :, 0:64]), reads=[("ps", qb)], writes=["qtokn"])
            if qlvl < 3:
                continue
            for hh in range(4):
                c0 = hh * 96
                rope(ps[qb][:, c0 + 64:c0 + 80], ps[qb][:, c0 + 80:c0 + 96], cosT[:, 64 + ot, :], sinT[:, 64 + ot, :],
                     qtok[:, x_ * 4 + hh, 64:80], qtok[:, x_ * 4 + hh, 80:96], [("ps", qb), "sincos"], "qtokr", False)
        if qlvl < 4:
            continue
        pq = psb(TPQ)
        for h in range(8):
            A("pe", lambda e, h=h: e.transpose(out=pq[0:96, h * 128:(h + 1) * 128], in_=qtok[:, h, :], identity=idb[:]),
              reads=["qtokn", "qtokr", "idb"], writes=[("ps", TPQ)])
        A("act", lambda e: e.copy(out=QT[:, :, ot * 128:(ot + 1) * 128], in_=pq[0:96, :].rearrange("p (h t) -> p h t", h=8)), reads=[("ps", TPQ)], writes=["QT"])

    if debug and debug.startswith("p4"):
        dtt = R([128, 1024], F32)
        for h in range(8):
            for half in range(2):
                A("dve", lambda e, h=h, half=half: e.tensor_copy(out=dtt[0:96, :], in_=QT[:, h, half * 1024:(half + 1) * 1024]), reads=["QT"], writes=["dtt"])
                r0 = (h * 2 + half) * 96
                A("sp", lambda e, r0=r0: e.dma_start(out=dbg[r0:r0 + 96, :], in_=dtt[0:96, :]), reads=["dtt"], writes=["dbgo"], dma=True)
        A("dve", lambda e: e.tensor_copy(out=dtt[0:32, :], in_=kropeT[:, 7168:8192]), reads=["kropeT"], writes=["dtt"])
        A("sp", lambda e: e.dma_start(out=dbg[1536:1568, :], in_=dtt[0:32, :]), reads=["dtt"], writes=["dbgo"], dma=True)
        A("dve", lambda e: e.tensor_copy(out=dtt[:], in_=ckvT[:, 1, 7168:8192]), reads=["ckvT"], writes=["dtt"])
        A("sp", lambda e: e.dma_start(out=dbg[1664:1792, :], in_=dtt[:]), reads=["dtt"], writes=["dbgo"], dma=True)
        A("sp", lambda e: e.nop(), reads=["dbgo"])
        S.emit(nc, st)
        st.close()
        return nc
    S.barrier(lambda e: e.memset(sst[:, 7:8], 0.0))
    al.regions = [list(r) for r in reg_p4]
    wkp = R([128, 2, 8, 96], BF16)
    wv = R([128, 2, 512], BF16)
    selb = R([32, 96], BF16)
    KT0 = R([96, 8192], BF16)
    KT = [KT0, KT0]
    VA = [R([128, 64, 65], BF16) for _ in range(2)]
    yattn = R([128, 16, 512], BF16)
    rden = R([128, 4], F32)
    A("pool", lambda e: e.memset(wkp[:], 0.0), writes=["wkp"])
    A("dve", lambda e: e.tensor_copy(out=selb[:], in_=sel), reads=["cst"], writes=["selb"])
    for b_ in range(2):
        A("pool", lambda e, b_=b_: e.memset(VA[b_][:, :, 64:65], 1.0), writes=[("VA1", b_)])
    for c in range(2):
        sl = wstate["i"] % 2
        wstate["i"] += 1
        A("sp", lambda e, c=c, sl=sl: e.dma_start(out=wst[sl][:], in_=w_ukv[c * 128:(c + 1) * 128, :]), writes=[("wst", sl)], dma=True)
        W3 = wst[sl][:].rearrange("p (h d) -> p h d", h=8)
        A("pool", lambda e, c=c: e.tensor_copy(out=wkp[:, c, :, 0:64], in_=W3[:, :, 0:64]), reads=[("wst", sl), "wkp"], writes=["wkp"])
        A("pool", lambda e, c=c: e.tensor_copy(out=wv[:, c, :].rearrange("p (h d) -> p h d", h=8), in_=W3[:, :, 64:128]), reads=[("wst", sl)], writes=["wv"])

    OB_, KB_, VB_ = (6, 7), (0, 1), 2
    SCALE = 96.0 ** -0.5
    Pb2 = [R([128, 1024], BF16) for _ in range(3)]
    kbs = {"i": 0}
    for h in range(8):
        bf = h % 2
        for grp in range(16):
            kb = KB_[kbs["i"] % 2]
            kbs["i"] += 1
            gs = slice(grp * 512, (grp + 1) * 512)
            A("pe", lambda e: e.matmul(ps[kb][0:96, :], lhsT=wkp[:, 0, h, :], rhs=ckvT[:, 0, gs], start=True, stop=False), reads=["wkp", "ckvT"], writes=[("ps", kb)])
            A("pe", lambda e: e.matmul(ps[kb][0:96, :], lhsT=wkp[:, 1, h, :], rhs=ckvT[:, 1, gs], start=False, stop=False), reads=["wkp", "ckvT"], writes=[("ps", kb)])
            A("pe", lambda e: e.matmul(ps[kb][0:96, :], lhsT=selb[:], rhs=kropeT[:, gs], start=False, stop=True), reads=["selb", "kropeT"], writes=[("ps", kb)])
            A("dve", lambda e: e.tensor_copy(out=KT[bf][:, gs], in_=ps[kb][0:96, :]), reads=[("ps", kb)], writes=[("KT", 0)])
        for tg in range(8):
            vb = 2 + (tg % 2)
            for tl in range(8):
                kt = tg * 8 + tl
                for c in range(2):
                    A("pe", lambda e, c=c, tl=tl, kt=kt: e.matmul(ps[vb][:, tl * 64:(tl + 1) * 64], lhsT=ckvT[:, c, kt * 128:(kt + 1) * 128],
                                                                 rhs=wv[:, c, h * 64:(h + 1) * 64], start=(c == 0), stop=(c == 1)),
                      reads=["ckvT", "wv"], writes=[("ps", vb)])
            A("act", lambda e: e.copy(out=VA[bf][:, tg * 8:(tg + 1) * 8, 0:64], in_=ps[vb][:].rearrange("p (t d) -> p t d", t=8)), reads=[("ps", vb)], writes=[("VA", bf)])
        for tt in range(4):
            ob = OB_[(h * 4 + tt) % 2]
            qs = slice(tt * 512, (tt + 1) * 512)

            def s_mm(sp_):
                j = sp_ % 3
                for half in range(2):
                    st_ = 2 * sp_ + half
                    A("pe", lambda e: e.matmul(ps[2 * j + half][:], lhsT=KT[bf][:, st_ * 128:(st_ + 1) * 128], rhs=QT[:, h, qs], start=True, stop=True),
                      reads=[("KT", 0), "QT"], writes=[("ps", 2 * j + half)])
                A("act", lambda e: e.activation(out=Pb2[j][:], in_=psbig[j][:], func=AF.Exp, scale=SCALE),
                  reads=[("ps", 2 * j), ("ps", 2 * j + 1)], writes=[("Pb", j)])

            s_mm(0)
            s_mm(1)
            for sp_ in range(32):
                j = sp_ % 3
                for half in range(2):
                    st_ = 2 * sp_ + half
                    for qq in range(4):
                        A("pe", lambda e, qq=qq: e.matmul(ps[ob][:, qq * 65:(qq + 1) * 65], lhsT=Pb2[j][:, half * 512 + qq * 128:half * 512 + (qq + 1) * 128],
                                                          rhs=VA[bf][:, st_, :], start=(st_ == 0 and qq == 0), stop=(st_ == 63), skip_group_check=True),
                          reads=[("Pb", j), ("VA", bf), ("VA1", bf)], writes=[("ps", ob)])
                if sp_ + 2 < 32:
                    s_mm(sp_ + 2)
            O3 = ps[ob][:, 0:260].rearrange("p (q d) -> p q d", q=4)
            A("dve", lambda e: e.reciprocal(out=rden[:], in_=O3[:, :, 64]), reads=[("ps", ob)], writes=["rden"])
            for qq in range(4):
                A("dve", lambda e, qq=qq: e.tensor_scalar(out=yattn[:, tt * 4 + qq, h * 64:(h + 1) * 64], in0=O3[:, qq, 0:64], scalar1=rden[:, qq:qq + 1],
                                                          scalar2=None, op0=ALU.mult), reads=[("ps", ob), "rden"], writes=["yattn"])
    for tile in range(16):
        pb = psb(VB_)
        for c in range(4):
            A("pe", lambda e, c=c: e.transpose(out=pb[:, c * 128:(c + 1) * 128], in_=yattn[:, tile, c * 128:(c + 1) * 128], identity=idb[:]),
              reads=["yattn", "idb"], writes=[("ps", VB_)])
        A("act", lambda e: e.copy(out=yaT[:, :, tile * 128:(tile + 1) * 128], in_=pb[:, 0:512].rearrange("p (c t) -> p c t", c=4)), reads=[("ps", VB_)], writes=["yaT"])

    S.barrier(lambda e: e.memset(sst[:, 7:8], 0.0))
    al.regions = [[m_phase, m_alias], [reg_p4[1][0], SB_HI]]
    wgab = R([128, 8, 2048], BF16)
    wbm = R([128, 4, 1024], BF16)
    wbl = R([128, 4, 1024], BF16)
    wo = R([128, 8, 1024], BF16)
    hT6 = [R([128, 8, 128], BF16) for _ in range(2)]
    sg = R([128, 2048], BF16)
    t1 = R([128, 1024], F32)
    t2 = R([128, 1024], F32)
    mrg = R([128, 1024], BF16)
    mT = R([128, 8, 128], BF16)
    x1t = [R([128, 1024], F32) for _ in range(2)]
    load_w(wgab, "wgab", w_in, 8, 2736, 4784)
    load_w(wbm, "wbm", w_bm, 4, 0, 1024)
    load_w(wbl, "wbl", w_bl, 4, 0, 1024)
    load_w(wo, "wo", w_out, 8, 0, 1024)
    for ot in range(16):
        z = ot % 2
        tsl = slice(ot * 128, (ot + 1) * 128)
        sl = xpipe(xown[tsl, :], gmix, "gmix", hT6[z][:], ("hT6", z), 7)
        for cb_ in range(4):
            for k in range(8):
                A("pe", lambda e, k=k, cb_=cb_: e.matmul(ps[cb_][:], lhsT=hT6[z][:, k, :], rhs=wgab[:, k, cb_ * 512:(cb_ + 1) * 512], start=(k == 0), stop=(k == 7)),
                  reads=[("hT6", z), ("wgab", k)], writes=[("ps", cb_)])
            A("act", lambda e, cb_=cb_: e.activation(out=sg[:, cb_ * 512:(cb_ + 1) * 512], in_=ps[cb_][:], func=AF.Sigmoid), reads=[("ps", cb_)], writes=[("sg", cb_)])
        for half in range(2):
            for (wt, wk, aT_, ak, bank0) in ((wbm, "wbm", yaT, "yaT", 4), (wbl, "wbl", ymT, "ymT", 5)):
                bank = bank0
                for c in range(4):
                    A("pe", lambda e, c=c, wt=wt, aT_=aT_, bank=bank: e.matmul(ps[bank][:], lhsT=aT_[:, c, tsl], rhs=wt[:, c, half * 512:(half + 1) * 512],
                                                                               start=(c == 0), stop=(c == 3)), reads=[ak, (wk, c)], writes=[("ps", bank)])
            hs = slice(half * 512, (half + 1) * 512)
            A("dve", lambda e: e.tensor_tensor(out=t1[:, hs], in0=ps[4][:], in1=sg[:, half * 512:(half + 1) * 512], op=ALU.mult), reads=[("ps", 4), ("sg", half)], writes=[("t1", half)])
            A("dve", lambda e: e.tensor_tensor(out=t2[:, hs], in0=ps[5][:], in1=sg[:, 1024 + half * 512:1024 + (half + 1) * 512], op=ALU.mult),
              reads=[("ps", 5), ("sg", 2 + half)], writes=[("t2", half)])
            A("pool", lambda e: e.tensor_tensor(out=mrg[:, hs], in0=t1[:, hs], in1=t2[:, hs], op=ALU.add), reads=[("t1", half), ("t2", half)], writes=[("mrg", half)])
        pb = psb(6)
        for k in range(8):
            A("pe", lambda e, k=k: e.transpose(out=pb[:, k * 128:(k + 1) * 128], in_=mrg[:, k * 128:(k + 1) * 128], identity=idb[:]),
              reads=[("mrg", k // 4), "idb"], writes=[("ps", 6)])
        A("act", lambda e: e.copy(out=mT[:], in_=pb.rearrange("p (k t) -> p k t", k=8)), reads=[("ps", 6)], writes=["mT"])
        for half in range(2):
            bank = 4 + half
            for k in range(8):
                A("pe", lambda e, k=k, bank=bank, half=half: e.matmul(ps[bank][:], lhsT=mT[:, k, :], rhs=wo[:, k, half * 512:(half + 1) * 512], start=(k == 0), stop=(k == 7)),
                  reads=["mT", ("wo", k)], writes=[("ps", bank)])
            A("dve", lambda e, bank=bank, half=half: e.tensor_tensor(out=x1t[z][:, half * 512:(half + 1) * 512], in0=ps[bank][:], in1=xt[sl][:, half * 512:(half + 1) * 512], op=ALU.add),
              reads=[("ps", bank), ("xt", sl)], writes=[("x1t", z)])
        A("sp", lambda e: e.dma_start(out=x1d[tsl, :], in_=x1t[z][:]), reads=[("x1t", z)], writes=["x1d"], dma=True)

    S.barrier(lambda e: e.memset(sst[:, 7:8], 0.0))
    al.regions = [[m_phase, SB_HI]]
    wup = R([128, 8, 4096], BF16)
    wdn = R([128, 32, 1024], BF16)
    gmlp = R([128, 1024], F32)
    gfin = R([128, 1024], F32)
    hTm = [R([128, 8, 256], BF16) for _ in range(2)]
    aT = R([128, 32, 256], BF16)
    rr = [R([128, 256], F32) for _ in range(2)]
    xres = [R([128, 1024], F32) for _ in range(2)]
    otile = xres
    A("sp", lambda e: e.dma_start(out=gmlp[:], in_=gmlpd), writes=["gmlp"], dma=True)
    A("sp", lambda e: e.dma_start(out=gfin[:], in_=gfind), writes=["gfin"], dma=True)
    load_w(wup, "wup", w_up, 8, 0, 4096)
    load_w(wdn, "wdn", w_down, 32, 0, 1024)
    for g in range(8):
        z = g % 2
        for i in range(2):
            r0 = g * 256 + i * 128
            xpipe(x1d[r0:r0 + 128, :], gmlp, "gmlp", hTm[z][:, :, i * 128:(i + 1) * 128], ("hTm", z), 7)
        for f in range(32):
            bank = f % 2
            for k in range(8):
                A("pe", lambda e, k=k, f=f, bank=bank: e.matmul(ps[bank][:, 0:256], lhsT=wup[:, k, f * 128:(f + 1) * 128], rhs=hTm[z][:, k, :], start=(k == 0), stop=(k == 7)),
                  reads=[("hTm", z), ("wup", k)], writes=[("ps", bank)])
            A("act", lambda e, bank=bank: e.activation(out=rr[bank][:], in_=ps[bank][:, 0:256], func=AF.Relu), reads=[("ps", bank)], writes=[("rr", bank)])
            A("dve", lambda e, f=f, bank=bank: e.tensor_tensor(out=aT[:, f, :], in0=rr[bank][:], in1=rr[bank][:], op=ALU.mult), reads=[("rr", bank)], writes=["aT"])
        for i in range(2):
            r0 = g * 256 + i * 128
            zz = (g * 2 + i) % 2
            A("sp", lambda e, r0=r0, zz=zz: e.dma_start(out=xres[zz][:], in_=x1d[r0:r0 + 128, :]), reads=["x1d"], writes=[("xres", zz)], dma=True)
            for half in range(2):
                bank = 2 + half
                for f in range(32):
                    A("pe", lambda e, f=f, bank=bank, half=half, i=i: e.matmul(ps[bank][:], lhsT=aT[:, f, i * 128:(i + 1) * 128], rhs=wdn[:, f, half * 512:(half + 1) * 512],
                                                                               start=(f == 0), stop=(f == 31)), reads=["aT", ("wdn", f)], writes=[("ps", bank)])
                A("dve", lambda e, bank=bank, half=half, zz=zz: e.tensor_tensor(out=xres[zz][:, half * 512:(half + 1) * 512], in0=ps[bank][:], in1=xres[zz][:, half * 512:(half + 1) * 512], op=ALU.add),
                  reads=[("ps", bank), ("xres", zz)], writes=[("xres", zz)])
            ssap = sst[:, 4 + zz:5 + zz]
            A("act", lambda e, zz=zz, ssap=ssap: e.activation(out=junk[:], in_=xres[zz][:], func=AF.Square, accum_out=ssap), reads=[("xres", zz)], writes=[("ssf", zz)])
            rstd_of(ssap, 1024, ("ssf", zz))
            A("dve", lambda e, zz=zz, ssap=ssap: e.scalar_tensor_tensor(out=otile[zz][:], in0=xres[zz][:], scalar=ssap, in1=gfin[:], op0=ALU.mult, op1=ALU.mult),
              reads=[("xres", zz), ("ssf", zz), "gfin"], writes=[("xres", zz)])
            A("sp", lambda e, r0=r0, zz=zz: e.dma_start(out=y[r0:r0 + 128, :], in_=otile[zz][:]), reads=[("xres", zz)], writes=[("yout", g * 2 + i)], dma=True)
    A("sp", lambda e: e.nop(), reads=[("yout", i_) for i_ in range(16)])
    S.emit(nc, st)
    st.close()
    return nc


def make_consts():
    c = np.zeros((128, 880), np.float32)
    r = np.arange(128)
    c[:, 0:128] = np.eye(128)
    c[:, 128:256] = (r[:, None] <= r[None, :])
    c[:, 256:384] = (r[:, None] >= r[None, :])
    c[:, 384:512] = (r[:, None] > r[None, :])
    c[:, 512:640] = (r[:, None] < r[None, :])
    c[:, 640:768] = 1.0
    for i in range(32):
        c[i, 784 + 64 + i] = 1.0
    return c


def bc(v, n=128):
    return np.ascontiguousarray(np.broadcast_to(np.asarray(v, np.float32).reshape(1, -1), (n, np.asarray(v).size)))


def prep_inputs(inp, core):
    b, j = divmod(core, 4)
    x = inp["x"][b]
    pos = inp["positions"][b]
    o0, o1 = NOWN * j, NOWN * (j + 1)
    xo = np.concatenate([x[:o0], x[o1:]], axis=0)
    xown = x[o0:o1]
    xh = np.zeros((128, 1024), np.float32)

    def row(n):
        return x[n] if 0 <= n < 8192 else np.zeros(1024, np.float32)

    for g in range(12):
        n0 = 512 * g if 512 * g < o0 else 512 * g + NOWN
        for q, n in enumerate((n0 - 2, n0 - 1, n0 + 512, n0 + 513)):
            xh[4 * g + q] = row(n)
    for g in range(4):
        n0 = o0 + 512 * g
        for q, n in enumerate((n0 - 2, n0 - 1, n0 + 512, n0 + 513)):
            xh[48 + 4 * g + q] = row(n)
    pos_all = np.concatenate([pos[:o0], pos[o1:], pos[o0:o1]])
    posT = np.concatenate([pos_all.reshape(64, 128).T, pos[o0:o1].reshape(16, 128).T], axis=1).astype(np.int32)
    tfv = (np.arange(48) < 16 * j).astype(np.float32)
    igb = inp["mlstm_igate_b"][0]
    fgb = inp["mlstm_fgate_b"][0]
    gb16 = np.concatenate([igb[0], igb[1], fgb[0], fgb[1]])
    cwv = inp["mlstm_conv_w"][0][:, 0, :]
    cw = np.ascontiguousarray(cwv.reshape(5, 8, 128).transpose(2, 1, 0)).reshape(128, 40)
    cb = np.ascontiguousarray(inp["mlstm_conv_b"][0].reshape(8, 128).T)
    d = {
        "xo": np.ascontiguousarray(xo), "xown": np.ascontiguousarray(xown), "xh": xh,
        "posT": np.ascontiguousarray(posT), "tf": bc(tfv), "cst": make_consts(),
        "gmix": bc(inp["norm_mix_g"][0]), "gmlp": bc(inp["norm_mlp_g"][0]), "gfin": bc(inp["norm_final_g"]),
        "gq": bc(inp["mla_q_norm_g"][0]), "gkv": bc(inp["mla_kv_norm_g"][0]), "gon": bc(inp["mlstm_out_norm_g"][0]),
        "gb": bc(np.tile(gb16, 4)), "cw": cw.astype(np.float32), "cb": cb.astype(np.float32),
        "w_in": inp["w_in"][0], "w_uq": inp["mla_w_uq"][0], "w_ukv": inp["mla_w_ukv"][0],
        "w_bm": inp["w_branch_mla"][0], "w_bl": inp["w_branch_mlstm"][0], "w_out": inp["w_out"][0],
        "w_up": inp["w_mlp_up"][0], "w_down": inp["w_mlp_down"][0],
    }
    return {k: np.ascontiguousarray(v) for k, v in d.items()}


def run(inputs, debug=None, cores=8):
    inp = {k: np.asarray(v) for k, v in inputs.items()}
    nc = build_program(debug)
    in_maps = [prep_inputs(inp, c) for c in range(cores)]
    res = run_bass_kernel_spmd(nc, in_maps, core_ids=list(range(cores)))
    return res


def kernel(**inputs):
    res = run(inputs)
    out = np.zeros((2, 8192, 1024), np.float32)
    for c in range(8):
        b, j = divmod(c, 4)
        out[b, NOWN * j:NOWN * (j + 1)] = res.results[c]["y"]
    return out
```

```python
import math
from contextlib import ExitStack
import numpy as np
import concourse.bass as bass
import concourse.mybir as mybir
from concourse.bass_utils import run_bass_kernel_spmd

F32 = mybir.dt.float32
BF16 = mybir.dt.bfloat16
I32 = mybir.dt.int32
AF = mybir.ActivationFunctionType
ALU = mybir.AluOpType

SEM_LIMIT = 20000
N_DSEM = 24
SB_LO = 16512
SB_HI = 229376
NOWN = 2048
NOTH = 6144
EPS = 1e-6
LNSC = -0.5 * math.log(128.0)
BIG = 30000.0


class Op:
    __slots__ = ("eng", "fn", "deps", "signal", "is_dma", "idx", "sem", "val", "dslot")

    def __init__(self, eng, fn, is_dma):
        self.eng = eng
        self.fn = fn
        self.deps = []
        self.signal = False
        self.is_dma = is_dma
        self.sem = None
        self.val = None
        self.dslot = None


class _Rec:
    def __init__(self):
        self.call = None

    def __getattr__(self, name):
        def f(*a, **k):
            self.call = (name, a, k)
            return self
        return f


class Sched:
    def __init__(self):
        self.ops = []
        self.last_w = {}
        self.readers = {}
        self.n_dma = 0
        self.dslot_last = {}
        self.fence_op = None

    def add(self, eng, fn, reads=(), writes=(), dma=False):
        rec = _Rec()
        fn(rec)
        call = rec.call
        op = Op(eng, call, dma)
        psk = [k for k in reads if isinstance(k, tuple) and k[0] == "ps"]
        if psk:
            reads = [k for k in reads if k not in psk]
            writes = list(writes) + psk
        deps = set()
        for k in reads:
            w = self.last_w.get(k)
            if w is not None:
                deps.add(w)
        for k in writes:
            w = self.last_w.get(k)
            if w is not None:
                deps.add(w)
            for r in self.readers.get(k, ()):
                deps.add(r)
        if self.fence_op is not None:
            deps.add(self.fence_op)
        if dma:
            slot = (eng, self.n_dma % N_DSEM)
            self.n_dma += 1
            prev = self.dslot_last.get(slot)
            if prev is not None:
                deps.add(prev)
            self.dslot_last[slot] = op
            op.dslot = slot
        for d in deps:
            if d is op:
                continue
            if (not d.is_dma) and d.eng == "pe" and eng == "pe" and not dma:
                continue
            d.signal = True
            op.deps.append(d)
        for k in reads:
            self.readers.setdefault(k, []).append(op)
        for k in writes:
            self.last_w[k] = op
            self.readers[k] = []
        self.ops.append(op)
        return op

    def barrier(self, fn):
        keys = set(self.last_w.keys()) | set(self.readers.keys())
        self.fence_op = None
        op = self.add("dve", fn, writes=list(keys))
        for o in self.dslot_last.values():
            if o is not op and o not in op.deps:
                o.signal = True
                op.deps.append(o)
        self.fence_op = op
        return op

    def emit(self, nc, stack):
        engs = ["pe", "act", "dve", "pool", "sp"]
        counts = {e: 0 for e in engs}
        dcount = {}
        for op in self.ops:
            if op.is_dma:
                c = dcount.get(op.dslot, 0) + 1
                dcount[op.dslot] = c
                op.sem = ("d", op.dslot)
                op.val = 16 * c
            elif op.signal:
                c = counts[op.eng]
                counts[op.eng] = c + 1
                op.sem = (op.eng, c // SEM_LIMIT)
                op.val = c % SEM_LIMIT + 1
        sems = {}
        for op in self.ops:
            if op.sem is not None and op.sem not in sems:
                sems[op.sem] = stack.enter_context(nc.semaphore("s_%d" % len(sems)))
        block = stack.enter_context(nc.Block())
        ops = self.ops

        def run(engname, e):
            waited = {}
            for op in ops:
                if op.eng != engname:
                    continue
                for d in op.deps:
                    key = d.sem
                    if waited.get(key, 0) >= d.val:
                        continue
                    e.wait_ge(sems[key], d.val)
                    waited[key] = d.val
                name, a_, k_ = op.fn
                ins = getattr(e, name)(*a_, **k_)
                if op.is_dma:
                    ins.then_inc(sems[op.sem], 16)
                elif op.signal:
                    ins.then_inc(sems[op.sem], 1)

        @block.tensor
        def _(e):
            run("pe", e)

        @block.scalar
        def _(e):
            run("act", e)

        @block.vector
        def _(e):
            run("dve", e)

        @block.gpsimd
        def _(e):
            run("pool", e)

        @block.sync
        def _(e):
            run("sp", e)


class Alloc:
    def __init__(self, nc):
        self.nc = nc
        self.base = SB_LO
        self.top = SB_LO
        self.n = 0

    def mark(self):
        return self.top

    def reset(self, m):
        self.top = m

    def set_regions(self, regions):
        self.regions = [list(r) for r in regions]

    def ralloc(self, shape, dt):
        nb = 1
        for s_ in shape[1:]:
            nb *= s_
        nb *= 2 if dt == BF16 else 4
        nb = (nb + 63) // 64 * 64
        for r in self.regions:
            if r[0] + nb <= r[1]:
                off = r[0]
                r[0] += nb
                self.n += 1
                return self.nc.alloc_sbuf_tensor_at("t%d" % self.n, list(shape), dt, offset=off)
        raise AssertionError(("SBUF overflow", shape, self.regions))

    def __call__(self, shape, dt):
        nb = 1
        for s in shape[1:]:
            nb *= s
        nb *= 2 if dt == BF16 else 4
        nb = (nb + 63) // 64 * 64
        off = self.top
        self.top += nb
        assert self.top <= SB_HI, ("SBUF overflow", self.top)
        self.n += 1
        return self.nc.alloc_sbuf_tensor_at("t%d" % self.n, list(shape), dt, offset=off)


def build_program(debug=None):
    nc = bass.Bass("TRN2", target_bir_lowering=False)

    def din(name, shape, dt=F32):
        return nc.dram_tensor(name, list(shape), dt, kind="ExternalInput").ap()

    xo = din("xo", [NOTH, 1024])
    xown = din("xown", [NOWN, 1024])
    xh = din("xh", [128, 1024])
    posT = din("posT", [128, 80], I32)
    tfd = din("tf", [128, 48])
    cstd = din("cst", [128, 880])
    gmixd = din("gmix", [128, 1024])
    gmlpd = din("gmlp", [128, 1024])
    gfind = din("gfin", [128, 1024])
    gqd = din("gq", [128, 384])
    gkvd = din("gkv", [128, 256])
    gond = din("gon", [128, 512])
    gbd = din("gb", [128, 64])
    cwd = din("cw", [128, 40])
    cbd = din("cb", [128, 8])
    w_in = din("w_in", [1024, 4784])
    w_uq = din("w_uq", [384, 768])
    w_ukv = din("w_ukv", [256, 1024])
    w_bm = din("w_bm", [512, 1024])
    w_bl = din("w_bl", [512, 1024])
    w_out = din("w_out", [1024, 1024])
    w_up = din("w_up", [1024, 4096])
    w_down = din("w_down", [4096, 1024])
    y = nc.dram_tensor("y", [NOWN, 1024], F32, kind="ExternalOutput").ap()
    x1d = nc.dram_tensor("x1d", [NOWN, 1024], F32).ap()
    dbg = None
    if debug:
        dbg = nc.dram_tensor("dbg", [NOWN, 1024], F32, kind="ExternalOutput").ap()

    S = Sched()
    A = S.add
    al = Alloc(nc)
    st = ExitStack()
    psbig = [st.enter_context(nc.psum_tensor("psb%d" % i, [128, 1024], F32)) for i in range(4)]
    ps = [psbig[i // 2][:, (i % 2) * 512:(i % 2 + 1) * 512] for i in range(8)]

    def psb(i):
        return ps[i][:].bitcast(BF16)

    cst = al([128, 880], F32)
    idb = al([128, 128], BF16)
    mLEb = al([128, 128], BF16)
    mGEb = al([128, 128], BF16)
    gmix = al([128, 1024], F32)
    xt = [al([128, 1024], F32) for _ in range(2)]
    junk = al([128, 1024], BF16)
    hb = [al([128, 1024], BF16) for _ in range(2)]
    sst = al([128, 8], F32)
    wst = [al([128, 1024], F32) for _ in range(3)]
    LNSCt = al([128, 1], F32)
    ONEt = al([128, 1], F32)
    EPSt = al([128, 1], F32)
    idf = cst[:, 0:128]
    mLE = cst[:, 128:256]
    mGE = cst[:, 256:384]
    mSU = cst[:, 384:512]
    mSL = cst[:, 512:640]
    ones = cst[:, 640:768]
    sel = cst[0:32, 784:880]

    A("sp", lambda e: e.dma_start(out=cst[:], in_=cstd), writes=["cst"], dma=True)
    A("sp", lambda e: e.dma_start(out=gmix[:], in_=gmixd), writes=["gmix"], dma=True)
    A("dve", lambda e: e.memset(ONEt[:], 1.0), writes=["onet"])
    A("dve", lambda e: e.memset(EPSt[:], EPS), writes=["epst"])
    A("dve", lambda e: e.tensor_copy(out=idb[:], in_=idf), reads=["cst"], writes=["idb"])
    A("dve", lambda e: e.tensor_copy(out=mLEb[:], in_=mLE), reads=["cst"], writes=["mLEb"])
    A("dve", lambda e: e.tensor_copy(out=mGEb[:], in_=mGE), reads=["cst"], writes=["mGEb"])

    wstate = {"i": 0}

    def load_w(dst, dkey, src, K, c_lo, c_hi, dcol=0, queue="sp"):
        for k in range(K):
            c0 = c_lo
            while c0 < c_hi:
                cc = min(1024, c_hi - c0)
                sl = wstate["i"] % 3
                wstate["i"] += 1
                A(queue, lambda e, sl=sl, k=k, c0=c0, cc=cc: e.dma_start(out=wst[sl][:, 0:cc], in_=src[k * 128:(k + 1) * 128, c0:c0 + cc]),
                  writes=[("wst", sl)], dma=True)
                d0 = dcol + (c0 - c_lo)
                ce = ("pool", "act", "dve")[wstate["i"] % 3] if cc >= 256 else "pool"
                if ce == "act":
                    A("act", lambda e, sl=sl, k=k, d0=d0, cc=cc: e.copy(out=dst[:, k, d0:d0 + cc], in_=wst[sl][:, 0:cc]),
                      reads=[("wst", sl)], writes=[(dkey, k)])
                else:
                    A(ce, lambda e, sl=sl, k=k, d0=d0, cc=cc: e.tensor_copy(out=dst[:, k, d0:d0 + cc], in_=wst[sl][:, 0:cc]),
                      reads=[("wst", sl)], writes=[(dkey, k)])
                c0 += cc

    xstate = {"i": 0}

    def rstd_of(sskey_ap, n, key):
        A("act", lambda e: e.activation(out=sskey_ap, in_=sskey_ap, func=AF.Ln, scale=1.0 / n, bias=EPSt[:, 0:1]), reads=[key, "epst"], writes=[key])
        A("act", lambda e: e.activation(out=sskey_ap, in_=sskey_ap, func=AF.Exp, scale=-0.5), reads=[key], writes=[key])

    def xpipe(src_rows, g_tile, gkey, hT_dst, hkey, tps, keep=False):
        i = xstate["i"]
        xstate["i"] += 1
        sl = i % 2
        A("sp", lambda e: e.dma_start(out=xt[sl][:], in_=src_rows), writes=[("xt", sl)], dma=True)
        ssap = sst[:, sl:sl + 1]
        A("act", lambda e: e.activation(out=junk[:], in_=xt[sl][:], func=AF.Square, accum_out=ssap), reads=[("xt", sl)], writes=[("ss", sl)])
        rstd_of(ssap, 1024, ("ss", sl))
        A("dve", lambda e: e.scalar_tensor_tensor(out=hb[sl][:], in0=xt[sl][:], scalar=ssap, in1=g_tile[:], op0=ALU.mult, op1=ALU.mult),
          reads=[("xt", sl), ("ss", sl), gkey], writes=[("hb", sl)])
        pb = psb(tps)
        for k in range(8):
            A("pe", lambda e, k=k: e.transpose(out=pb[:, k * 128:(k + 1) * 128], in_=hb[sl][:, k * 128:(k + 1) * 128], identity=idb[:]),
              reads=[("hb", sl), "idb"], writes=[("ps", tps)])
        A("act", lambda e: e.copy(out=hT_dst, in_=pb.rearrange("p (k t) -> p k t", k=8)), reads=[("ps", tps)], writes=[hkey])
        return sl

    m_phase = al.mark()
    LGe = [al([128, 8], F32) for _ in range(2)]
    Ie = [al([128, 8], F32) for _ in range(2)]
    exg = [al([128, 8], F32) for _ in range(2)]
    Wt = [al([128, 8], F32) for _ in range(2)]
    dect = [al([128, 4], F32) for _ in range(2)]
    Arun = al([128, 4], F32)
    ktil = [al([128, 128], BF16) for _ in range(4)]
    Cf = al([128, 4, 129], F32)
    Cb = al([128, 4, 129], F32)
    Cfb = al([128, 4, 129], BF16)
    Cbb = al([128, 4, 129], BF16)
    qT = al([128, 4, NOWN], BF16)
    kT = al([128, 4, NOWN], BF16)
    ktok = al([128, 16, 512], BF16)
    vaug = al([128, 16, 4, 129], BF16)
    sgo = al([128, 16, 512], BF16)
    Gown = al([128, 16, 16], F32)
    LFo = al([128, 16, 8], F32)
    m_alias = al.mark()
    wl = al([128, 8, 2064], BF16)
    tf = al([128, 48], F32)
    tb = al([128, 48], F32)
    pf = al([128, 48], F32)
    pbk = al([128, 48], F32)
    gbias = al([128, 64], F32)
    cw = al([128, 40], F32)
    cb = al([128, 8], F32)
    gon = al([128, 512], F32)
    haloT = al([128, 8, 128], F32)
    hTg = [al([128, 8, 512], BF16) for _ in range(2)]
    padb = [al([128, 516], F32) for _ in range(2)]
    acc = [al([128, 512], F32) for _ in range(2)]
    kTg1 = al([128, 4, 512], BF16)
    ktokg1 = al([128, 4, 512], BF16)
    vaugg1 = al([128, 4, 4, 129], BF16)
    kTg = [kTg1, kTg1]
    ktokg = [ktokg1, ktokg1]
    vaugg = [vaugg1, vaugg1]
    Gg = [al([128, 4, 16], F32) for _ in range(2)]
    LFg = [al([128, 4, 8], F32) for _ in range(2)]
    sgt = [al([128, 512], BF16) for _ in range(2)]
    m_p12 = al.mark()
    al.reset(m_alias)
    ymT = al([128, 4, NOWN], BF16)
    m_keep = al.mark()
    EBt = al([128, 16, 8], F32)
    ECt = al([128, 16, 8], F32)
    WSt = al([128, 16, 8], F32)
    DECt = al([128, 16, 8], F32)
    cmt = al([128, 8], F32)
    HB = al([128, 16, 512], F32)
    PT = [al([128, 4, 128], BF16) for _ in range(2)]
    den = al([128, 4], F32)
    scl = al([128, 4], F32)
    hsum = [al([128, 4, 128], F32) for _ in range(2)]
    ssq = al([128, 4], F32)
    ymt = [al([128, 4, 128], BF16) for _ in range(2)]
    assert al.mark() <= m_p12
    al.reset(m_p12)

    for (t_, d_, k_) in ((tf, tfd, "tf"), (gbias, gbd, "gbias"), (cw, cwd, "cw"), (cb, cbd, "cb"), (gon, gond, "gon")):
        A("sp", lambda e, t_=t_, d_=d_: e.dma_start(out=t_[:], in_=d_), writes=[k_], dma=True)
    A("dve", lambda e: e.tensor_scalar(out=tb[:], in0=tf[:], scalar1=-1.0, scalar2=1.0, op0=ALU.mult, op1=ALU.add), reads=["tf"], writes=["tb"])
    A("dve", lambda e: e.tensor_scalar(out=pf[:], in0=tf[:], scalar1=-1.0, scalar2=BIG, op0=ALU.add, op1=ALU.mult), reads=["tf"], writes=["pf"])
    A("dve", lambda e: e.tensor_scalar(out=pbk[:], in0=tf[:], scalar1=-BIG, scalar2=None, op0=ALU.mult), reads=["tf"], writes=["pbk"])
    for t_, k_ in ((Cf, "Cf"), (Cb, "Cb"), (Arun, "Arun")):
        A("dve", lambda e, t_=t_: e.memset(t_[:], 0.0), writes=[k_])
    A("pool", lambda e: e.memset(vaug[:, :, :, 128:129], 1.0), writes=["vaug1"])
    A("pool", lambda e: e.memset(vaugg1[:, :, :, 128:129], 1.0), writes=[("vaugg1", 0)])

    load_w(wl, "wl", w_in, 8, 672, 2720, 0)
    for (s0, d0) in ((2720, 2048), (2728, 2052), (2724, 2056), (2732, 2060)):
        load_w(wl, "wl", w_in, 8, s0, s0 + 4, d0)
    WLK = [("wl", k) for k in range(8)]

    TPSX, TPSK, CPS0, CPS1, MVPS, UPS0 = 0, 1, 2, 3, 4, 5

    hTh = hTg[1][:, :, 0:128]
    xpipe(xh, gmix, "gmix", hTh, ("hTg", 1), TPSX)
    for c in range(8):
        bank = CPS0 + (c % 2)
        for k in range(8):
            A("pe", lambda e, c=c, k=k, bank=bank: e.matmul(ps[bank][:, 0:128], lhsT=wl[:, k, c * 128:(c + 1) * 128], rhs=hTg[1][:, k, 0:128],
                                                           start=(k == 0), stop=(k == 7)),
              reads=[("hTg", 1), ("wl", k)], writes=[("ps", bank)])
        A("act", lambda e, c=c, bank=bank: e.copy(out=haloT[:, c, :], in_=ps[bank][:, 0:128]), reads=[("ps", bank)], writes=["haloT"])

    cstate = {"i": 0}

    def conv_chunk(par, wcol, chunk, hidx, dst_ap, dst_key):
        ci = cstate["i"]
        cstate["i"] += 1
        bank = CPS0 + (ci % 2)
        pz = ci % 2
        for k in range(8):
            A("pe", lambda e, k=k: e.matmul(ps[bank][:], lhsT=wl[:, k, wcol:wcol + 128], rhs=hTg[par][:, k, :], start=(k == 0), stop=(k == 7)),
              reads=[("hTg", par), ("wl", k)], writes=[("ps", bank)])
        A("act", lambda e: e.copy(out=padb[pz][:, 2:514], in_=ps[bank][:]), reads=[("ps", bank)], writes=[("padb", pz)])
        A("pool", lambda e: e.tensor_copy(out=padb[pz][:, 0:2], in_=haloT[:, chunk, hidx:hidx + 2]), reads=["haloT"], writes=[("padbL", pz)])
        A("pool", lambda e: e.tensor_copy(out=padb[pz][:, 514:516], in_=haloT[:, chunk, hidx + 2:hidx + 4]), reads=["haloT"], writes=[("padbR", pz)])
        rk = [("padb", pz), ("padbL", pz), ("padbR", pz), "cw", "cb"]
        A("dve", lambda e: e.tensor_scalar(out=acc[pz][:], in0=padb[pz][:, 0:512], scalar1=cw[:, chunk * 5:chunk * 5 + 1], scalar2=cb[:, chunk:chunk + 1],
                                           op0=ALU.mult, op1=ALU.add), reads=rk, writes=[("acc", pz)])
        for j in range(1, 5):
            A("dve", lambda e, j=j: e.scalar_tensor_tensor(out=acc[pz][:], in0=padb[pz][:, j:j + 512], scalar=cw[:, chunk * 5 + j:chunk * 5 + j + 1],
                                                           in1=acc[pz][:], op0=ALU.mult, op1=ALU.add), reads=rk + [("acc", pz)], writes=[("acc", pz)])
        A("act", lambda e: e.activation(out=dst_ap, in_=acc[pz][:], func=AF.Silu), reads=[("acc", pz)], writes=[dst_key])

    def mlstm_pre(gi):
        own = gi >= 12
        par = gi % 2
        src = xown if own else xo
        g0 = (gi - 12) if own else gi
        for i in range(4):
            r0 = g0 * 512 + i * 128
            xpipe(src[r0:r0 + 128, :], gmix, "gmix", hTg[par][:, :, i * 128:(i + 1) * 128], ("hTg", par), TPSX)

    def mlstm_group(gi, own, mid=None):
        par = gi % 2
        g0 = (gi - 12) if own else gi
        hidx = (48 + 4 * g0) if own else 4 * g0
        for h in range(4):
            if own:
                conv_chunk(par, h * 128, h, hidx, qT[:, h, g0 * 512:(g0 + 1) * 512], "qT")
                conv_chunk(par, 512 + h * 128, 4 + h, hidx, kT[:, h, g0 * 512:(g0 + 1) * 512], "kT")
            else:
                conv_chunk(par, 512 + h * 128, 4 + h, hidx, kTg[par][:, h, :], ("kTg", 0))
        if mid is not None:
            mid()
        for b2 in range(2):
            pb = psb(TPSK)
            for bb in range(2):
                blk = b2 * 2 + bb
                for h in range(4):
                    if own:
                        src_ap = kT[:, h, g0 * 512 + blk * 128:g0 * 512 + (blk + 1) * 128]
                        rkey = "kT"
                    else:
                        src_ap = kTg[par][:, h, blk * 128:(blk + 1) * 128]
                        rkey = ("kTg", 0)
                    o0 = (bb * 4 + h) * 128
                    A("pe", lambda e, src_ap=src_ap, o0=o0: e.transpose(out=pb[:, o0:o0 + 128], in_=src_ap, identity=idb[:]),
                      reads=[rkey, "idb"], writes=[("ps", TPSK)])
            if own:
                dst = ktok[:, g0 * 4 + b2 * 2:g0 * 4 + b2 * 2 + 2, :]
                dk = "ktok"
            else:
                dst = ktokg[par][:, b2 * 2:b2 * 2 + 2, :]
                dk = ("ktokg", 0)
            A("act", lambda e, dst=dst, pb=pb: e.copy(out=dst, in_=pb.rearrange("p (b c) -> p b c", b=2)), reads=[("ps", TPSK)], writes=[dk])
        for i in range(4):
            mvb = (4, 5)[i % 2]
            for k in range(8):
                A("pe", lambda e, i=i, k=k: e.matmul(ps[mvb][:], lhsT=hTg[par][:, k, i * 128:(i + 1) * 128], rhs=wl[:, k, 1024:1536],
                                                     start=(k == 0), stop=(k == 7)), reads=[("hTg", par), ("wl", k)], writes=[("ps", mvb)])
            if own:
                dst = vaug[:, g0 * 4 + i, :, 0:128]
                dk = "vaug"
            else:
                dst = vaugg[par][:, i, :, 0:128]
                dk = ("vaugg", 0)
            A("dve", lambda e, dst=dst: e.tensor_copy(out=dst, in_=ps[mvb][:].rearrange("p (h d) -> p h d", h=4)), reads=[("ps", mvb)], writes=[dk])
            if own:
                mob = (5, 4)[i % 2]
                for k in range(8):
                    A("pe", lambda e, i=i, k=k: e.matmul(ps[mob][:], lhsT=hTg[par][:, k, i * 128:(i + 1) * 128], rhs=wl[:, k, 1536:2048],
                                                         start=(k == 0), stop=(k == 7)), reads=[("hTg", par), ("wl", k)], writes=[("ps", mob)])
                sp_ = i % 2
                A("act", lambda e, sp_=sp_: e.activation(out=sgt[sp_][:], in_=ps[mob][:], func=AF.Sigmoid), reads=[("ps", mob)], writes=[("sgt", sp_)])
                A("pool", lambda e, sp_=sp_, i=i: e.tensor_tensor(out=sgo[:, g0 * 4 + i, :], in0=sgt[sp_][:], in1=gon[:], op=ALU.mult),
                  reads=[("sgt", sp_), "gon"], writes=["sgo"])
        for i in range(4):
            for k in range(8):
                A("pe", lambda e, i=i, k=k: e.matmul(ps[MVPS][:, i * 16:(i + 1) * 16], lhsT=hTg[par][:, k, i * 128:(i + 1) * 128], rhs=wl[:, k, 2048:2064],
                                                     start=(k == 0), stop=(k == 7)), reads=[("hTg", par), ("wl", k)], writes=[("ps", MVPS)])
        if own:
            Gd = Gown[:, g0 * 4:g0 * 4 + 4, :]
            gk = "Gown"
            LFd = LFo[:, g0 * 4:g0 * 4 + 4, :]
            lk = "LFo"
        else:
            Gd = Gg[par][:]
            gk = ("Gg", par)
            LFd = LFg[par][:]
            lk = ("LFg", par)
        A("dve", lambda e: e.tensor_tensor(out=Gd, in0=ps[MVPS][:, 0:64].rearrange("p (b c) -> p b c", b=4),
                                           in1=gbias[:].rearrange("p (b c) -> p b c", b=4), op=ALU.add), reads=[("ps", MVPS), "gbias"], writes=[gk])
        A("act", lambda e: e.activation(out=LFd, in_=Gd[:, :, 8:16], func=AF.Exp, scale=-1.0), reads=[gk], writes=[lk])
        A("act", lambda e: e.activation(out=LFd, in_=LFd, func=AF.Ln, bias=ONEt[:, 0:1]), reads=[lk, "onet"], writes=[lk])
        if own:
            return
        for blk in range(4):
            gb_ = gi * 4 + blk
            z = blk % 2
            A("dve", lambda e: e.tensor_scalar(out=LGe[z][:, 0:4], in0=LFg[par][:, blk, 0:4], scalar1=tf[:, gb_:gb_ + 1], scalar2=-1.0, op0=ALU.mult, op1=ALU.mult),
              reads=[lk, "tf"], writes=[("LGe", z)])
            A("dve", lambda e: e.tensor_scalar(out=LGe[z][:, 4:8], in0=LFg[par][:, blk, 4:8], scalar1=tb[:, gb_:gb_ + 1], scalar2=-1.0, op0=ALU.mult, op1=ALU.mult),
              reads=[lk, "tb"], writes=[("LGe", z)])
            A("dve", lambda e: e.tensor_scalar(out=Ie[z][:, 0:4], in0=Gg[par][:, blk, 0:4], scalar1=pf[:, gb_:gb_ + 1], scalar2=None, op0=ALU.add),
              reads=[gk, "pf"], writes=[("Ie", z)])
            A("dve", lambda e: e.tensor_scalar(out=Ie[z][:, 4:8], in0=Gg[par][:, blk, 4:8], scalar1=pbk[:, gb_:gb_ + 1], scalar2=None, op0=ALU.add),
              reads=[gk, "pbk"], writes=[("Ie", z)])
            gp = UPS0 + 2
            A("pe", lambda e: e.matmul(ps[gp][:, 0:4], lhsT=mSU, rhs=LGe[z][:, 0:4], start=True, stop=True), reads=["cst", ("LGe", z)], writes=[("ps", gp)])
            A("pe", lambda e: e.matmul(ps[gp][:, 4:8], lhsT=mSL, rhs=LGe[z][:, 4:8], start=True, stop=True), reads=["cst", ("LGe", z)], writes=[("ps", gp)])
            A("pe", lambda e: e.matmul(ps[gp][:, 8:16], lhsT=ones, rhs=LGe[z][:, 0:8], start=True, stop=True), reads=["cst", ("LGe", z)], writes=[("ps", gp)])
            A("dve", lambda e: e.tensor_tensor(out=exg[z][:], in0=ps[gp][:, 0:8], in1=Ie[z][:], op=ALU.add), reads=[("ps", gp), ("Ie", z)], writes=[("exg", z)])
            A("dve", lambda e: e.tensor_tensor(out=exg[z][:, 4:8], in0=exg[z][:, 4:8], in1=Arun[:], op=ALU.add), reads=[("exg", z), "Arun"], writes=[("exg", z)])
            A("act", lambda e: e.activation(out=Wt[z][:], in_=exg[z][:], func=AF.Exp, bias=LNSCt[:, 0:1]), reads=[("exg", z), "lnsc"], writes=[("Wt", z)])
            A("act", lambda e: e.activation(out=dect[z][:], in_=ps[gp][:, 8:12], func=AF.Exp), reads=[("ps", gp)], writes=[("dect", z)])
            A("dve", lambda e: e.tensor_tensor(out=Arun[:], in0=Arun[:], in1=ps[gp][:, 12:16], op=ALU.add), reads=["Arun", ("ps", gp)], writes=["Arun"])
            for u in range(8):
                ch, h = divmod(u, 4)
                kz = u % 4
                A("dve", lambda e, u=u, h=h, kz=kz: e.tensor_scalar(out=ktil[kz][:], in0=ktokg[par][:, blk, h * 128:(h + 1) * 128],
                                                                    scalar1=Wt[z][:, u:u + 1], scalar2=None, op0=ALU.mult),
                  reads=[("ktokg", 0), ("Wt", z)], writes=[("ktil", kz)])
                bank = UPS0 + (u % 3)
                c0 = (u // 3) * 129 + (128 if bank == UPS0 + 2 else 0)
                A("pe", lambda e, h=h, kz=kz, bank=bank, c0=c0: e.matmul(ps[bank][:, c0:c0 + 129], lhsT=ktil[kz][:], rhs=vaugg[par][:, blk, h, :],
                                                                         start=True, stop=True),
                  reads=[("ktil", kz), ("vaugg", 0), ("vaugg1", 0)], writes=[("ps", bank)])
                if ch == 0:
                    A("dve", lambda e, h=h, bank=bank, c0=c0: e.scalar_tensor_tensor(out=Cf[:, h, :], in0=Cf[:, h, :], scalar=dect[z][:, h:h + 1],
                                                                                     in1=ps[bank][:, c0:c0 + 129], op0=ALU.mult, op1=ALU.add),
                      reads=["Cf", ("dect", z), ("ps", bank)], writes=["Cf"])
                else:
                    A("dve", lambda e, h=h, bank=bank, c0=c0: e.tensor_tensor(out=Cb[:, h, :], in0=Cb[:, h, :], in1=ps[bank][:, c0:c0 + 129], op=ALU.add),
                      reads=["Cb", ("ps", bank)], writes=["Cb"])

    A("dve", lambda e: e.memset(LNSCt[:], LNSC), writes=["lnsc"])

    gseq = list(range(12 if not (debug and (debug.endswith("_fast") or debug.startswith("p4"))) else 0))
    gseq += list(range(12, 16 if not (debug and debug.startswith("p4")) else 12))
    if gseq:
        mlstm_pre(gseq[0])
    for gidx, gi in enumerate(gseq):
        nxt = gseq[gidx + 1] if gidx + 1 < len(gseq) else None
        mlstm_group(gi, gi >= 12, mid=(lambda nxt=nxt: mlstm_pre(nxt)) if nxt is not None else None)

    if debug and debug.split("_")[0] in ("qTe", "kTe"):
        src_t = qT if debug.startswith("qTe") else kT
        dt_ = al([128, 1024], F32)
        for h in range(4):
            for half in range(2):
                A("dve", lambda e, h=h, half=half: e.tensor_copy(out=dt_[:], in_=src_t[:, h, half * 1024:(half + 1) * 1024]), reads=["qT", "kT"], writes=["dt_"])
                r0 = (h * 2 + half) * 128
                A("sp", lambda e, r0=r0: e.dma_start(out=dbg[r0:r0 + 128, :], in_=dt_[:]), reads=["dt_"], writes=["dbgo"], dma=True)
        A("sp", lambda e: e.nop(), reads=["dbgo"])
        S.emit(nc, st)
        st.close()
        return nc
    p12keys = [("wl", k) for k in range(8)] + ["tf", "tb", "pf", "pbk", "gbias", "cw", "cb", "gon", "haloT", ("hTg", 0), ("hTg", 1),
               ("padb", 0), ("padb", 1), ("padbL", 0), ("padbL", 1), ("padbR", 0), ("padbR", 1), ("acc", 0), ("acc", 1),
               ("kTg", 0), ("ktokg", 0), ("vaugg", 0), ("vaugg1", 0), ("Gg", 0), ("Gg", 1), ("LFg", 0), ("LFg", 1), ("sgt", 0), ("sgt", 1)]
    p3keys = ["EBt", "ECt", "WSt", "DECt", "cmt", "HB", ("PT", 0), ("PT", 1), "den", "scl", ("hsum", 0), ("hsum", 1), "ssq", ("ymt", 0), ("ymt", 1), "ymT"]
    A("dve", lambda e: e.memset(cmt[:], 0.0), writes=p12keys + p3keys)
    A("pool", lambda e: e.tensor_copy(out=Cfb[:], in_=Cf[:]), reads=["Cf"], writes=["Cfb"])
    A("pool", lambda e: e.tensor_copy(out=Cbb[:], in_=Cb[:]), reads=["Cb"], writes=["Cbb"])
    GP = 7
    for blk in range(16 if not (debug and debug.startswith("p4")) else 0):
        z = blk % 2
        A("dve", lambda e, blk=blk: e.tensor_scalar(out=LGe[z][:], in0=LFo[:, blk, :], scalar1=-1.0, scalar2=None, op0=ALU.mult), reads=["LFo"], writes=[("LGe", z)])
        A("pe", lambda e: e.matmul(ps[GP][:, 0:4], lhsT=mLE, rhs=LGe[z][:, 0:4], start=True, stop=True), reads=["cst", ("LGe", z)], writes=[("ps", GP)])
        A("pe", lambda e: e.matmul(ps[GP][:, 4:8], lhsT=mGE, rhs=LGe[z][:, 4:8], start=True, stop=True), reads=["cst", ("LGe", z)], writes=[("ps", GP)])
        A("pe", lambda e: e.matmul(ps[GP][:, 8:16], lhsT=ones, rhs=LGe[z][:, 0:8], start=True, stop=True), reads=["cst", ("LGe", z)], writes=[("ps", GP)])
        A("act", lambda e, blk=blk: e.activation(out=EBt[:, blk, :], in_=ps[GP][:, 0:8], func=AF.Exp), reads=[("ps", GP)], writes=["EBt"])
        A("dve", lambda e, blk=blk: e.tensor_tensor(out=cmt[:], in0=Gown[:, blk, 0:8], in1=ps[GP][:, 0:8], op=ALU.subtract), reads=["Gown", ("ps", GP)], writes=["cmt"])
        A("act", lambda e, blk=blk: e.activation(out=ECt[:, blk, :], in_=cmt[:], func=AF.Exp, bias=LNSCt[:, 0:1]), reads=["cmt", "lnsc"], writes=["ECt"])
        A("act", lambda e, blk=blk: e.activation(out=DECt[:, blk, :], in_=ps[GP][:, 8:16], func=AF.Exp), reads=[("ps", GP)], writes=["DECt"])
        A("dve", lambda e, blk=blk: e.tensor_tensor(out=cmt[:], in0=cmt[:], in1=ps[GP][:, 8:16], op=ALU.add), reads=["cmt", ("ps", GP)], writes=["cmt"])
        A("act", lambda e, blk=blk: e.activation(out=WSt[:, blk, :], in_=cmt[:], func=AF.Exp, bias=LNSCt[:, 0:1]), reads=["cmt", "lnsc"], writes=["WSt"])

    SPS, ND0, ND1, UB0, UB1, TPY = 0, 1, 2, 3, 4, 5
    def dirpass(d):
        blocks = list(range(16)) if d == 0 else list(range(15, -1, -1))
        Cx, Cxb, ck, ckb = (Cf, Cfb, "Cf", "Cfb") if d == 0 else (Cb, Cbb, "Cb", "Cbb")
        maskb = mLEb if d == 0 else mGEb
        mk_ = "mLEb" if d == 0 else "mGEb"
        final = (d == 0)
        for bi, blk in enumerate(blocks):
            z = bi % 2
            tsl = slice(blk * 128, (blk + 1) * 128)
            for h in range(4):
                A("pe", lambda e, h=h: e.matmul(ps[SPS][:, h * 128:(h + 1) * 128], lhsT=kT[:, h, tsl], rhs=qT[:, h, tsl], start=True, stop=True),
                  reads=["kT", "qT"], writes=[("ps", SPS)])
            for h in range(4):
                A("dve", lambda e, h=h: e.scalar_tensor_tensor(out=PT[z][:, h, :], in0=ps[SPS][:, h * 128:(h + 1) * 128], scalar=ECt[:, blk, d * 4 + h:d * 4 + h + 1],
                                                               in1=maskb[:], op0=ALU.mult, op1=ALU.mult),
                  reads=[("ps", SPS), "ECt", mk_], writes=[("PT", z)])
            for h in range(4):
                bank = ND0 + h % 2
                c0 = (h // 2) * 129
                A("pe", lambda e, h=h, bank=bank, c0=c0: e.matmul(ps[bank][:, c0:c0 + 129], lhsT=PT[z][:, h, :], rhs=vaug[:, blk, h, :], start=True, stop=False),
                  reads=[("PT", z), "vaug", "vaug1"], writes=[("ps", bank)])
                A("pe", lambda e, h=h, bank=bank, c0=c0: e.matmul(ps[bank][:, c0:c0 + 129], lhsT=qT[:, h, tsl], rhs=Cxb[:, h, :], start=False, stop=True),
                  reads=["qT", ckb], writes=[("ps", bank)])
            for h in range(4):
                bank = ND0 + h % 2
                c0 = (h // 2) * 129
                A("dve", lambda e, h=h, bank=bank, c0=c0: e.tensor_tensor(out=den[:, h:h + 1], in0=ps[bank][:, c0 + 128:c0 + 129],
                                                                          in1=EBt[:, blk, d * 4 + h:d * 4 + h + 1], op=ALU.mult),
                  reads=[("ps", bank), "EBt"], writes=["den"])
            A("dve", lambda e: e.tensor_scalar(out=scl[:], in0=den[:], scalar1=-1.0, scalar2=None, op0=ALU.mult), reads=["den"], writes=["scl"])
            A("dve", lambda e: e.tensor_tensor(out=den[:], in0=den[:], in1=scl[:], op=ALU.max), reads=["den", "scl"], writes=["den"])
            A("dve", lambda e: e.tensor_scalar(out=den[:], in0=den[:], scalar1=1.0, scalar2=None, op0=ALU.max), reads=["den"], writes=["den"])
            A("dve", lambda e: e.reciprocal(out=den[:], in_=den[:]), reads=["den"], writes=["den"])
            A("dve", lambda e: e.tensor_tensor(out=scl[:], in0=den[:], in1=EBt[:, blk, d * 4:d * 4 + 4], op=ALU.mult), reads=["den", "EBt"], writes=["scl"])
            for h in range(4):
                bank = ND0 + h % 2
                c0 = (h // 2) * 129
                if not final:
                    A("dve", lambda e, h=h, bank=bank, c0=c0: e.tensor_scalar(out=HB[:, blk, h * 128:(h + 1) * 128], in0=ps[bank][:, c0:c0 + 128],
                                                                              scalar1=scl[:, h:h + 1], scalar2=None, op0=ALU.mult),
                      reads=[("ps", bank), "scl"], writes=["HB"])
                else:
                    A("dve", lambda e, h=h, bank=bank, c0=c0: e.scalar_tensor_tensor(out=hsum[z][:, h, :], in0=ps[bank][:, c0:c0 + 128], scalar=scl[:, h:h + 1],
                                                                                     in1=HB[:, blk, h * 128:(h + 1) * 128], op0=ALU.mult, op1=ALU.add),
                      reads=[("ps", bank), "scl", "HB"], writes=[("hsum", z)])
            if bi < 15:
                for h in range(4):
                    kz = h
                    bank = UB0 + h % 2
                    c0 = (h // 2) * 129
                    A("dve", lambda e, h=h, kz=kz: e.tensor_scalar(out=ktil[kz][:], in0=ktok[:, blk, h * 128:(h + 1) * 128],
                                                                   scalar1=WSt[:, blk, d * 4 + h:d * 4 + h + 1], scalar2=None, op0=ALU.mult),
                      reads=["ktok", "WSt"], writes=[("ktil", kz)])
                    A("pe", lambda e, h=h, kz=kz, bank=bank, c0=c0: e.matmul(ps[bank][:, c0:c0 + 129], lhsT=ktil[kz][:], rhs=vaug[:, blk, h, :], start=True, stop=True),
                      reads=[("ktil", kz), "vaug", "vaug1"], writes=[("ps", bank)])
                    A("dve", lambda e, h=h, bank=bank, c0=c0: e.scalar_tensor_tensor(out=Cx[:, h, :], in0=Cx[:, h, :], scalar=DECt[:, blk, d * 4 + h:d * 4 + h + 1],
                                                                                     in1=ps[bank][:, c0:c0 + 129], op0=ALU.mult, op1=ALU.add),
                      reads=[ck, "DECt", ("ps", bank)], writes=[ck])
                A("pool", lambda e: e.tensor_copy(out=Cxb[:], in_=Cx[:]), reads=[ck], writes=[ckb])
            if final:
                for h in range(4):
                    A("act", lambda e, h=h: e.activation(out=junk[:, 0:128], in_=hsum[z][:, h, :], func=AF.Square, accum_out=ssq[:, h:h + 1]),
                      reads=[("hsum", z)], writes=["ssq"])
                rstd_of(ssq[:], 128, "ssq")
                for h in range(4):
                    A("dve", lambda e, h=h: e.scalar_tensor_tensor(out=ymt[z][:, h, :], in0=hsum[z][:, h, :], scalar=ssq[:, h:h + 1],
                                                                   in1=sgo[:, blk, h * 128:(h + 1) * 128], op0=ALU.mult, op1=ALU.mult),
                      reads=[("hsum", z), "ssq", "sgo"], writes=[("ymt", z)])
                pb = psb(TPY)
                for h in range(4):
                    A("pe", lambda e, h=h: e.transpose(out=pb[:, h * 128:(h + 1) * 128], in_=ymt[z][:, h, :], identity=idb[:]),
                      reads=[("ymt", z), "idb"], writes=[("ps", TPY)])
                A("act", lambda e: e.copy(out=ymT[:, :, tsl], in_=pb[:, 0:512].rearrange("p (h t) -> p h t", h=4)), reads=[("ps", TPY)], writes=["ymT"])

    if not (debug and debug.startswith("p4")):
        dirpass(1)
    if debug and debug.startswith("HB"):
        for blk in range(16):
            A("sp", lambda e, blk=blk: e.dma_start(out=dbg[blk * 128:(blk + 1) * 128, 0:512], in_=HB[:, blk, :]), reads=["HB"], writes=["dbgo"], dma=True)
        A("sp", lambda e: e.nop(), reads=["dbgo"])
        S.emit(nc, st)
        st.close()
        return nc
    if not (debug and debug.startswith("p4")):
        dirpass(0)

    if debug and debug.split("_")[0] in ("mlstm", "qT", "kT"):
        debug = debug.split("_")[0]
        if debug == "qT":
            ymT = qT
        elif debug == "kT":
            ymT = kT
        ymk = {"mlstm": "ymT", "qT": "qT", "kT": "kT"}[debug]
        dt_ = al([128, 1024], F32)
        for h in range(4):
            for half in range(2):
                A("dve", lambda e, h=h, half=half: e.tensor_copy(out=dt_[:], in_=ymT[:, h, half * 1024:(half + 1) * 1024]), reads=[ymk], writes=["dt_"])
                r0 = (h * 2 + half) * 128
                A("sp", lambda e, r0=r0: e.dma_start(out=dbg[r0:r0 + 128, :], in_=dt_[:]), reads=["dt_"], writes=["dbgo"], dma=True)
        A("sp", lambda e: e.nop(), reads=["dbgo"])
        S.emit(nc, st)
        st.close()
        return nc

    S.barrier(lambda e: e.memset(sst[:, 7:8], 0.0))
    al.set_regions([(m_phase, m_alias), (m_keep, SB_HI)])
    R = al.ralloc
    ckvT = R([128, 2, 8192], BF16)
    kropeT = R([32, 8192], BF16)
    QT = R([96, 8, NOWN], BF16)
    yaT = R([128, 4, NOWN], BF16)
    reg_p4 = [list(r) for r in al.regions]
    wkv = R([128, 8, 288], BF16)
    wq = R([128, 8, 384], BF16)
    wuq = R([128, 3, 768], BF16)
    gq = R([128, 384], F32)
    gkv = R([128, 256], F32)
    hT4 = [R([128, 8, 128], BF16) for _ in range(2)]
    sinT = R([128, 80, 16], F32)
    cosT = R([128, 80, 16], F32)
    sinq = R([128, 16, 4, 16], F32)
    cosq = R([128, 16, 4, 16], F32)
    reg_tmp = [list(r) for r in al.regions]
    posi = R([128, 80], I32)
    posf = R([128, 80], F32)
    ang = R([128, 80, 16], F32)
    tq = R([128, 1280], F32)
    tk = R([128, 1280], I32)
    tkf = R([128, 1280], F32)

    load_w(wkv, "wkv", w_in, 8, 384, 672)
    load_w(wq, "wq", w_in, 8, 0, 384)
    load_w(wuq, "wuq", w_uq, 3, 0, 768)
    A("sp", lambda e: e.dma_start(out=gq[:], in_=gqd), writes=["gq"], dma=True)
    A("sp", lambda e: e.dma_start(out=gkv[:], in_=gkvd), writes=["gkv"], dma=True)
    A("sp", lambda e: e.dma_start(out=posi[:], in_=posT), writes=["posi"], dma=True)
    A("dve", lambda e: e.tensor_copy(out=posf[:], in_=posi[:]), reads=["posi"], writes=["posf"])
    invf = (np.float32(10000.0) ** (-np.arange(0, 32, 2, dtype=np.float32) / np.float32(32))).astype(np.float32)
    for f in range(16):
        A("dve", lambda e, f=f: e.tensor_scalar(out=ang[:, :, f], in0=posf[:], scalar1=float(invf[f]), scalar2=None, op0=ALU.mult), reads=["posf"], writes=["ang"])
    angf = ang[:].rearrange("p a b -> p (a b)")
    TWO_PI = 2.0 * math.pi
    for (dst, off) in ((sinT, 0.0), (cosT, 0.25)):
        dstf = dst[:].rearrange("p a b -> p (a b)")
        A("dve", lambda e, off=off: e.tensor_scalar(out=tq[:], in0=angf, scalar1=1.0 / TWO_PI, scalar2=off, op0=ALU.mult, op1=ALU.add), reads=["ang"], writes=["tq"])
        A("dve", lambda e: e.tensor_copy(out=tk[:], in_=tq[:]), reads=["tq"], writes=["tk"])
        A("dve", lambda e: e.tensor_copy(out=tkf[:], in_=tk[:]), reads=["tk"], writes=["tkf"])
        A("dve", lambda e: e.tensor_tensor(out=tq[:], in0=tq[:], in1=tkf[:], op=ALU.subtract), reads=["tq", "tkf"], writes=["tq"])
        A("dve", lambda e: e.tensor_scalar(out=tkf[:], in0=tq[:], scalar1=0.5, scalar2=None, op0=ALU.is_gt), reads=["tq"], writes=["tkf"])
        A("dve", lambda e: e.tensor_tensor(out=tq[:], in0=tq[:], in1=tkf[:], op=ALU.subtract), reads=["tq", "tkf"], writes=["tq"])
        A("dve", lambda e: e.tensor_scalar(out=tkf[:], in0=tq[:], scalar1=-0.5, scalar2=None, op0=ALU.is_lt), reads=["tq"], writes=["tkf"])
        A("dve", lambda e: e.tensor_tensor(out=tq[:], in0=tq[:], in1=tkf[:], op=ALU.add), reads=["tq", "tkf"], writes=["tq"])
        A("dve", lambda e: e.tensor_scalar(out=tq[:], in0=tq[:], scalar1=-0.4999, scalar2=0.4999, op0=ALU.max, op1=ALU.min), reads=["tq"], writes=["tq"])
        A("act", lambda e, dstf=dstf: e.activation(out=dstf, in_=tq[:], func=AF.Sin, scale=TWO_PI), reads=["tq"], writes=["sincos"])
    for hh in range(4):
        A("dve", lambda e, hh=hh: e.tensor_copy(out=sinq[:, :, hh, :], in_=sinT[:, 64:80, :]), reads=["sincos"], writes=["sinq"])
        A("dve", lambda e, hh=hh: e.tensor_copy(out=cosq[:, :, hh, :], in_=cosT[:, 64:80, :]), reads=["sincos"], writes=["cosq"])

    if debug == "p4a":
        A("sp", lambda e: e.dma_start(out=dbg[0:128, 0:1024], in_=sinT[:].rearrange("p a b -> p (a b)")[:, 0:1024]), reads=["sincos"], writes=["dbgo"], dma=True)
        A("sp", lambda e: e.nop(), reads=["dbgo"])
        S.emit(nc, st)
        st.close()
        return nc
    S.barrier(lambda e: e.memset(sst[:, 7:8], 0.0))
    al.regions = [list(r) for r in reg_tmp]
    cn = [R([128, 384], BF16) for _ in range(2)]
    krr = [R([128, 32], BF16) for _ in range(2)]
    rt = [R([128, 4, 16], F32) for _ in range(8)]
    cqnT = R([128, 3, 128], BF16)
    qtok = R([128, 8, 96], BF16)
    TPSX, LAT0, TPC, Q0, Q1, TPQ = 0, 1, 3, 4, 5, 6
    lstate = {"i": 0}

    def rope(x1, x2, cs, sn, o1, o2, rkeys, okey, shape4):
        rs_ = lstate["i"] % 2
        lstate["i"] += 1
        ta, tb_, tc, td = [t[:] if shape4 else t[:, 0, :] for t in rt[rs_ * 4:rs_ * 4 + 4]]
        k0, k1, k2, k3 = [("rt", rs_, q_) for q_ in range(4)]
        A("dve", lambda e: e.tensor_tensor(out=ta, in0=x1, in1=cs, op=ALU.mult), reads=rkeys, writes=[k0])
        A("dve", lambda e: e.tensor_tensor(out=tb_, in0=x2, in1=sn, op=ALU.mult), reads=rkeys, writes=[k1])
        A("dve", lambda e: e.tensor_tensor(out=o1, in0=ta, in1=tb_, op=ALU.subtract), reads=[k0, k1], writes=[okey])
        A("dve", lambda e: e.tensor_tensor(out=tc, in0=x2, in1=cs, op=ALU.mult), reads=rkeys, writes=[k2])
        A("dve", lambda e: e.tensor_tensor(out=td, in0=x1, in1=sn, op=ALU.mult), reads=rkeys, writes=[k3])
        A("dve", lambda e: e.tensor_tensor(out=o2, in0=tc, in1=td, op=ALU.add), reads=[k2, k3], writes=[okey])

    def kv_pre(kt):
        z = kt % 2
        src = xo[kt * 128:(kt + 1) * 128, :] if kt < 48 else xown[(kt - 48) * 128:(kt - 47) * 128, :]
        xpipe(src, gmix, "gmix", hT4[z][:], ("hT4", z), TPSX)

    kv_pre(0)
    for kt in range(64):
        z = kt % 2
        if kt + 1 < 64:
            kv_pre(kt + 1)
        lat = LAT0 + z
        for k in range(8):
            A("pe", lambda e, k=k: e.matmul(ps[lat][:, 0:288], lhsT=hT4[z][:, k, :], rhs=wkv[:, k, :], start=(k == 0), stop=(k == 7)),
              reads=[("hT4", z), ("wkv", k)], writes=[("ps", lat)])
        ssap = sst[:, 2 + z:3 + z]
        A("act", lambda e: e.activation(out=junk[:, 0:256], in_=ps[lat][:, 0:256], func=AF.Square, accum_out=ssap), reads=[("ps", lat)], writes=[("ssl", z)])
        rstd_of(ssap, 256, ("ssl", z))
        A("dve", lambda e: e.scalar_tensor_tensor(out=cn[z][:, 0:256], in0=ps[lat][:, 0:256], scalar=ssap, in1=gkv[:], op0=ALU.mult, op1=ALU.mult),
          reads=[("ps", lat), ("ssl", z), "gkv"], writes=[("cn", z)])
        rope(ps[lat][:, 256:272], ps[lat][:, 272:288], cosT[:, kt, :], sinT[:, kt, :], krr[z][:, 0:16], krr[z][:, 16:32],
             [("ps", lat), "sincos"], ("krr", z), False)
        tpc = (3, 7)[z]
        pb = psb(tpc)
        for c in range(2):
            A("pe", lambda e, c=c: e.transpose(out=pb[:, c * 128:(c + 1) * 128], in_=cn[z][:, c * 128:(c + 1) * 128], identity=idb[:]),
              reads=[("cn", z), "idb"], writes=[("ps", tpc)])
        A("pe", lambda e: e.transpose(out=pb[0:32, 256:384], in_=krr[z][:], identity=idb[:]), reads=[("krr", z), "idb"], writes=[("ps", tpc)])
        A("act", lambda e: e.copy(out=ckvT[:, :, kt * 128:(kt + 1) * 128], in_=pb[:, 0:256].rearrange("p (c t) -> p c t", c=2)), reads=[("ps", tpc)], writes=["ckvT"])
        A("act", lambda e: e.copy(out=kropeT[:, kt * 128:(kt + 1) * 128], in_=pb[0:32, 256:384]), reads=[("ps", tpc)], writes=["kropeT"])

    def q_pre(ot):
        z = ot % 2
        xpipe(xown[ot * 128:(ot + 1) * 128, :], gmix, "gmix", hT4[z][:], ("hT4", z), TPSX)

    n_ot = 16 if debug != "p4b" else 0
    if n_ot:
        q_pre(0)
    for ot in range(n_ot):
        z = ot % 2
        if ot + 1 < n_ot:
            q_pre(ot + 1)
        lat = LAT0 + z
        for k in range(8):
            A("pe", lambda e, k=k: e.matmul(ps[lat][:, 0:384], lhsT=hT4[z][:, k, :], rhs=wq[:, k, :], start=(k == 0), stop=(k == 7)),
              reads=[("hT4", z), ("wq", k)], writes=[("ps", lat)])
        ssap = sst[:, 2 + z:3 + z]
        A("act", lambda e: e.activation(out=junk[:, 0:384], in_=ps[lat][:, 0:384], func=AF.Square, accum_out=ssap), reads=[("ps", lat)], writes=[("ssl", z)])
        rstd_of(ssap, 384, ("ssl", z))
        A("dve", lambda e: e.scalar_tensor_tensor(out=cn[z][:], in0=ps[lat][:, 0:384], scalar=ssap, in1=gq[:], op0=ALU.mult, op1=ALU.mult),
          reads=[("ps", lat), ("ssl", z), "gq"], writes=[("cn", z)])
        pb = psb(TPC)
        for c in range(3):
            A("pe", lambda e, c=c: e.transpose(out=pb[:, c * 128:(c + 1) * 128], in_=cn[z][:, c * 128:(c + 1) * 128], identity=idb[:]),
              reads=[("cn", z), "idb"], writes=[("ps", TPC)])
        A("act", lambda e: e.copy(out=cqnT[:], in_=pb[:, 0:384].rearrange("p (c t) -> p c t", c=3)), reads=[("ps", TPC)], writes=["cqnT"])
        qlvl = int(debug[3:]) if (debug and debug.startswith("p4q")) else 9
        if qlvl < 2:
            continue
        for x_ in range(2):
            qb = Q0 + x_
            for c in range(3):
                A("pe", lambda e, c=c: e.matmul(ps[qb][:, 0:384], lhsT=cqnT[:, c, :], rhs=wuq[:, c, x_ * 384:(x_ + 1) * 384], start=(c == 0), stop=(c == 2)),
                  reads=["cqnT", ("wuq", c)], writes=[("ps", qb)])
            V4 = ps[qb][:, 0:384].rearrange("p (h d) -> p h d", h=4)
            A("act", lambda e: e.copy(out=qtok[:, x_ * 4:x_ * 4 + 4, 0:64], in_=V4[:, :, 0:64]), reads=[("ps", qb)], writes=["qtokn"])
            if qlvl < 3:
                continue
            for hh in range(4):
                c0 = hh * 96
                rope(ps[qb][:, c0 + 64:c0 + 80], ps[qb][:, c0 + 80:c0 + 96], cosT[:, 64 + ot, :], sinT[:, 64 + ot, :],
                     qtok[:, x_ * 4 + hh, 64:80], qtok[:, x_ * 4 + hh, 80:96], [("ps", qb), "sincos"], "qtokr", False)
        if qlvl < 4:
            continue
        pq = psb(TPQ)
        for h in range(8):
            A("pe", lambda e, h=h: e.transpose(out=pq[0:96, h * 128:(h + 1) * 128], in_=qtok[:, h, :], identity=idb[:]),
              reads=["qtokn", "qtokr", "idb"], writes=[("ps", TPQ)])
        A("act", lambda e: e.copy(out=QT[:, :, ot * 128:(ot + 1) * 128], in_=pq[0:96, :].rearrange("p (h t) -> p h t", h=8)), reads=[("ps", TPQ)], writes=["QT"])

    if debug and debug.startswith("p4"):
        dtt = R([128, 1024], F32)
        for h in range(8):
            for half in range(2):
                A("dve", lambda e, h=h, half=half: e.tensor_copy(out=dtt[0:96, :], in_=QT[:, h, half * 1024:(half + 1) * 1024]), reads=["QT"], writes=["dtt"])
                r0 = (h * 2 + half) * 96
                A("sp", lambda e, r0=r0: e.dma_start(out=dbg[r0:r0 + 96, :], in_=dtt[0:96, :]), reads=["dtt"], writes=["dbgo"], dma=True)
        A("dve", lambda e: e.tensor_copy(out=dtt[0:32, :], in_=kropeT[:, 7168:8192]), reads=["kropeT"], writes=["dtt"])
        A("sp", lambda e: e.dma_start(out=dbg[1536:1568, :], in_=dtt[0:32, :]), reads=["dtt"], writes=["dbgo"], dma=True)
        A("dve", lambda e: e.tensor_copy(out=dtt[:], in_=ckvT[:, 1, 7168:8192]), reads=["ckvT"], writes=["dtt"])
        A("sp", lambda e: e.dma_start(out=dbg[1664:1792, :], in_=dtt[:]), reads=["dtt"], writes=["dbgo"], dma=True)
        A("sp", lambda e: e.nop(), reads=["dbgo"])
        S.emit(nc, st)
        st.close()
        return nc
    S.barrier(lambda e: e.memset(sst[:, 7:8], 0.0))
    al.regions = [list(r) for r in reg_p4]
    wkp = R([128, 2, 8, 96], BF16)
    wv = R([128, 2, 512], BF16)
    selb = R([32, 96], BF16)
    KT0 = R([96, 8192], BF16)
    KT = [KT0, KT0]
    VA = [R([128, 64, 65], BF16) for _ in range(2)]
    yattn = R([128, 16, 512], BF16)
    rden = R([128, 4], F32)
    A("pool", lambda e: e.memset(wkp[:], 0.0), writes=["wkp"])
    A("dve", lambda e: e.tensor_copy(out=selb[:], in_=sel), reads=["cst"], writes=["selb"])
    for b_ in range(2):
        A("pool", lambda e, b_=b_: e.memset(VA[b_][:, :, 64:65], 1.0), writes=[("VA1", b_)])
    for c in range(2):
        sl = wstate["i"] % 2
        wstate["i"] += 1
        A("sp", lambda e, c=c, sl=sl: e.dma_start(out=wst[sl][:], in_=w_ukv[c * 128:(c + 1) * 128, :]), writes=[("wst", sl)], dma=True)
        W3 = wst[sl][:].rearrange("p (h d) -> p h d", h=8)
        A("pool", lambda e, c=c: e.tensor_copy(out=wkp[:, c, :, 0:64], in_=W3[:, :, 0:64]), reads=[("wst", sl), "wkp"], writes=["wkp"])
        A("pool", lambda e, c=c: e.tensor_copy(out=wv[:, c, :].rearrange("p (h d) -> p h d", h=8), in_=W3[:, :, 64:128]), reads=[("wst", sl)], writes=["wv"])

    OB_, KB_, VB_ = (6, 7), (0, 1), 2
    SCALE = 96.0 ** -0.5
    Pb2 = [R([128, 1024], BF16) for _ in range(3)]
    kbs = {"i": 0}
    for h in range(8):
        bf = h % 2
        for grp in range(16):
            kb = KB_[kbs["i"] % 2]
            kbs["i"] += 1
            gs = slice(grp * 512, (grp + 1) * 512)
            A("pe", lambda e: e.matmul(ps[kb][0:96, :], lhsT=wkp[:, 0, h, :], rhs=ckvT[:, 0, gs], start=True, stop=False), reads=["wkp", "ckvT"], writes=[("ps", kb)])
            A("pe", lambda e: e.matmul(ps[kb][0:96, :], lhsT=wkp[:, 1, h, :], rhs=ckvT[:, 1, gs], start=False, stop=False), reads=["wkp", "ckvT"], writes=[("ps", kb)])
            A("pe", lambda e: e.matmul(ps[kb][0:96, :], lhsT=selb[:], rhs=kropeT[:, gs], start=False, stop=True), reads=["selb", "kropeT"], writes=[("ps", kb)])
            A("dve", lambda e: e.tensor_copy(out=KT[bf][:, gs], in_=ps[kb][0:96, :]), reads=[("ps", kb)], writes=[("KT", 0)])
        for tg in range(8):
            vb = 2 + (tg % 2)
            for tl in range(8):
                kt = tg * 8 + tl
                for c in range(2):
                    A("pe", lambda e, c=c, tl=tl, kt=kt: e.matmul(ps[vb][:, tl * 64:(tl + 1) * 64], lhsT=ckvT[:, c, kt * 128:(kt + 1) * 128],
                                                                 rhs=wv[:, c, h * 64:(h + 1) * 64], start=(c == 0), stop=(c == 1)),
                      reads=["ckvT", "wv"], writes=[("ps", vb)])
            A("act", lambda e: e.copy(out=VA[bf][:, tg * 8:(tg + 1) * 8, 0:64], in_=ps[vb][:].rearrange("p (t d) -> p t d", t=8)), reads=[("ps", vb)], writes=[("VA", bf)])
        for tt in range(4):
            ob = OB_[(h * 4 + tt) % 2]
            qs = slice(tt * 512, (tt + 1) * 512)

            def s_mm(sp_):
                j = sp_ % 3
                for half in range(2):
                    st_ = 2 * sp_ + half
                    A("pe", lambda e: e.matmul(ps[2 * j + half][:], lhsT=KT[bf][:, st_ * 128:(st_ + 1) * 128], rhs=QT[:, h, qs], start=True, stop=True),
                      reads=[("KT", 0), "QT"], writes=[("ps", 2 * j + half)])
                A("act", lambda e: e.activation(out=Pb2[j][:], in_=psbig[j][:], func=AF.Exp, scale=SCALE),
                  reads=[("ps", 2 * j), ("ps", 2 * j + 1)], writes=[("Pb", j)])

            s_mm(0)
            s_mm(1)
            for sp_ in range(32):
                j = sp_ % 3
                for half in range(2):
                    st_ = 2 * sp_ + half
                    for qq in range(4):
                        A("pe", lambda e, qq=qq: e.matmul(ps[ob][:, qq * 65:(qq + 1) * 65], lhsT=Pb2[j][:, half * 512 + qq * 128:half * 512 + (qq + 1) * 128],
                                                          rhs=VA[bf][:, st_, :], start=(st_ == 0 and qq == 0), stop=(st_ == 63), skip_group_check=True),
                          reads=[("Pb", j), ("VA", bf), ("VA1", bf)], writes=[("ps", ob)])
                if sp_ + 2 < 32:
                    s_mm(sp_ + 2)
            O3 = ps[ob][:, 0:260].rearrange("p (q d) -> p q d", q=4)
            A("dve", lambda e: e.reciprocal(out=rden[:], in_=O3[:, :, 64]), reads=[("ps", ob)], writes=["rden"])
            for qq in range(4):
                A("dve", lambda e, qq=qq: e.tensor_scalar(out=yattn[:, tt * 4 + qq, h * 64:(h + 1) * 64], in0=O3[:, qq, 0:64], scalar1=rden[:, qq:qq + 1],
                                                          scalar2=None, op0=ALU.mult), reads=[("ps", ob), "rden"], writes=["yattn"])
    for tile in range(16):
        pb = psb(VB_)
        for c in range(4):
            A("pe", lambda e, c=c: e.transpose(out=pb[:, c * 128:(c + 1) * 128], in_=yattn[:, tile, c * 128:(c + 1) * 128], identity=idb[:]),
              reads=["yattn", "idb"], writes=[("ps", VB_)])
        A("act", lambda e: e.copy(out=yaT[:, :, tile * 128:(tile + 1) * 128], in_=pb[:, 0:512].rearrange("p (c t) -> p c t", c=4)), reads=[("ps", VB_)], writes=["yaT"])

    S.barrier(lambda e: e.memset(sst[:, 7:8], 0.0))
    al.regions = [[m_phase, m_alias], [reg_p4[1][0], SB_HI]]
    wgab = R([128, 8, 2048], BF16)
    wbm = R([128, 4, 1024], BF16)
    wbl = R([128, 4, 1024], BF16)
    wo = R([128, 8, 1024], BF16)
    hT6 = [R([128, 8, 128], BF16) for _ in range(2)]
    sg = R([128, 2048], BF16)
    t1 = R([128, 1024], F32)
    t2 = R([128, 1024], F32)
    mrg = R([128, 1024], BF16)
    mT = R([128, 8, 128], BF16)
    x1t = [R([128, 1024], F32) for _ in range(2)]
    load_w(wgab, "wgab", w_in, 8, 2736, 4784)
    load_w(wbm, "wbm", w_bm, 4, 0, 1024)
    load_w(wbl, "wbl", w_bl, 4, 0, 1024)
    load_w(wo, "wo", w_out, 8, 0, 1024)
    def m_pre(ot):
        z = ot % 2
        return xpipe(xown[ot * 128:(ot + 1) * 128, :], gmix, "gmix", hT6[z][:], ("hT6", z), 7)

    sl_next = m_pre(0)
    for ot in range(16):
        z = ot % 2
        tsl = slice(ot * 128, (ot + 1) * 128)
        sl = sl_next
        if ot + 1 < 16:
            sl_next = m_pre(ot + 1)
        for cb_ in range(4):
            for k in range(8):
                A("pe", lambda e, k=k, cb_=cb_: e.matmul(ps[cb_][:], lhsT=hT6[z][:, k, :], rhs=wgab[:, k, cb_ * 512:(cb_ + 1) * 512], start=(k == 0), stop=(k == 7)),
                  reads=[("hT6", z), ("wgab", k)], writes=[("ps", cb_)])
            A("act", lambda e, cb_=cb_: e.activation(out=sg[:, cb_ * 512:(cb_ + 1) * 512], in_=ps[cb_][:], func=AF.Sigmoid), reads=[("ps", cb_)], writes=[("sg", cb_)])
        for half in range(2):
            for (wt, wk, aT_, ak, bank0) in ((wbm, "wbm", yaT, "yaT", 4), (wbl, "wbl", ymT, "ymT", 5)):
                bank = bank0
                for c in range(4):
                    A("pe", lambda e, c=c, wt=wt, aT_=aT_, bank=bank: e.matmul(ps[bank][:], lhsT=aT_[:, c, tsl], rhs=wt[:, c, half * 512:(half + 1) * 512],
                                                                               start=(c == 0), stop=(c == 3)), reads=[ak, (wk, c)], writes=[("ps", bank)])
            hs = slice(half * 512, (half + 1) * 512)
            A("dve", lambda e: e.tensor_tensor(out=t1[:, hs], in0=ps[4][:], in1=sg[:, half * 512:(half + 1) * 512], op=ALU.mult), reads=[("ps", 4), ("sg", half)], writes=[("t1", half)])
            A("dve", lambda e: e.tensor_tensor(out=t2[:, hs], in0=ps[5][:], in1=sg[:, 1024 + half * 512:1024 + (half + 1) * 512], op=ALU.mult),
              reads=[("ps", 5), ("sg", 2 + half)], writes=[("t2", half)])
            A("pool", lambda e: e.tensor_tensor(out=mrg[:, hs], in0=t1[:, hs], in1=t2[:, hs], op=ALU.add), reads=[("t1", half), ("t2", half)], writes=[("mrg", half)])
        pb = psb(6)
        for k in range(8):
            A("pe", lambda e, k=k: e.transpose(out=pb[:, k * 128:(k + 1) * 128], in_=mrg[:, k * 128:(k + 1) * 128], identity=idb[:]),
              reads=[("mrg", k // 4), "idb"], writes=[("ps", 6)])
        A("act", lambda e: e.copy(out=mT[:], in_=pb.rearrange("p (k t) -> p k t", k=8)), reads=[("ps", 6)], writes=["mT"])
        for half in range(2):
            bank = 4 + half
            for k in range(8):
                A("pe", lambda e, k=k, bank=bank, half=half: e.matmul(ps[bank][:], lhsT=mT[:, k, :], rhs=wo[:, k, half * 512:(half + 1) * 512], start=(k == 0), stop=(k == 7)),
                  reads=["mT", ("wo", k)], writes=[("ps", bank)])
            A("dve", lambda e, bank=bank, half=half: e.tensor_tensor(out=x1t[z][:, half * 512:(half + 1) * 512], in0=ps[bank][:], in1=xt[sl][:, half * 512:(half + 1) * 512], op=ALU.add),
              reads=[("ps", bank), ("xt", sl)], writes=[("x1t", z)])
        A("sp", lambda e: e.dma_start(out=x1d[tsl, :], in_=x1t[z][:]), reads=[("x1t", z)], writes=["x1d"], dma=True)

    S.barrier(lambda e: e.memset(sst[:, 7:8], 0.0))
    al.regions = [[m_phase, SB_HI]]
    wup = R([128, 8, 4096], BF16)
    wdn = R([128, 32, 1024], BF16)
    gmlp = R([128, 1024], F32)
    gfin = R([128, 1024], F32)
    hTm = [R([128, 8, 256], BF16) for _ in range(2)]
    aT = R([128, 32, 256], BF16)
    rr = [R([128, 256], F32) for _ in range(2)]
    xres = [R([128, 1024], F32) for _ in range(2)]
    otile = xres
    A("sp", lambda e: e.dma_start(out=gmlp[:], in_=gmlpd), writes=["gmlp"], dma=True)
    A("sp", lambda e: e.dma_start(out=gfin[:], in_=gfind), writes=["gfin"], dma=True)
    load_w(wup, "wup", w_up, 8, 0, 4096)
    load_w(wdn, "wdn", w_down, 32, 0, 1024)
    def f_pre(g):
        z = g % 2
        for i in range(2):
            r0 = g * 256 + i * 128
            xpipe(x1d[r0:r0 + 128, :], gmlp, "gmlp", hTm[z][:, :, i * 128:(i + 1) * 128], ("hTm", z), 7)

    f_pre(0)
    for g in range(8):
        z = g % 2
        for f in range(32):
            bank = f % 2
            for k in range(8):
                A("pe", lambda e, k=k, f=f, bank=bank: e.matmul(ps[bank][:, 0:256], lhsT=wup[:, k, f * 128:(f + 1) * 128], rhs=hTm[z][:, k, :], start=(k == 0), stop=(k == 7)),
                  reads=[("hTm", z), ("wup", k)], writes=[("ps", bank)])
            A("act", lambda e, bank=bank: e.activation(out=rr[bank][:], in_=ps[bank][:, 0:256], func=AF.Relu), reads=[("ps", bank)], writes=[("rr", bank)])
            A("dve", lambda e, f=f, bank=bank: e.tensor_tensor(out=aT[:, f, :], in0=rr[bank][:], in1=rr[bank][:], op=ALU.mult), reads=[("rr", bank)], writes=["aT"])
        if g + 1 < 8:
            f_pre(g + 1)
        for i in range(2):
            r0 = g * 256 + i * 128
            zz = (g * 2 + i) % 2
            A("sp", lambda e, r0=r0, zz=zz: e.dma_start(out=xres[zz][:], in_=x1d[r0:r0 + 128, :]), reads=["x1d"], writes=[("xres", zz)], dma=True)
            for half in range(2):
                bank = 2 + half
                for f in range(32):
                    A("pe", lambda e, f=f, bank=bank, half=half, i=i: e.matmul(ps[bank][:], lhsT=aT[:, f, i * 128:(i + 1) * 128], rhs=wdn[:, f, half * 512:(half + 1) * 512],
                                                                               start=(f == 0), stop=(f == 31)), reads=["aT", ("wdn", f)], writes=[("ps", bank)])
                A("dve", lambda e, bank=bank, half=half, zz=zz: e.tensor_tensor(out=xres[zz][:, half * 512:(half + 1) * 512], in0=ps[bank][:], in1=xres[zz][:, half * 512:(half + 1) * 512], op=ALU.add),
                  reads=[("ps", bank), ("xres", zz)], writes=[("xres", zz)])
            ssap = sst[:, 4 + zz:5 + zz]
            A("act", lambda e, zz=zz, ssap=ssap: e.activation(out=junk[:], in_=xres[zz][:], func=AF.Square, accum_out=ssap), reads=[("xres", zz)], writes=[("ssf", zz)])
            rstd_of(ssap, 1024, ("ssf", zz))
            A("dve", lambda e, zz=zz, ssap=ssap: e.scalar_tensor_tensor(out=otile[zz][:], in0=xres[zz][:], scalar=ssap, in1=gfin[:], op0=ALU.mult, op1=ALU.mult),
              reads=[("xres", zz), ("ssf", zz), "gfin"], writes=[("xres", zz)])
            A("sp", lambda e, r0=r0, zz=zz: e.dma_start(out=y[r0:r0 + 128, :], in_=otile[zz][:]), reads=[("xres", zz)], writes=[("yout", g * 2 + i)], dma=True)
    A("sp", lambda e: e.nop(), reads=[("yout", i_) for i_ in range(16)])
    S.emit(nc, st)
    st.close()
    return nc


def make_consts():
    c = np.zeros((128, 880), np.float32)
    r = np.arange(128)
    c[:, 0:128] = np.eye(128)
    c[:, 128:256] = (r[:, None] <= r[None, :])
    c[:, 256:384] = (r[:, None] >= r[None, :])
    c[:, 384:512] = (r[:, None] > r[None, :])
    c[:, 512:640] = (r[:, None] < r[None, :])
    c[:, 640:768] = 1.0
    for i in range(32):
        c[i, 784 + 64 + i] = 1.0
    return c


def bc(v, n=128):
    return np.ascontiguousarray(np.broadcast_to(np.asarray(v, np.float32).reshape(1, -1), (n, np.asarray(v).size)))


def prep_inputs(inp, core):
    b, j = divmod(core, 4)
    x = inp["x"][b]
    pos = inp["positions"][b]
    o0, o1 = NOWN * j, NOWN * (j + 1)
    xo = np.concatenate([x[:o0], x[o1:]], axis=0)
    xown = x[o0:o1]
    xh = np.zeros((128, 1024), np.float32)

    def row(n):
        return x[n] if 0 <= n < 8192 else np.zeros(1024, np.float32)

    for g in range(12):
        n0 = 512 * g if 512 * g < o0 else 512 * g + NOWN
        for q, n in enumerate((n0 - 2, n0 - 1, n0 + 512, n0 + 513)):
            xh[4 * g + q] = row(n)
    for g in range(4):
        n0 = o0 + 512 * g
        for q, n in enumerate((n0 - 2, n0 - 1, n0 + 512, n0 + 513)):
            xh[48 + 4 * g + q] = row(n)
    pos_all = np.concatenate([pos[:o0], pos[o1:], pos[o0:o1]])
    posT = np.concatenate([pos_all.reshape(64, 128).T, pos[o0:o1].reshape(16, 128).T], axis=1).astype(np.int32)
    tfv = (np.arange(48) < 16 * j).astype(np.float32)
    igb = inp["mlstm_igate_b"][0]
    fgb = inp["mlstm_fgate_b"][0]
    gb16 = np.concatenate([igb[0], igb[1], fgb[0], fgb[1]])
    cwv = inp["mlstm_conv_w"][0][:, 0, :]
    cw = np.ascontiguousarray(cwv.reshape(5, 8, 128).transpose(2, 1, 0)).reshape(128, 40)
    cb = np.ascontiguousarray(inp["mlstm_conv_b"][0].reshape(8, 128).T)
    d = {
        "xo": np.ascontiguousarray(xo), "xown": np.ascontiguousarray(xown), "xh": xh,
        "posT": np.ascontiguousarray(posT), "tf": bc(tfv), "cst": make_consts(),
        "gmix": bc(inp["norm_mix_g"][0]), "gmlp": bc(inp["norm_mlp_g"][0]), "gfin": bc(inp["norm_final_g"]),
        "gq": bc(inp["mla_q_norm_g"][0]), "gkv": bc(inp["mla_kv_norm_g"][0]), "gon": bc(inp["mlstm_out_norm_g"][0]),
        "gb": bc(np.tile(gb16, 4)), "cw": cw.astype(np.float32), "cb": cb.astype(np.float32),
        "w_in": inp["w_in"][0], "w_uq": inp["mla_w_uq"][0], "w_ukv": inp["mla_w_ukv"][0],
        "w_bm": inp["w_branch_mla"][0], "w_bl": inp["w_branch_mlstm"][0], "w_out": inp["w_out"][0],
        "w_up": inp["w_mlp_up"][0], "w_down": inp["w_mlp_down"][0],
    }
    return {k: np.ascontiguousarray(v) for k, v in d.items()}


def run(inputs, debug=None, cores=8):
    inp = {k: np.asarray(v) for k, v in inputs.items()}
    nc = build_program(debug)
    in_maps = [prep_inputs(inp, c) for c in range(cores)]
    res = run_bass_kernel_spmd(nc, in_maps, core_ids=list(range(cores)))
    return res


def kernel(**inputs):
    res = run(inputs)
    out = np.zeros((2, 8192, 1024), np.float32)
    for c in range(8):
        b, j = divmod(c, 4)
        out[b, NOWN * j:NOWN * (j + 1)] = res.results[c]["y"]
    return out
```

```python
import math
from contextlib import ExitStack
import numpy as np
import concourse.bass as bass
import concourse.mybir as mybir
from concourse.bass_utils import run_bass_kernel_spmd

F32 = mybir.dt.float32
BF16 = mybir.dt.bfloat16
I32 = mybir.dt.int32
AF = mybir.ActivationFunctionType
ALU = mybir.AluOpType

SEM_LIMIT = 20000
N_DSEM = 24
SB_LO = 16512
SB_HI = 229376
NOWN = 2048
NOTH = 6144
EPS = 1e-6
LNSC = -0.5 * math.log(128.0)
BIG = 30000.0


class Op:
    __slots__ = ("eng", "fn", "deps", "signal", "is_dma", "idx", "sem", "val", "dslot")

    def __init__(self, eng, fn, is_dma):
        self.eng = eng
        self.fn = fn
        self.deps = []
        self.signal = False
        self.is_dma = is_dma
        self.sem = None
        self.val = None
        self.dslot = None


class _Rec:
    def __init__(self):
        self.call = None

    def __getattr__(self, name):
        def f(*a, **k):
            self.call = (name, a, k)
            return self
        return f


class Sched:
    def __init__(self):
        self.ops = []
        self.last_w = {}
        self.readers = {}
        self.n_dma = 0
        self.dslot_last = {}
        self.fence_op = None

    def add(self, eng, fn, reads=(), writes=(), dma=False):
        rec = _Rec()
        fn(rec)
        call = rec.call
        op = Op(eng, call, dma)
        psk = [k for k in reads if isinstance(k, tuple) and k[0] == "ps"]
        if psk:
            reads = [k for k in reads if k not in psk]
            writes = list(writes) + psk
        deps = set()
        for k in reads:
            w = self.last_w.get(k)
            if w is not None:
                deps.add(w)
        for k in writes:
            w = self.last_w.get(k)
            if w is not None:
                deps.add(w)
            for r in self.readers.get(k, ()):
                deps.add(r)
        if self.fence_op is not None:
            deps.add(self.fence_op)
        if dma:
            slot = (eng, self.n_dma % N_DSEM)
            self.n_dma += 1
            prev = self.dslot_last.get(slot)
            if prev is not None:
                deps.add(prev)
            self.dslot_last[slot] = op
            op.dslot = slot
        for d in deps:
            if d is op:
                continue
            if (not d.is_dma) and d.eng == "pe" and eng == "pe" and not dma:
                continue
            d.signal = True
            op.deps.append(d)
        for k in reads:
            self.readers.setdefault(k, []).append(op)
        for k in writes:
            self.last_w[k] = op
            self.readers[k] = []
        self.ops.append(op)
        return op

    def barrier(self, fn):
        keys = set(self.last_w.keys()) | set(self.readers.keys())
        self.fence_op = None
        op = self.add("dve", fn, writes=list(keys))
        for o in self.dslot_last.values():
            if o is not op and o not in op.deps:
                o.signal = True
                op.deps.append(o)
        self.fence_op = op
        return op

    def emit(self, nc, stack):
        engs = ["pe", "act", "dve", "pool", "sp"]
        counts = {e: 0 for e in engs}
        dcount = {}
        for op in self.ops:
            if op.is_dma:
                c = dcount.get(op.dslot, 0) + 1
                dcount[op.dslot] = c
                op.sem = ("d", op.dslot)
                op.val = 16 * c
            elif op.signal:
                c = counts[op.eng]
                counts[op.eng] = c + 1
                op.sem = (op.eng, c // SEM_LIMIT)
                op.val = c % SEM_LIMIT + 1
        sems = {}
        for op in self.ops:
            if op.sem is not None and op.sem not in sems:
                sems[op.sem] = stack.enter_context(nc.semaphore("s_%d" % len(sems)))
        block = stack.enter_context(nc.Block())
        ops = self.ops

        def run(engname, e):
            waited = {}
            for op in ops:
                if op.eng != engname:
                    continue
                for d in op.deps:
                    key = d.sem
                    if waited.get(key, 0) >= d.val:
                        continue
                    e.wait_ge(sems[key], d.val)
                    waited[key] = d.val
                name, a_, k_ = op.fn
                ins = getattr(e, name)(*a_, **k_)
                if op.is_dma:
                    ins.then_inc(sems[op.sem], 16)
                elif op.signal:
                    ins.then_inc(sems[op.sem], 1)

        @block.tensor
        def _(e):
            run("pe", e)

        @block.scalar
        def _(e):
            run("act", e)

        @block.vector
        def _(e):
            run("dve", e)

        @block.gpsimd
        def _(e):
            run("pool", e)

        @block.sync
        def _(e):
            run("sp", e)


class Alloc:
    def __init__(self, nc):
        self.nc = nc
        self.base = SB_LO
        self.top = SB_LO
        self.n = 0

    def mark(self):
        return self.top

    def reset(self, m):
        self.top = m

    def set_regions(self, regions):
        self.regions = [list(r) for r in regions]

    def ralloc(self, shape, dt):
        nb = 1
        for s_ in shape[1:]:
            nb *= s_
        nb *= 2 if dt == BF16 else 4
        nb = (nb + 63) // 64 * 64
        for r in self.regions:
            if r[0] + nb <= r[1]:
                off = r[0]
                r[0] += nb
                self.n += 1
                return self.nc.alloc_sbuf_tensor_at("t%d" % self.n, list(shape), dt, offset=off)
        raise AssertionError(("SBUF overflow", shape, self.regions))

    def __call__(self, shape, dt):
        nb = 1
        for s in shape[1:]:
            nb *= s
        nb *= 2 if dt == BF16 else 4
        nb = (nb + 63) // 64 * 64
        off = self.top
        self.top += nb
        assert self.top <= SB_HI, ("SBUF overflow", self.top)
        self.n += 1
        return self.nc.alloc_sbuf_tensor_at("t%d" % self.n, list(shape), dt, offset=off)


def build_program(debug=None):
    nc = bass.Bass("TRN2", target_bir_lowering=False)

    def din(name, shape, dt=F32):
        return nc.dram_tensor(name, list(shape), dt, kind="ExternalInput").ap()

    xo = din("xo", [NOTH, 1024])
    xown = din("xown", [NOWN, 1024])
    xh = din("xh", [128, 1024])
    posT = din("posT", [128, 80], I32)
    tfd = din("tf", [128, 48])
    cstd = din("cst", [128, 880])
    gmixd = din("gmix", [128, 1024])
    gmlpd = din("gmlp", [128, 1024])
    gfind = din("gfin", [128, 1024])
    gqd = din("gq", [128, 384])
    gkvd = din("gkv", [128, 256])
    gond = din("gon", [128, 512])
    gbd = din("gb", [128, 64])
    cwd = din("cw", [128, 40])
    cbd = din("cb", [128, 8])
    w_in = din("w_in", [1024, 4784])
    w_uq = din("w_uq", [384, 768])
    w_ukv = din("w_ukv", [256, 1024])
    w_bm = din("w_bm", [512, 1024])
    w_bl = din("w_bl", [512, 1024])
    w_out = din("w_out", [1024, 1024])
    w_up = din("w_up", [1024, 4096])
    w_down = din("w_down", [4096, 1024])
    y = nc.dram_tensor("y", [NOWN, 1024], F32, kind="ExternalOutput").ap()
    x1d = nc.dram_tensor("x1d", [NOWN, 1024], F32).ap()
    dbg = None
    if debug:
        dbg = nc.dram_tensor("dbg", [NOWN, 1024], F32, kind="ExternalOutput").ap()

    S = Sched()
    A = S.add
    al = Alloc(nc)
    st = ExitStack()
    psbig = [st.enter_context(nc.psum_tensor("psb%d" % i, [128, 1024], F32)) for i in range(4)]
    ps = [psbig[i // 2][:, (i % 2) * 512:(i % 2 + 1) * 512] for i in range(8)]

    def psb(i):
        return ps[i][:].bitcast(BF16)

    cst = al([128, 880], F32)
    idb = al([128, 128], BF16)
    mLEb = al([128, 128], BF16)
    mGEb = al([128, 128], BF16)
    gmix = al([128, 1024], F32)
    xt = [al([128, 1024], F32) for _ in range(2)]
    junk = al([128, 1024], BF16)
    hb = [al([128, 1024], BF16) for _ in range(2)]
    sst = al([128, 8], F32)
    wst = [al([128, 1024], F32) for _ in range(3)]
    LNSCt = al([128, 1], F32)
    ONEt = al([128, 1], F32)
    EPSt = al([128, 1], F32)
    idf = cst[:, 0:128]
    mLE = cst[:, 128:256]
    mGE = cst[:, 256:384]
    mSU = cst[:, 384:512]
    mSL = cst[:, 512:640]
    ones = cst[:, 640:768]
    sel = cst[0:32, 784:880]

    A("sp", lambda e: e.dma_start(out=cst[:], in_=cstd), writes=["cst"], dma=True)
    A("sp", lambda e: e.dma_start(out=gmix[:], in_=gmixd), writes=["gmix"], dma=True)
    A("dve", lambda e: e.memset(ONEt[:], 1.0), writes=["onet"])
    A("dve", lambda e: e.memset(EPSt[:], EPS), writes=["epst"])
    A("dve", lambda e: e.tensor_copy(out=idb[:], in_=idf), reads=["cst"], writes=["idb"])
    A("dve", lambda e: e.tensor_copy(out=mLEb[:], in_=mLE), reads=["cst"], writes=["mLEb"])
    A("dve", lambda e: e.tensor_copy(out=mGEb[:], in_=mGE), reads=["cst"], writes=["mGEb"])

    wstate = {"i": 0}

    def load_w(dst, dkey, src, K, c_lo, c_hi, dcol=0, queue="sp"):
        for k in range(K):
            c0 = c_lo
            while c0 < c_hi:
                cc = min(1024, c_hi - c0)
                sl = wstate["i"] % 3
                wstate["i"] += 1
                A(queue, lambda e, sl=sl, k=k, c0=c0, cc=cc: e.dma_start(out=wst[sl][:, 0:cc], in_=src[k * 128:(k + 1) * 128, c0:c0 + cc]),
                  writes=[("wst", sl)], dma=True)
                d0 = dcol + (c0 - c_lo)
                ce = ("pool", "act", "dve")[wstate["i"] % 3] if cc >= 256 else "pool"
                if ce == "act":
                    A("act", lambda e, sl=sl, k=k, d0=d0, cc=cc: e.copy(out=dst[:, k, d0:d0 + cc], in_=wst[sl][:, 0:cc]),
                      reads=[("wst", sl)], writes=[(dkey, k)])
                else:
                    A(ce, lambda e, sl=sl, k=k, d0=d0, cc=cc: e.tensor_copy(out=dst[:, k, d0:d0 + cc], in_=wst[sl][:, 0:cc]),
                      reads=[("wst", sl)], writes=[(dkey, k)])
                c0 += cc

    xstate = {"i": 0}

    def rstd_of(sskey_ap, n, key):
        A("act", lambda e: e.activation(out=sskey_ap, in_=sskey_ap, func=AF.Ln, scale=1.0 / n, bias=EPSt[:, 0:1]), reads=[key, "epst"], writes=[key])
        A("act", lambda e: e.activation(out=sskey_ap, in_=sskey_ap, func=AF.Exp, scale=-0.5), reads=[key], writes=[key])

    def xpipe(src_rows, g_tile, gkey, hT_dst, hkey, tps, keep=False):
        i = xstate["i"]
        xstate["i"] += 1
        sl = i % 2
        A("sp", lambda e: e.dma_start(out=xt[sl][:], in_=src_rows), writes=[("xt", sl)], dma=True)
        ssap = sst[:, sl:sl + 1]
        A("act", lambda e: e.activation(out=junk[:], in_=xt[sl][:], func=AF.Square, accum_out=ssap), reads=[("xt", sl)], writes=[("ss", sl)])
        rstd_of(ssap, 1024, ("ss", sl))
        A("dve", lambda e: e.scalar_tensor_tensor(out=hb[sl][:], in0=xt[sl][:], scalar=ssap, in1=g_tile[:], op0=ALU.mult, op1=ALU.mult),
          reads=[("xt", sl), ("ss", sl), gkey], writes=[("hb", sl)])
        pb = psb(tps)
        for k in range(8):
            A("pe", lambda e, k=k: e.transpose(out=pb[:, k * 128:(k + 1) * 128], in_=hb[sl][:, k * 128:(k + 1) * 128], identity=idb[:]),
              reads=[("hb", sl), "idb"], writes=[("ps", tps)])
        A("act", lambda e: e.copy(out=hT_dst, in_=pb.rearrange("p (k t) -> p k t", k=8)), reads=[("ps", tps)], writes=[hkey])
        return sl

    m_phase = al.mark()
    LGe = [al([128, 8], F32) for _ in range(2)]
    Ie = [al([128, 8], F32) for _ in range(2)]
    exg = [al([128, 8], F32) for _ in range(2)]
    Wt = [al([128, 8], F32) for _ in range(2)]
    dect = [al([128, 4], F32) for _ in range(2)]
    Arun = al([128, 4], F32)
    ktil = [al([128, 128], BF16) for _ in range(4)]
    Cf = al([128, 4, 129], F32)
    Cb = al([128, 4, 129], F32)
    Cfb = al([128, 4, 129], BF16)
    Cbb = al([128, 4, 129], BF16)
    qT = al([128, 4, NOWN], BF16)
    kT = al([128, 4, NOWN], BF16)
    ktok = al([128, 16, 512], BF16)
    vaug = al([128, 16, 4, 129], BF16)
    sgo = al([128, 16, 512], BF16)
    Gown = al([128, 16, 16], F32)
    LFo = al([128, 16, 8], F32)
    m_alias = al.mark()
    wl = al([128, 8, 2064], BF16)
    tf = al([128, 48], F32)
    tb = al([128, 48], F32)
    pf = al([128, 48], F32)
    pbk = al([128, 48], F32)
    gbias = al([128, 64], F32)
    cw = al([128, 40], F32)
    cb = al([128, 8], F32)
    gon = al([128, 512], F32)
    haloT = al([128, 8, 128], F32)
    hTg = [al([128, 8, 512], BF16) for _ in range(2)]
    padb = [al([128, 516], F32) for _ in range(2)]
    acc = [al([128, 512], F32) for _ in range(2)]
    kTg1 = al([128, 4, 512], BF16)
    ktokg1 = al([128, 4, 512], BF16)
    vaugg1 = al([128, 4, 4, 129], BF16)
    kTg = [kTg1, kTg1]
    ktokg = [ktokg1, ktokg1]
    vaugg = [vaugg1, vaugg1]
    Gg = [al([128, 4, 16], F32) for _ in range(2)]
    LFg = [al([128, 4, 8], F32) for _ in range(2)]
    sgt = [al([128, 512], BF16) for _ in range(2)]
    m_p12 = al.mark()
    al.reset(m_alias)
    ymT = al([128, 4, NOWN], BF16)
    m_keep = al.mark()
    EBt = al([128, 16, 8], F32)
    ECt = al([128, 16, 8], F32)
    WSt = al([128, 16, 8], F32)
    DECt = al([128, 16, 8], F32)
    cmt = al([128, 8], F32)
    HB = al([128, 16, 512], F32)
    PT = [al([128, 4, 128], BF16) for _ in range(2)]
    den2 = [al([128, 4], F32) for _ in range(2)]
    scl2 = [al([128, 4], F32) for _ in range(2)]
    hsum = [al([128, 4, 128], F32) for _ in range(2)]
    ssq2 = [al([128, 4], F32) for _ in range(2)]
    ymt = [al([128, 4, 128], BF16) for _ in range(2)]
    assert al.mark() <= m_p12
    al.reset(m_p12)

    for (t_, d_, k_) in ((tf, tfd, "tf"), (gbias, gbd, "gbias"), (cw, cwd, "cw"), (cb, cbd, "cb"), (gon, gond, "gon")):
        A("sp", lambda e, t_=t_, d_=d_: e.dma_start(out=t_[:], in_=d_), writes=[k_], dma=True)
    A("dve", lambda e: e.tensor_scalar(out=tb[:], in0=tf[:], scalar1=-1.0, scalar2=1.0, op0=ALU.mult, op1=ALU.add), reads=["tf"], writes=["tb"])
    A("dve", lambda e: e.tensor_scalar(out=pf[:], in0=tf[:], scalar1=-1.0, scalar2=BIG, op0=ALU.add, op1=ALU.mult), reads=["tf"], writes=["pf"])
    A("dve", lambda e: e.tensor_scalar(out=pbk[:], in0=tf[:], scalar1=-BIG, scalar2=None, op0=ALU.mult), reads=["tf"], writes=["pbk"])
    for t_, k_ in ((Cf, "Cf"), (Cb, "Cb"), (Arun, "Arun")):
        A("dve", lambda e, t_=t_: e.memset(t_[:], 0.0), writes=[k_])
    A("pool", lambda e: e.memset(vaug[:, :, :, 128:129], 1.0), writes=["vaug1"])
    A("pool", lambda e: e.memset(vaugg1[:, :, :, 128:129], 1.0), writes=[("vaugg1", 0)])

    load_w(wl, "wl", w_in, 8, 672, 2720, 0)
    for (s0, d0) in ((2720, 2048), (2728, 2052), (2724, 2056), (2732, 2060)):
        load_w(wl, "wl", w_in, 8, s0, s0 + 4, d0)
    WLK = [("wl", k) for k in range(8)]

    TPSX, TPSK, CPS0, CPS1, MVPS, UPS0 = 0, 1, 2, 3, 4, 5

    hTh = hTg[1][:, :, 0:128]
    xpipe(xh, gmix, "gmix", hTh, ("hTg", 1), TPSX)
    for c in range(8):
        bank = CPS0 + (c % 2)
        for k in range(8):
            A("pe", lambda e, c=c, k=k, bank=bank: e.matmul(ps[bank][:, 0:128], lhsT=wl[:, k, c * 128:(c + 1) * 128], rhs=hTg[1][:, k, 0:128],
                                                           start=(k == 0), stop=(k == 7)),
              reads=[("hTg", 1), ("wl", k)], writes=[("ps", bank)])
        A("act", lambda e, c=c, bank=bank: e.copy(out=haloT[:, c, :], in_=ps[bank][:, 0:128]), reads=[("ps", bank)], writes=["haloT"])

    cstate = {"i": 0}

    def conv_chunk(par, wcol, chunk, hidx, dst_ap, dst_key):
        ci = cstate["i"]
        cstate["i"] += 1
        bank = CPS0 + (ci % 2)
        pz = ci % 2
        for k in range(8):
            A("pe", lambda e, k=k: e.matmul(ps[bank][:], lhsT=wl[:, k, wcol:wcol + 128], rhs=hTg[par][:, k, :], start=(k == 0), stop=(k == 7)),
              reads=[("hTg", par), ("wl", k)], writes=[("ps", bank)])
        A("act", lambda e: e.copy(out=padb[pz][:, 2:514], in_=ps[bank][:]), reads=[("ps", bank)], writes=[("padb", pz)])
        A("pool", lambda e: e.tensor_copy(out=padb[pz][:, 0:2], in_=haloT[:, chunk, hidx:hidx + 2]), reads=["haloT"], writes=[("padbL", pz)])
        A("pool", lambda e: e.tensor_copy(out=padb[pz][:, 514:516], in_=haloT[:, chunk, hidx + 2:hidx + 4]), reads=["haloT"], writes=[("padbR", pz)])
        rk = [("padb", pz), ("padbL", pz), ("padbR", pz), "cw", "cb"]
        A("dve", lambda e: e.tensor_scalar(out=acc[pz][:], in0=padb[pz][:, 0:512], scalar1=cw[:, chunk * 5:chunk * 5 + 1], scalar2=cb[:, chunk:chunk + 1],
                                           op0=ALU.mult, op1=ALU.add), reads=rk, writes=[("acc", pz)])
        for j in range(1, 5):
            A("dve", lambda e, j=j: e.scalar_tensor_tensor(out=acc[pz][:], in0=padb[pz][:, j:j + 512], scalar=cw[:, chunk * 5 + j:chunk * 5 + j + 1],
                                                           in1=acc[pz][:], op0=ALU.mult, op1=ALU.add), reads=rk + [("acc", pz)], writes=[("acc", pz)])
        A("act", lambda e: e.activation(out=dst_ap, in_=acc[pz][:], func=AF.Silu), reads=[("acc", pz)], writes=[dst_key])

    def mlstm_pre(gi):
        own = gi >= 12
        par = gi % 2
        src = xown if own else xo
        g0 = (gi - 12) if own else gi
        for i in range(4):
            r0 = g0 * 512 + i * 128
            xpipe(src[r0:r0 + 128, :], gmix, "gmix", hTg[par][:, :, i * 128:(i + 1) * 128], ("hTg", par), TPSX)

    def mlstm_group(gi, own, mid=None):
        par = gi % 2
        g0 = (gi - 12) if own else gi
        hidx = (48 + 4 * g0) if own else 4 * g0
        for h in range(4):
            if own:
                conv_chunk(par, h * 128, h, hidx, qT[:, h, g0 * 512:(g0 + 1) * 512], "qT")
                conv_chunk(par, 512 + h * 128, 4 + h, hidx, kT[:, h, g0 * 512:(g0 + 1) * 512], "kT")
            else:
                conv_chunk(par, 512 + h * 128, 4 + h, hidx, kTg[par][:, h, :], ("kTg", 0))
        if mid is not None:
            mid()
        for b2 in range(2):
            pb = psb(TPSK)
            for bb in range(2):
                blk = b2 * 2 + bb
                for h in range(4):
                    if own:
                        src_ap = kT[:, h, g0 * 512 + blk * 128:g0 * 512 + (blk + 1) * 128]
                        rkey = "kT"
                    else:
                        src_ap = kTg[par][:, h, blk * 128:(blk + 1) * 128]
                        rkey = ("kTg", 0)
                    o0 = (bb * 4 + h) * 128
                    A("pe", lambda e, src_ap=src_ap, o0=o0: e.transpose(out=pb[:, o0:o0 + 128], in_=src_ap, identity=idb[:]),
                      reads=[rkey, "idb"], writes=[("ps", TPSK)])
            if own:
                dst = ktok[:, g0 * 4 + b2 * 2:g0 * 4 + b2 * 2 + 2, :]
                dk = "ktok"
            else:
                dst = ktokg[par][:, b2 * 2:b2 * 2 + 2, :]
                dk = ("ktokg", 0)
            A("act", lambda e, dst=dst, pb=pb: e.copy(out=dst, in_=pb.rearrange("p (b c) -> p b c", b=2)), reads=[("ps", TPSK)], writes=[dk])
        for i in range(4):
            mvb = (4, 5)[i % 2]
            for k in range(8):
                A("pe", lambda e, i=i, k=k: e.matmul(ps[mvb][:], lhsT=hTg[par][:, k, i * 128:(i + 1) * 128], rhs=wl[:, k, 1024:1536],
                                                     start=(k == 0), stop=(k == 7)), reads=[("hTg", par), ("wl", k)], writes=[("ps", mvb)])
            if own:
                dst = vaug[:, g0 * 4 + i, :, 0:128]
                dk = "vaug"
            else:
                dst = vaugg[par][:, i, :, 0:128]
                dk = ("vaugg", 0)
            A("dve", lambda e, dst=dst: e.tensor_copy(out=dst, in_=ps[mvb][:].rearrange("p (h d) -> p h d", h=4)), reads=[("ps", mvb)], writes=[dk])
            if own:
                mob = (5, 4)[i % 2]
                for k in range(8):
                    A("pe", lambda e, i=i, k=k: e.matmul(ps[mob][:], lhsT=hTg[par][:, k, i * 128:(i + 1) * 128], rhs=wl[:, k, 1536:2048],
                                                         start=(k == 0), stop=(k == 7)), reads=[("hTg", par), ("wl", k)], writes=[("ps", mob)])
                sp_ = i % 2
                A("act", lambda e, sp_=sp_: e.activation(out=sgt[sp_][:], in_=ps[mob][:], func=AF.Sigmoid), reads=[("ps", mob)], writes=[("sgt", sp_)])
                A("pool", lambda e, sp_=sp_, i=i: e.tensor_tensor(out=sgo[:, g0 * 4 + i, :], in0=sgt[sp_][:], in1=gon[:], op=ALU.mult),
                  reads=[("sgt", sp_), "gon"], writes=["sgo"])
        for i in range(4):
            for k in range(8):
                A("pe", lambda e, i=i, k=k: e.matmul(ps[MVPS][:, i * 16:(i + 1) * 16], lhsT=hTg[par][:, k, i * 128:(i + 1) * 128], rhs=wl[:, k, 2048:2064],
                                                     start=(k == 0), stop=(k == 7)), reads=[("hTg", par), ("wl", k)], writes=[("ps", MVPS)])
        if own:
            Gd = Gown[:, g0 * 4:g0 * 4 + 4, :]
            gk = "Gown"
            LFd = LFo[:, g0 * 4:g0 * 4 + 4, :]
            lk = "LFo"
        else:
            Gd = Gg[par][:]
            gk = ("Gg", par)
            LFd = LFg[par][:]
            lk = ("LFg", par)
        A("dve", lambda e: e.tensor_tensor(out=Gd, in0=ps[MVPS][:, 0:64].rearrange("p (b c) -> p b c", b=4),
                                           in1=gbias[:].rearrange("p (b c) -> p b c", b=4), op=ALU.add), reads=[("ps", MVPS), "gbias"], writes=[gk])
        A("act", lambda e: e.activation(out=LFd, in_=Gd[:, :, 8:16], func=AF.Exp, scale=-1.0), reads=[gk], writes=[lk])
        A("act", lambda e: e.activation(out=LFd, in_=LFd, func=AF.Ln, bias=ONEt[:, 0:1]), reads=[lk, "onet"], writes=[lk])
        if own:
            return
        for blk in range(4):
            gb_ = gi * 4 + blk
            z = blk % 2
            A("dve", lambda e: e.tensor_scalar(out=LGe[z][:, 0:4], in0=LFg[par][:, blk, 0:4], scalar1=tf[:, gb_:gb_ + 1], scalar2=-1.0, op0=ALU.mult, op1=ALU.mult),
              reads=[lk, "tf"], writes=[("LGe", z)])
            A("dve", lambda e: e.tensor_scalar(out=LGe[z][:, 4:8], in0=LFg[par][:, blk, 4:8], scalar1=tb[:, gb_:gb_ + 1], scalar2=-1.0, op0=ALU.mult, op1=ALU.mult),
              reads=[lk, "tb"], writes=[("LGe", z)])
            A("dve", lambda e: e.tensor_scalar(out=Ie[z][:, 0:4], in0=Gg[par][:, blk, 0:4], scalar1=pf[:, gb_:gb_ + 1], scalar2=None, op0=ALU.add),
              reads=[gk, "pf"], writes=[("Ie", z)])
            A("dve", lambda e: e.tensor_scalar(out=Ie[z][:, 4:8], in0=Gg[par][:, blk, 4:8], scalar1=pbk[:, gb_:gb_ + 1], scalar2=None, op0=ALU.add),
              reads=[gk, "pbk"], writes=[("Ie", z)])
            gp = UPS0 + 2
            A("pe", lambda e: e.matmul(ps[gp][:, 0:4], lhsT=mSU, rhs=LGe[z][:, 0:4], start=True, stop=True), reads=["cst", ("LGe", z)], writes=[("ps", gp)])
            A("pe", lambda e: e.matmul(ps[gp][:, 4:8], lhsT=mSL, rhs=LGe[z][:, 4:8], start=True, stop=True), reads=["cst", ("LGe", z)], writes=[("ps", gp)])
            A("pe", lambda e: e.matmul(ps[gp][:, 8:16], lhsT=ones, rhs=LGe[z][:, 0:8], start=True, stop=True), reads=["cst", ("LGe", z)], writes=[("ps", gp)])
            A("dve", lambda e: e.tensor_tensor(out=exg[z][:], in0=ps[gp][:, 0:8], in1=Ie[z][:], op=ALU.add), reads=[("ps", gp), ("Ie", z)], writes=[("exg", z)])
            A("dve", lambda e: e.tensor_tensor(out=exg[z][:, 4:8], in0=exg[z][:, 4:8], in1=Arun[:], op=ALU.add), reads=[("exg", z), "Arun"], writes=[("exg", z)])
            A("act", lambda e: e.activation(out=Wt[z][:], in_=exg[z][:], func=AF.Exp, bias=LNSCt[:, 0:1]), reads=[("exg", z), "lnsc"], writes=[("Wt", z)])
            A("act", lambda e: e.activation(out=dect[z][:], in_=ps[gp][:, 8:12], func=AF.Exp), reads=[("ps", gp)], writes=[("dect", z)])
            A("dve", lambda e: e.tensor_tensor(out=Arun[:], in0=Arun[:], in1=ps[gp][:, 12:16], op=ALU.add), reads=["Arun", ("ps", gp)], writes=["Arun"])
            for u in range(8):
                ch, h = divmod(u, 4)
                kz = u % 4
                A("dve", lambda e, u=u, h=h, kz=kz: e.tensor_scalar(out=ktil[kz][:], in0=ktokg[par][:, blk, h * 128:(h + 1) * 128],
                                                                    scalar1=Wt[z][:, u:u + 1], scalar2=None, op0=ALU.mult),
                  reads=[("ktokg", 0), ("Wt", z)], writes=[("ktil", kz)])
                bank = UPS0 + (u % 3)
                c0 = (u // 3) * 129 + (128 if bank == UPS0 + 2 else 0)
                A("pe", lambda e, h=h, kz=kz, bank=bank, c0=c0: e.matmul(ps[bank][:, c0:c0 + 129], lhsT=ktil[kz][:], rhs=vaugg[par][:, blk, h, :],
                                                                         start=True, stop=True),
                  reads=[("ktil", kz), ("vaugg", 0), ("vaugg1", 0)], writes=[("ps", bank)])
                if ch == 0:
                    A("dve", lambda e, h=h, bank=bank, c0=c0: e.scalar_tensor_tensor(out=Cf[:, h, :], in0=Cf[:, h, :], scalar=dect[z][:, h:h + 1],
                                                                                     in1=ps[bank][:, c0:c0 + 129], op0=ALU.mult, op1=ALU.add),
                      reads=["Cf", ("dect", z), ("ps", bank)], writes=["Cf"])
                else:
                    A("dve", lambda e, h=h, bank=bank, c0=c0: e.tensor_tensor(out=Cb[:, h, :], in0=Cb[:, h, :], in1=ps[bank][:, c0:c0 + 129], op=ALU.add),
                      reads=["Cb", ("ps", bank)], writes=["Cb"])

    A("dve", lambda e: e.memset(LNSCt[:], LNSC), writes=["lnsc"])

    gseq = list(range(12 if not (debug and (debug.endswith("_fast") or debug.startswith("p4"))) else 0))
    gseq += list(range(12, 16 if not (debug and debug.startswith("p4")) else 12))
    if gseq:
        mlstm_pre(gseq[0])
    for gidx, gi in enumerate(gseq):
        nxt = gseq[gidx + 1] if gidx + 1 < len(gseq) else None
        mlstm_group(gi, gi >= 12, mid=(lambda nxt=nxt: mlstm_pre(nxt)) if nxt is not None else None)

    if debug and debug.split("_")[0] in ("qTe", "kTe"):
        src_t = qT if debug.startswith("qTe") else kT
        dt_ = al([128, 1024], F32)
        for h in range(4):
            for half in range(2):
                A("dve", lambda e, h=h, half=half: e.tensor_copy(out=dt_[:], in_=src_t[:, h, half * 1024:(half + 1) * 1024]), reads=["qT", "kT"], writes=["dt_"])
                r0 = (h * 2 + half) * 128
                A("sp", lambda e, r0=r0: e.dma_start(out=dbg[r0:r0 + 128, :], in_=dt_[:]), reads=["dt_"], writes=["dbgo"], dma=True)
        A("sp", lambda e: e.nop(), reads=["dbgo"])
        S.emit(nc, st)
        st.close()
        return nc
    p12keys = [("wl", k) for k in range(8)] + ["tf", "tb", "pf", "pbk", "gbias", "cw", "cb", "gon", "haloT", ("hTg", 0), ("hTg", 1),
               ("padb", 0), ("padb", 1), ("padbL", 0), ("padbL", 1), ("padbR", 0), ("padbR", 1), ("acc", 0), ("acc", 1),
               ("kTg", 0), ("ktokg", 0), ("vaugg", 0), ("vaugg1", 0), ("Gg", 0), ("Gg", 1), ("LFg", 0), ("LFg", 1), ("sgt", 0), ("sgt", 1)]
    p3keys = ["EBt", "ECt", "WSt", "DECt", "cmt", ("PT", 0), ("PT", 1), ("hsum", 0), ("hsum", 1), ("ymt", 0), ("ymt", 1), "ymT"]
    p3keys += [("HB", b_) for b_ in range(16)] + [(n_, d_) for n_ in ("den", "scl", "ssq") for d_ in range(2)]
    A("dve", lambda e: e.memset(cmt[:], 0.0), writes=p12keys + p3keys)
    A("pool", lambda e: e.tensor_copy(out=Cfb[:], in_=Cf[:]), reads=["Cf"], writes=["Cfb"])
    A("pool", lambda e: e.tensor_copy(out=Cbb[:], in_=Cb[:]), reads=["Cb"], writes=["Cbb"])
    GP = 7
    for blk in range(16 if not (debug and debug.startswith("p4")) else 0):
        z = blk % 2
        A("dve", lambda e, blk=blk: e.tensor_scalar(out=LGe[z][:], in0=LFo[:, blk, :], scalar1=-1.0, scalar2=None, op0=ALU.mult), reads=["LFo"], writes=[("LGe", z)])
        A("pe", lambda e: e.matmul(ps[GP][:, 0:4], lhsT=mLE, rhs=LGe[z][:, 0:4], start=True, stop=True), reads=["cst", ("LGe", z)], writes=[("ps", GP)])
        A("pe", lambda e: e.matmul(ps[GP][:, 4:8], lhsT=mGE, rhs=LGe[z][:, 4:8], start=True, stop=True), reads=["cst", ("LGe", z)], writes=[("ps", GP)])
        A("pe", lambda e: e.matmul(ps[GP][:, 8:16], lhsT=ones, rhs=LGe[z][:, 0:8], start=True, stop=True), reads=["cst", ("LGe", z)], writes=[("ps", GP)])
        A("act", lambda e, blk=blk: e.activation(out=EBt[:, blk, :], in_=ps[GP][:, 0:8], func=AF.Exp), reads=[("ps", GP)], writes=["EBt"])
        A("dve", lambda e, blk=blk: e.tensor_tensor(out=cmt[:], in0=Gown[:, blk, 0:8], in1=ps[GP][:, 0:8], op=ALU.subtract), reads=["Gown", ("ps", GP)], writes=["cmt"])
        A("act", lambda e, blk=blk: e.activation(out=ECt[:, blk, :], in_=cmt[:], func=AF.Exp, bias=LNSCt[:, 0:1]), reads=["cmt", "lnsc"], writes=["ECt"])
        A("act", lambda e, blk=blk: e.activation(out=DECt[:, blk, :], in_=ps[GP][:, 8:16], func=AF.Exp), reads=[("ps", GP)], writes=["DECt"])
        A("dve", lambda e, blk=blk: e.tensor_tensor(out=cmt[:], in0=cmt[:], in1=ps[GP][:, 8:16], op=ALU.add), reads=["cmt", ("ps", GP)], writes=["cmt"])
        A("act", lambda e, blk=blk: e.activation(out=WSt[:, blk, :], in_=cmt[:], func=AF.Exp, bias=LNSCt[:, 0:1]), reads=["cmt", "lnsc"], writes=["WSt"])

    SPS, ND0, ND1, UB0, UB1, TPY = 0, 1, 2, 3, 4, 5
    def dstep(d, bi):
        Cx, Cxb, ck, ckb = (Cf, Cfb, "Cf", "Cfb") if d == 0 else (Cb, Cbb, "Cb", "Cbb")
        maskb = mLEb if d == 0 else mGEb
        mk_ = "mLEb" if d == 0 else "mGEb"
        final = (bi >= 8)
        if True:
            blk = bi if d == 0 else 15 - bi
            z = d
            SPS = (0, 6)[d]
            den = den2[d]
            scl = scl2[d]
            ssq = ssq2[d]
            tsl = slice(blk * 128, (blk + 1) * 128)
            for h in range(4):
                A("pe", lambda e, h=h: e.matmul(ps[SPS][:, h * 128:(h + 1) * 128], lhsT=kT[:, h, tsl], rhs=qT[:, h, tsl], start=True, stop=True),
                  reads=["kT", "qT"], writes=[("ps", SPS)])
            for h in range(4):
                A("dve", lambda e, h=h: e.scalar_tensor_tensor(out=PT[z][:, h, :], in0=ps[SPS][:, h * 128:(h + 1) * 128], scalar=ECt[:, blk, d * 4 + h:d * 4 + h + 1],
                                                               in1=maskb[:], op0=ALU.mult, op1=ALU.mult),
                  reads=[("ps", SPS), "ECt", mk_], writes=[("PT", z)])
            for h in range(4):
                bank = ND0 + h % 2
                c0 = (h // 2) * 129
                A("pe", lambda e, h=h, bank=bank, c0=c0: e.matmul(ps[bank][:, c0:c0 + 129], lhsT=PT[z][:, h, :], rhs=vaug[:, blk, h, :], start=True, stop=False),
                  reads=[("PT", z), "vaug", "vaug1"], writes=[("ps", bank)])
                A("pe", lambda e, h=h, bank=bank, c0=c0: e.matmul(ps[bank][:, c0:c0 + 129], lhsT=qT[:, h, tsl], rhs=Cxb[:, h, :], start=False, stop=True),
                  reads=["qT", ckb], writes=[("ps", bank)])
            for h in range(4):
                bank = ND0 + h % 2
                c0 = (h // 2) * 129
                A("dve", lambda e, h=h, bank=bank, c0=c0: e.tensor_tensor(out=den[:, h:h + 1], in0=ps[bank][:, c0 + 128:c0 + 129],
                                                                          in1=EBt[:, blk, d * 4 + h:d * 4 + h + 1], op=ALU.mult),
                  reads=[("ps", bank), "EBt"], writes=[("den", d)])
            A("dve", lambda e: e.tensor_scalar(out=scl[:], in0=den[:], scalar1=-1.0, scalar2=None, op0=ALU.mult), reads=[("den", d)], writes=[("scl", d)])
            A("dve", lambda e: e.tensor_tensor(out=den[:], in0=den[:], in1=scl[:], op=ALU.max), reads=[("den", d), ("scl", d)], writes=[("den", d)])
            A("dve", lambda e: e.tensor_scalar(out=den[:], in0=den[:], scalar1=1.0, scalar2=None, op0=ALU.max), reads=[("den", d)], writes=[("den", d)])
            A("dve", lambda e: e.reciprocal(out=den[:], in_=den[:]), reads=[("den", d)], writes=[("den", d)])
            A("dve", lambda e: e.tensor_tensor(out=scl[:], in0=den[:], in1=EBt[:, blk, d * 4:d * 4 + 4], op=ALU.mult), reads=[("den", d), "EBt"], writes=[("scl", d)])
            for h in range(4):
                bank = ND0 + h % 2
                c0 = (h // 2) * 129
                if not final:
                    A("dve", lambda e, h=h, bank=bank, c0=c0: e.tensor_scalar(out=HB[:, blk, h * 128:(h + 1) * 128], in0=ps[bank][:, c0:c0 + 128],
                                                                              scalar1=scl[:, h:h + 1], scalar2=None, op0=ALU.mult),
                      reads=[("ps", bank), ("scl", d)], writes=[("HB", blk)])
                else:
                    A("dve", lambda e, h=h, bank=bank, c0=c0: e.scalar_tensor_tensor(out=hsum[z][:, h, :], in0=ps[bank][:, c0:c0 + 128], scalar=scl[:, h:h + 1],
                                                                                     in1=HB[:, blk, h * 128:(h + 1) * 128], op0=ALU.mult, op1=ALU.add),
                      reads=[("ps", bank), ("scl", d), ("HB", blk)], writes=[("hsum", z)])
            if bi < 15:
                for h in range(4):
                    kz = d * 2 + (h % 2)
                    bank = UB0 + h % 2
                    c0 = (h // 2) * 129
                    A("dve", lambda e, h=h, kz=kz: e.tensor_scalar(out=ktil[kz][:], in0=ktok[:, blk, h * 128:(h + 1) * 128],
                                                                   scalar1=WSt[:, blk, d * 4 + h:d * 4 + h + 1], scalar2=None, op0=ALU.mult),
                      reads=["ktok", "WSt"], writes=[("ktil", kz)])
                    A("pe", lambda e, h=h, kz=kz, bank=bank, c0=c0: e.matmul(ps[bank][:, c0:c0 + 129], lhsT=ktil[kz][:], rhs=vaug[:, blk, h, :], start=True, stop=True),
                      reads=[("ktil", kz), "vaug", "vaug1"], writes=[("ps", bank)])
                    A("dve", lambda e, h=h, bank=bank, c0=c0: e.scalar_tensor_tensor(out=Cx[:, h, :], in0=Cx[:, h, :], scalar=DECt[:, blk, d * 4 + h:d * 4 + h + 1],
                                                                                     in1=ps[bank][:, c0:c0 + 129], op0=ALU.mult, op1=ALU.add),
                      reads=[ck, "DECt", ("ps", bank)], writes=[ck])
                A("pool", lambda e: e.tensor_copy(out=Cxb[:], in_=Cx[:]), reads=[ck], writes=[ckb])
            if final:
                for h in range(4):
                    A("act", lambda e, h=h: e.activation(out=junk[:, 0:128], in_=hsum[z][:, h, :], func=AF.Square, accum_out=ssq[:, h:h + 1]),
                      reads=[("hsum", z)], writes=[("ssq", d)])
                rstd_of(ssq[:], 128, ("ssq", d))
                for h in range(4):
                    A("dve", lambda e, h=h: e.scalar_tensor_tensor(out=ymt[z][:, h, :], in0=hsum[z][:, h, :], scalar=ssq[:, h:h + 1],
                                                                   in1=sgo[:, blk, h * 128:(h + 1) * 128], op0=ALU.mult, op1=ALU.mult),
                      reads=[("hsum", z), ("ssq", d), "sgo"], writes=[("ymt", z)])
                pb = psb(TPY)
                for h in range(4):
                    A("pe", lambda e, h=h: e.transpose(out=pb[:, h * 128:(h + 1) * 128], in_=ymt[z][:, h, :], identity=idb[:]),
                      reads=[("ymt", z), "idb"], writes=[("ps", TPY)])
                A("act", lambda e: e.copy(out=ymT[:, :, tsl], in_=pb[:, 0:512].rearrange("p (h t) -> p h t", h=4)), reads=[("ps", TPY)], writes=["ymT"])

    if not (debug and debug.startswith("p4")):
        for i_ in range(16):
            dstep(1, i_)
            dstep(0, i_)

    if debug and debug.split("_")[0] in ("mlstm", "qT", "kT"):
        debug = debug.split("_")[0]
        if debug == "qT":
            ymT = qT
        elif debug == "kT":
            ymT = kT
        ymk = {"mlstm": "ymT", "qT": "qT", "kT": "kT"}[debug]
        dt_ = al([128, 1024], F32)
        for h in range(4):
            for half in range(2):
                A("dve", lambda e, h=h, half=half: e.tensor_copy(out=dt_[:], in_=ymT[:, h, half * 1024:(half + 1) * 1024]), reads=[ymk], writes=["dt_"])
                r0 = (h * 2 + half) * 128
                A("sp", lambda e, r0=r0: e.dma_start(out=dbg[r0:r0 + 128, :], in_=dt_[:]), reads=["dt_"], writes=["dbgo"], dma=True)
        A("sp", lambda e: e.nop(), reads=["dbgo"])
        S.emit(nc, st)
        st.close()
        return nc

    S.barrier(lambda e: e.memset(sst[:, 7:8], 0.0))
    al.set_regions([(m_phase, m_alias), (m_keep, SB_HI)])
    R = al.ralloc
    ckvT = R([128, 2, 8192], BF16)
    kropeT = R([32, 8192], BF16)
    QT = R([96, 8, NOWN], BF16)
    yaT = R([128, 4, NOWN], BF16)
    reg_p4 = [list(r) for r in al.regions]
    wkv = R([128, 8, 288], BF16)
    wq = R([128, 8, 384], BF16)
    wuq = R([128, 3, 768], BF16)
    gq = R([128, 384], F32)
    gkv = R([128, 256], F32)
    hT4 = [R([128, 8, 128], BF16) for _ in range(2)]
    sinT = R([128, 80, 16], F32)
    cosT = R([128, 80, 16], F32)
    sinq = R([128, 16, 4, 16], F32)
    cosq = R([128, 16, 4, 16], F32)
    reg_tmp = [list(r) for r in al.regions]
    posi = R([128, 80], I32)
    posf = R([128, 80], F32)
    ang = R([128, 80, 16], F32)
    tq = R([128, 1280], F32)
    tk = R([128, 1280], I32)
    tkf = R([128, 1280], F32)

    load_w(wkv, "wkv", w_in, 8, 384, 672)
    load_w(wq, "wq", w_in, 8, 0, 384)
    load_w(wuq, "wuq", w_uq, 3, 0, 768)
    A("sp", lambda e: e.dma_start(out=gq[:], in_=gqd), writes=["gq"], dma=True)
    A("sp", lambda e: e.dma_start(out=gkv[:], in_=gkvd), writes=["gkv"], dma=True)
    A("sp", lambda e: e.dma_start(out=posi[:], in_=posT), writes=["posi"], dma=True)
    A("dve", lambda e: e.tensor_copy(out=posf[:], in_=posi[:]), reads=["posi"], writes=["posf"])
    invf = (np.float32(10000.0) ** (-np.arange(0, 32, 2, dtype=np.float32) / np.float32(32))).astype(np.float32)
    for f in range(16):
        A("dve", lambda e, f=f: e.tensor_scalar(out=ang[:, :, f], in0=posf[:], scalar1=float(invf[f]), scalar2=None, op0=ALU.mult), reads=["posf"], writes=["ang"])
    angf = ang[:].rearrange("p a b -> p (a b)")
    TWO_PI = 2.0 * math.pi
    for (dst, off) in ((sinT, 0.0), (cosT, 0.25)):
        dstf = dst[:].rearrange("p a b -> p (a b)")
        A("dve", lambda e, off=off: e.tensor_scalar(out=tq[:], in0=angf, scalar1=1.0 / TWO_PI, scalar2=off, op0=ALU.mult, op1=ALU.add), reads=["ang"], writes=["tq"])
        A("dve", lambda e: e.tensor_copy(out=tk[:], in_=tq[:]), reads=["tq"], writes=["tk"])
        A("dve", lambda e: e.tensor_copy(out=tkf[:], in_=tk[:]), reads=["tk"], writes=["tkf"])
        A("dve", lambda e: e.tensor_tensor(out=tq[:], in0=tq[:], in1=tkf[:], op=ALU.subtract), reads=["tq", "tkf"], writes=["tq"])
        A("dve", lambda e: e.tensor_scalar(out=tkf[:], in0=tq[:], scalar1=0.5, scalar2=None, op0=ALU.is_gt), reads=["tq"], writes=["tkf"])
        A("dve", lambda e: e.tensor_tensor(out=tq[:], in0=tq[:], in1=tkf[:], op=ALU.subtract), reads=["tq", "tkf"], writes=["tq"])
        A("dve", lambda e: e.tensor_scalar(out=tkf[:], in0=tq[:], scalar1=-0.5, scalar2=None, op0=ALU.is_lt), reads=["tq"], writes=["tkf"])
        A("dve", lambda e: e.tensor_tensor(out=tq[:], in0=tq[:], in1=tkf[:], op=ALU.add), reads=["tq", "tkf"], writes=["tq"])
        A("dve", lambda e: e.tensor_scalar(out=tq[:], in0=tq[:], scalar1=-0.4999, scalar2=0.4999, op0=ALU.max, op1=ALU.min), reads=["tq"], writes=["tq"])
        A("act", lambda e, dstf=dstf: e.activation(out=dstf, in_=tq[:], func=AF.Sin, scale=TWO_PI), reads=["tq"], writes=["sincos"])
    for hh in range(4):
        A("dve", lambda e, hh=hh: e.tensor_copy(out=sinq[:, :, hh, :], in_=sinT[:, 64:80, :]), reads=["sincos"], writes=["sinq"])
        A("dve", lambda e, hh=hh: e.tensor_copy(out=cosq[:, :, hh, :], in_=cosT[:, 64:80, :]), reads=["sincos"], writes=["cosq"])

    if debug == "p4a":
        A("sp", lambda e: e.dma_start(out=dbg[0:128, 0:1024], in_=sinT[:].rearrange("p a b -> p (a b)")[:, 0:1024]), reads=["sincos"], writes=["dbgo"], dma=True)
        A("sp", lambda e: e.nop(), reads=["dbgo"])
        S.emit(nc, st)
        st.close()
        return nc
    S.barrier(lambda e: e.memset(sst[:, 7:8], 0.0))
    al.regions = [list(r) for r in reg_tmp]
    cn = [R([128, 384], BF16) for _ in range(2)]
    krr = [R([128, 32], BF16) for _ in range(2)]
    rt = [R([128, 4, 16], F32) for _ in range(8)]
    cqnT = R([128, 3, 128], BF16)
    qtok = R([128, 8, 96], BF16)
    TPSX, LAT0, TPC, Q0, Q1, TPQ = 0, 1, 3, 4, 5, 6
    lstate = {"i": 0}

    def rope(x1, x2, cs, sn, o1, o2, rkeys, okey, shape4):
        rs_ = lstate["i"] % 2
        lstate["i"] += 1
        ta, tb_, tc, td = [t[:] if shape4 else t[:, 0, :] for t in rt[rs_ * 4:rs_ * 4 + 4]]
        k0, k1, k2, k3 = [("rt", rs_, q_) for q_ in range(4)]
        A("dve", lambda e: e.tensor_tensor(out=ta, in0=x1, in1=cs, op=ALU.mult), reads=rkeys, writes=[k0])
        A("dve", lambda e: e.tensor_tensor(out=tb_, in0=x2, in1=sn, op=ALU.mult), reads=rkeys, writes=[k1])
        A("dve", lambda e: e.tensor_tensor(out=o1, in0=ta, in1=tb_, op=ALU.subtract), reads=[k0, k1], writes=[okey])
        A("dve", lambda e: e.tensor_tensor(out=tc, in0=x2, in1=cs, op=ALU.mult), reads=rkeys, writes=[k2])
        A("dve", lambda e: e.tensor_tensor(out=td, in0=x1, in1=sn, op=ALU.mult), reads=rkeys, writes=[k3])
        A("dve", lambda e: e.tensor_tensor(out=o2, in0=tc, in1=td, op=ALU.add), reads=[k2, k3], writes=[okey])

    def kv_pre(kt):
        z = kt % 2
        src = xo[kt * 128:(kt + 1) * 128, :] if kt < 48 else xown[(kt - 48) * 128:(kt - 47) * 128, :]
        xpipe(src, gmix, "gmix", hT4[z][:], ("hT4", z), TPSX)

    kv_pre(0)
    for kt in range(64):
        z = kt % 2
        if kt + 1 < 64:
            kv_pre(kt + 1)
        lat = LAT0 + z
        for k in range(8):
            A("pe", lambda e, k=k: e.matmul(ps[lat][:, 0:288], lhsT=hT4[z][:, k, :], rhs=wkv[:, k, :], start=(k == 0), stop=(k == 7)),
              reads=[("hT4", z), ("wkv", k)], writes=[("ps", lat)])
        ssap = sst[:, 2 + z:3 + z]
        A("act", lambda e: e.activation(out=junk[:, 0:256], in_=ps[lat][:, 0:256], func=AF.Square, accum_out=ssap), reads=[("ps", lat)], writes=[("ssl", z)])
        rstd_of(ssap, 256, ("ssl", z))
        A("dve", lambda e: e.scalar_tensor_tensor(out=cn[z][:, 0:256], in0=ps[lat][:, 0:256], scalar=ssap, in1=gkv[:], op0=ALU.mult, op1=ALU.mult),
          reads=[("ps", lat), ("ssl", z), "gkv"], writes=[("cn", z)])
        rope(ps[lat][:, 256:272], ps[lat][:, 272:288], cosT[:, kt, :], sinT[:, kt, :], krr[z][:, 0:16], krr[z][:, 16:32],
             [("ps", lat), "sincos"], ("krr", z), False)
        tpc = (3, 7)[z]
        pb = psb(tpc)
        for c in range(2):
            A("pe", lambda e, c=c: e.transpose(out=pb[:, c * 128:(c + 1) * 128], in_=cn[z][:, c * 128:(c + 1) * 128], identity=idb[:]),
              reads=[("cn", z), "idb"], writes=[("ps", tpc)])
        A("pe", lambda e: e.transpose(out=pb[0:32, 256:384], in_=krr[z][:], identity=idb[:]), reads=[("krr", z), "idb"], writes=[("ps", tpc)])
        A("act", lambda e: e.copy(out=ckvT[:, :, kt * 128:(kt + 1) * 128], in_=pb[:, 0:256].rearrange("p (c t) -> p c t", c=2)), reads=[("ps", tpc)], writes=["ckvT"])
        A("act", lambda e: e.copy(out=kropeT[:, kt * 128:(kt + 1) * 128], in_=pb[0:32, 256:384]), reads=[("ps", tpc)], writes=["kropeT"])

    def q_pre(ot):
        z = ot % 2
        xpipe(xown[ot * 128:(ot + 1) * 128, :], gmix, "gmix", hT4[z][:], ("hT4", z), TPSX)

    n_ot = 16 if debug != "p4b" else 0
    if n_ot:
        q_pre(0)
    for ot in range(n_ot):
        z = ot % 2
        if ot + 1 < n_ot:
            q_pre(ot + 1)
        lat = LAT0 + z
        for k in range(8):
            A("pe", lambda e, k=k: e.matmul(ps[lat][:, 0:384], lhsT=hT4[z][:, k, :], rhs=wq[:, k, :], start=(k == 0), stop=(k == 7)),
              reads=[("hT4", z), ("wq", k)], writes=[("ps", lat)])
        ssap = sst[:, 2 + z:3 + z]
        A("act", lambda e: e.activation(out=junk[:, 0:384], in_=ps[lat][:, 0:384], func=AF.Square, accum_out=ssap), reads=[("ps", lat)], writes=[("ssl", z)])
        rstd_of(ssap, 384, ("ssl", z))
        A("dve", lambda e: e.scalar_tensor_tensor(out=cn[z][:], in0=ps[lat][:, 0:384], scalar=ssap, in1=gq[:], op0=ALU.mult, op1=ALU.mult),
          reads=[("ps", lat), ("ssl", z), "gq"], writes=[("cn", z)])
        pb = psb(TPC)
        for c in range(3):
            A("pe", lambda e, c=c: e.transpose(out=pb[:, c * 128:(c + 1) * 128], in_=cn[z][:, c * 128:(c + 1) * 128], identity=idb[:]),
              reads=[("cn", z), "idb"], writes=[("ps", TPC)])
        A("act", lambda e: e.copy(out=cqnT[:], in_=pb[:, 0:384].rearrange("p (c t) -> p c t", c=3)), reads=[("ps", TPC)], writes=["cqnT"])
        qlvl = int(debug[3:]) if (debug and debug.startswith("p4q")) else 9
        if qlvl < 2:
            continue
        for x_ in range(2):
            qb = Q0 + x_
            for c in range(3):
                A("pe", lambda e, c=c: e.matmul(ps[qb][:, 0:384], lhsT=cqnT[:, c, :], rhs=wuq[:, c, x_ * 384:(x_ + 1) * 384], start=(c == 0), stop=(c == 2)),
                  reads=["cqnT", ("wuq", c)], writes=[("ps", qb)])
            V4 = ps[qb][:, 0:384].rearrange("p (h d) -> p h d", h=4)
            A("act", lambda e: e.copy(out=qtok[:, x_ * 4:x_ * 4 + 4, 0:64], in_=V4[:, :, 0:64]), reads=[("ps", qb)], writes=["qtokn"])
            if qlvl < 3:
                continue
            for hh in range(4):
                c0 = hh * 96
                rope(ps[qb][:, c0 + 64:c0 + 80], ps[qb][:, c0 + 80:c0 + 96], cosT[:, 64 + ot, :], sinT[:, 64 + ot, :],
                     qtok[:, x_ * 4 + hh, 64:80], qtok[:, x_ * 4 + hh, 80:96], [("ps", qb), "sincos"], "qtokr", False)
        if qlvl < 4:
            continue
        pq = psb(TPQ)
        for h in range(8):
            A("pe", lambda e, h=h: e.transpose(out=pq[0:96, h * 128:(h + 1) * 128], in_=qtok[:, h, :], identity=idb[:]),
              reads=["qtokn", "qtokr", "idb"], writes=[("ps", TPQ)])
        A("act", lambda e: e.copy(out=QT[:, :, ot * 128:(ot + 1) * 128], in_=pq[0:96, :].rearrange("p (h t) -> p h t", h=8)), reads=[("ps", TPQ)], writes=["QT"])

    if debug and debug.startswith("p4"):
        dtt = R([128, 1024], F32)
        for h in range(8):
            for half in range(2):
                A("dve", lambda e, h=h, half=half: e.tensor_copy(out=dtt[0:96, :], in_=QT[:, h, half * 1024:(half + 1) * 1024]), reads=["QT"], writes=["dtt"])
                r0 = (h * 2 + half) * 96
                A("sp", lambda e, r0=r0: e.dma_start(out=dbg[r0:r0 + 96, :], in_=dtt[0:96, :]), reads=["dtt"], writes=["dbgo"], dma=True)
        A("dve", lambda e: e.tensor_copy(out=dtt[0:32, :], in_=kropeT[:, 7168:8192]), reads=["kropeT"], writes=["dtt"])
        A("sp", lambda e: e.dma_start(out=dbg[1536:1568, :], in_=dtt[0:32, :]), reads=["dtt"], writes=["dbgo"], dma=True)
        A("dve", lambda e: e.tensor_copy(out=dtt[:], in_=ckvT[:, 1, 7168:8192]), reads=["ckvT"], writes=["dtt"])
        A("sp", lambda e: e.dma_start(out=dbg[1664:1792, :], in_=dtt[:]), reads=["dtt"], writes=["dbgo"], dma=True)
        A("sp", lambda e: e.nop(), reads=["dbgo"])
        S.emit(nc, st)
        st.close()
        return nc
    S.barrier(lambda e: e.memset(sst[:, 7:8], 0.0))
    al.regions = [list(r) for r in reg_p4]
    wkp = R([128, 2, 8, 96], BF16)
    wv = R([128, 2, 512], BF16)
    selb = R([32, 96], BF16)
    KT0 = R([96, 8192], BF16)
    KT = [KT0, KT0]
    VA = [R([128, 64, 65], BF16) for _ in range(2)]
    yattn = R([128, 16, 512], BF16)
    rden = R([128, 4], F32)
    A("pool", lambda e: e.memset(wkp[:], 0.0), writes=["wkp"])
    A("dve", lambda e: e.tensor_copy(out=selb[:], in_=sel), reads=["cst"], writes=["selb"])
    for b_ in range(2):
        A("pool", lambda e, b_=b_: e.memset(VA[b_][:, :, 64:65], 1.0), writes=[("VA1", b_)])
    for c in range(2):
        sl = wstate["i"] % 2
        wstate["i"] += 1
        A("sp", lambda e, c=c, sl=sl: e.dma_start(out=wst[sl][:], in_=w_ukv[c * 128:(c + 1) * 128, :]), writes=[("wst", sl)], dma=True)
        W3 = wst[sl][:].rearrange("p (h d) -> p h d", h=8)
        A("pool", lambda e, c=c: e.tensor_copy(out=wkp[:, c, :, 0:64], in_=W3[:, :, 0:64]), reads=[("wst", sl), "wkp"], writes=["wkp"])
        A("pool", lambda e, c=c: e.tensor_copy(out=wv[:, c, :].rearrange("p (h d) -> p h d", h=8), in_=W3[:, :, 64:128]), reads=[("wst", sl)], writes=["wv"])

    OB_, KB_, VB_ = (6, 7), (0, 1), 2
    SCALE = 96.0 ** -0.5
    Pb2 = [R([128, 1024], BF16) for _ in range(3)]
    kbs = {"i": 0}
    for h in range(8):
        bf = h % 2
        for grp in range(16):
            kb = KB_[kbs["i"] % 2]
            kbs["i"] += 1
            gs = slice(grp * 512, (grp + 1) * 512)
            A("pe", lambda e: e.matmul(ps[kb][0:96, :], lhsT=wkp[:, 0, h, :], rhs=ckvT[:, 0, gs], start=True, stop=False), reads=["wkp", "ckvT"], writes=[("ps", kb)])
            A("pe", lambda e: e.matmul(ps[kb][0:96, :], lhsT=wkp[:, 1, h, :], rhs=ckvT[:, 1, gs], start=False, stop=False), reads=["wkp", "ckvT"], writes=[("ps", kb)])
            A("pe", lambda e: e.matmul(ps[kb][0:96, :], lhsT=selb[:], rhs=kropeT[:, gs], start=False, stop=True), reads=["selb", "kropeT"], writes=[("ps", kb)])
            A("dve", lambda e: e.tensor_copy(out=KT[bf][:, gs], in_=ps[kb][0:96, :]), reads=[("ps", kb)], writes=[("KT", 0)])
        for tg in range(8):
            vb = 2 + (tg % 2)
            for tl in range(8):
                kt = tg * 8 + tl
                for c in range(2):
                    A("pe", lambda e, c=c, tl=tl, kt=kt: e.matmul(ps[vb][:, tl * 64:(tl + 1) * 64], lhsT=ckvT[:, c, kt * 128:(kt + 1) * 128],
                                                                 rhs=wv[:, c, h * 64:(h + 1) * 64], start=(c == 0), stop=(c == 1)),
                      reads=["ckvT", "wv"], writes=[("ps", vb)])
            A("act", lambda e: e.copy(out=VA[bf][:, tg * 8:(tg + 1) * 8, 0:64], in_=ps[vb][:].rearrange("p (t d) -> p t d", t=8)), reads=[("ps", vb)], writes=[("VA", bf)])
        for tt in range(4):
            ob = OB_[(h * 4 + tt) % 2]
            qs = slice(tt * 512, (tt + 1) * 512)

            def s_mm(sp_):
                j = sp_ % 3
                for half in range(2):
                    st_ = 2 * sp_ + half
                    A("pe", lambda e: e.matmul(ps[2 * j + half][:], lhsT=KT[bf][:, st_ * 128:(st_ + 1) * 128], rhs=QT[:, h, qs], start=True, stop=True),
                      reads=[("KT", 0), "QT"], writes=[("ps", 2 * j + half)])
                A("act", lambda e: e.activation(out=Pb2[j][:], in_=psbig[j][:], func=AF.Exp, scale=SCALE),
                  reads=[("ps", 2 * j), ("ps", 2 * j + 1)], writes=[("Pb", j)])

            s_mm(0)
            s_mm(1)
            for sp_ in range(32):
                j = sp_ % 3
                for half in range(2):
                    st_ = 2 * sp_ + half
                    for qq in range(4):
                        A("pe", lambda e, qq=qq: e.matmul(ps[ob][:, qq * 65:(qq + 1) * 65], lhsT=Pb2[j][:, half * 512 + qq * 128:half * 512 + (qq + 1) * 128],
                                                          rhs=VA[bf][:, st_, :], start=(st_ == 0 and qq == 0), stop=(st_ == 63), skip_group_check=True),
                          reads=[("Pb", j), ("VA", bf), ("VA1", bf)], writes=[("ps", ob)])
                if sp_ + 2 < 32:
                    s_mm(sp_ + 2)
            O3 = ps[ob][:, 0:260].rearrange("p (q d) -> p q d", q=4)
            A("dve", lambda e: e.reciprocal(out=rden[:], in_=O3[:, :, 64]), reads=[("ps", ob)], writes=["rden"])
            for qq in range(4):
                A("dve", lambda e, qq=qq: e.tensor_scalar(out=yattn[:, tt * 4 + qq, h * 64:(h + 1) * 64], in0=O3[:, qq, 0:64], scalar1=rden[:, qq:qq + 1],
                                                          scalar2=None, op0=ALU.mult), reads=[("ps", ob), "rden"], writes=["yattn"])
    for tile in range(16):
        pb = psb(VB_)
        for c in range(4):
            A("pe", lambda e, c=c: e.transpose(out=pb[:, c * 128:(c + 1) * 128], in_=yattn[:, tile, c * 128:(c + 1) * 128], identity=idb[:]),
              reads=["yattn", "idb"], writes=[("ps", VB_)])
        A("act", lambda e: e.copy(out=yaT[:, :, tile * 128:(tile + 1) * 128], in_=pb[:, 0:512].rearrange("p (c t) -> p c t", c=4)), reads=[("ps", VB_)], writes=["yaT"])

    S.barrier(lambda e: e.memset(sst[:, 7:8], 0.0))
    al.regions = [[m_phase, m_alias], [reg_p4[1][0], SB_HI]]
    wgab = R([128, 8, 2048], BF16)
    wbm = R([128, 4, 1024], BF16)
    wbl = R([128, 4, 1024], BF16)
    wo = R([128, 8, 1024], BF16)
    hT6 = [R([128, 8, 128], BF16) for _ in range(2)]
    sg = R([128, 2048], BF16)
    t1 = R([128, 1024], F32)
    t2 = R([128, 1024], F32)
    mrg = R([128, 1024], BF16)
    mT = R([128, 8, 128], BF16)
    x1t = [R([128, 1024], F32) for _ in range(2)]
    load_w(wgab, "wgab", w_in, 8, 2736, 4784)
    load_w(wbm, "wbm", w_bm, 4, 0, 1024)
    load_w(wbl, "wbl", w_bl, 4, 0, 1024)
    load_w(wo, "wo", w_out, 8, 0, 1024)
    def m_pre(ot):
        z = ot % 2
        return xpipe(xown[ot * 128:(ot + 1) * 128, :], gmix, "gmix", hT6[z][:], ("hT6", z), 7)

    sl_next = m_pre(0)
    for ot in range(16):
        z = ot % 2
        tsl = slice(ot * 128, (ot + 1) * 128)
        sl = sl_next
        if ot + 1 < 16:
            sl_next = m_pre(ot + 1)
        for cb_ in range(4):
            for k in range(8):
                A("pe", lambda e, k=k, cb_=cb_: e.matmul(ps[cb_][:], lhsT=hT6[z][:, k, :], rhs=wgab[:, k, cb_ * 512:(cb_ + 1) * 512], start=(k == 0), stop=(k == 7)),
                  reads=[("hT6", z), ("wgab", k)], writes=[("ps", cb_)])
            A("act", lambda e, cb_=cb_: e.activation(out=sg[:, cb_ * 512:(cb_ + 1) * 512], in_=ps[cb_][:], func=AF.Sigmoid), reads=[("ps", cb_)], writes=[("sg", cb_)])
        for half in range(2):
            for (wt, wk, aT_, ak, bank0) in ((wbm, "wbm", yaT, "yaT", 4), (wbl, "wbl", ymT, "ymT", 5)):
                bank = bank0
                for c in range(4):
                    A("pe", lambda e, c=c, wt=wt, aT_=aT_, bank=bank: e.matmul(ps[bank][:], lhsT=aT_[:, c, tsl], rhs=wt[:, c, half * 512:(half + 1) * 512],
                                                                               start=(c == 0), stop=(c == 3)), reads=[ak, (wk, c)], writes=[("ps", bank)])
            hs = slice(half * 512, (half + 1) * 512)
            A("dve", lambda e: e.tensor_tensor(out=t1[:, hs], in0=ps[4][:], in1=sg[:, half * 512:(half + 1) * 512], op=ALU.mult), reads=[("ps", 4), ("sg", half)], writes=[("t1", half)])
            A("dve", lambda e: e.tensor_tensor(out=t2[:, hs], in0=ps[5][:], in1=sg[:, 1024 + half * 512:1024 + (half + 1) * 512], op=ALU.mult),
              reads=[("ps", 5), ("sg", 2 + half)], writes=[("t2", half)])
            A("pool", lambda e: e.tensor_tensor(out=mrg[:, hs], in0=t1[:, hs], in1=t2[:, hs], op=ALU.add), reads=[("t1", half), ("t2", half)], writes=[("mrg", half)])
        pb = psb(6)
        for k in range(8):
            A("pe", lambda e, k=k: e.transpose(out=pb[:, k * 128:(k + 1) * 128], in_=mrg[:, k * 128:(k + 1) * 128], identity=idb[:]),
              reads=[("mrg", k // 4), "idb"], writes=[("ps", 6)])
        A("act", lambda e: e.copy(out=mT[:], in_=pb.rearrange("p (k t) -> p k t", k=8)), reads=[("ps", 6)], writes=["mT"])
        for half in range(2):
            bank = 4 + half
            for k in range(8):
                A("pe", lambda e, k=k, bank=bank, half=half: e.matmul(ps[bank][:], lhsT=mT[:, k, :], rhs=wo[:, k, half * 512:(half + 1) * 512], start=(k == 0), stop=(k == 7)),
                  reads=["mT", ("wo", k)], writes=[("ps", bank)])
            A("dve", lambda e, bank=bank, half=half: e.tensor_tensor(out=x1t[z][:, half * 512:(half + 1) * 512], in0=ps[bank][:], in1=xt[sl][:, half * 512:(half + 1) * 512], op=ALU.add),
              reads=[("ps", bank), ("xt", sl)], writes=[("x1t", z)])
        A("sp", lambda e: e.dma_start(out=x1d[tsl, :], in_=x1t[z][:]), reads=[("x1t", z)], writes=["x1d"], dma=True)

    S.barrier(lambda e: e.memset(sst[:, 7:8], 0.0))
    al.regions = [[m_phase, SB_HI]]
    wup = R([128, 8, 4096], BF16)
    wdn = R([128, 32, 1024], BF16)
    gmlp = R([128, 1024], F32)
    gfin = R([128, 1024], F32)
    hTm = [R([128, 8, 256], BF16) for _ in range(2)]
    aT = R([128, 32, 256], BF16)
    rr = [R([128, 256], F32) for _ in range(2)]
    xres = [R([128, 1024], F32) for _ in range(2)]
    otile = xres
    A("sp", lambda e: e.dma_start(out=gmlp[:], in_=gmlpd), writes=["gmlp"], dma=True)
    A("sp", lambda e: e.dma_start(out=gfin[:], in_=gfind), writes=["gfin"], dma=True)
    load_w(wup, "wup", w_up, 8, 0, 4096)
    load_w(wdn, "wdn", w_down, 32, 0, 1024)
    def f_pre(g):
        z = g % 2
        for i in range(2):
            r0 = g * 256 + i * 128
            xpipe(x1d[r0:r0 + 128, :], gmlp, "gmlp", hTm[z][:, :, i * 128:(i + 1) * 128], ("hTm", z), 7)

    f_pre(0)
    for g in range(8):
        z = g % 2
        for f in range(32):
            bank = f % 2
            for k in range(8):
                A("pe", lambda e, k=k, f=f, bank=bank: e.matmul(ps[bank][:, 0:256], lhsT=wup[:, k, f * 128:(f + 1) * 128], rhs=hTm[z][:, k, :], start=(k == 0), stop=(k == 7)),
                  reads=[("hTm", z), ("wup", k)], writes=[("ps", bank)])
            A("act", lambda e, bank=bank: e.activation(out=rr[bank][:], in_=ps[bank][:, 0:256], func=AF.Relu), reads=[("ps", bank)], writes=[("rr", bank)])
            A("dve", lambda e, f=f, bank=bank: e.tensor_tensor(out=aT[:, f, :], in0=rr[bank][:], in1=rr[bank][:], op=ALU.mult), reads=[("rr", bank)], writes=["aT"])
        if g + 1 < 8:
            f_pre(g + 1)
        for i in range(2):
            r0 = g * 256 + i * 128
            zz = (g * 2 + i) % 2
            A("sp", lambda e, r0=r0, zz=zz: e.dma_start(out=xres[zz][:], in_=x1d[r0:r0 + 128, :]), reads=["x1d"], writes=[("xres", zz)], dma=True)
            for half in range(2):
                bank = 2 + half
                for f in range(32):
                    A("pe", lambda e, f=f, bank=bank, half=half, i=i: e.matmul(ps[bank][:], lhsT=aT[:, f, i * 128:(i + 1) * 128], rhs=wdn[:, f, half * 512:(half + 1) * 512],
                                                                               start=(f == 0), stop=(f == 31)), reads=["aT", ("wdn", f)], writes=[("ps", bank)])
                A("dve", lambda e, bank=bank, half=half, zz=zz: e.tensor_tensor(out=xres[zz][:, half * 512:(half + 1) * 512], in0=ps[bank][:], in1=xres[zz][:, half * 512:(half + 1) * 512], op=ALU.add),
                  reads=[("ps", bank), ("xres", zz)], writes=[("xres", zz)])
            ssap = sst[:, 4 + zz:5 + zz]
            A("act", lambda e, zz=zz, ssap=ssap: e.activation(out=junk[:], in_=xres[zz][:], func=AF.Square, accum_out=ssap), reads=[("xres", zz)], writes=[("ssf", zz)])
            rstd_of(ssap, 1024, ("ssf", zz))
            A("dve", lambda e, zz=zz, ssap=ssap: e.scalar_tensor_tensor(out=otile[zz][:], in0=xres[zz][:], scalar=ssap, in1=gfin[:], op0=ALU.mult, op1=ALU.mult),
              reads=[("xres", zz), ("ssf", zz), "gfin"], writes=[("xres", zz)])
            A("sp", lambda e, r0=r0, zz=zz: e.dma_start(out=y[r0:r0 + 128, :], in_=otile[zz][:]), reads=[("xres", zz)], writes=[("yout", g * 2 + i)], dma=True)
    A("sp", lambda e: e.nop(), reads=[("yout", i_) for i_ in range(16)])
    S.emit(nc, st)
    st.close()
    return nc


def make_consts():
    c = np.zeros((128, 880), np.float32)
    r = np.arange(128)
    c[:, 0:128] = np.eye(128)
    c[:, 128:256] = (r[:, None] <= r[None, :])
    c[:, 256:384] = (r[:, None] >= r[None, :])
    c[:, 384:512] = (r[:, None] > r[None, :])
    c[:, 512:640] = (r[:, None] < r[None, :])
    c[:, 640:768] = 1.0
    for i in range(32):
        c[i, 784 + 64 + i] = 1.0
    return c


def bc(v, n=128):
    return np.ascontiguousarray(np.broadcast_to(np.asarray(v, np.float32).reshape(1, -1), (n, np.asarray(v).size)))


def prep_inputs(inp, core):
    b, j = divmod(core, 4)
    x = inp["x"][b]
    pos = inp["positions"][b]
    o0, o1 = NOWN * j, NOWN * (j + 1)
    xo = np.concatenate([x[:o0], x[o1:]], axis=0)
    xown = x[o0:o1]
    xh = np.zeros((128, 1024), np.float32)

    def row(n):
        return x[n] if 0 <= n < 8192 else np.zeros(1024, np.float32)

    for g in range(12):
        n0 = 512 * g if 512 * g < o0 else 512 * g + NOWN
        for q, n in enumerate((n0 - 2, n0 - 1, n0 + 512, n0 + 513)):
            xh[4 * g + q] = row(n)
    for g in range(4):
        n0 = o0 + 512 * g
        for q, n in enumerate((n0 - 2, n0 - 1, n0 + 512, n0 + 513)):
            xh[48 + 4 * g + q] = row(n)
    pos_all = np.concatenate([pos[:o0], pos[o1:], pos[o0:o1]])
    posT = np.concatenate([pos_all.reshape(64, 128).T, pos[o0:o1].reshape(16, 128).T], axis=1).astype(np.int32)
    tfv = (np.arange(48) < 16 * j).astype(np.float32)
    igb = inp["mlstm_igate_b"][0]
    fgb = inp["mlstm_fgate_b"][0]
    gb16 = np.concatenate([igb[0], igb[1], fgb[0], fgb[1]])
    cwv = inp["mlstm_conv_w"][0][:, 0, :]
    cw = np.ascontiguousarray(cwv.reshape(5, 8, 128).transpose(2, 1, 0)).reshape(128, 40)
    cb = np.ascontiguousarray(inp["mlstm_conv_b"][0].reshape(8, 128).T)
    d = {
        "xo": np.ascontiguousarray(xo), "xown": np.ascontiguousarray(xown), "xh": xh,
        "posT": np.ascontiguousarray(posT), "tf": bc(tfv), "cst": make_consts(),
        "gmix": bc(inp["norm_mix_g"][0]), "gmlp": bc(inp["norm_mlp_g"][0]), "gfin": bc(inp["norm_final_g"]),
        "gq": bc(inp["mla_q_norm_g"][0]), "gkv": bc(inp["mla_kv_norm_g"][0]), "gon": bc(inp["mlstm_out_norm_g"][0]),
        "gb": bc(np.tile(gb16, 4)), "cw": cw.astype(np.float32), "cb": cb.astype(np.float32),
        "w_in": inp["w_in"][0], "w_uq": inp["mla_w_uq"][0], "w_ukv": inp["mla_w_ukv"][0],
        "w_bm": inp["w_branch_mla"][0], "w_bl": inp["w_branch_mlstm"][0], "w_out": inp["w_out"][0],
        "w_up": inp["w_mlp_up"][0], "w_down": inp["w_mlp_down"][0],
    }
    return {k: np.ascontiguousarray(v) for k, v in d.items()}


def run(inputs, debug=None, cores=8):
    inp = {k: np.asarray(v) for k, v in inputs.items()}
    nc = build_program(debug)
    in_maps = [prep_inputs(inp, c) for c in range(cores)]
    res = run_bass_kernel_spmd(nc, in_maps, core_ids=list(range(cores)))
    return res


def kernel(**inputs):
    res = run(inputs)
    out = np.zeros((2, 8192, 1024), np.float32)
    for c in range(8):
        b, j = divmod(c, 4)
        out[b, NOWN * j:NOWN * (j + 1)] = res.results[c]["y"]
    return out
```

```python
import math
from contextlib import ExitStack
import numpy as np
import concourse.bass as bass
import concourse.mybir as mybir
from concourse.bass_utils import run_bass_kernel_spmd

F32 = mybir.dt.float32
BF16 = mybir.dt.bfloat16
I32 = mybir.dt.int32
AF = mybir.ActivationFunctionType
ALU = mybir.AluOpType

SEM_LIMIT = 20000
N_DSEM = 24
SB_LO = 16512
SB_HI = 229376
NOWN = 2048
NOTH = 6144
EPS = 1e-6
LNSC = -0.5 * math.log(128.0)
BIG = 30000.0


class Op:
    __slots__ = ("eng", "fn", "deps", "signal", "is_dma", "idx", "sem", "val", "dslot")

    def __init__(self, eng, fn, is_dma):
        self.eng = eng
        self.fn = fn
        self.deps = []
        self.signal = False
        self.is_dma = is_dma
        self.sem = None
        self.val = None
        self.dslot = None


class _Rec:
    def __init__(self):
        self.call = None

    def __getattr__(self, name):
        def f(*a, **k):
            self.call = (name, a, k)
            return self
        return f


class Sched:
    def __init__(self):
        self.ops = []
        self.last_w = {}
        self.readers = {}
        self.n_dma = 0
        self.dslot_last = {}
        self.fence_op = None

    def add(self, eng, fn, reads=(), writes=(), dma=False):
        rec = _Rec()
        fn(rec)
        call = rec.call
        op = Op(eng, call, dma)
        psk = [k for k in reads if isinstance(k, tuple) and k[0] == "ps"]
        if psk:
            reads = [k for k in reads if k not in psk]
            writes = list(writes) + psk
        deps = set()
        for k in reads:
            w = self.last_w.get(k)
            if w is not None:
                deps.add(w)
        for k in writes:
            w = self.last_w.get(k)
            if w is not None:
                deps.add(w)
            for r in self.readers.get(k, ()):
                deps.add(r)
        if self.fence_op is not None:
            deps.add(self.fence_op)
        if dma:
            slot = (eng, self.n_dma % N_DSEM)
            self.n_dma += 1
            prev = self.dslot_last.get(slot)
            if prev is not None:
                deps.add(prev)
            self.dslot_last[slot] = op
            op.dslot = slot
        for d in deps:
            if d is op:
                continue
            if (not d.is_dma) and d.eng == "pe" and eng == "pe" and not dma:
                continue
            d.signal = True
            op.deps.append(d)
        for k in reads:
            self.readers.setdefault(k, []).append(op)
        for k in writes:
            self.last_w[k] = op
            self.readers[k] = []
        self.ops.append(op)
        return op

    def barrier(self, fn):
        keys = set(self.last_w.keys()) | set(self.readers.keys())
        self.fence_op = None
        op = self.add("dve", fn, writes=list(keys))
        for o in self.dslot_last.values():
            if o is not op and o not in op.deps:
                o.signal = True
                op.deps.append(o)
        self.fence_op = op
        return op

    def emit(self, nc, stack):
        engs = ["pe", "act", "dve", "pool", "sp"]
        counts = {e: 0 for e in engs}
        dcount = {}
        for op in self.ops:
            if op.is_dma:
                c = dcount.get(op.dslot, 0) + 1
                dcount[op.dslot] = c
                op.sem = ("d", op.dslot)
                op.val = 16 * c
            elif op.signal:
                c = counts[op.eng]
                counts[op.eng] = c + 1
                op.sem = (op.eng, c // SEM_LIMIT)
                op.val = c % SEM_LIMIT + 1
        sems = {}
        for op in self.ops:
            if op.sem is not None and op.sem not in sems:
                sems[op.sem] = stack.enter_context(nc.semaphore("s_%d" % len(sems)))
        block = stack.enter_context(nc.Block())
        ops = self.ops

        def run(engname, e):
            waited = {}
            for op in ops:
                if op.eng != engname:
                    continue
                for d in op.deps:
                    key = d.sem
                    if waited.get(key, 0) >= d.val:
                        continue
                    e.wait_ge(sems[key], d.val)
                    waited[key] = d.val
                name, a_, k_ = op.fn
                ins = getattr(e, name)(*a_, **k_)
                if op.is_dma:
                    ins.then_inc(sems[op.sem], 16)
                elif op.signal:
                    ins.then_inc(sems[op.sem], 1)

        @block.tensor
        def _(e):
            run("pe", e)

        @block.scalar
        def _(e):
            run("act", e)

        @block.vector
        def _(e):
            run("dve", e)

        @block.gpsimd
        def _(e):
            run("pool", e)

        @block.sync
        def _(e):
            run("sp", e)


class Alloc:
    def __init__(self, nc):
        self.nc = nc
        self.base = SB_LO
        self.top = SB_LO
        self.n = 0

    def mark(self):
        return self.top

    def reset(self, m):
        self.top = m

    def set_regions(self, regions):
        self.regions = [list(r) for r in regions]

    def ralloc(self, shape, dt):
        nb = 1
        for s_ in shape[1:]:
            nb *= s_
        nb *= 2 if dt == BF16 else 4
        nb = (nb + 63) // 64 * 64
        for r in self.regions:
            if r[0] + nb <= r[1]:
                off = r[0]
                r[0] += nb
                self.n += 1
                return self.nc.alloc_sbuf_tensor_at("t%d" % self.n, list(shape), dt, offset=off)
        raise AssertionError(("SBUF overflow", shape, self.regions))

    def __call__(self, shape, dt):
        nb = 1
        for s in shape[1:]:
            nb *= s
        nb *= 2 if dt == BF16 else 4
        nb = (nb + 63) // 64 * 64
        off = self.top
        self.top += nb
        assert self.top <= SB_HI, ("SBUF overflow", self.top)
        self.n += 1
        return self.nc.alloc_sbuf_tensor_at("t%d" % self.n, list(shape), dt, offset=off)


def build_program(debug=None):
    nc = bass.Bass("TRN2", target_bir_lowering=False)

    def din(name, shape, dt=F32):
        return nc.dram_tensor(name, list(shape), dt, kind="ExternalInput").ap()

    xo = din("xo", [NOTH, 1024])
    xown = din("xown", [NOWN, 1024])
    xh = din("xh", [128, 1024])
    posT = din("posT", [128, 80], I32)
    tfd = din("tf", [128, 48])
    cstd = din("cst", [128, 880])
    gmixd = din("gmix", [128, 1024])
    gmlpd = din("gmlp", [128, 1024])
    gfind = din("gfin", [128, 1024])
    gqd = din("gq", [128, 384])
    gkvd = din("gkv", [128, 256])
    gond = din("gon", [128, 512])
    gbd = din("gb", [128, 64])
    cwd = din("cw", [128, 40])
    cbd = din("cb", [128, 8])
    w_in = din("w_in", [1024, 4784])
    w_uq = din("w_uq", [384, 768])
    w_ukv = din("w_ukv", [256, 1024])
    w_bm = din("w_bm", [512, 1024])
    w_bl = din("w_bl", [512, 1024])
    w_out = din("w_out", [1024, 1024])
    w_up = din("w_up", [1024, 4096])
    w_down = din("w_down", [4096, 1024])
    y = nc.dram_tensor("y", [NOWN, 1024], F32, kind="ExternalOutput").ap()
    x1d = nc.dram_tensor("x1d", [NOWN, 1024], F32).ap()
    dbg = None
    if debug:
        dbg = nc.dram_tensor("dbg", [NOWN, 1024], F32, kind="ExternalOutput").ap()

    S = Sched()
    A = S.add
    al = Alloc(nc)
    st = ExitStack()
    psbig = [st.enter_context(nc.psum_tensor("psb%d" % i, [128, 1024], F32)) for i in range(4)]
    ps = [psbig[i // 2][:, (i % 2) * 512:(i % 2 + 1) * 512] for i in range(8)]

    def psb(i):
        return ps[i][:].bitcast(BF16)

    cst = al([128, 880], F32)
    idb = al([128, 128], BF16)
    mLEb = al([128, 128], BF16)
    mGEb = al([128, 128], BF16)
    gmix = al([128, 1024], F32)
    xt = [al([128, 1024], F32) for _ in range(2)]
    junk = al([128, 1024], BF16)
    hb = [al([128, 1024], BF16) for _ in range(2)]
    sst = al([128, 8], F32)
    wst = [al([128, 1024], F32) for _ in range(3)]
    LNSCt = al([128, 1], F32)
    ONEt = al([128, 1], F32)
    EPSt = al([128, 1], F32)
    idf = cst[:, 0:128]
    mLE = cst[:, 128:256]
    mGE = cst[:, 256:384]
    mSU = cst[:, 384:512]
    mSL = cst[:, 512:640]
    ones = cst[:, 640:768]
    sel = cst[0:32, 784:880]

    A("sp", lambda e: e.dma_start(out=cst[:], in_=cstd), writes=["cst"], dma=True)
    A("sp", lambda e: e.dma_start(out=gmix[:], in_=gmixd), writes=["gmix"], dma=True)
    A("dve", lambda e: e.memset(ONEt[:], 1.0), writes=["onet"])
    A("dve", lambda e: e.memset(EPSt[:], EPS), writes=["epst"])
    A("dve", lambda e: e.tensor_copy(out=idb[:], in_=idf), reads=["cst"], writes=["idb"])
    A("dve", lambda e: e.tensor_copy(out=mLEb[:], in_=mLE), reads=["cst"], writes=["mLEb"])
    A("dve", lambda e: e.tensor_copy(out=mGEb[:], in_=mGE), reads=["cst"], writes=["mGEb"])

    wstate = {"i": 0}

    def load_w(dst, dkey, src, K, c_lo, c_hi, dcol=0, queue="sp"):
        for k in range(K):
            c0 = c_lo
            while c0 < c_hi:
                cc = min(1024, c_hi - c0)
                sl = wstate["i"] % 3
                wstate["i"] += 1
                A(queue, lambda e, sl=sl, k=k, c0=c0, cc=cc: e.dma_start(out=wst[sl][:, 0:cc], in_=src[k * 128:(k + 1) * 128, c0:c0 + cc]),
                  writes=[("wst", sl)], dma=True)
                d0 = dcol + (c0 - c_lo)
                ce = ("pool", "act", "dve")[wstate["i"] % 3] if cc >= 256 else "pool"
                if ce == "act":
                    A("act", lambda e, sl=sl, k=k, d0=d0, cc=cc: e.copy(out=dst[:, k, d0:d0 + cc], in_=wst[sl][:, 0:cc]),
                      reads=[("wst", sl)], writes=[(dkey, k)])
                else:
                    A(ce, lambda e, sl=sl, k=k, d0=d0, cc=cc: e.tensor_copy(out=dst[:, k, d0:d0 + cc], in_=wst[sl][:, 0:cc]),
                      reads=[("wst", sl)], writes=[(dkey, k)])
                c0 += cc

    xstate = {"i": 0}

    def rstd_of(sskey_ap, n, key):
        A("act", lambda e: e.activation(out=sskey_ap, in_=sskey_ap, func=AF.Ln, scale=1.0 / n, bias=EPSt[:, 0:1]), reads=[key, "epst"], writes=[key])
        A("act", lambda e: e.activation(out=sskey_ap, in_=sskey_ap, func=AF.Exp, scale=-0.5), reads=[key], writes=[key])

    def xpipe(src_rows, g_tile, gkey, hT_dst, hkey, tps, keep=False):
        i = xstate["i"]
        xstate["i"] += 1
        sl = i % 2
        A("sp", lambda e: e.dma_start(out=xt[sl][:], in_=src_rows), writes=[("xt", sl)], dma=True)
        ssap = sst[:, sl:sl + 1]
        A("act", lambda e: e.activation(out=junk[:], in_=xt[sl][:], func=AF.Square, accum_out=ssap), reads=[("xt", sl)], writes=[("ss", sl)])
        rstd_of(ssap, 1024, ("ss", sl))
        A("dve", lambda e: e.scalar_tensor_tensor(out=hb[sl][:], in0=xt[sl][:], scalar=ssap, in1=g_tile[:], op0=ALU.mult, op1=ALU.mult),
          reads=[("xt", sl), ("ss", sl), gkey], writes=[("hb", sl)])
        pb = psb(tps)
        for k in range(8):
            A("pe", lambda e, k=k: e.transpose(out=pb[:, k * 128:(k + 1) * 128], in_=hb[sl][:, k * 128:(k + 1) * 128], identity=idb[:]),
              reads=[("hb", sl), "idb"], writes=[("ps", tps)])
        A("act", lambda e: e.copy(out=hT_dst, in_=pb.rearrange("p (k t) -> p k t", k=8)), reads=[("ps", tps)], writes=[hkey])
        return sl

    m_phase = al.mark()
    LGe = [al([128, 8], F32) for _ in range(2)]
    Ie = [al([128, 8], F32) for _ in range(2)]
    exg = [al([128, 8], F32) for _ in range(2)]
    Wt = [al([128, 8], F32) for _ in range(2)]
    dect = [al([128, 4], F32) for _ in range(2)]
    Arun = al([128, 4], F32)
    ktil = [al([128, 128], BF16) for _ in range(4)]
    Cf = al([128, 4, 129], F32)
    Cb = al([128, 4, 129], F32)
    Cfb = al([128, 4, 129], BF16)
    Cbb = al([128, 4, 129], BF16)
    qT = al([128, 4, NOWN], BF16)
    kT = al([128, 4, NOWN], BF16)
    ktok = al([128, 16, 512], BF16)
    vaug = al([128, 16, 4, 129], BF16)
    sgo = al([128, 16, 512], BF16)
    Gown = al([128, 16, 16], F32)
    LFo = al([128, 16, 8], F32)
    m_alias = al.mark()
    wl = al([128, 8, 2064], BF16)
    tf = al([128, 48], F32)
    tb = al([128, 48], F32)
    pf = al([128, 48], F32)
    pbk = al([128, 48], F32)
    gbias = al([128, 64], F32)
    cw = al([128, 40], F32)
    cb = al([128, 8], F32)
    gon = al([128, 512], F32)
    haloT = al([128, 8, 128], F32)
    hTg = [al([128, 8, 512], BF16) for _ in range(2)]
    padb = [al([128, 516], F32) for _ in range(2)]
    acc = [al([128, 512], F32) for _ in range(2)]
    kTg1 = al([128, 4, 512], BF16)
    ktokg1 = al([128, 4, 512], BF16)
    vaugg1 = al([128, 4, 4, 129], BF16)
    kTg = [kTg1, kTg1]
    ktokg = [ktokg1, ktokg1]
    vaugg = [vaugg1, vaugg1]
    Gg = [al([128, 4, 16], F32) for _ in range(2)]
    LFg = [al([128, 4, 8], F32) for _ in range(2)]
    sgt = [al([128, 512], BF16) for _ in range(2)]
    m_p12 = al.mark()
    al.reset(m_alias)
    ymT = al([128, 4, NOWN], BF16)
    m_keep = al.mark()
    EBt = al([128, 16, 8], F32)
    ECt = al([128, 16, 8], F32)
    WSt = al([128, 16, 8], F32)
    DECt = al([128, 16, 8], F32)
    cmt = al([128, 8], F32)
    HB = al([128, 16, 512], F32)
    PT = [al([128, 4, 128], BF16) for _ in range(2)]
    den = al([128, 4], F32)
    scl = al([128, 4], F32)
    hsum = [al([128, 4, 128], F32) for _ in range(2)]
    ssq = al([128, 4], F32)
    ymt = [al([128, 4, 128], BF16) for _ in range(2)]
    assert al.mark() <= m_p12
    al.reset(m_p12)

    for (t_, d_, k_) in ((tf, tfd, "tf"), (gbias, gbd, "gbias"), (cw, cwd, "cw"), (cb, cbd, "cb"), (gon, gond, "gon")):
        A("sp", lambda e, t_=t_, d_=d_: e.dma_start(out=t_[:], in_=d_), writes=[k_], dma=True)
    A("dve", lambda e: e.tensor_scalar(out=tb[:], in0=tf[:], scalar1=-1.0, scalar2=1.0, op0=ALU.mult, op1=ALU.add), reads=["tf"], writes=["tb"])
    A("dve", lambda e: e.tensor_scalar(out=pf[:], in0=tf[:], scalar1=-1.0, scalar2=BIG, op0=ALU.add, op1=ALU.mult), reads=["tf"], writes=["pf"])
    A("dve", lambda e: e.tensor_scalar(out=pbk[:], in0=tf[:], scalar1=-BIG, scalar2=None, op0=ALU.mult), reads=["tf"], writes=["pbk"])
    for t_, k_ in ((Cf, "Cf"), (Cb, "Cb"), (Arun, "Arun")):
        A("dve", lambda e, t_=t_: e.memset(t_[:], 0.0), writes=[k_])
    A("pool", lambda e: e.memset(vaug[:, :, :, 128:129], 1.0), writes=["vaug1"])
    A("pool", lambda e: e.memset(vaugg1[:, :, :, 128:129], 1.0), writes=[("vaugg1", 0)])

    load_w(wl, "wl", w_in, 8, 672, 2720, 0)
    for (s0, d0) in ((2720, 2048), (2728, 2052), (2724, 2056), (2732, 2060)):
        load_w(wl, "wl", w_in, 8, s0, s0 + 4, d0)
    WLK = [("wl", k) for k in range(8)]

    TPSX, TPSK, CPS0, CPS1, MVPS, UPS0 = 0, 1, 2, 3, 4, 5

    hTh = hTg[1][:, :, 0:128]
    xpipe(xh, gmix, "gmix", hTh, ("hTg", 1), TPSX)
    for c in range(8):
        bank = CPS0 + (c % 2)
        for k in range(8):
            A("pe", lambda e, c=c, k=k, bank=bank: e.matmul(ps[bank][:, 0:128], lhsT=wl[:, k, c * 128:(c + 1) * 128], rhs=hTg[1][:, k, 0:128],
                                                           start=(k == 0), stop=(k == 7)),
              reads=[("hTg", 1), ("wl", k)], writes=[("ps", bank)])
        A("act", lambda e, c=c, bank=bank: e.copy(out=haloT[:, c, :], in_=ps[bank][:, 0:128]), reads=[("ps", bank)], writes=["haloT"])

    cstate = {"i": 0}

    def conv_chunk(par, wcol, chunk, hidx, dst_ap, dst_key):
        ci = cstate["i"]
        cstate["i"] += 1
        bank = CPS0 + (ci % 2)
        pz = ci % 2
        for k in range(8):
            A("pe", lambda e, k=k: e.matmul(ps[bank][:], lhsT=wl[:, k, wcol:wcol + 128], rhs=hTg[par][:, k, :], start=(k == 0), stop=(k == 7)),
              reads=[("hTg", par), ("wl", k)], writes=[("ps", bank)])
        A("act", lambda e: e.copy(out=padb[pz][:, 2:514], in_=ps[bank][:]), reads=[("ps", bank)], writes=[("padb", pz)])
        A("pool", lambda e: e.tensor_copy(out=padb[pz][:, 0:2], in_=haloT[:, chunk, hidx:hidx + 2]), reads=["haloT"], writes=[("padbL", pz)])
        A("pool", lambda e: e.tensor_copy(out=padb[pz][:, 514:516], in_=haloT[:, chunk, hidx + 2:hidx + 4]), reads=["haloT"], writes=[("padbR", pz)])
        rk = [("padb", pz), ("padbL", pz), ("padbR", pz), "cw", "cb"]
        A("dve", lambda e: e.tensor_scalar(out=acc[pz][:], in0=padb[pz][:, 0:512], scalar1=cw[:, chunk * 5:chunk * 5 + 1], scalar2=cb[:, chunk:chunk + 1],
                                           op0=ALU.mult, op1=ALU.add), reads=rk, writes=[("acc", pz)])
        for j in range(1, 5):
            A("dve", lambda e, j=j: e.scalar_tensor_tensor(out=acc[pz][:], in0=padb[pz][:, j:j + 512], scalar=cw[:, chunk * 5 + j:chunk * 5 + j + 1],
                                                           in1=acc[pz][:], op0=ALU.mult, op1=ALU.add), reads=rk + [("acc", pz)], writes=[("acc", pz)])
        A("act", lambda e: e.activation(out=dst_ap, in_=acc[pz][:], func=AF.Silu), reads=[("acc", pz)], writes=[dst_key])

    def mlstm_pre(gi):
        own = gi >= 12
        par = gi % 2
        src = xown if own else xo
        g0 = (gi - 12) if own else gi
        for i in range(4):
            r0 = g0 * 512 + i * 128
            xpipe(src[r0:r0 + 128, :], gmix, "gmix", hTg[par][:, :, i * 128:(i + 1) * 128], ("hTg", par), TPSX)

    def mlstm_group(gi, own, mid=None):
        par = gi % 2
        g0 = (gi - 12) if own else gi
        hidx = (48 + 4 * g0) if own else 4 * g0
        for h in range(4):
            if own:
                conv_chunk(par, h * 128, h, hidx, qT[:, h, g0 * 512:(g0 + 1) * 512], "qT")
                conv_chunk(par, 512 + h * 128, 4 + h, hidx, kT[:, h, g0 * 512:(g0 + 1) * 512], "kT")
            else:
                conv_chunk(par, 512 + h * 128, 4 + h, hidx, kTg[par][:, h, :], ("kTg", 0))
        if mid is not None:
            mid()
        for b2 in range(2):
            pb = psb(TPSK)
            for bb in range(2):
                blk = b2 * 2 + bb
                for h in range(4):
                    if own:
                        src_ap = kT[:, h, g0 * 512 + blk * 128:g0 * 512 + (blk + 1) * 128]
                        rkey = "kT"
                    else:
                        src_ap = kTg[par][:, h, blk * 128:(blk + 1) * 128]
                        rkey = ("kTg", 0)
                    o0 = (bb * 4 + h) * 128
                    A("pe", lambda e, src_ap=src_ap, o0=o0: e.transpose(out=pb[:, o0:o0 + 128], in_=src_ap, identity=idb[:]),
                      reads=[rkey, "idb"], writes=[("ps", TPSK)])
            if own:
                dst = ktok[:, g0 * 4 + b2 * 2:g0 * 4 + b2 * 2 + 2, :]
                dk = "ktok"
            else:
                dst = ktokg[par][:, b2 * 2:b2 * 2 + 2, :]
                dk = ("ktokg", 0)
            A("act", lambda e, dst=dst, pb=pb: e.copy(out=dst, in_=pb.rearrange("p (b c) -> p b c", b=2)), reads=[("ps", TPSK)], writes=[dk])
        for i in range(4):
            mvb = (4, 5)[i % 2]
            for k in range(8):
                A("pe", lambda e, i=i, k=k: e.matmul(ps[mvb][:], lhsT=hTg[par][:, k, i * 128:(i + 1) * 128], rhs=wl[:, k, 1024:1536],
                                                     start=(k == 0), stop=(k == 7)), reads=[("hTg", par), ("wl", k)], writes=[("ps", mvb)])
            if own:
                dst = vaug[:, g0 * 4 + i, :, 0:128]
                dk = "vaug"
            else:
                dst = vaugg[par][:, i, :, 0:128]
                dk = ("vaugg", 0)
            A("dve", lambda e, dst=dst: e.tensor_copy(out=dst, in_=ps[mvb][:].rearrange("p (h d) -> p h d", h=4)), reads=[("ps", mvb)], writes=[dk])
            if own:
                mob = (5, 4)[i % 2]
                for k in range(8):
                    A("pe", lambda e, i=i, k=k: e.matmul(ps[mob][:], lhsT=hTg[par][:, k, i * 128:(i + 1) * 128], rhs=wl[:, k, 1536:2048],
                                                         start=(k == 0), stop=(k == 7)), reads=[("hTg", par), ("wl", k)], writes=[("ps", mob)])
                sp_ = i % 2
                A("act", lambda e, sp_=sp_: e.activation(out=sgt[sp_][:], in_=ps[mob][:], func=AF.Sigmoid), reads=[("ps", mob)], writes=[("sgt", sp_)])
                A("pool", lambda e, sp_=sp_, i=i: e.tensor_tensor(out=sgo[:, g0 * 4 + i, :], in0=sgt[sp_][:], in1=gon[:], op=ALU.mult),
                  reads=[("sgt", sp_), "gon"], writes=["sgo"])
        for i in range(4):
            for k in range(8):
                A("pe", lambda e, i=i, k=k: e.matmul(ps[MVPS][:, i * 16:(i + 1) * 16], lhsT=hTg[par][:, k, i * 128:(i + 1) * 128], rhs=wl[:, k, 2048:2064],
                                                     start=(k == 0), stop=(k == 7)), reads=[("hTg", par), ("wl", k)], writes=[("ps", MVPS)])
        if own:
            Gd = Gown[:, g0 * 4:g0 * 4 + 4, :]
            gk = "Gown"
            LFd = LFo[:, g0 * 4:g0 * 4 + 4, :]
            lk = "LFo"
        else:
            Gd = Gg[par][:]
            gk = ("Gg", par)
            LFd = LFg[par][:]
            lk = ("LFg", par)
        A("dve", lambda e: e.tensor_tensor(out=Gd, in0=ps[MVPS][:, 0:64].rearrange("p (b c) -> p b c", b=4),
                                           in1=gbias[:].rearrange("p (b c) -> p b c", b=4), op=ALU.add), reads=[("ps", MVPS), "gbias"], writes=[gk])
        A("act", lambda e: e.activation(out=LFd, in_=Gd[:, :, 8:16], func=AF.Exp, scale=-1.0), reads=[gk], writes=[lk])
        A("act", lambda e: e.activation(out=LFd, in_=LFd, func=AF.Ln, bias=ONEt[:, 0:1]), reads=[lk, "onet"], writes=[lk])
        if own:
            return
        for blk in range(4):
            gb_ = gi * 4 + blk
            z = blk % 2
            A("dve", lambda e: e.tensor_scalar(out=LGe[z][:, 0:4], in0=LFg[par][:, blk, 0:4], scalar1=tf[:, gb_:gb_ + 1], scalar2=-1.0, op0=ALU.mult, op1=ALU.mult),
              reads=[lk, "tf"], writes=[("LGe", z)])
            A("dve", lambda e: e.tensor_scalar(out=LGe[z][:, 4:8], in0=LFg[par][:, blk, 4:8], scalar1=tb[:, gb_:gb_ + 1], scalar2=-1.0, op0=ALU.mult, op1=ALU.mult),
              reads=[lk, "tb"], writes=[("LGe", z)])
            A("dve", lambda e: e.tensor_scalar(out=Ie[z][:, 0:4], in0=Gg[par][:, blk, 0:4], scalar1=pf[:, gb_:gb_ + 1], scalar2=None, op0=ALU.add),
              reads=[gk, "pf"], writes=[("Ie", z)])
            A("dve", lambda e: e.tensor_scalar(out=Ie[z][:, 4:8], in0=Gg[par][:, blk, 4:8], scalar1=pbk[:, gb_:gb_ + 1], scalar2=None, op0=ALU.add),
              reads=[gk, "pbk"], writes=[("Ie", z)])
            gp = UPS0 + 2
            A("pe", lambda e: e.matmul(ps[gp][:, 0:4], lhsT=mSU, rhs=LGe[z][:, 0:4], start=True, stop=True), reads=["cst", ("LGe", z)], writes=[("ps", gp)])
            A("pe", lambda e: e.matmul(ps[gp][:, 4:8], lhsT=mSL, rhs=LGe[z][:, 4:8], start=True, stop=True), reads=["cst", ("LGe", z)], writes=[("ps", gp)])
            A("pe", lambda e: e.matmul(ps[gp][:, 8:16], lhsT=ones, rhs=LGe[z][:, 0:8], start=True, stop=True), reads=["cst", ("LGe", z)], writes=[("ps", gp)])
            A("dve", lambda e: e.tensor_tensor(out=exg[z][:], in0=ps[gp][:, 0:8], in1=Ie[z][:], op=ALU.add), reads=[("ps", gp), ("Ie", z)], writes=[("exg", z)])
            A("dve", lambda e: e.tensor_tensor(out=exg[z][:, 4:8], in0=exg[z][:, 4:8], in1=Arun[:], op=ALU.add), reads=[("exg", z), "Arun"], writes=[("exg", z)])
            A("act", lambda e: e.activation(out=Wt[z][:], in_=exg[z][:], func=AF.Exp, bias=LNSCt[:, 0:1]), reads=[("exg", z), "lnsc"], writes=[("Wt", z)])
            A("act", lambda e: e.activation(out=dect[z][:], in_=ps[gp][:, 8:12], func=AF.Exp), reads=[("ps", gp)], writes=[("dect", z)])
            A("dve", lambda e: e.tensor_tensor(out=Arun[:], in0=Arun[:], in1=ps[gp][:, 12:16], op=ALU.add), reads=["Arun", ("ps", gp)], writes=["Arun"])
            for u in range(8):
                ch, h = divmod(u, 4)
                kz = u % 4
                A("dve", lambda e, u=u, h=h, kz=kz: e.tensor_scalar(out=ktil[kz][:], in0=ktokg[par][:, blk, h * 128:(h + 1) * 128],
                                                                    scalar1=Wt[z][:, u:u + 1], scalar2=None, op0=ALU.mult),
                  reads=[("ktokg", 0), ("Wt", z)], writes=[("ktil", kz)])
                bank = UPS0 + (u % 3)
                c0 = (u // 3) * 129 + (128 if bank == UPS0 + 2 else 0)
                A("pe", lambda e, h=h, kz=kz, bank=bank, c0=c0: e.matmul(ps[bank][:, c0:c0 + 129], lhsT=ktil[kz][:], rhs=vaugg[par][:, blk, h, :],
                                                                         start=True, stop=True),
                  reads=[("ktil", kz), ("vaugg", 0), ("vaugg1", 0)], writes=[("ps", bank)])
                if ch == 0:
                    A("dve", lambda e, h=h, bank=bank, c0=c0: e.scalar_tensor_tensor(out=Cf[:, h, :], in0=Cf[:, h, :], scalar=dect[z][:, h:h + 1],
                                                                                     in1=ps[bank][:, c0:c0 + 129], op0=ALU.mult, op1=ALU.add),
                      reads=["Cf", ("dect", z), ("ps", bank)], writes=["Cf"])
                else:
                    A("dve", lambda e, h=h, bank=bank, c0=c0: e.tensor_tensor(out=Cb[:, h, :], in0=Cb[:, h, :], in1=ps[bank][:, c0:c0 + 129], op=ALU.add),
                      reads=["Cb", ("ps", bank)], writes=["Cb"])

    A("dve", lambda e: e.memset(LNSCt[:], LNSC), writes=["lnsc"])

    gseq = list(range(12 if not (debug and (debug.endswith("_fast") or debug.startswith("p4"))) else 0))
    gseq += list(range(12, 16 if not (debug and debug.startswith("p4")) else 12))
    if gseq:
        mlstm_pre(gseq[0])
    for gidx, gi in enumerate(gseq):
        nxt = gseq[gidx + 1] if gidx + 1 < len(gseq) else None
        mlstm_group(gi, gi >= 12, mid=(lambda nxt=nxt: mlstm_pre(nxt)) if nxt is not None else None)

    if debug and debug.split("_")[0] in ("qTe", "kTe"):
        src_t = qT if debug.startswith("qTe") else kT
        dt_ = al([128, 1024], F32)
        for h in range(4):
            for half in range(2):
                A("dve", lambda e, h=h, half=half: e.tensor_copy(out=dt_[:], in_=src_t[:, h, half * 1024:(half + 1) * 1024]), reads=["qT", "kT"], writes=["dt_"])
                r0 = (h * 2 + half) * 128
                A("sp", lambda e, r0=r0: e.dma_start(out=dbg[r0:r0 + 128, :], in_=dt_[:]), reads=["dt_"], writes=["dbgo"], dma=True)
        A("sp", lambda e: e.nop(), reads=["dbgo"])
        S.emit(nc, st)
        st.close()
        return nc
    p12keys = [("wl", k) for k in range(8)] + ["tf", "tb", "pf", "pbk", "gbias", "cw", "cb", "gon", "haloT", ("hTg", 0), ("hTg", 1),
               ("padb", 0), ("padb", 1), ("padbL", 0), ("padbL", 1), ("padbR", 0), ("padbR", 1), ("acc", 0), ("acc", 1),
               ("kTg", 0), ("ktokg", 0), ("vaugg", 0), ("vaugg1", 0), ("Gg", 0), ("Gg", 1), ("LFg", 0), ("LFg", 1), ("sgt", 0), ("sgt", 1)]
    p3keys = ["EBt", "ECt", "WSt", "DECt", "cmt", "HB", ("PT", 0), ("PT", 1), "den", "scl", ("hsum", 0), ("hsum", 1), "ssq", ("ymt", 0), ("ymt", 1), "ymT"]
    A("dve", lambda e: e.memset(cmt[:], 0.0), writes=p12keys + p3keys)
    A("pool", lambda e: e.tensor_copy(out=Cfb[:], in_=Cf[:]), reads=["Cf"], writes=["Cfb"])
    A("pool", lambda e: e.tensor_copy(out=Cbb[:], in_=Cb[:]), reads=["Cb"], writes=["Cbb"])
    GP = 7
    for blk in range(16 if not (debug and debug.startswith("p4")) else 0):
        z = blk % 2
        A("dve", lambda e, blk=blk: e.tensor_scalar(out=LGe[z][:], in0=LFo[:, blk, :], scalar1=-1.0, scalar2=None, op0=ALU.mult), reads=["LFo"], writes=[("LGe", z)])
        A("pe", lambda e: e.matmul(ps[GP][:, 0:4], lhsT=mLE, rhs=LGe[z][:, 0:4], start=True, stop=True), reads=["cst", ("LGe", z)], writes=[("ps", GP)])
        A("pe", lambda e: e.matmul(ps[GP][:, 4:8], lhsT=mGE, rhs=LGe[z][:, 4:8], start=True, stop=True), reads=["cst", ("LGe", z)], writes=[("ps", GP)])
        A("pe", lambda e: e.matmul(ps[GP][:, 8:16], lhsT=ones, rhs=LGe[z][:, 0:8], start=True, stop=True), reads=["cst", ("LGe", z)], writes=[("ps", GP)])
        A("act", lambda e, blk=blk: e.activation(out=EBt[:, blk, :], in_=ps[GP][:, 0:8], func=AF.Exp), reads=[("ps", GP)], writes=["EBt"])
        A("dve", lambda e, blk=blk: e.tensor_tensor(out=cmt[:], in0=Gown[:, blk, 0:8], in1=ps[GP][:, 0:8], op=ALU.subtract), reads=["Gown", ("ps", GP)], writes=["cmt"])
        A("act", lambda e, blk=blk: e.activation(out=ECt[:, blk, :], in_=cmt[:], func=AF.Exp, bias=LNSCt[:, 0:1]), reads=["cmt", "lnsc"], writes=["ECt"])
        A("act", lambda e, blk=blk: e.activation(out=DECt[:, blk, :], in_=ps[GP][:, 8:16], func=AF.Exp), reads=[("ps", GP)], writes=["DECt"])
        A("dve", lambda e, blk=blk: e.tensor_tensor(out=cmt[:], in0=cmt[:], in1=ps[GP][:, 8:16], op=ALU.add), reads=["cmt", ("ps", GP)], writes=["cmt"])
        A("act", lambda e, blk=blk: e.activation(out=WSt[:, blk, :], in_=cmt[:], func=AF.Exp, bias=LNSCt[:, 0:1]), reads=["cmt", "lnsc"], writes=["WSt"])

    SPS, ND0, ND1, UB0, UB1, TPY = 0, 1, 2, 3, 4, 5
    def dirpass(d):
        blocks = list(range(16)) if d == 0 else list(range(15, -1, -1))
        Cx, Cxb, ck, ckb = (Cf, Cfb, "Cf", "Cfb") if d == 0 else (Cb, Cbb, "Cb", "Cbb")
        maskb = mLEb if d == 0 else mGEb
        mk_ = "mLEb" if d == 0 else "mGEb"
        final = (d == 0)
        for bi, blk in enumerate(blocks):
            z = bi % 2
            tsl = slice(blk * 128, (blk + 1) * 128)
            for h in range(4):
                A("pe", lambda e, h=h: e.matmul(ps[SPS][:, h * 128:(h + 1) * 128], lhsT=kT[:, h, tsl], rhs=qT[:, h, tsl], start=True, stop=True),
                  reads=["kT", "qT"], writes=[("ps", SPS)])
            for h in range(4):
                A("dve", lambda e, h=h: e.scalar_tensor_tensor(out=PT[z][:, h, :], in0=ps[SPS][:, h * 128:(h + 1) * 128], scalar=ECt[:, blk, d * 4 + h:d * 4 + h + 1],
                                                               in1=maskb[:], op0=ALU.mult, op1=ALU.mult),
                  reads=[("ps", SPS), "ECt", mk_], writes=[("PT", z)])
            for h in range(4):
                bank = ND0 + h % 2
                c0 = (h // 2) * 129
                A("pe", lambda e, h=h, bank=bank, c0=c0: e.matmul(ps[bank][:, c0:c0 + 129], lhsT=PT[z][:, h, :], rhs=vaug[:, blk, h, :], start=True, stop=False),
                  reads=[("PT", z), "vaug", "vaug1"], writes=[("ps", bank)])
                A("pe", lambda e, h=h, bank=bank, c0=c0: e.matmul(ps[bank][:, c0:c0 + 129], lhsT=qT[:, h, tsl], rhs=Cxb[:, h, :], start=False, stop=True),
                  reads=["qT", ckb], writes=[("ps", bank)])
            for h in range(4):
                bank = ND0 + h % 2
                c0 = (h // 2) * 129
                A("dve", lambda e, h=h, bank=bank, c0=c0: e.tensor_tensor(out=den[:, h:h + 1], in0=ps[bank][:, c0 + 128:c0 + 129],
                                                                          in1=EBt[:, blk, d * 4 + h:d * 4 + h + 1], op=ALU.mult),
                  reads=[("ps", bank), "EBt"], writes=["den"])
            A("dve", lambda e: e.tensor_scalar(out=scl[:], in0=den[:], scalar1=-1.0, scalar2=None, op0=ALU.mult), reads=["den"], writes=["scl"])
            A("dve", lambda e: e.tensor_tensor(out=den[:], in0=den[:], in1=scl[:], op=ALU.max), reads=["den", "scl"], writes=["den"])
            A("dve", lambda e: e.tensor_scalar(out=den[:], in0=den[:], scalar1=1.0, scalar2=None, op0=ALU.max), reads=["den"], writes=["den"])
            A("dve", lambda e: e.reciprocal(out=den[:], in_=den[:]), reads=["den"], writes=["den"])
            A("dve", lambda e: e.tensor_tensor(out=scl[:], in0=den[:], in1=EBt[:, blk, d * 4:d * 4 + 4], op=ALU.mult), reads=["den", "EBt"], writes=["scl"])
            for h in range(4):
                bank = ND0 + h % 2
                c0 = (h // 2) * 129
                if not final:
                    A("dve", lambda e, h=h, bank=bank, c0=c0: e.tensor_scalar(out=HB[:, blk, h * 128:(h + 1) * 128], in0=ps[bank][:, c0:c0 + 128],
                                                                              scalar1=scl[:, h:h + 1], scalar2=None, op0=ALU.mult),
                      reads=[("ps", bank), "scl"], writes=["HB"])
                else:
                    A("dve", lambda e, h=h, bank=bank, c0=c0: e.scalar_tensor_tensor(out=hsum[z][:, h, :], in0=ps[bank][:, c0:c0 + 128], scalar=scl[:, h:h + 1],
                                                                                     in1=HB[:, blk, h * 128:(h + 1) * 128], op0=ALU.mult, op1=ALU.add),
                      reads=[("ps", bank), "scl", "HB"], writes=[("hsum", z)])
            if bi < 15:
                for h in range(4):
                    kz = h
                    bank = UB0 + h % 2
                    c0 = (h // 2) * 129
                    A("dve", lambda e, h=h, kz=kz: e.tensor_scalar(out=ktil[kz][:], in0=ktok[:, blk, h * 128:(h + 1) * 128],
                                                                   scalar1=WSt[:, blk, d * 4 + h:d * 4 + h + 1], scalar2=None, op0=ALU.mult),
                      reads=["ktok", "WSt"], writes=[("ktil", kz)])
                    A("pe", lambda e, h=h, kz=kz, bank=bank, c0=c0: e.matmul(ps[bank][:, c0:c0 + 129], lhsT=ktil[kz][:], rhs=vaug[:, blk, h, :], start=True, stop=True),
                      reads=[("ktil", kz), "vaug", "vaug1"], writes=[("ps", bank)])
                    A("dve", lambda e, h=h, bank=bank, c0=c0: e.scalar_tensor_tensor(out=Cx[:, h, :], in0=Cx[:, h, :], scalar=DECt[:, blk, d * 4 + h:d * 4 + h + 1],
                                                                                     in1=ps[bank][:, c0:c0 + 129], op0=ALU.mult, op1=ALU.add),
                      reads=[ck, "DECt", ("ps", bank)], writes=[ck])
                A("pool", lambda e: e.tensor_copy(out=Cxb[:], in_=Cx[:]), reads=[ck], writes=[ckb])
            if final:
                for h in range(4):
                    A("act", lambda e, h=h: e.activation(out=junk[:, 0:128], in_=hsum[z][:, h, :], func=AF.Square, accum_out=ssq[:, h:h + 1]),
                      reads=[("hsum", z)], writes=["ssq"])
                rstd_of(ssq[:], 128, "ssq")
                for h in range(4):
                    A("dve", lambda e, h=h: e.scalar_tensor_tensor(out=ymt[z][:, h, :], in0=hsum[z][:, h, :], scalar=ssq[:, h:h + 1],
                                                                   in1=sgo[:, blk, h * 128:(h + 1) * 128], op0=ALU.mult, op1=ALU.mult),
                      reads=[("hsum", z), "ssq", "sgo"], writes=[("ymt", z)])
                pb = psb(TPY)
                for h in range(4):
                    A("pe", lambda e, h=h: e.transpose(out=pb[:, h * 128:(h + 1) * 128], in_=ymt[z][:, h, :], identity=idb[:]),
                      reads=[("ymt", z), "idb"], writes=[("ps", TPY)])
                A("act", lambda e: e.copy(out=ymT[:, :, tsl], in_=pb[:, 0:512].rearrange("p (h t) -> p h t", h=4)), reads=[("ps", TPY)], writes=["ymT"])

    if not (debug and debug.startswith("p4")):
        dirpass(1)
    if debug and debug.startswith("HB"):
        for blk in range(16):
            A("sp", lambda e, blk=blk: e.dma_start(out=dbg[blk * 128:(blk + 1) * 128, 0:512], in_=HB[:, blk, :]), reads=["HB"], writes=["dbgo"], dma=True)
        A("sp", lambda e: e.nop(), reads=["dbgo"])
        S.emit(nc, st)
        st.close()
        return nc
    if not (debug and debug.startswith("p4")):
        dirpass(0)

    if debug and debug.split("_")[0] in ("mlstm", "qT", "kT"):
        debug = debug.split("_")[0]
        if debug == "qT":
            ymT = qT
        elif debug == "kT":
            ymT = kT
        ymk = {"mlstm": "ymT", "qT": "qT", "kT": "kT"}[debug]
        dt_ = al([128, 1024], F32)
        for h in range(4):
            for half in range(2):
                A("dve", lambda e, h=h, half=half: e.tensor_copy(out=dt_[:], in_=ymT[:, h, half * 1024:(half + 1) * 1024]), reads=[ymk], writes=["dt_"])
                r0 = (h * 2 + half) * 128
                A("sp", lambda e, r0=r0: e.dma_start(out=dbg[r0:r0 + 128, :], in_=dt_[:]), reads=["dt_"], writes=["dbgo"], dma=True)
        A("sp", lambda e: e.nop(), reads=["dbgo"])
        S.emit(nc, st)
        st.close()
        return nc

    S.barrier(lambda e: e.memset(sst[:, 7:8], 0.0))
    al.set_regions([(m_phase, m_alias), (m_keep, SB_HI)])
    R = al.ralloc
    ckvT = R([128, 2, 8192], BF16)
    kropeT = R([32, 8192], BF16)
    QT = R([96, 8, NOWN], BF16)
    yaT = R([128, 4, NOWN], BF16)
    reg_p4 = [list(r) for r in al.regions]
    wkv = R([128, 8, 288], BF16)
    wq = R([128, 8, 384], BF16)
    wuq = R([128, 3, 768], BF16)
    gq = R([128, 384], F32)
    gkv = R([128, 256], F32)
    hT4 = [R([128, 8, 128], BF16) for _ in range(2)]
    sinT = R([128, 80, 16], F32)
    cosT = R([128, 80, 16], F32)
    sinq = R([128, 16, 4, 16], F32)
    cosq = R([128, 16, 4, 16], F32)
    reg_tmp = [list(r) for r in al.regions]
    posi = R([128, 80], I32)
    posf = R([128, 80], F32)
    ang = R([128, 80, 16], F32)
    tq = R([128, 1280], F32)
    tk = R([128, 1280], I32)
    tkf = R([128, 1280], F32)

    load_w(wkv, "wkv", w_in, 8, 384, 672)
    load_w(wq, "wq", w_in, 8, 0, 384)
    load_w(wuq, "wuq", w_uq, 3, 0, 768)
    A("sp", lambda e: e.dma_start(out=gq[:], in_=gqd), writes=["gq"], dma=True)
    A("sp", lambda e: e.dma_start(out=gkv[:], in_=gkvd), writes=["gkv"], dma=True)
    A("sp", lambda e: e.dma_start(out=posi[:], in_=posT), writes=["posi"], dma=True)
    A("dve", lambda e: e.tensor_copy(out=posf[:], in_=posi[:]), reads=["posi"], writes=["posf"])
    invf = (np.float32(10000.0) ** (-np.arange(0, 32, 2, dtype=np.float32) / np.float32(32))).astype(np.float32)
    for f in range(16):
        A("dve", lambda e, f=f: e.tensor_scalar(out=ang[:, :, f], in0=posf[:], scalar1=float(invf[f]), scalar2=None, op0=ALU.mult), reads=["posf"], writes=["ang"])
    angf = ang[:].rearrange("p a b -> p (a b)")
    TWO_PI = 2.0 * math.pi
    for (dst, off) in ((sinT, 0.0), (cosT, 0.25)):
        dstf = dst[:].rearrange("p a b -> p (a b)")
        A("dve", lambda e, off=off: e.tensor_scalar(out=tq[:], in0=angf, scalar1=1.0 / TWO_PI, scalar2=off, op0=ALU.mult, op1=ALU.add), reads=["ang"], writes=["tq"])
        A("dve", lambda e: e.tensor_copy(out=tk[:], in_=tq[:]), reads=["tq"], writes=["tk"])
        A("dve", lambda e: e.tensor_copy(out=tkf[:], in_=tk[:]), reads=["tk"], writes=["tkf"])
        A("dve", lambda e: e.tensor_tensor(out=tq[:], in0=tq[:], in1=tkf[:], op=ALU.subtract), reads=["tq", "tkf"], writes=["tq"])
        A("dve", lambda e: e.tensor_scalar(out=tkf[:], in0=tq[:], scalar1=0.5, scalar2=None, op0=ALU.is_gt), reads=["tq"], writes=["tkf"])
        A("dve", lambda e: e.tensor_tensor(out=tq[:], in0=tq[:], in1=tkf[:], op=ALU.subtract), reads=["tq", "tkf"], writes=["tq"])
        A("dve", lambda e: e.tensor_scalar(out=tkf[:], in0=tq[:], scalar1=-0.5, scalar2=None, op0=ALU.is_lt), reads=["tq"], writes=["tkf"])
        A("dve", lambda e: e.tensor_tensor(out=tq[:], in0=tq[:], in1=tkf[:], op=ALU.add), reads=["tq", "tkf"], writes=["tq"])
        A("dve", lambda e: e.tensor_scalar(out=tq[:], in0=tq[:], scalar1=-0.4999, scalar2=0.4999, op0=ALU.max, op1=ALU.min), reads=["tq"], writes=["tq"])
        A("act", lambda e, dstf=dstf: e.activation(out=dstf, in_=tq[:], func=AF.Sin, scale=TWO_PI), reads=["tq"], writes=["sincos"])
    for hh in range(4):
        A("dve", lambda e, hh=hh: e.tensor_copy(out=sinq[:, :, hh, :], in_=sinT[:, 64:80, :]), reads=["sincos"], writes=["sinq"])
        A("dve", lambda e, hh=hh: e.tensor_copy(out=cosq[:, :, hh, :], in_=cosT[:, 64:80, :]), reads=["sincos"], writes=["cosq"])

    if debug == "p4a":
        A("sp", lambda e: e.dma_start(out=dbg[0:128, 0:1024], in_=sinT[:].rearrange("p a b -> p (a b)")[:, 0:1024]), reads=["sincos"], writes=["dbgo"], dma=True)
        A("sp", lambda e: e.nop(), reads=["dbgo"])
        S.emit(nc, st)
        st.close()
        return nc
    S.barrier(lambda e: e.memset(sst[:, 7:8], 0.0))
    al.regions = [list(r) for r in reg_tmp]
    cn = [R([128, 384], BF16) for _ in range(2)]
    krr = [R([128, 32], BF16) for _ in range(2)]
    rt = [R([128, 4, 16], F32) for _ in range(8)]
    cqnT = R([128, 3, 128], BF16)
    qtok = R([128, 8, 96], BF16)
    TPSX, LAT0, TPC, Q0, Q1, TPQ = 0, 1, 3, 4, 5, 6
    lstate = {"i": 0}

    def rope(x1, x2, cs, sn, o1, o2, rkeys, okey, shape4):
        rs_ = lstate["i"] % 2
        lstate["i"] += 1
        ta, tb_, tc, td = [t[:] if shape4 else t[:, 0, :] for t in rt[rs_ * 4:rs_ * 4 + 4]]
        k0, k1, k2, k3 = [("rt", rs_, q_) for q_ in range(4)]
        A("dve", lambda e: e.tensor_tensor(out=ta, in0=x1, in1=cs, op=ALU.mult), reads=rkeys, writes=[k0])
        A("dve", lambda e: e.tensor_tensor(out=tb_, in0=x2, in1=sn, op=ALU.mult), reads=rkeys, writes=[k1])
        A("dve", lambda e: e.tensor_tensor(out=o1, in0=ta, in1=tb_, op=ALU.subtract), reads=[k0, k1], writes=[okey])
        A("dve", lambda e: e.tensor_tensor(out=tc, in0=x2, in1=cs, op=ALU.mult), reads=rkeys, writes=[k2])
        A("dve", lambda e: e.tensor_tensor(out=td, in0=x1, in1=sn, op=ALU.mult), reads=rkeys, writes=[k3])
        A("dve", lambda e: e.tensor_tensor(out=o2, in0=tc, in1=td, op=ALU.add), reads=[k2, k3], writes=[okey])

    def kv_pre(kt):
        z = kt % 2
        src = xo[kt * 128:(kt + 1) * 128, :] if kt < 48 else xown[(kt - 48) * 128:(kt - 47) * 128, :]
        xpipe(src, gmix, "gmix", hT4[z][:], ("hT4", z), TPSX)

    kv_pre(0)
    for kt in range(64):
        z = kt % 2
        if kt + 1 < 64:
            kv_pre(kt + 1)
        lat = LAT0 + z
        for k in range(8):
            A("pe", lambda e, k=k: e.matmul(ps[lat][:, 0:288], lhsT=hT4[z][:, k, :], rhs=wkv[:, k, :], start=(k == 0), stop=(k == 7)),
              reads=[("hT4", z), ("wkv", k)], writes=[("ps", lat)])
        ssap = sst[:, 2 + z:3 + z]
        A("act", lambda e: e.activation(out=junk[:, 0:256], in_=ps[lat][:, 0:256], func=AF.Square, accum_out=ssap), reads=[("ps", lat)], writes=[("ssl", z)])
        rstd_of(ssap, 256, ("ssl", z))
        A("dve", lambda e: e.scalar_tensor_tensor(out=cn[z][:, 0:256], in0=ps[lat][:, 0:256], scalar=ssap, in1=gkv[:], op0=ALU.mult, op1=ALU.mult),
          reads=[("ps", lat), ("ssl", z), "gkv"], writes=[("cn", z)])
        rope(ps[lat][:, 256:272], ps[lat][:, 272:288], cosT[:, kt, :], sinT[:, kt, :], krr[z][:, 0:16], krr[z][:, 16:32],
             [("ps", lat), "sincos"], ("krr", z), False)
        tpc = (3, 7)[z]
        pb = psb(tpc)
        for c in range(2):
            A("pe", lambda e, c=c: e.transpose(out=pb[:, c * 128:(c + 1) * 128], in_=cn[z][:, c * 128:(c + 1) * 128], identity=idb[:]),
              reads=[("cn", z), "idb"], writes=[("ps", tpc)])
        A("pe", lambda e: e.transpose(out=pb[0:32, 256:384], in_=krr[z][:], identity=idb[:]), reads=[("krr", z), "idb"], writes=[("ps", tpc)])
        A("act", lambda e: e.copy(out=ckvT[:, :, kt * 128:(kt + 1) * 128], in_=pb[:, 0:256].rearrange("p (c t) -> p c t", c=2)), reads=[("ps", tpc)], writes=["ckvT"])
        A("act", lambda e: e.copy(out=kropeT[:, kt * 128:(kt + 1) * 128], in_=pb[0:32, 256:384]), reads=[("ps", tpc)], writes=["kropeT"])

    def q_pre(ot):
        z = ot % 2
        xpipe(xown[ot * 128:(ot + 1) * 128, :], gmix, "gmix", hT4[z][:], ("hT4", z), TPSX)

    n_ot = 16 if debug != "p4b" else 0
    if n_ot:
        q_pre(0)
    for ot in range(n_ot):
        z = ot % 2
        if ot + 1 < n_ot:
            q_pre(ot + 1)
        lat = LAT0 + z
        for k in range(8):
            A("pe", lambda e, k=k: e.matmul(ps[lat][:, 0:384], lhsT=hT4[z][:, k, :], rhs=wq[:, k, :], start=(k == 0), stop=(k == 7)),
              reads=[("hT4", z), ("wq", k)], writes=[("ps", lat)])
        ssap = sst[:, 2 + z:3 + z]
        A("act", lambda e: e.activation(out=junk[:, 0:384], in_=ps[lat][:, 0:384], func=AF.Square, accum_out=ssap), reads=[("ps", lat)], writes=[("ssl", z)])
        rstd_of(ssap, 384, ("ssl", z))
        A("dve", lambda e: e.scalar_tensor_tensor(out=cn[z][:], in0=ps[lat][:, 0:384], scalar=ssap, in1=gq[:], op0=ALU.mult, op1=ALU.mult),
          reads=[("ps", lat), ("ssl", z), "gq"], writes=[("cn", z)])
        pb = psb(TPC)
        for c in range(3):
            A("pe", lambda e, c=c: e.transpose(out=pb[:, c * 128:(c + 1) * 128], in_=cn[z][:, c * 128:(c + 1) * 128], identity=idb[:]),
              reads=[("cn", z), "idb"], writes=[("ps", TPC)])
        A("act", lambda e: e.copy(out=cqnT[:], in_=pb[:, 0:384].rearrange("p (c t) -> p c t", c=3)), reads=[("ps", TPC)], writes=["cqnT"])
        qlvl = int(debug[3:]) if (debug and debug.startswith("p4q")) else 9
        if qlvl < 2:
            continue
        for x_ in range(2):
            qb = Q0 + x_
            for c in range(3):
                A("pe", lambda e, c=c: e.matmul(ps[qb][:, 0:384], lhsT=cqnT[:, c, :], rhs=wuq[:, c, x_ * 384:(x_ + 1) * 384], start=(c == 0), stop=(c == 2)),
                  reads=["cqnT", ("wuq", c)], writes=[("ps", qb)])
            V4 = ps[qb][:, 0:384].rearrange("p (h d) -> p h d", h=4)
            A("act", lambda e: e.copy(out=qtok[:, x_ * 4:x_ * 4 + 4, 0:64], in_=V4[:, :, 0:64]), reads=[("ps", qb)], writes=["qtokn"])
            if qlvl < 3:
                continue
            for hh in range(4):
                c0 = hh * 96
                rope(ps[qb][:, c0 + 64:c0 + 80], ps[qb][:, c0 + 80:c0 + 96], cosT[:, 64 + ot, :], sinT[:, 64 + ot, :],
                     qtok[:, x_ * 4 + hh, 64:80], qtok[:, x_ * 4 + hh, 80:96], [("ps", qb), "sincos"], "qtokr", False)
        if qlvl < 4:
            continue
        pq = psb(TPQ)
        for h in range(8):
            A("pe", lambda e, h=h: e.transpose(out=pq[0:96, h * 128:(h + 1) * 128], in_=qtok[:, h, :], identity=idb[:]),
              reads=["qtokn", "qtokr", "idb"], writes=[("ps", TPQ)])
        A("act", lambda e: e.copy(out=QT[:, :, ot * 128:(ot + 1) * 128], in_=pq[0:96, :].rearrange("p (h t) -> p h t", h=8)), reads=[("ps", TPQ)], writes=["QT"])

    if debug and debug.startswith("p4"):
        dtt = R([128, 1024], F32)
        for h in range(8):
            for half in range(2):
                A("dve", lambda e, h=h, half=half: e.tensor_copy(out=dtt[0:96, :], in_=QT[:, h, half * 1024:(half + 1) * 1024]), reads=["QT"], writes=["dtt"])
                r0 = (h * 2 + half) * 96
                A("sp", lambda e, r0=r0: e.dma_start(out=dbg[r0:r0 + 96, :], in_=dtt[0:96, :]), reads=["dtt"], writes=["dbgo"], dma=True)
        A("dve", lambda e: e.tensor_copy(out=dtt[0:32, :], in_=kropeT[:, 7168:8192]), reads=["kropeT"], writes=["dtt"])
        A("sp", lambda e: e.dma_start(out=dbg[1536:1568, :], in_=dtt[0:32, :]), reads=["dtt"], writes=["dbgo"], dma=True)
        A("dve", lambda e: e.tensor_copy(out=dtt[:], in_=ckvT[:, 1, 7168:8192]), reads=["ckvT"], writes=["dtt"])
        A("sp", lambda e: e.dma_start(out=dbg[1664:1792, :], in_=dtt[:]), reads=["dtt"], writes=["dbgo"], dma=True)
        A("sp", lambda e: e.nop(), reads=["dbgo"])
        S.emit(nc, st)
        st.close()
        return nc
    S.barrier(lambda e: e.memset(sst[:, 7:8], 0.0))
    al.regions = [list(r) for r in reg_p4]
    wkp = R([128, 2, 8, 96], BF16)
    wv = R([128, 2, 512], BF16)
    selb = R([32, 96], BF16)
    KT0 = R([96, 8192], BF16)
    KT = [KT0, KT0]
    VA = [R([128, 64, 65], BF16) for _ in range(2)]
    yattn = R([128, 16, 512], BF16)
    rden = R([128, 4], F32)
    A("pool", lambda e: e.memset(wkp[:], 0.0), writes=["wkp"])
    A("dve", lambda e: e.tensor_copy(out=selb[:], in_=sel), reads=["cst"], writes=["selb"])
    for b_ in range(2):
        A("pool", lambda e, b_=b_: e.memset(VA[b_][:, :, 64:65], 1.0), writes=[("VA1", b_)])
    for c in range(2):
        sl = wstate["i"] % 2
        wstate["i"] += 1
        A("sp", lambda e, c=c, sl=sl: e.dma_start(out=wst[sl][:], in_=w_ukv[c * 128:(c + 1) * 128, :]), writes=[("wst", sl)], dma=True)
        W3 = wst[sl][:].rearrange("p (h d) -> p h d", h=8)
        A("pool", lambda e, c=c: e.tensor_copy(out=wkp[:, c, :, 0:64], in_=W3[:, :, 0:64]), reads=[("wst", sl), "wkp"], writes=["wkp"])
        A("pool", lambda e, c=c: e.tensor_copy(out=wv[:, c, :].rearrange("p (h d) -> p h d", h=8), in_=W3[:, :, 64:128]), reads=[("wst", sl)], writes=["wv"])

    OB_, KB_, VB_ = (6, 7), (0, 1), 2
    SCALE = 96.0 ** -0.5
    Pb2 = [R([128, 1024], BF16) for _ in range(3)]
    kbs = {"i": 0}

    def v_build(h, tg):
        bf_ = h % 2
        vb = 4 + (tg % 2)
        for tl in range(8):
            kt = tg * 8 + tl
            for c in range(2):
                A("pe", lambda e, c=c, tl=tl, kt=kt: e.matmul(ps[vb][:, tl * 64:(tl + 1) * 64], lhsT=ckvT[:, c, kt * 128:(kt + 1) * 128],
                                                             rhs=wv[:, c, h * 64:(h + 1) * 64], start=(c == 0), stop=(c == 1)),
                  reads=["ckvT", "wv"], writes=[("ps", vb)])
        A("dve", lambda e: e.tensor_copy(out=VA[bf_][:, tg * 8:(tg + 1) * 8, 0:64], in_=ps[vb][:].rearrange("p (t d) -> p t d", t=8)),
          reads=[("ps", vb)], writes=[("VA", bf_)])

    for tg in range(8):
        v_build(0, tg)
    for h in range(8):
        bf = h % 2
        for grp in range(16):
            kb = KB_[kbs["i"] % 2]
            kbs["i"] += 1
            gs = slice(grp * 512, (grp + 1) * 512)
            A("pe", lambda e: e.matmul(ps[kb][0:96, :], lhsT=wkp[:, 0, h, :], rhs=ckvT[:, 0, gs], start=True, stop=False), reads=["wkp", "ckvT"], writes=[("ps", kb)])
            A("pe", lambda e: e.matmul(ps[kb][0:96, :], lhsT=wkp[:, 1, h, :], rhs=ckvT[:, 1, gs], start=False, stop=False), reads=["wkp", "ckvT"], writes=[("ps", kb)])
            A("pe", lambda e: e.matmul(ps[kb][0:96, :], lhsT=selb[:], rhs=kropeT[:, gs], start=False, stop=True), reads=["selb", "kropeT"], writes=[("ps", kb)])
            A("dve", lambda e: e.tensor_copy(out=KT[bf][:, gs], in_=ps[kb][0:96, :]), reads=[("ps", kb)], writes=[("KT", 0)])
        for tt in range(4):
            ob = OB_[(h * 4 + tt) % 2]
            qs = slice(tt * 512, (tt + 1) * 512)

            def s_mm(sp_):
                j = sp_ % 2
                pj = sp_ % 3
                for half in range(2):
                    st_ = 2 * sp_ + half
                    A("pe", lambda e: e.matmul(ps[2 * j + half][:], lhsT=KT[bf][:, st_ * 128:(st_ + 1) * 128], rhs=QT[:, h, qs], start=True, stop=True),
                      reads=[("KT", 0), "QT"], writes=[("ps", 2 * j + half)])
                A("act", lambda e: e.activation(out=Pb2[pj][:], in_=psbig[j][:], func=AF.Exp, scale=SCALE),
                  reads=[("ps", 2 * j), ("ps", 2 * j + 1)], writes=[("Pb", pj)])

            s_mm(0)
            s_mm(1)
            for sp_ in range(32):
                pj = sp_ % 3
                for half in range(2):
                    st_ = 2 * sp_ + half
                    for qq in range(4):
                        A("pe", lambda e, qq=qq: e.matmul(ps[ob][:, qq * 65:(qq + 1) * 65], lhsT=Pb2[pj][:, half * 512 + qq * 128:half * 512 + (qq + 1) * 128],
                                                          rhs=VA[bf][:, st_, :], start=(st_ == 0 and qq == 0), stop=(st_ == 63), skip_group_check=True),
                          reads=[("Pb", pj), ("VA", bf), ("VA1", bf)], writes=[("ps", ob)])
                if sp_ + 2 < 32:
                    s_mm(sp_ + 2)
                step = tt * 32 + sp_
                if h + 1 < 8 and step % 16 == 8:
                    v_build(h + 1, step // 16)
            O3 = ps[ob][:, 0:260].rearrange("p (q d) -> p q d", q=4)
            A("dve", lambda e: e.reciprocal(out=rden[:], in_=O3[:, :, 64]), reads=[("ps", ob)], writes=["rden"])
            for qq in range(4):
                A("dve", lambda e, qq=qq: e.tensor_scalar(out=yattn[:, tt * 4 + qq, h * 64:(h + 1) * 64], in0=O3[:, qq, 0:64], scalar1=rden[:, qq:qq + 1],
                                                          scalar2=None, op0=ALU.mult), reads=[("ps", ob), "rden"], writes=["yattn"])
    for tile in range(16):
        pb = psb(VB_)
        for c in range(4):
            A("pe", lambda e, c=c: e.transpose(out=pb[:, c * 128:(c + 1) * 128], in_=yattn[:, tile, c * 128:(c + 1) * 128], identity=idb[:]),
              reads=["yattn", "idb"], writes=[("ps", VB_)])
        A("act", lambda e: e.copy(out=yaT[:, :, tile * 128:(tile + 1) * 128], in_=pb[:, 0:512].rearrange("p (c t) -> p c t", c=4)), reads=[("ps", VB_)], writes=["yaT"])

    S.barrier(lambda e: e.memset(sst[:, 7:8], 0.0))
    al.regions = [[m_phase, m_alias], [reg_p4[1][0], SB_HI]]
    wgab = R([128, 8, 2048], BF16)
    wbm = R([128, 4, 1024], BF16)
    wbl = R([128, 4, 1024], BF16)
    wo = R([128, 8, 1024], BF16)
    hT6 = [R([128, 8, 128], BF16) for _ in range(2)]
    sg = R([128, 2048], BF16)
    t1 = R([128, 1024], F32)
    t2 = R([128, 1024], F32)
    mrg = R([128, 1024], BF16)
    mT = R([128, 8, 128], BF16)
    x1t = [R([128, 1024], F32) for _ in range(2)]
    load_w(wgab, "wgab", w_in, 8, 2736, 4784)
    load_w(wbm, "wbm", w_bm, 4, 0, 1024)
    load_w(wbl, "wbl", w_bl, 4, 0, 1024)
    load_w(wo, "wo", w_out, 8, 0, 1024)
    def m_pre(ot):
        z = ot % 2
        return xpipe(xown[ot * 128:(ot + 1) * 128, :], gmix, "gmix", hT6[z][:], ("hT6", z), 7)

    sl_next = m_pre(0)
    for ot in range(16):
        z = ot % 2
        tsl = slice(ot * 128, (ot + 1) * 128)
        sl = sl_next
        if ot + 1 < 16:
            sl_next = m_pre(ot + 1)
        for cb_ in range(4):
            for k in range(8):
                A("pe", lambda e, k=k, cb_=cb_: e.matmul(ps[cb_][:], lhsT=hT6[z][:, k, :], rhs=wgab[:, k, cb_ * 512:(cb_ + 1) * 512], start=(k == 0), stop=(k == 7)),
                  reads=[("hT6", z), ("wgab", k)], writes=[("ps", cb_)])
            A("act", lambda e, cb_=cb_: e.activation(out=sg[:, cb_ * 512:(cb_ + 1) * 512], in_=ps[cb_][:], func=AF.Sigmoid), reads=[("ps", cb_)], writes=[("sg", cb_)])
        for half in range(2):
            for (wt, wk, aT_, ak, bank0) in ((wbm, "wbm", yaT, "yaT", 4), (wbl, "wbl", ymT, "ymT", 5)):
                bank = bank0
                for c in range(4):
                    A("pe", lambda e, c=c, wt=wt, aT_=aT_, bank=bank: e.matmul(ps[bank][:], lhsT=aT_[:, c, tsl], rhs=wt[:, c, half * 512:(half + 1) * 512],
                                                                               start=(c == 0), stop=(c == 3)), reads=[ak, (wk, c)], writes=[("ps", bank)])
            hs = slice(half * 512, (half + 1) * 512)
            A("dve", lambda e: e.tensor_tensor(out=t1[:, hs], in0=ps[4][:], in1=sg[:, half * 512:(half + 1) * 512], op=ALU.mult), reads=[("ps", 4), ("sg", half)], writes=[("t1", half)])
            A("dve", lambda e: e.tensor_tensor(out=t2[:, hs], in0=ps[5][:], in1=sg[:, 1024 + half * 512:1024 + (half + 1) * 512], op=ALU.mult),
              reads=[("ps", 5), ("sg", 2 + half)], writes=[("t2", half)])
            A("pool", lambda e: e.tensor_tensor(out=mrg[:, hs], in0=t1[:, hs], in1=t2[:, hs], op=ALU.add), reads=[("t1", half), ("t2", half)], writes=[("mrg", half)])
        pb = psb(6)
        for k in range(8):
            A("pe", lambda e, k=k: e.transpose(out=pb[:, k * 128:(k + 1) * 128], in_=mrg[:, k * 128:(k + 1) * 128], identity=idb[:]),
              reads=[("mrg", k // 4), "idb"], writes=[("ps", 6)])
        A("act", lambda e: e.copy(out=mT[:], in_=pb.rearrange("p (k t) -> p k t", k=8)), reads=[("ps", 6)], writes=["mT"])
        for half in range(2):
            bank = 4 + half
            for k in range(8):
                A("pe", lambda e, k=k, bank=bank, half=half: e.matmul(ps[bank][:], lhsT=mT[:, k, :], rhs=wo[:, k, half * 512:(half + 1) * 512], start=(k == 0), stop=(k == 7)),
                  reads=["mT", ("wo", k)], writes=[("ps", bank)])
            A("dve", lambda e, bank=bank, half=half: e.tensor_tensor(out=x1t[z][:, half * 512:(half + 1) * 512], in0=ps[bank][:], in1=xt[sl][:, half * 512:(half + 1) * 512], op=ALU.add),
              reads=[("ps", bank), ("xt", sl)], writes=[("x1t", z)])
        A("sp", lambda e: e.dma_start(out=x1d[tsl, :], in_=x1t[z][:]), reads=[("x1t", z)], writes=["x1d"], dma=True)

    S.barrier(lambda e: e.memset(sst[:, 7:8], 0.0))
    al.regions = [[m_phase, SB_HI]]
    wup = R([128, 8, 4096], BF16)
    wdn = R([128, 32, 1024], BF16)
    gmlp = R([128, 1024], F32)
    gfin = R([128, 1024], F32)
    hTm = [R([128, 8, 256], BF16) for _ in range(2)]
    aT = R([128, 32, 256], BF16)
    rr = [R([128, 256], F32) for _ in range(2)]
    xres = [R([128, 1024], F32) for _ in range(2)]
    otile = xres
    A("sp", lambda e: e.dma_start(out=gmlp[:], in_=gmlpd), writes=["gmlp"], dma=True)
    A("sp", lambda e: e.dma_start(out=gfin[:], in_=gfind), writes=["gfin"], dma=True)
    load_w(wup, "wup", w_up, 8, 0, 4096)
    load_w(wdn, "wdn", w_down, 32, 0, 1024)
    def f_pre(g):
        z = g % 2
        for i in range(2):
            r0 = g * 256 + i * 128
            xpipe(x1d[r0:r0 + 128, :], gmlp, "gmlp", hTm[z][:, :, i * 128:(i + 1) * 128], ("hTm", z), 7)

    f_pre(0)
    for g in range(8):
        z = g % 2
        for f in range(32):
            bank = f % 2
            for k in range(8):
                A("pe", lambda e, k=k, f=f, bank=bank: e.matmul(ps[bank][:, 0:256], lhsT=wup[:, k, f * 128:(f + 1) * 128], rhs=hTm[z][:, k, :], start=(k == 0), stop=(k == 7)),
                  reads=[("hTm", z), ("wup", k)], writes=[("ps", bank)])
            A("act", lambda e, bank=bank: e.activation(out=rr[bank][:], in_=ps[bank][:, 0:256], func=AF.Relu), reads=[("ps", bank)], writes=[("rr", bank)])
            A("dve", lambda e, f=f, bank=bank: e.tensor_tensor(out=aT[:, f, :], in0=rr[bank][:], in1=rr[bank][:], op=ALU.mult), reads=[("rr", bank)], writes=["aT"])
        if g + 1 < 8:
            f_pre(g + 1)
        for i in range(2):
            r0 = g * 256 + i * 128
            zz = (g * 2 + i) % 2
            A("sp", lambda e, r0=r0, zz=zz: e.dma_start(out=xres[zz][:], in_=x1d[r0:r0 + 128, :]), reads=["x1d"], writes=[("xres", zz)], dma=True)
            for half in range(2):
                bank = 2 + half
                for f in range(32):
                    A("pe", lambda e, f=f, bank=bank, half=half, i=i: e.matmul(ps[bank][:], lhsT=aT[:, f, i * 128:(i + 1) * 128], rhs=wdn[:, f, half * 512:(half + 1) * 512],
                                                                               start=(f == 0), stop=(f == 31)), reads=["aT", ("wdn", f)], writes=[("ps", bank)])
                A("dve", lambda e, bank=bank, half=half, zz=zz: e.tensor_tensor(out=xres[zz][:, half * 512:(half + 1) * 512], in0=ps[bank][:], in1=xres[zz][:, half * 512:(half + 1) * 512], op=ALU.add),
                  reads=[("ps", bank), ("xres", zz)], writes=[("xres", zz)])
            ssap = sst[:, 4 + zz:5 + zz]
            A("act", lambda e, zz=zz, ssap=ssap: e.activation(out=junk[:], in_=xres[zz][:], func=AF.Square, accum_out=ssap), reads=[("xres", zz)], writes=[("ssf", zz)])
            rstd_of(ssap, 1024, ("ssf", zz))
            A("dve", lambda e, zz=zz, ssap=ssap: e.scalar_tensor_tensor(out=otile[zz][:], in0=xres[zz][:], scalar=ssap, in1=gfin[:], op0=ALU.mult, op1=ALU.mult),
              reads=[("xres", zz), ("ssf", zz), "gfin"], writes=[("xres", zz)])
            A("sp", lambda e, r0=r0, zz=zz: e.dma_start(out=y[r0:r0 + 128, :], in_=otile[zz][:]), reads=[("xres", zz)], writes=[("yout", g * 2 + i)], dma=True)
    A("sp", lambda e: e.nop(), reads=[("yout", i_) for i_ in range(16)])
    S.emit(nc, st)
    st.close()
    return nc


def make_consts():
    c = np.zeros((128, 880), np.float32)
    r = np.arange(128)
    c[:, 0:128] = np.eye(128)
    c[:, 128:256] = (r[:, None] <= r[None, :])
    c[:, 256:384] = (r[:, None] >= r[None, :])
    c[:, 384:512] = (r[:, None] > r[None, :])
    c[:, 512:640] = (r[:, None] < r[None, :])
    c[:, 640:768] = 1.0
    for i in range(32):
        c[i, 784 + 64 + i] = 1.0
    return c


def bc(v, n=128):
    return np.ascontiguousarray(np.broadcast_to(np.asarray(v, np.float32).reshape(1, -1), (n, np.asarray(v).size)))


def prep_inputs(inp, core):
    b, j = divmod(core, 4)
    x = inp["x"][b]
    pos = inp["positions"][b]
    o0, o1 = NOWN * j, NOWN * (j + 1)
    xo = np.concatenate([x[:o0], x[o1:]], axis=0)
    xown = x[o0:o1]
    xh = np.zeros((128, 1024), np.float32)

    def row(n):
        return x[n] if 0 <= n < 8192 else np.zeros(1024, np.float32)

    for g in range(12):
        n0 = 512 * g if 512 * g < o0 else 512 * g + NOWN
        for q, n in enumerate((n0 - 2, n0 - 1, n0 + 512, n0 + 513)):
            xh[4 * g + q] = row(n)
    for g in range(4):
        n0 = o0 + 512 * g
        for q, n in enumerate((n0 - 2, n0 - 1, n0 + 512, n0 + 513)):
            xh[48 + 4 * g + q] = row(n)
    pos_all = np.concatenate([pos[:o0], pos[o1:], pos[o0:o1]])
    posT = np.concatenate([pos_all.reshape(64, 128).T, pos[o0:o1].reshape(16, 128).T], axis=1).astype(np.int32)
    tfv = (np.arange(48) < 16 * j).astype(np.float32)
    igb = inp["mlstm_igate_b"][0]
    fgb = inp["mlstm_fgate_b"][0]
    gb16 = np.concatenate([igb[0], igb[1], fgb[0], fgb[1]])
    cwv = inp["mlstm_conv_w"][0][:, 0, :]
    cw = np.ascontiguousarray(cwv.reshape(5, 8, 128).transpose(2, 1, 0)).reshape(128, 40)
    cb = np.ascontiguousarray(inp["mlstm_conv_b"][0].reshape(8, 128).T)
    d = {
        "xo": np.ascontiguousarray(xo), "xown": np.ascontiguousarray(xown), "xh": xh,
        "posT": np.ascontiguousarray(posT), "tf": bc(tfv), "cst": make_consts(),
        "gmix": bc(inp["norm_mix_g"][0]), "gmlp": bc(inp["norm_mlp_g"][0]), "gfin": bc(inp["norm_final_g"]),
        "gq": bc(inp["mla_q_norm_g"][0]), "gkv": bc(inp["mla_kv_norm_g"][0]), "gon": bc(inp["mlstm_out_norm_g"][0]),
        "gb": bc(np.tile(gb16, 4)), "cw": cw.astype(np.float32), "cb": cb.astype(np.float32),
        "w_in": inp["w_in"][0], "w_uq": inp["mla_w_uq"][0], "w_ukv": inp["mla_w_ukv"][0],
        "w_bm": inp["w_branch_mla"][0], "w_bl": inp["w_branch_mlstm"][0], "w_out": inp["w_out"][0],
        "w_up": inp["w_mlp_up"][0], "w_down": inp["w_mlp_down"][0],
    }
    return {k: np.ascontiguousarray(v) for k, v in d.items()}


def run(inputs, debug=None, cores=8):
    inp = {k: np.asarray(v) for k, v in inputs.items()}
    nc = build_program(debug)
    in_maps = [prep_inputs(inp, c) for c in range(cores)]
    res = run_bass_kernel_spmd(nc, in_maps, core_ids=list(range(cores)))
    return res


def kernel(**inputs):
    res = run(inputs)
    out = np.zeros((2, 8192, 1024), np.float32)
    for c in range(8):
        b, j = divmod(c, 4)
        out[b, NOWN * j:NOWN * (j + 1)] = res.results[c]["y"]
    return out
```
